# Optimizing a Trainium2 kernel written in Bass

```python
import math
import jax, jax.numpy as jnp
from jax import lax
import numpy as np

D_MODEL = 1024
BATCH = 8
SEQ = 2048
DEPTH = 2
DEC_BATCH = 16
DEC_SEQ = 4096
PAST_LEN = 128

EPS = 1e-6
ROPE_THETA = 500000.0
HY_WIDTH = 1024
HY_SHORT = 3
HY_EMB = 33
HY_BANDS = (HY_EMB - 1) // 2
HY_FILT_HIDDEN = 64
HY_TARGET = 1e-2
HY_FAST_PCT = 0.3
HY_SLOW_PCT = 1.5
GDN_HEADS = 8
GDN_HEAD_DIM = 128
GDN_WIDTH = GDN_HEADS * GDN_HEAD_DIM
GDN_SHORT = 5
GDN_CHUNK = 64
DIL_PATTERNS = ((128, 1), (512, 4), (2048, 16))
N_DIL = len(DIL_PATTERNS)
DIL_HEADS = 4
DIL_HEAD_DIM = 128
DIL_WIDTH = DIL_HEADS * DIL_HEAD_DIM
DIL_BLOCK = 64
SWA_Q_HEADS = 16
SWA_KV_HEADS = 2
SWA_HEAD_DIM = 64
SWA_WIDTH = SWA_Q_HEADS * SWA_HEAD_DIM
SWA_HALF_WINDOW = 128
SWA_BLOCK = 128
MEM_TOKENS = 256
X_HEADS = 4
X_HEAD_DIM = 128
X_WIDTH = X_HEADS * X_HEAD_DIM

EVEN_IN_WIDTHS = (3 * HY_WIDTH, HY_WIDTH, 3 * GDN_WIDTH, GDN_WIDTH, 2 * GDN_HEADS, 2 * GDN_HEADS, X_WIDTH, X_WIDTH)
EVEN_IN = sum(EVEN_IN_WIDTHS)
EVEN_OUT = HY_WIDTH + GDN_WIDTH + X_WIDTH
ODD_IN_WIDTHS = (3 * N_DIL * DIL_WIDTH, DIL_WIDTH, SWA_WIDTH, 2 * SWA_KV_HEADS * SWA_HEAD_DIM, SWA_WIDTH, X_WIDTH, X_WIDTH)
ODD_IN = sum(ODD_IN_WIDTHS)
ODD_OUT = DIL_WIDTH + SWA_WIDTH + X_WIDTH

kernel_name = 'bidir_hybrid_encoder'


def rms_norm(x, g):
    xf = x.astype(jnp.float32)
    y = xf * lax.rsqrt(jnp.mean(xf * xf, axis=-1, keepdims=True) + EPS)
    return (y * g.astype(jnp.float32)).astype(x.dtype)


def l2_normalize(x):
    return x * lax.rsqrt(jnp.sum(x * x, axis=-1, keepdims=True) + EPS)


def split_cols(z, widths):
    cuts = [int(c) for c in np.cumsum(widths)[:-1]]
    return jnp.split(z, cuts, axis=-1)


def centred_depthwise_conv(x, w):
    k_width, seq = w.shape[0], x.shape[1]
    pad = k_width // 2
    xp = jnp.pad(x, ((0, 0), (pad, pad), (0, 0)))
    y = xp[:, 0:seq] * w[0]
    for j in range(1, k_width):
        y = y + xp[:, j:j + seq] * w[j]
    return y


def partial_rope(x):
    seq, dh = x.shape[1], x.shape[-1]
    rd = dh // 4
    half = rd // 2
    inv = ROPE_THETA ** (-jnp.arange(half, dtype=jnp.float32) / half)
    ang = jnp.arange(seq, dtype=jnp.float32)[:, None] * inv[None, :]
    shape = (1, seq) + (1,) * (x.ndim - 3) + (half,)
    cos = jnp.cos(ang).reshape(shape)
    sin = jnp.sin(ang).reshape(shape)
    xf = x.astype(jnp.float32)
    x1, x2, rest = xf[..., :half], xf[..., half:rd], xf[..., rd:]
    return jnp.concatenate([x1 * cos - x2 * sin, x2 * cos + x1 * sin, rest], axis=-1).astype(x.dtype)


def hyena_filters(seq, w1, b1, w2, b2, w3, freq):
    t = jnp.linspace(0.0, 1.0, seq, dtype=jnp.float32)[:, None]
    w = (2.0 * math.pi / seq) * jnp.arange(seq, dtype=jnp.float32)[:, None]
    f = jnp.linspace(1e-4, HY_BANDS - 1, HY_BANDS, dtype=jnp.float32)[None, :]
    emb = jnp.concatenate([t, jnp.cos(f * w), -jnp.sin(f * w)], axis=-1)
    freq = freq.astype(jnp.float32)
    hid = jnp.sin(freq * (emb @ w1.astype(jnp.float32) + b1.astype(jnp.float32)))
    hid = jnp.sin(freq * (hid @ w2.astype(jnp.float32) + b2.astype(jnp.float32)))
    filt = hid @ w3.astype(jnp.float32)
    deltas = jnp.abs(jnp.linspace(math.log(HY_TARGET) / HY_SLOW_PCT, math.log(HY_TARGET) / HY_FAST_PCT, HY_WIDTH, dtype=jnp.float32))
    decay = jnp.exp(-t * jnp.tile(deltas, 2)[None, :])
    filt = filt * decay
    return filt[:, :HY_WIDTH], filt[:, HY_WIDTH:]


def bidir_long_conv(u, h_fwd, h_bwd, skip):
    seq, ch = u.shape[1], u.shape[2]
    n_fft = 2 * seq
    filt = jnp.concatenate([h_fwd, jnp.zeros((1, ch), jnp.float32), h_bwd[1:][::-1]], axis=0)
    filt_f = jnp.fft.rfft(filt, n=n_fft, axis=0)
    u_f = jnp.fft.rfft(u.astype(jnp.float32), n=n_fft, axis=1)
    y = jnp.fft.irfft(u_f * filt_f[None], n=n_fft, axis=1)[:, :seq]
    return (y + u.astype(jnp.float32) * skip.astype(jnp.float32)).astype(u.dtype)


def gated_delta_chunked(q, k, v, g, beta):
    bsz, heads, seq, dk = k.shape
    dv = v.shape[-1]
    c = GDN_CHUNK
    n = seq // c
    q = q.reshape(bsz, heads, n, c, dk)
    k = k.reshape(bsz, heads, n, c, dk)
    v = v.reshape(bsz, heads, n, c, dv)
    g = jnp.cumsum(g.reshape(bsz, heads, n, c), axis=-1)
    beta = beta.reshape(bsz, heads, n, c)
    kb = k * beta[..., None]
    vb = v * beta[..., None]
    lower = jnp.tril(jnp.ones((c, c), dtype=bool))
    strict = jnp.tril(jnp.ones((c, c), dtype=bool), -1)
    decay = jnp.exp(jnp.where(lower, g[..., :, None] - g[..., None, :], -jnp.inf))
    a = jnp.where(strict, jnp.einsum('bhnid,bhnjd->bhnij', kb, k) * decay, 0.0)
    eye = jnp.eye(c, dtype=jnp.float32)
    t_inv = lax.linalg.triangular_solve(a + eye, jnp.broadcast_to(eye, a.shape), left_side=True, lower=True, unit_diagonal=True)
    u = jnp.einsum('bhnij,bhnjd->bhnid', t_inv, vb)
    w = jnp.einsum('bhnij,bhnjd->bhnid', t_inv, kb * jnp.exp(g)[..., None])
    qk = jnp.where(lower, jnp.einsum('bhnid,bhnjd->bhnij', q, k) * decay, 0.0)
    qg = q * jnp.exp(g)[..., None]
    kd = k * jnp.exp(g[..., -1:] - g)[..., None]
    g_last = jnp.exp(g[..., -1])

    def step(state, xs):
        qk_i, qg_i, kd_i, u_i, w_i, gl_i = xs
        v_new = u_i - jnp.einsum('bhik,bhkv->bhiv', w_i, state)
        o_i = jnp.einsum('bhik,bhkv->bhiv', qg_i, state) + jnp.einsum('bhij,bhjv->bhiv', qk_i, v_new)
        state = state * gl_i[..., None, None] + jnp.einsum('bhik,bhiv->bhkv', kd_i, v_new)
        return state, o_i

    xs = tuple(jnp.moveaxis(z, 2, 0) for z in (qk, qg, kd, u, w, g_last))
    s0 = jnp.zeros((bsz, heads, dk, dv), jnp.float32)
    _, o = lax.scan(step, s0, xs)
    return jnp.moveaxis(o, 0, 2).reshape(bsz, heads, seq, dv)


def banded_attention(q, k, v, half_window, block, sink=None):
    nbat, seq, hk, grp, dh = q.shape
    nb = -(-seq // block)
    lp = nb * block
    width = block + 2 * half_window
    qb = jnp.pad(q, ((0, 0), (0, lp - seq), (0, 0), (0, 0), (0, 0))).reshape(nbat, nb, block, hk, grp, dh)
    pad_kv = ((0, 0), (half_window, lp - seq + half_window), (0, 0), (0, 0))
    kp = jnp.pad(k, pad_kv)
    vp = jnp.pad(v, pad_kv)
    idx = jnp.arange(nb)[:, None] * block + jnp.arange(width)[None, :]
    kb = kp[:, idx]
    vb = vp[:, idx]
    q_pos = jnp.arange(lp).reshape(nb, block)
    k_pos = (idx - half_window)[:, None, :]
    valid = (jnp.abs(q_pos[:, :, None] - k_pos) <= half_window) & (k_pos >= 0) & (k_pos < seq)
    s = jnp.einsum('nbqhgd,nbkhd->nbhgqk', qb, kb, preferred_element_type=jnp.float32) * (dh ** -0.5)
    s = jnp.where(valid[None, :, None, None], s, -jnp.inf)
    m = jnp.max(s, axis=-1)
    if sink is not None:
        sk = sink.astype(jnp.float32)[None, None, :, :, None]
        m = jnp.maximum(m, sk)
    p = jnp.exp(s - m[..., None])
    den = jnp.sum(p, axis=-1)
    if sink is not None:
        den = den + jnp.exp(sk - m)
    o = jnp.einsum('nbhgqk,nbkhd->nbqhgd', p, vb.astype(jnp.float32)) / jnp.moveaxis(den, -1, 2)[..., None]
    o = o.reshape(nbat, lp, hk, grp, dh)[:, :seq].astype(q.dtype)
    lse = jnp.moveaxis(m + jnp.log(den), -1, 2).reshape(nbat, lp, hk, grp)[:, :seq]
    return o, lse


def dilated_attention(q, k, v, dilation, half_span):
    bsz, seq, heads, dh = q.shape
    d = dilation
    ls = seq // d

    def to_res(t):
        return jnp.swapaxes(t.reshape(bsz, ls, d, heads, dh), 1, 2).reshape(bsz * d, ls, heads, dh)

    o, lse = banded_attention(to_res(q)[:, :, :, None], to_res(k), to_res(v), half_span, DIL_BLOCK)
    o = jnp.swapaxes(o[:, :, :, 0].reshape(bsz, d, ls, heads, dh), 1, 2).reshape(bsz, seq, heads, dh)
    lse = jnp.swapaxes(lse[..., 0].reshape(bsz, d, ls, heads), 1, 2).reshape(bsz, seq, heads)
    return o, lse


def memory_cross_attention(z_q, mem, mem_g, w_mem_kv):
    bsz, seq = z_q.shape[0], z_q.shape[1]
    n_mem = mem.shape[1]
    q = z_q.reshape(bsz, seq, X_HEADS, X_HEAD_DIM)
    kv = (rms_norm(mem, mem_g) @ w_mem_kv).reshape(bsz, n_mem, 2, X_HEADS, X_HEAD_DIM)
    s = jnp.einsum('blhd,bmhd->bhlm', q, kv[:, :, 0], preferred_element_type=jnp.float32) * (X_HEAD_DIM ** -0.5)
    p = jax.nn.softmax(s, axis=-1)
    o = jnp.einsum('bhlm,bmhd->blhd', p, kv[:, :, 1].astype(jnp.float32))
    return o.reshape(bsz, seq, X_WIDTH).astype(z_q.dtype)


def even_layer_mixer(h, mem, w_in, w_out, hy_conv_w, hy_conv_b, hy_filt_w1, hy_filt_b1, hy_filt_w2, hy_filt_b2,
                     hy_filt_w3, hy_freq, hy_skip, gdn_conv_w, gdn_A_log, gdn_dt_bias, gdn_norm_g, mem_g, w_mem_kv):
    bsz, seq, _ = h.shape
    z = h @ w_in
    z_hy, g_hy, z_qkv, g_gdn, z_beta, z_a, z_xq, g_x = split_cols(z, EVEN_IN_WIDTHS)
    uc = centred_depthwise_conv(z_hy, hy_conv_w) + hy_conv_b
    x0, x1, v_hy = jnp.split(uc, 3, axis=-1)
    h_fwd, h_bwd = hyena_filters(seq, hy_filt_w1, hy_filt_b1, hy_filt_w2, hy_filt_b2, hy_filt_w3, hy_freq)
    y_a = x0 * bidir_long_conv(v_hy * x1, h_fwd, h_bwd, hy_skip)
    y_a = y_a * jax.nn.silu(g_hy)
    qkv = jax.nn.silu(centred_depthwise_conv(z_qkv, gdn_conv_w)).astype(jnp.float32)
    qkv = qkv.reshape(bsz, seq, 3, GDN_HEADS, GDN_HEAD_DIM)
    q = l2_normalize(qkv[:, :, 0]) * (GDN_HEAD_DIM ** -0.5)
    k = l2_normalize(qkv[:, :, 1])
    v = qkv[:, :, 2]
    beta = jax.nn.sigmoid(z_beta.astype(jnp.float32)).reshape(bsz, seq, 2, GDN_HEADS)
    g = -jnp.exp(gdn_A_log.astype(jnp.float32)) * jax.nn.softplus(
        z_a.astype(jnp.float32).reshape(bsz, seq, 2, GDN_HEADS) + gdn_dt_bias.astype(jnp.float32))
    qh, kh, vh = jnp.moveaxis(q, 1, 2), jnp.moveaxis(k, 1, 2), jnp.moveaxis(v, 1, 2)
    gh, bh = jnp.moveaxis(g, 1, 3), jnp.moveaxis(beta, 1, 3)
    o_f = gated_delta_chunked(qh, kh, vh, gh[:, 0], bh[:, 0])
    o_b = jnp.flip(gated_delta_chunked(jnp.flip(qh, 2), jnp.flip(kh, 2), jnp.flip(vh, 2),
                                       jnp.flip(gh[:, 1], 2), jnp.flip(bh[:, 1], 2)), 2)
    o = jnp.moveaxis(o_f + o_b, 1, 2)
    y_b = rms_norm(o, gdn_norm_g).reshape(bsz, seq, GDN_WIDTH).astype(h.dtype) * jax.nn.silu(g_gdn)
    y_x = memory_cross_attention(z_xq, mem, mem_g, w_mem_kv) * jax.nn.silu(g_x)
    return jnp.concatenate([y_a, y_b, y_x], axis=-1) @ w_out


def odd_layer_mixer(h, mem, w_in, w_out, swa_sink, mem_g, w_mem_kv):
    bsz, seq, _ = h.shape
    z = h @ w_in
    z_cqkv, g_c, z_dq, z_dkv, g_d, z_xq, g_x = split_cols(z, ODD_IN_WIDTHS)
    cqkv = z_cqkv.reshape(bsz, seq, 3, N_DIL, DIL_HEADS, DIL_HEAD_DIM)
    cq = partial_rope(cqkv[:, :, 0])
    ck = partial_rope(cqkv[:, :, 1])
    cv = cqkv[:, :, 2]
    outs, lses = [], []
    for gi, (window, dilation) in enumerate(DIL_PATTERNS):
        o_g, lse_g = dilated_attention(cq[:, :, gi], ck[:, :, gi], cv[:, :, gi], dilation, window // (2 * dilation))
        outs.append(o_g)
        lses.append(lse_g)
    wts = jax.nn.softmax(jnp.stack(lses, axis=0), axis=0)
    y_c = jnp.sum(wts[..., None] * jnp.stack(outs, axis=0).astype(jnp.float32), axis=0)
    y_c = y_c.reshape(bsz, seq, DIL_WIDTH).astype(h.dtype) * jax.nn.silu(g_c)
    dq = partial_rope(z_dq.reshape(bsz, seq, SWA_Q_HEADS, SWA_HEAD_DIM))
    dq = dq.reshape(bsz, seq, SWA_KV_HEADS, SWA_Q_HEADS // SWA_KV_HEADS, SWA_HEAD_DIM)
    dkv = z_dkv.reshape(bsz, seq, 2, SWA_KV_HEADS, SWA_HEAD_DIM)
    dk = partial_rope(dkv[:, :, 0])
    dv = dkv[:, :, 1]
    y_d, _ = banded_attention(dq, dk, dv, SWA_HALF_WINDOW, SWA_BLOCK,
                              sink=swa_sink.reshape(SWA_KV_HEADS, SWA_Q_HEADS // SWA_KV_HEADS))
    y_d = y_d.reshape(bsz, seq, SWA_WIDTH) * jax.nn.silu(g_d)
    y_x = memory_cross_attention(z_xq, mem, mem_g, w_mem_kv) * jax.nn.silu(g_x)
    return jnp.concatenate([y_c, y_d, y_x], axis=-1) @ w_out


def encoder_trunk(x, mem, even_params, odd_params):
    for layer in range(DEPTH):
        i = layer // 2
        if layer % 2 == 0:
            pre_g, post_g, *mix = [p[i] for p in even_params]
            out = even_layer_mixer(rms_norm(x, pre_g), mem, *mix)
        else:
            pre_g, post_g, *mix = [p[i] for p in odd_params]
            out = odd_layer_mixer(rms_norm(x, pre_g), mem, *mix)
        x = x + rms_norm(out, post_g)
    return x


def setup_inputs(seed: int = 0) -> dict:
    key = jax.random.key(seed)
    keys = list(jax.random.split(key, 32))
    nk = keys.pop
    n_even = (DEPTH + 1) // 2
    n_odd = DEPTH // 2
    d = D_MODEL

    def normal(shape, scale=1.0):
        return scale * jax.random.normal(nk(), shape, jnp.float32)

    def gain(shape):
        return 1.0 + normal(shape, 0.02)

    dt = jnp.exp(jax.random.uniform(nk(), (n_even, 2, GDN_HEADS), jnp.float32, math.log(1e-3), math.log(1e-1)))
    a_log = jnp.log(jax.random.uniform(nk(), (n_even, 2, GDN_HEADS), jnp.float32, 1.0, 16.0))
    return {
        'x_prompt': normal((BATCH, SEQ, d)),
        'x_sample': normal((DEC_BATCH, DEC_SEQ, d)),
        'mem_prompt': normal((BATCH, MEM_TOKENS, d)),
        'mem_sample': normal((DEC_BATCH, MEM_TOKENS, d)),
        'e_pre_g': gain((n_even, d)),
        'e_post_g': gain((n_even, d)),
        'e_w_in': normal((n_even, d, EVEN_IN), d ** -0.5),
        'e_w_out': normal((n_even, EVEN_OUT, d), EVEN_OUT ** -0.5),
        'hy_conv_w': normal((n_even, HY_SHORT, 3 * HY_WIDTH), HY_SHORT ** -0.5),
        'hy_conv_b': normal((n_even, 3 * HY_WIDTH), 0.02),
        'hy_filt_w1': normal((n_even, HY_EMB, HY_FILT_HIDDEN), HY_EMB ** -0.5),
        'hy_filt_b1': normal((n_even, HY_FILT_HIDDEN), 0.1),
        'hy_filt_w2': normal((n_even, HY_FILT_HIDDEN, HY_FILT_HIDDEN), HY_FILT_HIDDEN ** -0.5),
        'hy_filt_b2': normal((n_even, HY_FILT_HIDDEN), 0.1),
        'hy_filt_w3': normal((n_even, HY_FILT_HIDDEN, 2 * HY_WIDTH), HY_FILT_HIDDEN ** -0.5),
        'hy_freq': gain((n_even, HY_FILT_HIDDEN)),
        'hy_skip': normal((n_even, HY_WIDTH)),
        'gdn_conv_w': normal((n_even, GDN_SHORT, 3 * GDN_WIDTH), GDN_SHORT ** -0.5),
        'gdn_A_log': a_log,
        'gdn_dt_bias': dt + jnp.log(-jnp.expm1(-dt)),
        'gdn_norm_g': gain((n_even, GDN_HEAD_DIM)),
        'e_mem_g': gain((n_even, d)),
        'e_w_mem_kv': normal((n_even, d, 2 * X_WIDTH), d ** -0.5),
        'o_pre_g': gain((n_odd, d)),
        'o_post_g': gain((n_odd, d)),
        'o_w_in': normal((n_odd, d, ODD_IN), d ** -0.5),
        'o_w_out': normal((n_odd, ODD_OUT, d), ODD_OUT ** -0.5),
        'swa_sink': normal((n_odd, SWA_Q_HEADS), 0.5),
        'o_mem_g': gain((n_odd, d)),
        'o_w_mem_kv': normal((n_odd, d, 2 * X_WIDTH), d ** -0.5),
    }


def reference(x_prompt, x_sample, mem_prompt, mem_sample, e_pre_g, e_post_g, e_w_in, e_w_out, hy_conv_w, hy_conv_b,
              hy_filt_w1, hy_filt_b1, hy_filt_w2, hy_filt_b2, hy_filt_w3, hy_freq, hy_skip, gdn_conv_w, gdn_A_log,
              gdn_dt_bias, gdn_norm_g, e_mem_g, e_w_mem_kv, o_pre_g, o_post_g, o_w_in, o_w_out, swa_sink, o_mem_g,
              o_w_mem_kv):
    even_params = (e_pre_g, e_post_g, e_w_in, e_w_out, hy_conv_w, hy_conv_b, hy_filt_w1, hy_filt_b1, hy_filt_w2,
                   hy_filt_b2, hy_filt_w3, hy_freq, hy_skip, gdn_conv_w, gdn_A_log, gdn_dt_bias, gdn_norm_g,
                   e_mem_g, e_w_mem_kv)
    odd_params = (o_pre_g, o_post_g, o_w_in, o_w_out, swa_sink, o_mem_g, o_w_mem_kv)
    y_prompt = encoder_trunk(x_prompt, mem_prompt, even_params, odd_params)
    y_sample = encoder_trunk(x_sample, mem_sample, even_params, odd_params)
    return (y_prompt, y_sample)
```

```python
import math
from contextlib import ExitStack

import numpy as np
import ml_dtypes

import concourse.bass as bass
import concourse.mybir as mybir
from concourse.bass_utils import run_bass_kernel_spmd

F32 = mybir.dt.float32
BF16 = mybir.dt.bfloat16
F32R = mybir.dt.float32r
AF = mybir.ActivationFunctionType
ALU = mybir.AluOpType
AX = mybir.AxisListType

D = 1024
EPS = 1e-6
NEG = -30000.0
ROPE_THETA = 500000.0
MEM_TOKENS = 256
EVEN_IN = 9248
ODD_IN = 8448
E_HY, E_GHY, E_QKV, E_GG, E_BETA, E_A, E_XQ, E_GX = 0, 3072, 4096, 7168, 8192, 8208, 8224, 8736
O_CQKV, O_GC, O_DQ, O_DKV, O_GD, O_XQ, O_GX = 0, 4608, 5120, 6144, 6400, 7424, 7936
DIL = (1, 4, 16)


class Tok:
    __slots__ = ("w", "r")

    def __init__(self):
        self.w = None
        self.r = {}


class Sch:
    RING = 8

    def __init__(self, nc, es):
        self.nc = nc
        self.eng = {"pe": nc.tensor, "act": nc.scalar, "dve": nc.vector, "pool": nc.gpsimd, "sp": nc.sync}
        self.sem = {}
        self.cnt = {}
        self.known = {}
        for e in self.eng:
            self.sem[e] = es.enter_context(nc.semaphore("s_" + e))
            self.cnt[e] = 0
            self.known[e] = {}
        self.dcnt = {}
        for q in ("sp", "act", "pool"):
            self.dcnt[q] = 0
            for i in range(self.RING):
                key = ("d", q, i)
                self.sem[key] = es.enter_context(nc.semaphore("d_%s_%d" % (q, i)))
                self.cnt[key] = 0
        self.ninst = 0

    def _wait(self, e, deps):
        need = {}
        for key, val in deps:
            if key == e and e == "pe":
                continue
            if self.known[e].get(key, 0) >= val:
                continue
            if need.get(key, 0) < val:
                need[key] = val
        for key, val in need.items():
            self.eng[e].wait_ge(self.sem[key], val)
            self.known[e][key] = val

    @staticmethod
    def _deps(r, w):
        deps = []
        for t in r:
            if t.w is not None:
                deps.append(t.w)
        for t in w:
            if t.w is not None:
                deps.append(t.w)
            deps.extend(t.r.items())
        return deps

    @staticmethod
    def _mark(me, r, w):
        key, val = me
        for t in r:
            if t.r.get(key, 0) < val:
                t.r[key] = val
        for t in w:
            t.w = me
            t.r = {}

    def op(self, e, fn, r=(), w=()):
        self._wait(e, self._deps(r, w))
        inst = fn(self.eng[e])
        self.cnt[e] += 1
        inst.then_inc(self.sem[e], 1)
        self._mark((e, self.cnt[e]), r, w)
        self.ninst += 1

    def dma(self, q, out, in_, r=(), w=()):
        i = self.dcnt[q]
        self.dcnt[q] += 1
        slot = i % self.RING
        key = ("d", q, slot)
        val = 16 * (i // self.RING + 1)
        deps = self._deps(r, w)
        if val > 16:
            deps.append((key, val - 16))
        self._wait(q, deps)
        self.eng[q].dma_start(out=out, in_=in_).then_inc(self.sem[key], 16)
        self.cnt[key] = val
        self._mark((key, val), r, w)
        self.ninst += 1

    def barrier(self):
        allv = [(k, v) for k, v in self.cnt.items() if v > 0]
        for e in self.eng:
            self._wait(e, allv)

    def finish(self):
        allv = [(k, v) for k, v in self.cnt.items() if v > 0]
        self._wait("sp", allv)


def rope_tables(L, half):
    inv = ROPE_THETA ** (-np.arange(half, dtype=np.float32) / half)
    ang = np.arange(L, dtype=np.float32)[:, None] * inv[None, :]
    return np.cos(ang).astype(np.float32), np.sin(ang).astype(np.float32)


def tok_layout(a):
    L, Fd = a.shape
    return np.ascontiguousarray(a.reshape(L // 128, 128, Fd).transpose(1, 0, 2))


def band_mask(lo_off, hi_off):
    i = np.arange(128)[:, None]
    c = np.arange(384)[None, :]
    ok = (c >= i + lo_off) & (c <= i + hi_off)
    return np.where(ok, 0.0, NEG).astype(np.float32)


class KB:
    def __init__(self, seq_lens, dbg=False):
        self.seq_lens = list(seq_lens)
        self.NS = len(seq_lens)
        self.NTOK = sum(seq_lens)
        self.LMAX = max(seq_lens)
        self.dbg = dbg
        self.nc = bass.Bass("TRN2", target_bir_lowering=False)
        self.consts = {}

    def din(self, name, shape, dt=F32):
        return self.nc.dram_tensor(name, list(shape), dt, kind="ExternalInput").ap()

    def dscr(self, name, shape, dt=F32, out=False):
        kind = "ExternalOutput" if (out and self.dbg) else "Internal"
        return self.nc.dram_tensor(name, list(shape), dt, kind=kind).ap()

    def sb(self, es, name, shape, dt):
        self.uid = getattr(self, "uid", 0) + 1
        return es.enter_context(self.nc.sbuf_tensor("%s_%d" % (name, self.uid), list(shape), dt))

    def build(self):
        nc = self.nc
        NTOK, NS = self.NTOK, self.NS
        self.x = self.din("x", [NTOK, D])
        self.mem = self.din("mem", [NS * MEM_TOKENS, D])
        self.y = nc.dram_tensor("y", [NTOK, D], F32, kind="ExternalOutput").ap()
        self.x1 = self.dscr("x1", [NTOK, D])
        self.w = {}
        for nm, shp in (("e_w_in", [D, EVEN_IN]), ("e_w_out", [2560, D]), ("e_w_mem_kv", [D, 1024]),
                        ("o_w_in", [D, ODD_IN]), ("o_w_out", [2048, D]), ("o_w_mem_kv", [D, 1024])):
            self.w[nm] = self.din(nm, shp)
        for nm in ("e_pre_g", "e_mem_g", "o_pre_g", "o_mem_g"):
            self.w[nm] = self.din(nm, [128, 8])
        for nm in ("e_post_g", "o_post_g"):
            self.w[nm] = self.din(nm, [1, D])
        self.w["swa_sink"] = self.din("swa_sink", [1, 16])
        self.c_ident = self.din("c_ident", [128, 128])
        self.c_mask_dil = self.din("c_mask_dil", [128, 384])
        self.c_mask_swa = self.din("c_mask_swa", [128, 384])
        self.c_rope = {}
        for L in sorted(set(self.seq_lens)):
            T = L // 128
            self.c_rope[L] = (self.din("c_cos16_%d" % L, [128, T, 16]), self.din("c_sin16_%d" % L, [128, T, 16]),
                              self.din("c_cos8_%d" % L, [128, T, 8]), self.din("c_sin8_%d" % L, [128, T, 8]))
        self.YT = self.dscr("YT", [20 * 128, self.LMAX], BF16, out=True)
        self.OG = self.dscr("OG", [3, self.LMAX, 512], F32)
        TM = self.LMAX // 128
        for nm, shp in (("hy_filt_w1", [33, 64]), ("hy_filt_w2", [64, 64]), ("hy_filt_w3", [64, 2048]), ("hy_fvec", [64, 3]),
                        ("hy_conv_w", [128, 24, 3]), ("hy_conv_b", [128, 24]), ("hy_skip", [128, 8])):
            self.w[nm] = self.din(nm, shp)
        self.c_delta = self.din("c_delta", [1, 1024])
        self.c_dft, self.c_filt, self.H = {}, {}, {}
        for L in sorted(set(self.seq_lens)):
            T = L // 128
            self.c_dft[L] = (self.din("c_gc_%d" % L, [T, 128, T, 128], BF16), self.din("c_gs_%d" % L, [T, 128, T, 128], BF16))
            self.c_filt[L] = {"embT": self.din("c_embT_%d" % L, [33, L]), "tcol": self.din("c_tcol_%d" % L, [128, T]),
                              "psi": self.din("c_psi_%d" % L, [128, 2, T])}
            self.H[L] = (self.dscr("HR_%d" % L, [T, 128, 1024]), self.dscr("HI_%d" % L, [T, 128, 1024]))
        self.CFSF = (self.dscr("CF", [TM, 128, 1024]), self.dscr("SF", [TM, 128, 1024]))
        self.ZRS = (self.dscr("ZR", [TM, 128, 1024], BF16), self.dscr("ZS", [TM, 128, 1024], BF16))
        self.UT = self.dscr("UT", [1024, self.LMAX], BF16)
        self.AT = self.dscr("AT", [1024, self.LMAX], BF16)
        self.BT = self.dscr("BT", [1024, self.LMAX], BF16)
        self.t_cf, self.t_H, self.t_uab, self.t_Z = Tok(), Tok(), Tok(), Tok()
        for nm, shp in (("gdn_conv_w", [128, 24, 5]), ("gdn_A_log", [1, 16]), ("gdn_dt_bias", [1, 16]), ("gdn_norm_g", [1, 128])):
            self.w[nm] = self.din(nm, shp)
        self.c_gmask = self.din("c_gmask", [128, 18, 128])
        self.QT = self.dscr("QT", [1024, self.LMAX])
        self.KT = self.dscr("KT", [1024, self.LMAX])
        self.KTOK = self.dscr("KTOK", [self.LMAX, 1024])
        self.VTOK = self.dscr("VTOK", [self.LMAX, 1024])
        self.BG = self.dscr("BG", [self.LMAX, 32])
        self.SGG = self.dscr("SGG", [self.LMAX, 1024], BF16)
        self.OFB = self.dscr("OFB", [2, self.LMAX, 1024])
        self.G_UB = self.dscr("G_UB", [2, TM, 128, 1024])
        self.G_KD = self.dscr("G_KD", [2, TM, 128, 1024])
        self.G_WT = self.dscr("G_WT", [2, TM, 128, 8, 128])
        self.G_QKM = self.dscr("G_QKM", [2, TM, 128, 8, 128])
        self.G_SCC = self.dscr("G_SCC", [2, TM, 64, 2, 8])
        self.G_GLB = self.dscr("G_GLB", [2, TM, 128, 8, 2])
        self.t_gd, self.t_of, self.t_gp = Tok(), Tok(), Tok()
        self.LSE = self.dscr("LSE", [3, self.LMAX, 4], F32)

        with ExitStack() as es:
            self.es = es
            s = self.s = Sch(nc, es)
            self.ident32 = self.sb(es, "ident32", [128, 128], F32)
            self.ident16 = self.sb(es, "ident16", [128, 128], BF16)
            self.t_const = Tok()
            s.dma("sp", self.ident32[:], self.c_ident, w=[self.t_const])
            s.dma("pool", self.ident16[:], self.c_ident, w=[self.t_const])
            self.mask_dil = self.sb(es, "mask_dil", [128, 384], F32)
            self.mask_swa = self.sb(es, "mask_swa", [128, 384], F32)
            self.eps_col = self.sb(es, "eps_col", [128, 2], F32)
            s.op("dve", lambda e: e.memset(self.eps_col[:, 0:1], EPS), w=[self.t_const])
            s.op("dve", lambda e: e.memset(self.eps_col[:, 1:2], 1.0), w=[self.t_const])
            s.dma("sp", self.mask_dil[:], self.c_mask_dil, w=[self.t_const])
            s.dma("sp", self.mask_swa[:], self.c_mask_swa, w=[self.t_const])
            self.ps = es.enter_context(nc.psum_tensor("ps", [128, 4096], F32))
            self.ps_tok = [Tok() for _ in range(8)]
            self.ps_rr = 0
            self.t_og_dram = Tok()
            s.barrier()

            if getattr(self, "en_even", [1, 1, 1])[0]:
                for L in sorted(set(self.seq_lens)):
                    self.phase_filter(L)
            r0 = 0
            for si, L in enumerate(self.seq_lens):
                self.run_seq(si, r0, L)
                r0 += L
            s.finish()
        return nc

    def bank(self, b):
        return self.ps[:, b * 512:(b + 1) * 512]

    def bank16(self, b):
        return self.ps[:, b * 512:(b + 1) * 512].bitcast(BF16)

    def run_seq(self, si, r0, L):
        T = L // 128
        for layer in range(2):
            src = self.x if layer == 0 else self.x1
            dst = self.x1 if layer == 0 else self.y
            with ExitStack() as les:
                hT = self.sb(les, "hT", [128, 8, L], BF16)
                hT_tok = [Tok() for _ in range(T)]
                pre = "e_" if layer == 0 else "o_"
                self.phase_norm(lambda t: src[r0 + t * 128:r0 + (t + 1) * 128, :], T, self.w[pre + "pre_g"], hT, hT_tok)
                if layer == 0:
                    nch = 20
                    en = getattr(self, "en_even", [1, 1, 1])
                    if en[1]:
                        self.phase_gdn1(si, L, hT, hT_tok)
                    if en[2]:
                        self.phase_xattn(si, L, hT, hT_tok, "e_", E_XQ, E_GX, 16)
                    if en[0]:
                        self.phase_hy1(si, L, hT, hT_tok)
                    if not all(en):
                        self.zero_YT(L, [c for c in range(20) if not en[0 if c < 8 else (1 if c < 16 else 2)]])
                else:
                    nch = 16
                    en = getattr(self, "en_odd", [1, 1, 1])
                    if en[0]:
                        self.phase_dil(si, L, hT, hT_tok)
                    if en[1]:
                        self.phase_swa(si, L, hT, hT_tok)
                    if en[2]:
                        self.phase_xattn(si, L, hT, hT_tok, "o_", O_XQ, O_GX, 12)
                    if not all(en):
                        self.zero_YT(L, [c for c in range(16) if not en[0 if c < 4 else (1 if c < 12 else 2)]])
                self.s.barrier()
            if layer == 0 and getattr(self, "en_even", [1, 1, 1])[1]:
                self.phase_gdn2(si, L)
            if layer == 0 and getattr(self, "en_even", [1, 1, 1])[0]:
                self.phase_hy2(si, L)
            self.phase_out(r0, L, src, dst, pre, nch)

    def phase_norm(self, src, T, g_dram, hT, hT_tok, tag="n"):
        nc, s = self.nc, self.s
        with ExitStack() as es:
            gcol = self.sb(es, tag + "gcol", [128, 8], F32)
            gfull = self.sb(es, tag + "gfull", [128, 8, 128], F32)
            t_g = Tok()
            s.dma("sp", gcol[:], g_dram, w=[t_g])
            for k in range(8):
                s.op("dve", lambda e, k=k: e.tensor_scalar(out=gfull[:, k, :], in0=self.ident32[:], scalar1=0.0,
                                                             scalar2=gcol[:, k:k + 1], op0=ALU.mult, op1=ALU.add),
                     r=[t_g, self.t_const], w=[t_g])
            xt = [self.sb(es, tag + "xt%d" % i, [128, D], F32) for i in range(2)]
            xn = [self.sb(es, tag + "xn%d" % i, [128, D], BF16) for i in range(2)]
            st = [self.sb(es, tag + "st%d" % i, [128, 4], F32) for i in range(2)]
            junk = self.sb(es, tag + "junk", [128, D], BF16)
            t_xt = [Tok(), Tok()]
            t_xn = [Tok(), Tok()]
            t_st = [Tok(), Tok()]
            t_junk = Tok()
            for t in range(T):
                b = t % 2
                s.dma("sp", xt[b][:], src(t), w=[t_xt[b]])
                s.op("act", lambda e: e.activation(out=junk[:], in_=xt[b][:], func=AF.Square, accum_out=st[b][:, 0:1]),
                     r=[t_xt[b]], w=[t_junk, t_st[b]])
                s.op("dve", lambda e: e.tensor_scalar(out=st[b][:, 1:2], in0=st[b][:, 0:1], scalar1=1.0 / D, scalar2=EPS,
                                                      op0=ALU.mult, op1=ALU.add), r=[t_st[b]], w=[t_st[b]])
                s.op("act", lambda e: e.activation(out=st[b][:, 2:3], in_=st[b][:, 1:2], func=AF.Sqrt), r=[t_st[b]], w=[t_st[b]])
                s.op("dve", lambda e: e.reciprocal(out=st[b][:, 3:4], in_=st[b][:, 2:3]), r=[t_st[b]], w=[t_st[b]])
                s.op("act", lambda e: e.activation(out=xn[b][:], in_=xt[b][:], func=AF.Copy, scale=st[b][:, 3:4]),
                     r=[t_xt[b], t_st[b]], w=[t_xn[b]])
                pb = self.next_bank()
                pv = self.bank16(pb).rearrange("p (k c) -> p k c", k=8)
                for k in range(8):
                    s.op("pe", lambda e, k=k: e.transpose(out=pv[:, k, :], in_=xn[b][:, k * 128:(k + 1) * 128],
                                                          identity=self.ident16[:]),
                         r=[t_xn[b], self.t_const], w=[self.ps_tok[pb]])
                s.op("dve", lambda e: e.tensor_tensor(out=hT[:, :, t * 128:(t + 1) * 128], in0=pv, in1=gfull[:],
                                                      op=ALU.mult), r=[self.ps_tok[pb], t_g], w=[hT_tok[t]])
            s.barrier()

    def next_bank(self, n=1):
        b = self.ps_rr
        if n == 2 and b % 2:
            b += 1
        if b + n > 8:
            b = 0
        self.ps_rr = (b + n) % 8
        return b

    def pick(self, cls, banks):
        rr = self.__dict__.setdefault("_rr", {})
        i = rr.get(cls, 0)
        rr[cls] = i + 1
        return banks[i % len(banks)]

    def rope(self, src, t_src, dst, t_dst, tmp, t_tmp, cos, sin, t_tab, H, half, dh):
        s = self.s
        A = src[:, :, 0:half]
        B = src[:, :, half:2 * half]
        cb = cos.unsqueeze(1).to_broadcast([128, H, half])
        sb_ = sin.unsqueeze(1).to_broadcast([128, H, half])
        s.op("dve", lambda e: e.tensor_tensor(out=tmp[:, 0, 0:H, :], in0=A, in1=cb, op=ALU.mult), r=[t_src, t_tab], w=[t_tmp])
        s.op("dve", lambda e: e.tensor_tensor(out=tmp[:, 1, 0:H, :], in0=B, in1=sb_, op=ALU.mult), r=[t_src, t_tab], w=[t_tmp])
        s.op("dve", lambda e: e.tensor_tensor(out=tmp[:, 2, 0:H, :], in0=B, in1=cb, op=ALU.mult), r=[t_src, t_tab], w=[t_tmp])
        s.op("dve", lambda e: e.tensor_tensor(out=tmp[:, 3, 0:H, :], in0=A, in1=sb_, op=ALU.mult), r=[t_src, t_tab], w=[t_tmp])
        s.op("dve", lambda e: e.tensor_tensor(out=dst[:, :, 0:half], in0=tmp[:, 0, 0:H, :], in1=tmp[:, 1, 0:H, :], op=ALU.subtract),
             r=[t_tmp], w=[t_dst])
        s.op("dve", lambda e: e.tensor_tensor(out=dst[:, :, half:2 * half], in0=tmp[:, 2, 0:H, :], in1=tmp[:, 3, 0:H, :], op=ALU.add),
             r=[t_tmp], w=[t_dst])
        s.op("act", lambda e: e.activation(out=dst[:, :, 2 * half:dh], in_=src[:, :, 2 * half:dh], func=AF.Copy),
             r=[t_src], w=[t_dst])

    def softmax_block(self, sbanks, nh, nk, mask_ap, t_mask, Sm, t_Sm, Pe, t_Pe, st, t_st, sink_ap=None, t_sink=None):
        s = self.s
        for j in range(nh):
            s.op("dve", lambda e, j=j: e.tensor_tensor(out=Sm[:, j, 0:nk], in0=self.bank(sbanks[j])[:, 0:nk], in1=mask_ap, op=ALU.add),
                 r=[self.ps_tok[sbanks[j]], t_mask], w=[t_Sm])
        if sink_ap is None:
            s.op("dve", lambda e: e.tensor_reduce(out=st[:, 0:nh], in_=Sm[:, 0:nh, 0:nk], op=ALU.max, axis=AX.X, negate=True),
                 r=[t_Sm], w=[t_st])
        else:
            s.op("dve", lambda e: e.tensor_reduce(out=st[:, 2 * nh:3 * nh], in_=Sm[:, 0:nh, 0:nk], op=ALU.max, axis=AX.X),
                 r=[t_Sm], w=[t_st])
            s.op("dve", lambda e: e.tensor_tensor(out=st[:, 2 * nh:3 * nh], in0=st[:, 2 * nh:3 * nh], in1=sink_ap, op=ALU.max),
                 r=[t_st, t_sink], w=[t_st])
            s.op("dve", lambda e: e.tensor_scalar(out=st[:, 0:nh], in0=st[:, 2 * nh:3 * nh], scalar1=-1.0, scalar2=None, op0=ALU.mult),
                 r=[t_st], w=[t_st])
            s.op("dve", lambda e: e.tensor_tensor(out=st[:, 3 * nh:4 * nh], in0=sink_ap, in1=st[:, 0:nh], op=ALU.add),
                 r=[t_st, t_sink], w=[t_st])
            s.op("act", lambda e: e.activation(out=st[:, 3 * nh:4 * nh], in_=st[:, 3 * nh:4 * nh], func=AF.Exp), r=[t_st], w=[t_st])
        for j in range(nh):
            s.op("act", lambda e, j=j: e.activation(out=Pe[:, j, 0:nk], in_=Sm[:, j, 0:nk], func=AF.Exp, bias=st[:, j:j + 1],
                                                    accum_out=st[:, nh + j:nh + j + 1]),
                 r=[t_Sm, t_st], w=[t_Pe, t_st])
        if sink_ap is not None:
            s.op("dve", lambda e: e.tensor_tensor(out=st[:, nh:2 * nh], in0=st[:, nh:2 * nh], in1=st[:, 3 * nh:4 * nh], op=ALU.add),
                 r=[t_st], w=[t_st])

    def transpose_P(self, Pe, t_Pe, nh, nkt, PT, t_PT, ptb):
        s = self.s
        ptv = self.bank16(ptb).rearrange("p (j c) -> p j c", j=8)
        n = 0
        for j in range(nh):
            for kc in range(nkt):
                s.op("pe", lambda e, j=j, kc=kc, n=n: e.transpose(out=ptv[:, n, :], in_=Pe[:, j, kc * 128:(kc + 1) * 128],
                                                                  identity=self.ident16[:]),
                     r=[t_Pe, self.t_const], w=[self.ps_tok[ptb]])
                n += 1
        s.op("act", lambda e: e.activation(out=PT[:, 0:n, :], in_=ptv[:, 0:n, :], func=AF.Copy), r=[self.ps_tok[ptb]], w=[t_PT])

    def load_w(self, wt, t_w, w_dram, c0, n):
        self.s.dma("pool", wt[:, :, 0:n], w_dram[:, c0:c0 + n].rearrange("(k p) c -> p k c", p=128), w=[t_w])

    def proj_feat(self, out_ap, pb, wt, t_w, wc0, hT, hT_tok, tok0, ntok, extra_r=()):
        s = self.s
        toks = [hT_tok[t] for t in range(tok0 // 128, (tok0 + ntok + 127) // 128)]
        for k in range(8):
            s.op("pe", lambda e, k=k: e.matmul(out_ap, wt[:, k, wc0:wc0 + 128], hT[:, k, tok0:tok0 + ntok],
                                               start=(k == 0), stop=(k == 7)),
                 r=[t_w] + toks + list(extra_r), w=[self.ps_tok[pb]])

    def proj_tok(self, out_ap, pb, wt, t_w, wc0, ncol, hT, hT_tok_list, tok_ap_fn):
        s = self.s
        for k in range(8):
            s.op("pe", lambda e, k=k: e.matmul(out_ap, tok_ap_fn(k), wt[:, k, wc0:wc0 + ncol],
                                               start=(k == 0), stop=(k == 7)),
                 r=[t_w] + list(hT_tok_list), w=[self.ps_tok[pb]])

    def phase_xattn(self, si, L, hT, hT_tok, pre, off_q, off_g, ch0):
        nc, s = self.nc, self.s
        T = L // 128
        w_in = self.w[pre + "w_in"]
        with ExitStack() as es:
            memT = self.sb(es, "memT", [128, 8, MEM_TOKENS], BF16)
            memT_tok = [Tok(), Tok()]
            self.phase_norm(lambda t: self.mem[si * MEM_TOKENS + t * 128: si * MEM_TOKENS + (t + 1) * 128, :], 2,
                            self.w[pre + "mem_g"], memT, memT_tok, tag="m")
            wkv = self.sb(es, "wkv", [128, 8, 1024], BF16)
            t_wkv = Tok()
            self.load_w(wkv, t_wkv, self.w[pre + "w_mem_kv"], 0, 1024)
            KmT = self.sb(es, "KmT", [128, 4, MEM_TOKENS], BF16)
            Vm = self.sb(es, "Vm", [128, 2, 512], BF16)
            t_km, t_vm = Tok(), Tok()
            for h in range(4):
                pb = self.next_bank()
                self.proj_feat(self.bank(pb)[:, 0:256], pb, wkv, t_wkv, h * 128, memT, memT_tok, 0, 256)
                s.op("act", lambda e: e.activation(out=KmT[:, h, :], in_=self.bank(pb)[:, 0:256], func=AF.Copy),
                     r=[self.ps_tok[pb]], w=[t_km])
            for mt in range(2):
                pb = self.next_bank()
                self.proj_tok(self.bank(pb), pb, wkv, t_wkv, 512, 512, memT, memT_tok,
                              lambda k: memT[:, k, mt * 128:(mt + 1) * 128])
                s.op("act", lambda e: e.activation(out=Vm[:, mt, :], in_=self.bank(pb), func=AF.Copy),
                     r=[self.ps_tok[pb]], w=[t_vm])
            wq = self.sb(es, "wq", [128, 8, 512], BF16)
            wg = self.sb(es, "wg", [128, 8, 512], BF16)
            t_wq, t_wg = Tok(), Tok()
            self.load_w(wq, t_wq, w_in, off_q, 512)
            self.load_w(wg, t_wg, w_in, off_g, 512)
            qT = self.sb(es, "qT", [128, 4, 512], BF16)
            t_qT = Tok()
            sg = [self.sb(es, "sg%d" % i, [128, 512], F32) for i in range(2)]
            t_sg = [Tok(), Tok()]
            Sm = [self.sb(es, "Sm%d" % i, [128, 4, 256], F32) for i in range(2)]
            Pe = [self.sb(es, "Pe%d" % i, [128, 4, 256], BF16) for i in range(2)]
            t_Pe = [Tok(), Tok()]
            stt = [self.sb(es, "stt%d" % i, [128, 16], F32) for i in range(2)]
            t_stt = [Tok(), Tok()]
            PT = [self.sb(es, "PT%d" % i, [128, 8, 128], BF16) for i in range(2)]
            t_PT = [Tok(), Tok()]
            yx = [self.sb(es, "yx%d" % i, [128, 512], BF16) for i in range(2)]
            t_yx = [Tok(), Tok()]
            ystage = [self.sb(es, "ystage%d" % i, [128, 4, 512], BF16) for i in range(2)]
            t_ys = [Tok(), Tok()]
            scale = 128.0 ** -0.5
            nblk = (L + 511) // 512
            for blk in range(nblk):
                tok0 = blk * 512
                ntok = min(512, L - tok0)
                yb = blk % 2
                for h in range(4):
                    pb = self.next_bank()
                    self.proj_feat(self.bank(pb)[:, 0:ntok], pb, wq, t_wq, h * 128, hT, hT_tok, tok0, ntok)
                    s.op("act", lambda e: e.activation(out=qT[:, h, 0:ntok], in_=self.bank(pb)[:, 0:ntok], func=AF.Copy,
                                                       scale=scale), r=[self.ps_tok[pb]], w=[t_qT])
                for tl in range(ntok // 128):
                    t = blk * 4 + tl
                    b = t % 2
                    pg = self.next_bank()
                    self.proj_tok(self.bank(pg), pg, wg, t_wg, 0, 512, hT, [hT_tok[t]],
                                  lambda k: hT[:, k, t * 128:(t + 1) * 128])
                    s.op("act", lambda e: e.activation(out=sg[b][:], in_=self.bank(pg), func=AF.Silu),
                         r=[self.ps_tok[pg]], w=[t_sg[b]])
                    p2 = self.next_bank(2)
                    sv = self.ps[:, p2 * 512:(p2 + 2) * 512].rearrange("p (h m) -> p h m", h=4)
                    for h in range(4):
                        pbh = p2 + h // 2
                        s.op("pe", lambda e, h=h: e.matmul(sv[:, h, :], qT[:, h, tl * 128:(tl + 1) * 128], KmT[:, h, :],
                                                           start=True, stop=True),
                             r=[t_qT, t_km], w=[self.ps_tok[pbh]])
                    s.op("dve", lambda e: e.tensor_reduce(out=stt[b][:, 0:4], in_=sv, op=ALU.max, axis=AX.X, negate=True),
                         r=[self.ps_tok[p2], self.ps_tok[p2 + 1]], w=[t_stt[b]])
                    for h in range(4):
                        s.op("act", lambda e, h=h: e.activation(out=Pe[b][:, h, :], in_=sv[:, h, :], func=AF.Exp,
                                                                bias=stt[b][:, h:h + 1], accum_out=stt[b][:, 4 + h:5 + h]),
                             r=[self.ps_tok[p2], self.ps_tok[p2 + 1], t_stt[b]], w=[t_Pe[b], t_stt[b]])
                    s.op("dve", lambda e: e.reciprocal(out=stt[b][:, 8:12], in_=stt[b][:, 4:8]), r=[t_stt[b]], w=[t_stt[b]])
                    pt = self.next_bank()
                    ptv = self.bank16(pt).rearrange("p (j c) -> p j c", j=8)
                    for h in range(4):
                        for mc in range(2):
                            s.op("pe", lambda e, h=h, mc=mc: e.transpose(out=ptv[:, h * 2 + mc, :],
                                                                         in_=Pe[b][:, h, mc * 128:(mc + 1) * 128],
                                                                         identity=self.ident16[:]),
                                 r=[t_Pe[b], self.t_const], w=[self.ps_tok[pt]])
                    s.op("act", lambda e: e.activation(out=PT[b][:], in_=ptv, func=AF.Copy), r=[self.ps_tok[pt]], w=[t_PT[b]])
                    po = self.next_bank()
                    for h in range(4):
                        for mc in range(2):
                            s.op("pe", lambda e, h=h, mc=mc: e.matmul(self.bank(po)[:, h * 128:(h + 1) * 128],
                                                                      PT[b][:, h * 2 + mc, :], Vm[:, mc, h * 128:(h + 1) * 128],
                                                                      start=(mc == 0), stop=(mc == 1)),
                                 r=[t_PT[b], t_vm], w=[self.ps_tok[po]])
                    for h in range(4):
                        s.op("dve", lambda e, h=h: e.scalar_tensor_tensor(out=yx[b][:, h * 128:(h + 1) * 128],
                                                                          in0=self.bank(po)[:, h * 128:(h + 1) * 128],
                                                                          scalar=stt[b][:, 8 + h:9 + h], in1=sg[b][:, h * 128:(h + 1) * 128],
                                                                          op0=ALU.mult, op1=ALU.mult),
                             r=[self.ps_tok[po], t_stt[b], t_sg[b]], w=[t_yx[b]])
                    self.to_ystage(yx[b], t_yx[b], 4, ystage[yb], t_ys[yb], tl)
                self.store_ystage(ystage[yb], t_ys[yb], 4, ch0, tok0, ntok)
            s.barrier()

    def to_ystage(self, ytile, t_y, nch, ystage, t_ys, tl, banks=None):
        s = self.s
        for c0 in range(0, nch, 8):
            n = min(8, nch - c0)
            pt = self.next_bank() if banks is None else self.pick("pt", banks)
            ptv = self.bank16(pt).rearrange("p (j c) -> p j c", j=8)
            for c in range(n):
                s.op("pe", lambda e, c=c: e.transpose(out=ptv[:, c, :], in_=ytile[:, (c0 + c) * 128:(c0 + c + 1) * 128],
                                                      identity=self.ident16[:]),
                     r=[t_y, self.t_const], w=[self.ps_tok[pt]])
            s.op("act", lambda e: e.activation(out=ystage[:, c0:c0 + n, tl * 128:(tl + 1) * 128], in_=ptv[:, 0:n, :],
                                               func=AF.Copy), r=[self.ps_tok[pt]], w=[t_ys])

    def store_ystage(self, ystage, t_ys, nch, ch0, tok0, ntok):
        dst = self.YT[ch0 * 128:(ch0 + nch) * 128, tok0:tok0 + ntok].rearrange("(c p) n -> p c n", p=128)
        self.s.dma("sp", dst, ystage[:, 0:nch, 0:ntok], r=[t_ys])

    def phase_swa(self, si, L, hT, hT_tok):
        nc, s = self.nc, self.s
        T = L // 128
        w_in = self.w["o_w_in"]
        with ExitStack() as es:
            wkv = self.sb(es, "swkv", [128, 8, 256], BF16)
            wq = self.sb(es, "swq", [128, 8, 1024], BF16)
            wg = self.sb(es, "swg", [128, 8, 1024], BF16)
            t_wkv, t_wq, t_wg = Tok(), Tok(), Tok()
            self.load_w(wkv, t_wkv, w_in, O_DKV, 256)
            self.load_w(wq, t_wq, w_in, O_DQ, 1024)
            self.load_w(wg, t_wg, w_in, O_GD, 1024)
            cos8 = self.sb(es, "cos8", [128, T, 8], F32)
            sin8 = self.sb(es, "sin8", [128, T, 8], F32)
            t_tab = Tok()
            s.dma("sp", cos8[:], self.c_rope[L][2], w=[t_tab])
            s.dma("sp", sin8[:], self.c_rope[L][3], w=[t_tab])
            sink = self.sb(es, "sink", [128, 16], F32)
            t_sink = Tok()
            s.dma("sp", sink[:], self.w["swa_sink"].partition_broadcast(128), w=[t_sink])
            kTd = self.sb(es, "kTd", [128, 2, L], BF16)
            t_kT = [Tok() for _ in range(T)]
            vtok = self.sb(es, "vtok", [128, T, 128], BF16)
            t_v = [Tok() for _ in range(T)]
            raw = [self.sb(es, "sraw%d" % i, [128, 16, 64], F32) for i in range(2)]
            t_raw = [Tok(), Tok()]
            rtmp = self.sb(es, "srtmp", [128, 4, 16, 8], F32)
            t_rtmp = Tok()
            kr = self.sb(es, "skr", [128, 2, 64], BF16)
            t_kr = Tok()
            kd = self.sb(es, "skd", [128, 2, 2, 64], BF16)
            t_kd = Tok()
            for t in range(T):
                b = t % 2
                pb = self.pick("proj", [2, 3])
                self.proj_tok(self.bank(pb)[:, 0:256], pb, wkv, t_wkv, 0, 256, hT, [hT_tok[t]],
                              lambda k: hT[:, k, t * 128:(t + 1) * 128])
                s.op("act", lambda e: e.activation(out=raw[b][:, 0:2, :], in_=self.bank(pb)[:, 0:128].rearrange("p (h d) -> p h d", h=2),
                                                   func=AF.Copy), r=[self.ps_tok[pb]], w=[t_raw[b]])
                s.op("act", lambda e: e.activation(out=vtok[:, t, :], in_=self.bank(pb)[:, 128:256], func=AF.Copy),
                     r=[self.ps_tok[pb]], w=[t_v[t]])
                self.rope(raw[b][:, 0:2, :], t_raw[b], kr[:], t_kr, rtmp, t_rtmp, cos8[:, t, :], sin8[:, t, :], t_tab, 2, 8, 64)
                for r_ in range(2):
                    s.op("dve", lambda e, r_=r_: e.tensor_copy(out=kd[:, :, r_, :], in_=kr[:]), r=[t_kr], w=[t_kd])
                pt = self.pick("pt", [6, 7])
                ptv = self.bank16(pt).rearrange("p (j c) -> p j c", j=8)
                for kv in range(2):
                    s.op("pe", lambda e, kv=kv: e.transpose(out=ptv[:, kv, :], in_=kd[:, kv, :, :].rearrange("p r d -> p (r d)"),
                                                            identity=self.ident16[:]), r=[t_kd, self.t_const], w=[self.ps_tok[pt]])
                s.op("act", lambda e: e.activation(out=kTd[:, :, t * 128:(t + 1) * 128], in_=ptv[:, 0:2, :], func=AF.Copy),
                     r=[self.ps_tok[pt]], w=[t_kT[t]])
            q16 = [self.sb(es, "sq16%d" % i, [128, 16, 64], BF16) for i in range(2)]
            t_q16 = [Tok(), Tok()]
            qT = [self.sb(es, "sqT%d" % i, [128, 8, 128], BF16) for i in range(2)]
            t_qT = [Tok(), Tok()]
            sgd = [self.sb(es, "sgd%d" % i, [128, 1024], F32) for i in range(2)]
            t_sgd = [Tok(), Tok()]
            Sm = [self.sb(es, "sSm%d" % i, [128, 2, 384], F32) for i in range(2)]
            t_Sm = [Tok(), Tok()]
            Pe = [self.sb(es, "sPe%d" % i, [128, 2, 384], BF16) for i in range(2)]
            t_Pe = [Tok(), Tok()]
            stt = [self.sb(es, "sst%d" % i, [128, 8], F32) for i in range(2)]
            t_stt = [Tok(), Tok()]
            PT = [self.sb(es, "sPT%d" % i, [128, 8, 128], BF16) for i in range(2)]
            t_PT = [Tok(), Tok()]
            dens = [self.sb(es, "sdens%d" % i, [128, 32], F32) for i in range(2)]
            t_dens = [Tok(), Tok()]
            yd = [self.sb(es, "syd%d" % i, [128, 1024], BF16) for i in range(2)]
            t_yd = [Tok(), Tok()]
            ystage = [self.sb(es, "systage%d" % i, [128, 8, 512], BF16) for i in range(2)]
            t_ys = [Tok(), Tok()]
            for t in range(T):
                b = t % 2
                blk, tl = t // 4, t % 4
                yb = blk % 2
                for half in range(2):
                    pb = 2 + half
                    self.proj_tok(self.bank(pb), pb, wq, t_wq, half * 512, 512, hT, [hT_tok[t]],
                                  lambda k: hT[:, k, t * 128:(t + 1) * 128])
                    s.op("act", lambda e: e.activation(out=raw[b][:, half * 8:(half + 1) * 8, :],
                                                       in_=self.bank(pb).rearrange("p (h d) -> p h d", h=8), func=AF.Copy,
                                                       scale=0.125), r=[self.ps_tok[pb]], w=[t_raw[b]])
                self.rope(raw[b][:], t_raw[b], q16[b][:], t_q16[b], rtmp, t_rtmp, cos8[:, t, :], sin8[:, t, :], t_tab, 16, 8, 64)
                pt = self.pick("pt", [6, 7])
                ptv = self.bank16(pt).rearrange("p (j c) -> p j c", j=8)
                for c in range(8):
                    s.op("pe", lambda e, c=c: e.transpose(out=ptv[:, c, :], in_=q16[b][:, 2 * c:2 * c + 2, :].rearrange("p h d -> p (h d)"),
                                                          identity=self.ident16[:]), r=[t_q16[b], self.t_const], w=[self.ps_tok[pt]])
                s.op("act", lambda e: e.activation(out=qT[b][:], in_=ptv, func=AF.Copy), r=[self.ps_tok[pt]], w=[t_qT[b]])
                for half in range(2):
                    pb = 2 + half
                    self.proj_tok(self.bank(pb), pb, wg, t_wg, half * 512, 512, hT, [hT_tok[t]],
                                  lambda k: hT[:, k, t * 128:(t + 1) * 128])
                    s.op("act", lambda e: e.activation(out=sgd[b][:, half * 512:(half + 1) * 512], in_=self.bank(pb), func=AF.Silu),
                         r=[self.ps_tok[pb]], w=[t_sgd[b]])
                kts = [kt for kt in (t - 1, t, t + 1) if 0 <= kt < T]
                nkt = len(kts)
                nk = 128 * nkt
                m0 = (kts[0] - (t - 1)) * 128
                for c in range(8):
                    pp = c % 2
                    kv = c // 4
                    for j in range(2):
                        s.op("pe", lambda e, j=j: e.matmul(self.bank(4 + j)[:, 0:nk], qT[b][j * 64:(j + 1) * 64, c, :],
                                                           kTd[j * 64:(j + 1) * 64, kv, kts[0] * 128:kts[0] * 128 + nk],
                                                           start=True, stop=True),
                             r=[t_qT[b]] + [t_kT[kt] for kt in kts], w=[self.ps_tok[4 + j]])
                    self.softmax_block([4, 5], 2, nk, self.mask_swa[:, m0:m0 + nk], self.t_const, Sm[pp], t_Sm[pp], Pe[pp], t_Pe[pp],
                                       stt[pp], t_stt[pp], sink_ap=sink[:, 2 * c:2 * c + 2], t_sink=t_sink)
                    s.op("dve", lambda e: e.tensor_copy(out=dens[b][:, 2 * c:2 * c + 2], in_=stt[pp][:, 2:4]), r=[t_stt[pp]], w=[t_dens[b]])
                    ptb = self.pick("pt", [6, 7])
                    self.transpose_P(Pe[pp], t_Pe[pp], 2, nkt, PT[pp], t_PT[pp], ptb)
                    for j in range(2):
                        hq = 2 * c + j
                        ob = hq // 8
                        for kc in range(nkt):
                            s.op("pe", lambda e, j=j, kc=kc: e.matmul(self.bank(ob)[:, (hq % 8) * 64:(hq % 8 + 1) * 64],
                                                                      PT[pp][:, j * nkt + kc, :], vtok[:, kts[kc], kv * 64:(kv + 1) * 64],
                                                                      start=(kc == 0), stop=(kc == nkt - 1)),
                                 r=[t_PT[pp]] + [t_v[kt] for kt in kts], w=[self.ps_tok[ob]])
                s.op("dve", lambda e: e.reciprocal(out=dens[b][:, 16:32], in_=dens[b][:, 0:16]), r=[t_dens[b]], w=[t_dens[b]])
                for hq in range(16):
                    ob = hq // 8
                    s.op("dve", lambda e, hq=hq: e.scalar_tensor_tensor(out=yd[b][:, hq * 64:(hq + 1) * 64],
                                                                        in0=self.bank(ob)[:, (hq % 8) * 64:(hq % 8 + 1) * 64],
                                                                        scalar=dens[b][:, 16 + hq:17 + hq], in1=sgd[b][:, hq * 64:(hq + 1) * 64],
                                                                        op0=ALU.mult, op1=ALU.mult),
                         r=[self.ps_tok[ob], t_dens[b], t_sgd[b]], w=[t_yd[b]])
                self.to_ystage(yd[b], t_yd[b], 8, ystage[yb], t_ys[yb], tl, banks=[6, 7])
                if tl == 3 or t == T - 1:
                    self.store_ystage(ystage[yb], t_ys[yb], 8, 4, blk * 512, (tl + 1) * 128)
            s.barrier()

    def dft_fwd(self, L, x_tok, t_x, ncols, cb):
        s = self.s
        T = L // 128
        gc_d, gs_d = self.c_dft[L]
        with ExitStack() as es:
            tc = [self.sb(es, "tc%d" % i, [128, T, 128], BF16) for i in range(2)]
            ts = [self.sb(es, "ts%d" % i, [128, T, 128], BF16) for i in range(2)]
            t_tab = [Tok(), Tok()]
            for kc in range(T):
                b = kc % 2
                s.dma("sp", tc[b][:], gc_d[kc], w=[t_tab[b]])
                s.dma("sp", ts[b][:], gs_d[kc], w=[t_tab[b]])
                for half in range(ncols // 512):
                    bC = self.pick("dftC", [0, 2])
                    bS = bC + 1
                    for nci in range(T):
                        s.op("pe", lambda e, nci=nci: e.matmul(self.bank(bC), tc[b][:, nci, :], x_tok[:, nci, half * 512:(half + 1) * 512],
                                                               start=(nci == 0), stop=(nci == T - 1)),
                             r=[t_tab[b], t_x], w=[self.ps_tok[bC]])
                    for nci in range(T):
                        s.op("pe", lambda e, nci=nci: e.matmul(self.bank(bS), ts[b][:, nci, :], x_tok[:, nci, half * 512:(half + 1) * 512],
                                                               start=(nci == 0), stop=(nci == T - 1)),
                             r=[t_tab[b], t_x], w=[self.ps_tok[bS]])
                    cb(kc, half, bC, bS)
            s.barrier()

    def phase_filter(self, L):
        nc, s = self.nc, self.s
        T = L // 128
        HR, HI = self.H[L]
        CF, SF = self.CFSF
        c = self.c_filt[L]
        with ExitStack() as es:
            embT = self.sb(es, "embT", [33, L], F32)
            w1 = self.sb(es, "fw1", [33, 64], F32)
            w2 = self.sb(es, "fw2", [64, 64], F32)
            w3 = self.sb(es, "fw3", [64, 2048], F32)
            vec = self.sb(es, "fvec", [64, 3], F32)
            hid1 = self.sb(es, "hid1", [64, L], F32)
            hid2 = self.sb(es, "hid2", [64, L], F32)
            tcol = self.sb(es, "tcol", [128, T], F32)
            dbc = self.sb(es, "dbc", [128, 1024], F32)
            psi = self.sb(es, "psi", [128, 2, T], F32)
            t_c = Tok()
            s.dma("sp", embT[:], c["embT"], w=[t_c])
            s.dma("sp", w1[:], self.w["hy_filt_w1"], w=[t_c])
            s.dma("sp", w2[:], self.w["hy_filt_w2"], w=[t_c])
            s.dma("sp", w3[:], self.w["hy_filt_w3"], w=[t_c])
            s.dma("sp", vec[:], self.w["hy_fvec"], w=[t_c])
            s.dma("sp", tcol[:], c["tcol"], w=[t_c])
            s.dma("sp", dbc[:], self.c_delta.partition_broadcast(128), w=[t_c])
            s.dma("sp", psi[:], c["psi"], w=[t_c])
            arg = [self.sb(es, "farg%d" % i, [64, 512], F32) for i in range(2)]
            sn = [self.sb(es, "fsn%d" % i, [64, 512], F32) for i in range(2)]
            t_arg = [Tok(), Tok()]
            t_hid = Tok()
            for layer_i, (wm, kdim, src, dst, bcol) in enumerate(((w1, 33, embT, hid1, 0), (w2, 64, hid1, hid2, 1))):
                for blk in range(L // 512):
                    b = blk % 2
                    pb = self.pick("f", [4, 5])
                    s.op("pe", lambda e: e.matmul(self.bank(pb)[0:64, :], wm[0:kdim, :], src[0:kdim, blk * 512:(blk + 1) * 512],
                                                  start=True, stop=True), r=[t_c, t_hid], w=[self.ps_tok[pb]])
                    s.op("dve", lambda e: e.tensor_scalar(out=arg[b][:], in0=self.bank(pb)[0:64, :], scalar1=vec[:, bcol:bcol + 1],
                                                          scalar2=vec[:, 2:3], op0=ALU.add, op1=ALU.mult),
                         r=[self.ps_tok[pb], t_c], w=[t_arg[b]])
                    s.op("act", lambda e: e.activation(out=sn[b][:], in_=arg[b][:], func=AF.Sin, scale=1.0 / 3.0), r=[t_arg[b]], w=[t_arg[b]])
                    s.op("dve", lambda e: e.tensor_tensor(out=arg[b][:], in0=sn[b][:], in1=sn[b][:], op=ALU.mult), r=[t_arg[b]], w=[t_arg[b]])
                    s.op("dve", lambda e: e.tensor_scalar(out=arg[b][:], in0=arg[b][:], scalar1=-4.0, scalar2=3.0, op0=ALU.mult, op1=ALU.add),
                         r=[t_arg[b]], w=[t_arg[b]])
                    s.op("dve", lambda e: e.tensor_tensor(out=dst[:, blk * 512:(blk + 1) * 512], in0=sn[b][:], in1=arg[b][:], op=ALU.mult),
                         r=[t_arg[b]], w=[t_hid])
            filt = self.sb(es, "filt", [128, T, 1024], BF16)
            t_filt = Tok()
            dec = [self.sb(es, "fdec%d" % i, [128, 1024], F32) for i in range(2)]
            t_dec = [Tok(), Tok()]
            zc = [self.sb(es, "fzc%d" % i, [128, 4, 512], F32) for i in range(2)]
            t_zc = [Tok(), Tok()]
            ho = [self.sb(es, "fho%d" % i, [128, 2, 512], F32) for i in range(2)]
            t_ho = [Tok(), Tok()]
            for dirn in range(2):
                for t in range(T):
                    b = t % 2
                    s.op("act", lambda e: e.activation(out=dec[b][:], in_=dbc[:], func=AF.Exp, scale=tcol[:, t:t + 1]),
                         r=[t_c], w=[t_dec[b]])
                    for hb in range(2):
                        pb = self.pick("f", [4, 5])
                        s.op("pe", lambda e: e.matmul(self.bank(pb), hid2[:, t * 128:(t + 1) * 128],
                                                      w3[:, dirn * 1024 + hb * 512:dirn * 1024 + (hb + 1) * 512], start=True, stop=True),
                             r=[t_hid, t_c], w=[self.ps_tok[pb]])
                        s.op("dve", lambda e: e.tensor_tensor(out=filt[:, t, hb * 512:(hb + 1) * 512], in0=self.bank(pb),
                                                              in1=dec[b][:, hb * 512:(hb + 1) * 512], op=ALU.mult),
                             r=[self.ps_tok[pb], t_dec[b]], w=[t_filt])
                if dirn == 1:
                    s.op("dve", lambda e: e.memset(filt[0:1, 0, :], 0.0), w=[t_filt])

                def cb(kc, half, bC, bS, dirn=dirn):
                    b = (kc * 2 + half) % 2
                    hs = slice(half * 512, (half + 1) * 512)
                    if dirn == 0:
                        s.op("act", lambda e: e.activation(out=zc[b][:, 0, :], in_=self.bank(bC), func=AF.Copy), r=[self.ps_tok[bC]], w=[t_zc[b]])
                        s.op("act", lambda e: e.activation(out=zc[b][:, 1, :], in_=self.bank(bS), func=AF.Copy), r=[self.ps_tok[bS]], w=[t_zc[b]])
                        s.dma("sp", CF[kc, :, hs], zc[b][:, 0, :], r=[t_zc[b]], w=[self.t_cf])
                        s.dma("sp", SF[kc, :, hs], zc[b][:, 1, :], r=[t_zc[b]], w=[self.t_cf])
                    else:
                        s.dma("sp", zc[b][:, 0, :], CF[kc, :, hs], r=[self.t_cf], w=[t_zc[b]])
                        s.dma("sp", zc[b][:, 1, :], SF[kc, :, hs], r=[self.t_cf], w=[t_zc[b]])
                        Z = zc[b]
                        s.op("dve", lambda e: e.tensor_tensor(out=Z[:, 2, :], in0=Z[:, 0, :], in1=self.bank(bC), op=ALU.add),
                             r=[self.ps_tok[bC], t_zc[b]], w=[t_zc[b]])
                        s.op("dve", lambda e: e.tensor_tensor(out=Z[:, 0, :], in0=Z[:, 0, :], in1=self.bank(bC), op=ALU.subtract),
                             r=[self.ps_tok[bC], t_zc[b]], w=[t_zc[b]])
                        s.op("dve", lambda e: e.tensor_tensor(out=Z[:, 3, :], in0=Z[:, 1, :], in1=self.bank(bS), op=ALU.add),
                             r=[self.ps_tok[bS], t_zc[b]], w=[t_zc[b]])
                        s.op("dve", lambda e: e.tensor_tensor(out=Z[:, 1, :], in0=self.bank(bS), in1=Z[:, 1, :], op=ALU.subtract),
                             r=[self.ps_tok[bS], t_zc[b]], w=[t_zc[b]])
                        cps = psi[:, 0, kc:kc + 1]
                        sps = psi[:, 1, kc:kc + 1]
                        s.op("dve", lambda e: e.tensor_scalar(out=ho[b][:, 0, :], in0=Z[:, 2, :], scalar1=cps, scalar2=None, op0=ALU.mult),
                             r=[t_zc[b], t_c], w=[t_ho[b]])
                        s.op("dve", lambda e: e.scalar_tensor_tensor(out=ho[b][:, 0, :], in0=Z[:, 3, :], scalar=sps, in1=ho[b][:, 0, :],
                                                                     op0=ALU.mult, op1=ALU.add), r=[t_zc[b], t_c, t_ho[b]], w=[t_ho[b]])
                        s.op("dve", lambda e: e.tensor_scalar(out=ho[b][:, 1, :], in0=Z[:, 0, :], scalar1=sps, scalar2=None, op0=ALU.mult),
                             r=[t_zc[b], t_c], w=[t_ho[b]])
                        s.op("dve", lambda e: e.scalar_tensor_tensor(out=ho[b][:, 1, :], in0=Z[:, 1, :], scalar=cps, in1=ho[b][:, 1, :],
                                                                     op0=ALU.mult, op1=ALU.add), r=[t_zc[b], t_c, t_ho[b]], w=[t_ho[b]])
                        s.dma("sp", HR[kc, :, hs], ho[b][:, 0, :], r=[t_ho[b]], w=[self.t_H])
                        s.dma("sp", HI[kc, :, hs], ho[b][:, 1, :], r=[t_ho[b]], w=[self.t_H])

                self.dft_fwd(L, filt, t_filt, 1024, cb)
            s.barrier()

    def phase_hy1(self, si, L, hT, hT_tok):
        nc, s = self.nc, self.s
        w_in = self.w["e_w_in"]
        nblk = L // 512
        with ExitStack() as es:
            cw = self.sb(es, "hcw", [128, 24, 3], F32)
            cbias = self.sb(es, "hcb", [128, 24], F32)
            skip = self.sb(es, "hskip", [128, 8], F32)
            t_c = Tok()
            s.dma("sp", cw[:], self.w["hy_conv_w"], w=[t_c])
            s.dma("sp", cbias[:], self.w["hy_conv_b"], w=[t_c])
            s.dma("sp", skip[:], self.w["hy_skip"], w=[t_c])
            wts = [self.sb(es, "hw%d" % i, [128, 8, 4, 128], BF16) for i in range(2)]
            t_wts = [Tok(), Tok()]
            diag = [self.sb(es, "hdiag%d" % i, [128, 9, 128], BF16) for i in range(2)]
            t_diag = [Tok(), Tok()]
            zT = self.sb(es, "hzT", [128, 3, L + 2], BF16)
            t_zT = Tok()
            s.op("dve", lambda e: e.memset(zT[:, :, 0:1], 0.0), w=[t_zT])
            s.op("dve", lambda e: e.memset(zT[:, :, L + 1:L + 2], 0.0), w=[t_zT])
            xa = [self.sb(es, "hxa%d" % i, [128, 4, 512], F32) for i in range(2)]
            t_xa = [Tok(), Tok()]
            o16 = [self.sb(es, "ho16%d" % i, [128, 3, 512], BF16) for i in range(2)]
            t_o16 = [Tok(), Tok()]
            for c in range(8):
                wt, t_w = wts[c % 2], t_wts[c % 2]
                dg, t_dg = diag[c % 2], t_diag[c % 2]
                for a in range(4):
                    s.dma("pool", wt[:, :, a, :], w_in[:, a * 1024 + c * 128:a * 1024 + (c + 1) * 128].rearrange("(k p) c -> p k c", p=128),
                          w=[t_w])
                for a in range(3):
                    for j in range(3):
                        s.op("dve", lambda e, a=a, j=j: e.tensor_scalar(out=dg[:, a * 3 + j, :], in0=self.ident16[:],
                                                                        scalar1=cw[:, a * 8 + c, j:j + 1], scalar2=None, op0=ALU.mult),
                             r=[t_c, self.t_const], w=[t_dg])
                for blk in range(nblk):
                    for a in range(3):
                        pb = self.pick("proj", [2, 3])
                        self.proj_feat(self.bank(pb), pb, wt[:, :, a, :], t_w, 0, hT, hT_tok, blk * 512, 512)
                        s.op("act", lambda e, a=a: e.activation(out=zT[:, a, 1 + blk * 512:1 + (blk + 1) * 512], in_=self.bank(pb), func=AF.Copy),
                             r=[self.ps_tok[pb]], w=[t_zT])
                for blk in range(nblk):
                    b = blk % 2
                    X = xa[b]
                    for a in range(3):
                        pb = self.pick("conv", [4, 5])
                        for j in range(3):
                            s.op("pe", lambda e, a=a, j=j: e.matmul(self.bank(pb), dg[:, a * 3 + j, :], zT[:, a, blk * 512 + j:blk * 512 + j + 512],
                                                                    start=(j == 0), stop=(j == 2)), r=[t_dg, t_zT], w=[self.ps_tok[pb]])
                        s.op("act", lambda e, a=a: e.activation(out=X[:, a, :], in_=self.bank(pb), func=AF.Identity,
                                                                bias=cbias[:, a * 8 + c:a * 8 + c + 1]), r=[self.ps_tok[pb], t_c], w=[t_xa[b]])
                    pb = self.pick("proj", [2, 3])
                    self.proj_feat(self.bank(pb), pb, wt[:, :, 3, :], t_w, 0, hT, hT_tok, blk * 512, 512)
                    s.op("act", lambda e: e.activation(out=X[:, 3, :], in_=self.bank(pb), func=AF.Silu), r=[self.ps_tok[pb]], w=[t_xa[b]])
                    O = o16[b]
                    s.op("dve", lambda e: e.tensor_tensor(out=X[:, 2, :], in0=X[:, 2, :], in1=X[:, 1, :], op=ALU.mult), r=[t_xa[b]], w=[t_xa[b]])
                    s.op("pool", lambda e: e.tensor_tensor(out=X[:, 0, :], in0=X[:, 0, :], in1=X[:, 3, :], op=ALU.mult), r=[t_xa[b]], w=[t_xa[b]])
                    s.op("act", lambda e: e.activation(out=O[:, 0, :], in_=X[:, 2, :], func=AF.Copy), r=[t_xa[b]], w=[t_o16[b]])
                    s.op("act", lambda e: e.activation(out=O[:, 1, :], in_=X[:, 0, :], func=AF.Copy), r=[t_xa[b]], w=[t_o16[b]])
                    s.op("dve", lambda e: e.scalar_tensor_tensor(out=O[:, 2, :], in0=X[:, 2, :], scalar=skip[:, c:c + 1], in1=X[:, 0, :],
                                                                 op0=ALU.mult, op1=ALU.mult), r=[t_xa[b], t_c], w=[t_o16[b]])
                    for a, dr in enumerate((self.UT, self.AT, self.BT)):
                        s.dma("sp", dr[c * 128:(c + 1) * 128, blk * 512:(blk + 1) * 512], O[:, a, :], r=[t_o16[b]], w=[self.t_uab])
            s.barrier()

    def phase_hy2(self, si, L):
        nc, s = self.nc, self.s
        T = L // 128
        HR, HI = self.H[L]
        ZR, ZS = self.ZRS
        gc_d, gs_d = self.c_dft[L]
        with ExitStack() as es:
            u_tok = self.sb(es, "u_tok", [128, T, 1024], BF16)
            t_u = Tok()
            with ExitStack() as es2:
                ut = [self.sb(es2, "utl%d" % i, [128, L], BF16) for i in range(2)]
                t_ut = [Tok(), Tok()]
                for c in range(8):
                    b = c % 2
                    s.dma("sp", ut[b][:], self.UT[c * 128:(c + 1) * 128, 0:L], r=[self.t_uab], w=[t_ut[b]])
                    for t0 in range(0, T, 8):
                        n = min(8, T - t0)
                        pt = self.pick("pt", [6, 7])
                        ptv = self.bank16(pt).rearrange("p (j c) -> p j c", j=8)
                        for i in range(n):
                            s.op("pe", lambda e, i=i: e.transpose(out=ptv[:, i, :], in_=ut[b][:, (t0 + i) * 128:(t0 + i + 1) * 128],
                                                                  identity=self.ident16[:]), r=[t_ut[b], self.t_const], w=[self.ps_tok[pt]])
                        s.op("act", lambda e: e.activation(out=u_tok[:, t0:t0 + n, c * 128:(c + 1) * 128], in_=ptv[:, 0:n, :], func=AF.Copy),
                             r=[self.ps_tok[pt]], w=[t_u])
                s.barrier()
            hh = [self.sb(es, "hh%d" % i, [128, 2, 512], F32) for i in range(2)]
            t_hh = [Tok(), Tok()]
            cs = [self.sb(es, "cs%d" % i, [128, 2, 512], F32) for i in range(2)]
            t_cs = [Tok(), Tok()]
            tt = [self.sb(es, "tt%d" % i, [128, 4, 512], F32) for i in range(2)]
            t_tt = [Tok(), Tok()]
            zz = [self.sb(es, "zz%d" % i, [128, 2, 512], BF16) for i in range(2)]
            t_zz = [Tok(), Tok()]

            def cb(kc, half, bC, bS):
                b = (kc * 2 + half) % 2
                hs = slice(half * 512, (half + 1) * 512)
                s.dma("sp", hh[b][:, 0, :], HR[kc, :, hs], r=[self.t_H], w=[t_hh[b]])
                s.dma("sp", hh[b][:, 1, :], HI[kc, :, hs], r=[self.t_H], w=[t_hh[b]])
                s.op("act", lambda e: e.activation(out=cs[b][:, 0, :], in_=self.bank(bC), func=AF.Copy), r=[self.ps_tok[bC]], w=[t_cs[b]])
                s.op("act", lambda e: e.activation(out=cs[b][:, 1, :], in_=self.bank(bS), func=AF.Copy), r=[self.ps_tok[bS]], w=[t_cs[b]])
                TT = tt[b]
                s.op("dve", lambda e: e.tensor_tensor(out=TT[:, 0, :], in0=hh[b][:, 0, :], in1=cs[b][:, 0, :], op=ALU.mult), r=[t_hh[b], t_cs[b]], w=[t_tt[b]])
                s.op("pool", lambda e: e.tensor_tensor(out=TT[:, 1, :], in0=hh[b][:, 1, :], in1=cs[b][:, 1, :], op=ALU.mult), r=[t_hh[b], t_cs[b]], w=[t_tt[b]])
                s.op("pool", lambda e: e.tensor_tensor(out=TT[:, 2, :], in0=hh[b][:, 0, :], in1=cs[b][:, 1, :], op=ALU.mult), r=[t_hh[b], t_cs[b]], w=[t_tt[b]])
                s.op("dve", lambda e: e.tensor_tensor(out=TT[:, 3, :], in0=hh[b][:, 1, :], in1=cs[b][:, 0, :], op=ALU.mult), r=[t_hh[b], t_cs[b]], w=[t_tt[b]])
                s.op("dve", lambda e: e.tensor_tensor(out=zz[b][:, 0, :], in0=TT[:, 0, :], in1=TT[:, 1, :], op=ALU.add), r=[t_tt[b]], w=[t_zz[b]])
                s.op("pool", lambda e: e.tensor_tensor(out=zz[b][:, 1, :], in0=TT[:, 2, :], in1=TT[:, 3, :], op=ALU.subtract), r=[t_tt[b]], w=[t_zz[b]])
                s.dma("sp", ZR[kc, :, hs], zz[b][:, 0, :], r=[t_zz[b]], w=[self.t_Z])
                s.dma("sp", ZS[kc, :, hs], zz[b][:, 1, :], r=[t_zz[b]], w=[self.t_Z])

            self.dft_fwd(L, u_tok, t_u, 1024, cb)
        with ExitStack() as es:
            zr = self.sb(es, "zr", [128, T, 512], BF16)
            zs = self.sb(es, "zs", [128, T, 512], BF16)
            t_z = Tok()
            tc = [self.sb(es, "itc%d" % i, [128, T, 128], BF16) for i in range(2)]
            ts = [self.sb(es, "its%d" % i, [128, T, 128], BF16) for i in range(2)]
            t_tab = [Tok(), Tok()]
            y16 = [self.sb(es, "y16%d" % i, [128, 512], BF16) for i in range(2)]
            t_y16 = [Tok(), Tok()]
            ystage = [self.sb(es, "hystage%d" % i, [128, 4, 512], BF16) for i in range(2)]
            t_ys = [Tok(), Tok()]
            ab = [self.sb(es, "hab%d" % i, [128, 2, 4, 512], BF16) for i in range(2)]
            t_ab = [Tok(), Tok()]
            for ch in range(2):
                cs_ = slice(ch * 512, (ch + 1) * 512)
                s.dma("sp", zr[:], ZR[0:T, :, cs_].rearrange("k p c -> p k c"), r=[self.t_Z], w=[t_z])
                s.dma("sp", zs[:], ZS[0:T, :, cs_].rearrange("k p c -> p k c"), r=[self.t_Z], w=[t_z])
                for nci in range(T):
                    b = nci % 2
                    blk, tl = nci // 4, nci % 4
                    yb = blk % 2
                    if tl == 0:
                        rows = slice(ch * 512, (ch + 1) * 512)
                        s.dma("sp", ab[yb][:, 0, :, :], self.AT[rows, blk * 512:(blk + 1) * 512].rearrange("(c p) n -> p c n", p=128),
                              r=[self.t_uab], w=[t_ab[yb]])
                        s.dma("sp", ab[yb][:, 1, :, :], self.BT[rows, blk * 512:(blk + 1) * 512].rearrange("(c p) n -> p c n", p=128),
                              r=[self.t_uab], w=[t_ab[yb]])
                    s.dma("sp", tc[b][:], gc_d[nci], w=[t_tab[b]])
                    s.dma("sp", ts[b][:], gs_d[nci], w=[t_tab[b]])
                    pb = self.pick("inv", [0, 1, 2, 3])
                    for kc in range(T):
                        s.op("pe", lambda e, kc=kc: e.matmul(self.bank(pb), tc[b][:, kc, :], zr[:, kc, :], start=(kc == 0), stop=False),
                             r=[t_tab[b], t_z], w=[self.ps_tok[pb]])
                    for kc in range(T):
                        s.op("pe", lambda e, kc=kc: e.matmul(self.bank(pb), ts[b][:, kc, :], zs[:, kc, :], start=False, stop=(kc == T - 1)),
                             r=[t_tab[b], t_z], w=[self.ps_tok[pb]])
                    s.op("act", lambda e: e.activation(out=y16[b][:], in_=self.bank(pb), func=AF.Copy), r=[self.ps_tok[pb]], w=[t_y16[b]])
                    self.to_ystage(y16[b], t_y16[b], 4, ystage[yb], t_ys[yb], tl, banks=[6, 7])
                    if tl == 3:
                        s.op("dve", lambda e: e.tensor_tensor(out=ystage[yb][:], in0=ystage[yb][:], in1=ab[yb][:, 0, :, :], op=ALU.mult),
                             r=[t_ys[yb], t_ab[yb]], w=[t_ys[yb]])
                        s.op("dve", lambda e: e.tensor_tensor(out=ystage[yb][:], in0=ystage[yb][:], in1=ab[yb][:, 1, :, :], op=ALU.add),
                             r=[t_ys[yb], t_ab[yb]], w=[t_ys[yb]])
                        self.store_ystage(ystage[yb], t_ys[yb], 4, ch * 4, blk * 512, 512)
            s.barrier()

    def phase_gdn1(self, si, L, hT, hT_tok):
        nc, s = self.nc, self.s
        T = L // 128
        nblk = L // 512
        w_in = self.w["e_w_in"]
        with ExitStack() as es:
            cw = self.sb(es, "gcw", [128, 24, 5], F32)
            t_c = Tok()
            s.dma("sp", cw[:], self.w["gdn_conv_w"], w=[t_c])
            ones_r = self.sb(es, "ones_r", [128, 128], F32R)
            s.op("dve", lambda e: e.tensor_scalar(out=ones_r[:], in0=self.ident32[:], scalar1=0.0, scalar2=1.0, op0=ALU.mult, op1=ALU.add),
                 r=[self.t_const], w=[t_c])
            wts = [self.sb(es, "gw%d" % i, [128, 8, 3, 128], BF16) for i in range(2)]
            t_wts = [Tok(), Tok()]
            diag = [self.sb(es, "gdiag%d" % i, [128, 15, 128], BF16) for i in range(2)]
            t_diag = [Tok(), Tok()]
            zT = self.sb(es, "gzT", [128, 3, L + 4], BF16)
            t_zT = Tok()
            s.op("dve", lambda e: e.memset(zT[:, :, 0:2], 0.0), w=[t_zT])
            s.op("dve", lambda e: e.memset(zT[:, :, L + 2:L + 4], 0.0), w=[t_zT])
            xs = [self.sb(es, "gx%d" % i, [128, 3, 512], F32) for i in range(2)]
            t_xs = [Tok(), Tok()]
            sq = [self.sb(es, "gsq%d" % i, [128, 512], F32R) for i in range(2)]
            t_sq = [Tok(), Tok()]
            rs = [self.sb(es, "grs%d" % i, [128, 512], F32) for i in range(2)]
            t_rs = [Tok(), Tok()]
            kst = [self.sb(es, "gkst%d" % i, [128, 4, 128], F32) for i in range(2)]
            t_kst = [Tok(), Tok()]
            for h in range(8):
                wt, t_w = wts[h % 2], t_wts[h % 2]
                dg, t_dg = diag[h % 2], t_diag[h % 2]
                for a in range(3):
                    c0 = E_QKV + a * 1024 + h * 128
                    s.dma("pool", wt[:, :, a, :], w_in[:, c0:c0 + 128].rearrange("(k p) c -> p k c", p=128), w=[t_w])
                    for j in range(5):
                        s.op("dve", lambda e, a=a, j=j: e.tensor_scalar(out=dg[:, a * 5 + j, :], in0=self.ident16[:],
                                                                        scalar1=cw[:, a * 8 + h, j:j + 1], scalar2=None, op0=ALU.mult),
                             r=[t_c, self.t_const], w=[t_dg])
                for blk in range(nblk):
                    for a in range(3):
                        pb = self.pick("proj", [2, 3])
                        self.proj_feat(self.bank(pb), pb, wt[:, :, a, :], t_w, 0, hT, hT_tok, blk * 512, 512)
                        s.op("act", lambda e, a=a: e.activation(out=zT[:, a, 2 + blk * 512:2 + (blk + 1) * 512], in_=self.bank(pb), func=AF.Copy),
                             r=[self.ps_tok[pb]], w=[t_zT])
                for blk in range(nblk):
                    b = blk % 2
                    X = xs[b]
                    for a in range(3):
                        pb = self.pick("conv", [4, 5])
                        for j in range(5):
                            s.op("pe", lambda e, a=a, j=j: e.matmul(self.bank(pb), dg[:, a * 5 + j, :], zT[:, a, blk * 512 + j:blk * 512 + j + 512],
                                                                    start=(j == 0), stop=(j == 4)), r=[t_dg, t_zT], w=[self.ps_tok[pb]])
                        s.op("act", lambda e, a=a: e.activation(out=X[:, a, :], in_=self.bank(pb), func=AF.Silu), r=[self.ps_tok[pb]], w=[t_xs[b]])
                    for a in range(2):
                        bb = (blk * 2 + a) % 2
                        s.op("dve", lambda e, a=a: e.tensor_tensor(out=sq[bb][:], in0=X[:, a, :], in1=X[:, a, :], op=ALU.mult), r=[t_xs[b]], w=[t_sq[bb]])
                        pb = self.pick("ss", [0, 1])
                        s.op("pe", lambda e: e.matmul(self.bank(pb), ones_r[:], sq[bb][:], start=True, stop=True), r=[t_c, t_sq[bb]], w=[self.ps_tok[pb]])
                        s.op("act", lambda e: e.activation(out=rs[bb][:], in_=self.bank(pb), func=AF.Ln, bias=self.eps_col[:, 0:1]), r=[self.ps_tok[pb], self.t_const], w=[t_rs[bb]])
                        s.op("act", lambda e: e.activation(out=rs[bb][:], in_=rs[bb][:], func=AF.Exp, scale=-0.5), r=[t_rs[bb]], w=[t_rs[bb]])
                        sc = (128.0 ** -0.5) if a == 0 else 1.0
                        s.op("dve", lambda e, a=a: e.scalar_tensor_tensor(out=X[:, a, :], in0=X[:, a, :], scalar=sc, in1=rs[bb][:],
                                                                          op0=ALU.mult, op1=ALU.mult), r=[t_xs[b], t_rs[bb]], w=[t_xs[b]])
                        dr = self.QT if a == 0 else self.KT
                        s.dma("sp", dr[h * 128:(h + 1) * 128, blk * 512:(blk + 1) * 512], X[:, a, :], r=[t_xs[b]], w=[self.t_gd])
                    for a in (1, 2):
                        bb = (blk * 2 + a) % 2
                        pt = self.pick("pt", [6, 7])
                        ptv = self.bank(pt).rearrange("p (j c) -> p j c", j=4)
                        for i in range(4):
                            s.op("pe", lambda e, a=a, i=i: e.transpose(out=ptv[:, i, :], in_=X[:, a, i * 128:(i + 1) * 128], identity=self.ident32[:]),
                                 r=[t_xs[b], self.t_const], w=[self.ps_tok[pt]])
                        s.op("act", lambda e: e.activation(out=kst[bb][:], in_=ptv, func=AF.Copy), r=[self.ps_tok[pt]], w=[t_kst[bb]])
                        dr = self.KTOK if a == 1 else self.VTOK
                        s.dma("sp", dr[blk * 512:(blk + 1) * 512, h * 128:(h + 1) * 128].rearrange("(i p) d -> p i d", p=128), kst[bb][:],
                              r=[t_kst[bb]], w=[self.t_gd])
            wg = self.sb(es, "gwg", [128, 8, 1024], BF16)
            wbg = self.sb(es, "gwbg", [128, 8, 32], BF16)
            t_wg = Tok()
            self.load_w(wg, t_wg, w_in, E_GG, 1024)
            self.load_w(wbg, t_wg, w_in, E_BETA, 32)
            rows = self.sb(es, "grows", [128, 2, 16], F32)
            s.dma("sp", rows[:, 0, :], self.w["gdn_A_log"].partition_broadcast(128), w=[t_c])
            s.dma("sp", rows[:, 1, :], self.w["gdn_dt_bias"].partition_broadcast(128), w=[t_c])
            s.op("act", lambda e: e.activation(out=rows[:, 0, :], in_=rows[:, 0, :], func=AF.Exp), r=[t_c], w=[t_c])
            s.op("dve", lambda e: e.tensor_scalar(out=rows[:, 0, :], in0=rows[:, 0, :], scalar1=-1.0, scalar2=None, op0=ALU.mult), r=[t_c], w=[t_c])
            sg = [self.sb(es, "gsg%d" % i, [128, 1024], BF16) for i in range(2)]
            t_sg = [Tok(), Tok()]
            bgt = [self.sb(es, "gbgt%d" % i, [128, 4, 16], F32) for i in range(2)]
            t_bgt = [Tok(), Tok()]
            bgo = [self.sb(es, "gbgo%d" % i, [128, 32], F32) for i in range(2)]
            t_bgo = [Tok(), Tok()]
            for t in range(T):
                b = t % 2
                for half in range(2):
                    pb = 2 + half
                    self.proj_tok(self.bank(pb), pb, wg, t_wg, half * 512, 512, hT, [hT_tok[t]], lambda k: hT[:, k, t * 128:(t + 1) * 128])
                    s.op("act", lambda e: e.activation(out=sg[b][:, half * 512:(half + 1) * 512], in_=self.bank(pb), func=AF.Silu),
                         r=[self.ps_tok[pb]], w=[t_sg[b]])
                s.dma("sp", self.SGG[t * 128:(t + 1) * 128, :], sg[b][:], r=[t_sg[b]], w=[self.t_gd])
                pb = self.pick("conv", [4, 5])
                self.proj_tok(self.bank(pb)[:, 0:32], pb, wbg, t_wg, 0, 32, hT, [hT_tok[t]], lambda k: hT[:, k, t * 128:(t + 1) * 128])
                B = bgt[b]
                s.op("act", lambda e: e.activation(out=bgo[b][:, 0:16], in_=self.bank(pb)[:, 0:16], func=AF.Sigmoid), r=[self.ps_tok[pb]], w=[t_bgo[b]])
                s.op("dve", lambda e: e.tensor_tensor(out=B[:, 0, :], in0=self.bank(pb)[:, 16:32], in1=rows[:, 1, :], op=ALU.add), r=[self.ps_tok[pb], t_c], w=[t_bgt[b]])
                s.op("act", lambda e: e.activation(out=B[:, 1, :], in_=B[:, 0, :], func=AF.Abs), r=[t_bgt[b]], w=[t_bgt[b]])
                s.op("act", lambda e: e.activation(out=B[:, 1, :], in_=B[:, 1, :], func=AF.Exp, scale=-1.0), r=[t_bgt[b]], w=[t_bgt[b]])
                s.op("act", lambda e: e.activation(out=B[:, 1, :], in_=B[:, 1, :], func=AF.Ln, bias=self.eps_col[:, 1:2]), r=[t_bgt[b], self.t_const], w=[t_bgt[b]])
                s.op("dve", lambda e: e.tensor_scalar(out=B[:, 2, :], in0=B[:, 0, :], scalar1=0.0, scalar2=None, op0=ALU.max), r=[t_bgt[b]], w=[t_bgt[b]])
                s.op("dve", lambda e: e.tensor_tensor(out=B[:, 2, :], in0=B[:, 2, :], in1=B[:, 1, :], op=ALU.add), r=[t_bgt[b]], w=[t_bgt[b]])
                s.op("dve", lambda e: e.tensor_tensor(out=bgo[b][:, 16:32], in0=B[:, 2, :], in1=rows[:, 0, :], op=ALU.mult), r=[t_bgt[b], t_c], w=[t_bgo[b]])
                s.dma("sp", self.BG[t * 128:(t + 1) * 128, :], bgo[b][:], r=[t_bgo[b]], w=[self.t_gd])
            s.barrier()

    @staticmethod
    def lockstep(gens):
        gens = list(gens)
        while gens:
            for g in list(gens):
                try:
                    next(g)
                except StopIteration:
                    gens.remove(g)

    def phase_gdn2(self, si, L):
        self.phase_gdnP(L)
        self.phase_gdnR(L)
        self.phase_gdnC(L)

    def phase_gdnP(self, L):
        nc, s = self.nc, self.s
        T = L // 128
        QT3 = self.QT.rearrange("(h d) n -> d h n", d=128)
        KT3 = self.KT.rearrange("(h d) n -> d h n", d=128)
        with ExitStack() as es:
            cm = self.sb(es, "gmask", [128, 18, 128], F32)
            t_c = Tok()
            s.dma("sp", cm[:], self.c_gmask, w=[t_c])
            ones32 = self.sb(es, "ones32", [128, 128], F32)
            s.op("dve", lambda e: e.memset(ones32[:], 1.0), w=[t_c])
            ident_r = self.sb(es, "ident_r", [128, 128], F32R)
            s.op("dve", lambda e: e.tensor_copy(out=ident_r[:], in_=self.ident32[:]), r=[self.t_const], w=[t_c])
            idb = self.ident32[:].unsqueeze(1).to_broadcast([128, 4, 128])
            units = [(dirn, t, qd) for dirn in range(2) for t in range(T) for qd in range(2)]
            NL = 4

            def lane(li):
                def A(nm, dt=F32, shp=(128, 4, 128)):
                    return self.sb(es, "L%d%s" % (li, nm), list(shp), dt), Tok()
                lq, t_lq = A("lq")
                lk, t_lk = A("lk")
                qr, t_qr = A("qr", F32R)
                kr, t_kr = A("kr", F32R)
                lkt, t_lkt = A("lkt")
                lvt, t_lvt = A("lvt")
                ET, t_ET = A("ET")
                Bt, t_Bt = A("Bt")
                Ct, t_Ct = A("Ct")
                Bm, t_Bm = A("Bm", F32R)
                Cm, t_Cm = A("Cm", F32R)
                P, t_P = A("P", F32R)
                Q, t_Q = A("Q", F32R)
                Wn, t_Wn = A("Wn", F32R)
                Vn, t_Vn = A("Vn", F32R)
                QKm, t_QKm = A("QKm")
                ub, t_ub = A("ub")
                wT, t_wT = A("wT")
                kd, t_kd = A("kd")
                bg, t_bg = A("bg", F32, (128, 32))
                sc, t_sc = A("sc", F32, (128, 4, 4))
                scc, t_scc = A("scc", F32, (64, 2, 4))
                glb, t_glb = A("glb", F32, (128, 4, 2))
                Dm, t_Dm = lq, t_lq
                dgG, t_dgG = lk, t_lk
                vr, t_vr = qr, t_qr
                kg, t_kg = kr, t_kr
                bA, bB = 2 * li, 2 * li + 1
                pA, pB = self.ps_tok[bA], self.ps_tok[bB]
                v4 = lambda bk: self.bank(bk).rearrange("p (h c) -> p h c", h=4)
                for ui in range(li, len(units), NL):
                    dirn, t, qd = units[ui]
                    mo = 8 * dirn
                    mo2 = 8 * (1 - dirn)
                    hs = slice(qd * 4, qd * 4 + 4)
                    cs_ = slice(t * 128, (t + 1) * 128)
                    fs = slice(qd * 512, (qd + 1) * 512)
                    s.dma("sp", lq[:], QT3[:, hs, cs_], r=[self.t_gd], w=[t_lq])
                    s.dma("sp", lk[:], KT3[:, hs, cs_], r=[self.t_gd], w=[t_lk])
                    s.dma("sp", lkt[:], self.KTOK[cs_, fs].rearrange("p (h d) -> p h d", h=4), r=[self.t_gd], w=[t_lkt])
                    s.dma("sp", lvt[:], self.VTOK[cs_, fs].rearrange("p (h d) -> p h d", h=4), r=[self.t_gd], w=[t_lvt])
                    s.dma("sp", bg[:], self.BG[cs_, :], r=[self.t_gd], w=[t_bg])
                    yield
                    s.op("act", lambda e: e.activation(out=qr[:], in_=lq[:], func=AF.Copy), r=[t_lq], w=[t_qr])
                    s.op("act", lambda e: e.activation(out=kr[:], in_=lk[:], func=AF.Copy), r=[t_lk], w=[t_kr])
                    beta = bg[:, dirn * 8 + qd * 4:dirn * 8 + qd * 4 + 4]
                    gcol = bg[:, 16 + dirn * 8 + qd * 4:16 + dirn * 8 + qd * 4 + 4]
                    s.op("pe", lambda e: e.matmul(self.bank(bA)[:, 0:4], cm[:, 16 + dirn, :], gcol, start=True, stop=True),
                         r=[t_bg, t_c], w=[pA])
                    for cc in range(2):
                        s.op("pe", lambda e, cc=cc: e.matmul(self.bank(bA)[0:64, 16 + cc * 4:20 + cc * 4], cm[:, 16 + dirn, cc * 64:(cc + 1) * 64], gcol,
                                                             start=True, stop=True), r=[t_bg, t_c], w=[pA])
                    for hh in range(4):
                        s.op("pe", lambda e, hh=hh: e.matmul(self.bank(bB)[:, hh * 128:(hh + 1) * 128], kr[:, hh, :], kr[:, hh, :], start=True, stop=True),
                             r=[t_kr], w=[pB])
                    yield
                    s.op("dve", lambda e: e.tensor_copy(out=sc[:, :, 0], in_=self.bank(bA)[:, 0:4]), r=[pA], w=[t_sc])
                    s.op("act", lambda e: e.activation(out=sc[:, :, 1], in_=self.bank(bA)[:, 0:4], func=AF.Exp), r=[pA], w=[t_sc])
                    s.op("act", lambda e: e.activation(out=scc[:], in_=self.bank(bA)[0:64, 16:24].rearrange("p (c h) -> p c h", c=2), func=AF.Exp),
                         r=[pA], w=[t_scc])
                    yield
                    for hh in range(4):
                        s.op("dve", lambda e, hh=hh: e.tensor_scalar(out=dgG[:, hh, :], in0=self.ident32[:], scalar1=sc[:, hh, 0:1], scalar2=None, op0=ALU.mult),
                             r=[t_sc, self.t_const], w=[t_dgG])
                    s.op("pe", lambda e: e.matmul(self.bank(bA), ones32[:], dgG[:].rearrange("p h c -> p (h c)"), start=True, stop=True),
                         r=[t_c, t_dgG], w=[pA])
                    yield
                    gbc = v4(bA)
                    lastc = (63, 127) if dirn == 0 else (0, 64)
                    s.op("dve", lambda e: e.tensor_tensor(out=Dm[:], in0=gbc, in1=sc[:, :, 0:1].to_broadcast([128, 4, 128]), op=ALU.subtract),
                         r=[pA, t_sc], w=[t_Dm])
                    s.op("act", lambda e: e.activation(out=glb[:], in_=gbc[:, :, lastc[0]:lastc[1] + 1:64], func=AF.Exp), r=[pA], w=[t_glb])
                    for cc in range(2):
                        rr = slice(cc * 64, cc * 64 + 64)
                        s.op("dve", lambda e, cc=cc, rr=rr: e.tensor_tensor(out=sc[rr, :, 2], in0=gbc[rr, :, lastc[cc]], in1=sc[rr, :, 0], op=ALU.subtract),
                             r=[pA, t_sc], w=[t_sc])
                    yield
                    s.op("dve", lambda e: e.scalar_tensor_tensor(out=Dm[:], in0=Dm[:], scalar=0.0, in1=cm[:, mo + 0, :].unsqueeze(1).to_broadcast([128, 4, 128]),
                                                                 op0=ALU.min, op1=ALU.add), r=[t_Dm, t_c], w=[t_Dm])
                    s.op("act", lambda e: e.activation(out=sc[:, :, 2], in_=sc[:, :, 2], func=AF.Exp), r=[t_sc], w=[t_sc])
                    yield
                    s.op("act", lambda e: e.activation(out=ET[:], in_=Dm[:], func=AF.Exp), r=[t_Dm], w=[t_ET])
                    yield
                    s.op("dve", lambda e: e.tensor_tensor(out=Bt[:], in0=v4(bB), in1=ET[:], op=ALU.mult), r=[pB, t_ET], w=[t_Bt])
                    yield
                    for hh in range(4):
                        s.op("pe", lambda e, hh=hh: e.matmul(self.bank(bB)[:, hh * 128:(hh + 1) * 128], kr[:, hh, :], qr[:, hh, :], start=True, stop=True),
                             r=[t_kr, t_qr], w=[pB])
                    s.op("dve", lambda e: e.tensor_tensor(out=Bt[:], in0=Bt[:], in1=cm[:, mo + 1, :].unsqueeze(1).to_broadcast([128, 4, 128]), op=ALU.mult),
                         r=[t_Bt, t_c], w=[t_Bt])
                    yield
                    s.op("dve", lambda e: e.tensor_tensor(out=Bt[:], in0=Bt[:], in1=beta.unsqueeze(2).to_broadcast([128, 4, 128]), op=ALU.mult),
                         r=[t_Bt, t_bg], w=[t_Bt])
                    s.op("dve", lambda e: e.tensor_tensor(out=QKm[:], in0=v4(bB), in1=ET[:], op=ALU.mult), r=[pB, t_ET], w=[t_QKm])
                    s.dma("sp", self.G_QKM[dirn, t, :, hs, :], QKm[:], r=[t_QKm], w=[self.t_gp])
                    yield
                    for hh in range(4):
                        s.op("pe", lambda e, hh=hh: e.transpose(out=self.bank(bA)[:, hh * 128:(hh + 1) * 128], in_=Bt[:, hh, :], identity=self.ident32[:]),
                             r=[t_Bt, self.t_const], w=[pA])
                    s.op("pool", lambda e: e.tensor_tensor(out=kd[:], in0=lkt[:], in1=sc[:, :, 2:3].to_broadcast([128, 4, 128]), op=ALU.mult),
                         r=[t_lkt, t_sc], w=[t_kd])
                    s.dma("sp", self.G_KD[dirn, t, :, fs].rearrange("p (h d) -> p h d", h=4), kd[:], r=[t_kd], w=[self.t_gp])
                    s.dma("sp", self.G_SCC[dirn, t, :, :, hs], scc[:], r=[t_scc], w=[self.t_gp])
                    s.dma("sp", self.G_GLB[dirn, t, :, hs, :], glb[:], r=[t_glb], w=[self.t_gp])
                    yield
                    s.op("act", lambda e: e.activation(out=Ct[:], in_=v4(bA), func=AF.Copy), r=[pA], w=[t_Ct])
                    s.op("dve", lambda e: e.tensor_tensor(out=Bm[:], in0=Bt[:], in1=cm[:, mo + 2, :].unsqueeze(1).to_broadcast([128, 4, 128]), op=ALU.mult),
                         r=[t_Bt, t_c], w=[t_Bm])
                    yield
                    s.op("dve", lambda e: e.tensor_tensor(out=Cm[:], in0=Ct[:], in1=cm[:, mo2 + 2, :].unsqueeze(1).to_broadcast([128, 4, 128]), op=ALU.mult),
                         r=[t_Ct, t_c], w=[t_Cm])
                    s.op("dve", lambda e: e.scalar_tensor_tensor(out=P[:], in0=Bm[:], scalar=-1.0, in1=idb, op0=ALU.mult, op1=ALU.add),
                         r=[t_Bm, self.t_const], w=[t_P])
                    yield
                    s.op("dve", lambda e: e.scalar_tensor_tensor(out=Q[:], in0=Cm[:], scalar=-1.0, in1=idb, op0=ALU.mult, op1=ALU.add),
                         r=[t_Cm, self.t_const], w=[t_Q])
                    yield
                    for lv in range(1, 6):
                        last = (lv == 5)
                        s.op("dve", lambda e, lv=lv: e.tensor_tensor(out=Bm[:], in0=Bt[:], in1=cm[:, mo + 2 + lv, :].unsqueeze(1).to_broadcast([128, 4, 128]),
                                                                     op=ALU.mult), r=[t_Bt, t_c], w=[t_Bm])
                        if not last:
                            s.op("pool", lambda e, lv=lv: e.tensor_tensor(out=Cm[:], in0=Ct[:], in1=cm[:, mo2 + 2 + lv, :].unsqueeze(1).to_broadcast([128, 4, 128]),
                                                                          op=ALU.mult), r=[t_Ct, t_c], w=[t_Cm])
                        yield
                        for hh in range(4):
                            s.op("pe", lambda e, hh=hh: e.matmul(self.bank(bA)[:, hh * 128:(hh + 1) * 128], Bm[:, hh, :], Q[:, hh, :], start=True, stop=True),
                                 r=[t_Bm, t_Q], w=[pA])
                        if not last:
                            for hh in range(4):
                                s.op("pe", lambda e, hh=hh: e.matmul(self.bank(bB)[:, hh * 128:(hh + 1) * 128], Cm[:, hh, :], P[:, hh, :], start=True, stop=True),
                                     r=[t_Cm, t_P], w=[pB])
                        yield
                        s.op("act", lambda e: e.activation(out=Wn[:], in_=v4(bA), func=AF.Copy, scale=-1.0), r=[pA], w=[t_Wn])
                        if not last:
                            s.op("dve", lambda e: e.tensor_scalar(out=Vn[:], in0=v4(bB), scalar1=-1.0, scalar2=None, op0=ALU.mult), r=[pB], w=[t_Vn])
                        yield
                        for hh in range(4):
                            s.op("pe", lambda e, hh=hh: e.matmul(self.bank(bA)[:, hh * 128:(hh + 1) * 128], ident_r[:], P[:, hh, :], start=True, stop=False),
                                 r=[t_c, t_P], w=[pA])
                            s.op("pe", lambda e, hh=hh: e.matmul(self.bank(bA)[:, hh * 128:(hh + 1) * 128], Wn[:, hh, :], P[:, hh, :], start=False, stop=True),
                                 r=[t_Wn, t_P], w=[pA])
                        if not last:
                            for hh in range(4):
                                s.op("pe", lambda e, hh=hh: e.matmul(self.bank(bB)[:, hh * 128:(hh + 1) * 128], ident_r[:], Q[:, hh, :], start=True, stop=False),
                                     r=[t_c, t_Q], w=[pB])
                                s.op("pe", lambda e, hh=hh: e.matmul(self.bank(bB)[:, hh * 128:(hh + 1) * 128], Vn[:, hh, :], Q[:, hh, :], start=False, stop=True),
                                     r=[t_Vn, t_Q], w=[pB])
                        yield
                        s.op("act", lambda e: e.activation(out=P[:], in_=v4(bA), func=AF.Copy), r=[pA], w=[t_P])
                        if not last:
                            s.op("dve", lambda e: e.tensor_copy(out=Q[:], in_=v4(bB)), r=[pB], w=[t_Q])
                        yield
                    s.op("dve", lambda e: e.tensor_tensor(out=kg[:], in0=lkt[:], in1=sc[:, :, 1:2].to_broadcast([128, 4, 128]), op=ALU.mult),
                         r=[t_lkt, t_sc], w=[t_kg])
                    s.op("act", lambda e: e.activation(out=vr[:], in_=lvt[:], func=AF.Copy), r=[t_lvt], w=[t_vr])
                    yield
                    for hh in range(4):
                        s.op("pe", lambda e, hh=hh: e.matmul(self.bank(bA)[:, hh * 128:(hh + 1) * 128], P[:, hh, :], vr[:, hh, :], start=True, stop=True),
                             r=[t_P, t_vr], w=[pA])
                        s.op("pe", lambda e, hh=hh: e.matmul(self.bank(bB)[:, hh * 128:(hh + 1) * 128], kg[:, hh, :], P[:, hh, :], start=True, stop=True),
                             r=[t_P, t_kg], w=[pB])
                    yield
                    s.op("dve", lambda e: e.tensor_tensor(out=ub[:], in0=v4(bA), in1=beta.unsqueeze(2).to_broadcast([128, 4, 128]), op=ALU.mult),
                         r=[pA, t_bg], w=[t_ub])
                    s.op("act", lambda e: e.activation(out=wT[:], in_=v4(bB), func=AF.Copy), r=[pB], w=[t_wT])
                    s.dma("sp", self.G_UB[dirn, t, :, fs].rearrange("p (h d) -> p h d", h=4), ub[:], r=[t_ub], w=[self.t_gp])
                    s.dma("sp", self.G_WT[dirn, t, :, hs, :], wT[:], r=[t_wT], w=[self.t_gp])
                    yield

            self.lockstep([lane(i) for i in range(NL)])
            s.barrier()

    def phase_gdnR(self, L):
        nc, s = self.nc, self.s
        T = L // 128
        QT3 = self.QT.rearrange("(h d) n -> d h n", d=128)
        with ExitStack() as es:
            def lane(li):
                dirn, qd = li // 2, li % 2
                hs = slice(qd * 4, qd * 4 + 4)
                fs = slice(qd * 512, (qd + 1) * 512)

                def A(nm, dt=F32, shp=(128, 4, 128)):
                    return self.sb(es, "R%d%s" % (li, nm), list(shp), dt), Tok()
                ldb = []
                for i in range(2):
                    ldb.append(dict(wT=A("lwT%d" % i), qk=A("lqk%d" % i), kd=A("lkd%d" % i), ub=A("lub%d" % i), q=A("lq%d" % i),
                                    bg=A("lbg%d" % i, F32, (128, 32)), scc=A("lscc%d" % i, F32, (64, 2, 4)), glb=A("lglb%d" % i, F32, (128, 4, 2))))
                wTr, t_wTr = A("wTr", F32R)
                qkr, t_qkr = A("qkr", F32R)
                kdr, t_kdr = A("kdr", F32R)
                qr, t_qr = A("qr", F32R)
                nb, t_nb = A("nb", F32, (128, 4))
                S, t_S = A("S", F32R)
                vn, t_vn = A("vn", F32R)
                o1s, t_o1s = A("o1s", F32, (64, 4, 128))
                o2s, t_o2s = A("o2s", F32, (64, 4, 128))
                OTs = [A("OT%d" % i, F32, (64, 2, 512)) for i in range(2)]
                bA, bB = 2 * li, 2 * li + 1
                pA, pB = self.ps_tok[bA], self.ps_tok[bB]
                v4 = lambda bk: self.bank(bk).rearrange("p (h c) -> p h c", h=4)
                s.op("dve", lambda e: e.tensor_scalar(out=S[:], in0=self.ident32[:].unsqueeze(1).to_broadcast([128, 4, 128]), scalar1=0.0, scalar2=None,
                                                      op0=ALU.mult), r=[self.t_const], w=[t_S])
                tiles = list(range(T)) if dirn == 0 else list(range(T - 1, -1, -1))

                def load(it):
                    t = tiles[it]
                    Ld = ldb[it % 2]
                    cs_ = slice(t * 128, (t + 1) * 128)
                    s.dma("sp", Ld["wT"][0][:], self.G_WT[dirn, t, :, hs, :], r=[self.t_gp], w=[Ld["wT"][1]])
                    s.dma("sp", Ld["qk"][0][:], self.G_QKM[dirn, t, :, hs, :], r=[self.t_gp], w=[Ld["qk"][1]])
                    s.dma("sp", Ld["kd"][0][:], self.G_KD[dirn, t, :, fs].rearrange("p (h d) -> p h d", h=4), r=[self.t_gp], w=[Ld["kd"][1]])
                    s.dma("sp", Ld["ub"][0][:], self.G_UB[dirn, t, :, fs].rearrange("p (h d) -> p h d", h=4), r=[self.t_gp], w=[Ld["ub"][1]])
                    s.dma("sp", Ld["q"][0][:], QT3[:, hs, cs_], r=[self.t_gd], w=[Ld["q"][1]])
                    s.dma("sp", Ld["bg"][0][:], self.BG[cs_, :], r=[self.t_gd], w=[Ld["bg"][1]])
                    s.dma("sp", Ld["scc"][0][:], self.G_SCC[dirn, t, :, :, hs], r=[self.t_gp], w=[Ld["scc"][1]])
                    s.dma("sp", Ld["glb"][0][:], self.G_GLB[dirn, t, :, hs, :], r=[self.t_gp], w=[Ld["glb"][1]])

                load(0)
                for it, t in enumerate(tiles):
                    Ld = ldb[it % 2]
                    if it + 1 < T:
                        load(it + 1)
                    OT, t_OT = OTs[it % 2]
                    s.op("pool", lambda e: e.tensor_copy(out=wTr[:], in_=Ld["wT"][0][:]), r=[Ld["wT"][1]], w=[t_wTr])
                    s.op("act", lambda e: e.activation(out=qr[:], in_=Ld["q"][0][:], func=AF.Copy), r=[Ld["q"][1]], w=[t_qr])
                    s.op("pool", lambda e: e.tensor_copy(out=qkr[:], in_=Ld["qk"][0][:]), r=[Ld["qk"][1]], w=[t_qkr])
                    s.op("pool", lambda e: e.tensor_copy(out=kdr[:], in_=Ld["kd"][0][:]), r=[Ld["kd"][1]], w=[t_kdr])
                    s.op("dve", lambda e: e.tensor_scalar(out=nb[:], in0=Ld["bg"][0][:, dirn * 8 + qd * 4:dirn * 8 + qd * 4 + 4], scalar1=-1.0, scalar2=None,
                                                          op0=ALU.mult), r=[Ld["bg"][1]], w=[t_nb])
                    ubv, t_ubv = Ld["ub"]
                    sccv, t_sccv = Ld["scc"]
                    glbv, t_glbv = Ld["glb"]
                    yield
                    for cc in ((0, 1) if dirn == 0 else (1, 0)):
                        rr = slice(cc * 64, cc * 64 + 64)
                        M = (cc + 1) * 64
                        for hh in range(4):
                            s.op("pe", lambda e, hh=hh: e.matmul(self.bank(bA)[0:M, hh * 128:(hh + 1) * 128], wTr[:, hh, 0:M], S[:, hh, :], start=True, stop=True),
                                 r=[t_wTr, t_S], w=[pA])
                        for hh in range(4):
                            s.op("pe", lambda e, hh=hh: e.matmul(self.bank(bB)[0:64, hh * 128:(hh + 1) * 128], qr[:, hh, rr], S[:, hh, :], start=True, stop=True),
                                 r=[t_qr, t_S], w=[pB])
                        yield
                        for hh in range(4):
                            s.op("dve", lambda e, hh=hh: e.scalar_tensor_tensor(out=vn[rr, hh, :], in0=self.bank(bA)[rr, hh * 128:(hh + 1) * 128],
                                                                                scalar=nb[rr, hh:hh + 1], in1=ubv[rr, hh, :], op0=ALU.mult, op1=ALU.add),
                                 r=[pA, t_nb, t_ubv], w=[t_vn])
                        s.op("act", lambda e: e.activation(out=o1s[:], in_=self.bank(bB)[0:64, :].rearrange("p (h c) -> p h c", h=4), func=AF.Copy),
                             r=[pB], w=[t_o1s])
                        yield
                        for hh in range(4):
                            s.op("pe", lambda e, hh=hh: e.matmul(self.bank(bB)[:, hh * 128:(hh + 1) * 128], kdr[rr, hh, :], vn[rr, hh, :], start=True, stop=True),
                                 r=[t_kdr, t_vn], w=[pB])
                        for hh in range(4):
                            s.op("pe", lambda e, hh=hh: e.matmul(self.bank(bA)[0:64, hh * 128:(hh + 1) * 128], qkr[rr, hh, rr], vn[rr, hh, :], start=True, stop=True),
                                 r=[t_qkr, t_vn], w=[pA])
                        yield
                        for hh in range(4):
                            s.op("dve", lambda e, hh=hh: e.scalar_tensor_tensor(out=S[:, hh, :], in0=S[:, hh, :].bitcast(F32), scalar=glbv[:, hh, cc:cc + 1],
                                                                                in1=self.bank(bB)[:, hh * 128:(hh + 1) * 128], op0=ALU.mult, op1=ALU.add),
                                 r=[pB, t_glbv, t_S], w=[t_S])
                        s.op("act", lambda e: e.activation(out=o2s[:], in_=self.bank(bA)[0:64, :].rearrange("p (h c) -> p h c", h=4), func=AF.Copy),
                             r=[pA], w=[t_o2s])
                        yield
                        for hh in range(4):
                            s.op("dve", lambda e, hh=hh: e.scalar_tensor_tensor(out=OT[:, cc, hh * 128:(hh + 1) * 128], in0=o1s[:, hh, :],
                                                                                 scalar=sccv[:, cc, hh:hh + 1], in1=o2s[:, hh, :], op0=ALU.mult, op1=ALU.add),
                                 r=[t_o1s, t_sccv, t_o2s], w=[t_OT])
                    s.dma("sp", self.OFB[dirn, t * 128:(t + 1) * 128, fs].rearrange("(c p) f -> p c f", p=64), OT[:], r=[t_OT], w=[self.t_of])
                    yield

            self.lockstep([lane(i) for i in range(4)])
            s.barrier()

    def phase_gdnC(self, L):
        nc, s = self.nc, self.s
        T = L // 128
        with ExitStack() as es:
            gnorm = self.sb(es, "gnormg", [128, 128], F32)
            t_c = Tok()
            s.dma("sp", gnorm[:], self.w["gdn_norm_g"].partition_broadcast(128), w=[t_c])
            of = [self.sb(es, "cof%d" % i, [128, 1024], F32) for i in range(2)]
            ob = [self.sb(es, "cob%d" % i, [128, 1024], F32) for i in range(2)]
            sg = [self.sb(es, "csg%d" % i, [128, 1024], BF16) for i in range(2)]
            t_in = [Tok(), Tok()]
            rst = [self.sb(es, "crst%d" % i, [128, 24], F32) for i in range(2)]
            t_rst = [Tok(), Tok()]
            junk = self.sb(es, "cjunk", [128, 128], F32)
            t_junk = Tok()
            yb16 = [self.sb(es, "cyb%d" % i, [128, 1024], BF16) for i in range(2)]
            t_yb = [Tok(), Tok()]
            ystage = [self.sb(es, "cystage%d" % i, [128, 8, 512], BF16) for i in range(2)]
            t_ys = [Tok(), Tok()]
            for t in range(T):
                b = t % 2
                blk, tl = t // 4, t % 4
                yb = blk % 2
                cs_ = slice(t * 128, (t + 1) * 128)
                s.dma("sp", of[b][:], self.OFB[0, cs_, :], r=[self.t_of], w=[t_in[b]])
                s.dma("sp", ob[b][:], self.OFB[1, cs_, :], r=[self.t_of], w=[t_in[b]])
                s.dma("sp", sg[b][:], self.SGG[cs_, :], r=[self.t_gd], w=[t_in[b]])
                O = of[b]
                s.op("pool", lambda e: e.tensor_tensor(out=O[:], in0=O[:], in1=ob[b][:], op=ALU.add), r=[t_in[b]], w=[t_in[b]])
                for h in range(8):
                    s.op("act", lambda e, h=h: e.activation(out=junk[:], in_=O[:, h * 128:(h + 1) * 128], func=AF.Square, accum_out=rst[b][:, h:h + 1]),
                         r=[t_in[b]], w=[t_junk, t_rst[b]])
                s.op("dve", lambda e: e.tensor_scalar(out=rst[b][:, 8:16], in0=rst[b][:, 0:8], scalar1=1.0 / 128.0, scalar2=EPS, op0=ALU.mult, op1=ALU.add),
                     r=[t_rst[b]], w=[t_rst[b]])
                s.op("act", lambda e: e.activation(out=rst[b][:, 8:16], in_=rst[b][:, 8:16], func=AF.Ln), r=[t_rst[b]], w=[t_rst[b]])
                s.op("act", lambda e: e.activation(out=rst[b][:, 16:24], in_=rst[b][:, 8:16], func=AF.Exp, scale=-0.5), r=[t_rst[b]], w=[t_rst[b]])
                O3 = O[:].rearrange("p (h d) -> p h d", h=8)
                s.op("dve", lambda e: e.tensor_tensor(out=O3, in0=O3, in1=rst[b][:, 16:24].unsqueeze(2).to_broadcast([128, 8, 128]), op=ALU.mult),
                     r=[t_in[b], t_rst[b]], w=[t_in[b]])
                s.op("pool", lambda e: e.tensor_tensor(out=O3, in0=O3, in1=gnorm[:].unsqueeze(1).to_broadcast([128, 8, 128]), op=ALU.mult),
                     r=[t_in[b], t_c], w=[t_in[b]])
                s.op("dve", lambda e: e.tensor_tensor(out=yb16[b][:], in0=O[:], in1=sg[b][:], op=ALU.mult), r=[t_in[b]], w=[t_yb[b]])
                self.to_ystage(yb16[b], t_yb[b], 8, ystage[yb], t_ys[yb], tl, banks=[6, 7])
                if tl == 3 or t == T - 1:
                    self.store_ystage(ystage[yb], t_ys[yb], 8, 8, blk * 512, (tl + 1) * 128)
            s.barrier()

    def phase_dil(self, si, L, hT, hT_tok):
        nc, s = self.nc, self.s
        T = L // 128
        w_in = self.w["o_w_in"]
        OG, LSE = self.OG, self.LSE
        with ExitStack() as es:
            wts = [self.sb(es, "dw%d" % i, [128, 8, 768], BF16) for i in range(2)]
            t_wts = [Tok(), Tok()]
            cos16 = self.sb(es, "cos16", [128, T, 16], F32)
            sin16 = self.sb(es, "sin16", [128, T, 16], F32)
            t_tab = Tok()
            s.dma("sp", cos16[:], self.c_rope[L][0], w=[t_tab])
            s.dma("sp", sin16[:], self.c_rope[L][1], w=[t_tab])
            qT = self.sb(es, "dqT", [128, 2, L], BF16)
            kT = self.sb(es, "dkT", [128, 2, L], BF16)
            t_qk = [Tok() for _ in range(T)]
            vP = self.sb(es, "dvP", [128, T, 256], BF16)
            t_vP = [Tok() for _ in range(T)]
            raw = [self.sb(es, "draw%d" % i, [128, 4, 128], F32) for i in range(2)]
            t_raw = [Tok(), Tok()]
            rtmp = self.sb(es, "drtmp", [128, 4, 4, 16], F32)
            t_rtmp = Tok()
            qk16 = [self.sb(es, "dqk16%d" % i, [128, 4, 128], BF16) for i in range(2)]
            t_qk16 = [Tok(), Tok()]
            Sm = [self.sb(es, "dSm%d" % i, [128, 2, 384], F32) for i in range(2)]
            t_Sm = [Tok(), Tok()]
            Pe = [self.sb(es, "dPe%d" % i, [128, 2, 384], BF16) for i in range(2)]
            t_Pe = [Tok(), Tok()]
            stt = [self.sb(es, "dst%d" % i, [128, 8], F32) for i in range(2)]
            t_stt = [Tok(), Tok()]
            PT = [self.sb(es, "dPT%d" % i, [128, 8, 128], BF16) for i in range(2)]
            t_PT = [Tok(), Tok()]
            og = [self.sb(es, "dog%d" % i, [128, 256], F32) for i in range(2)]
            t_og = [Tok(), Tok()]
            lse = [self.sb(es, "dlse%d" % i, [128, 4], F32) for i in range(2)]
            t_lse = [Tok(), Tok()]
            scale = 128.0 ** -0.5
            it = 0
            for g, d in enumerate(DIL):
                ls = L // d
                tps = ls // 128
                for hp in range(2):
                    wt, t_w = wts[it % 2], t_wts[it % 2]
                    it += 1
                    for qkv in range(3):
                        c0 = O_CQKV + ((qkv * 3 + g) * 4 + hp * 2) * 128
                        s.dma("pool", wt[:, :, qkv * 256:(qkv + 1) * 256],
                              w_in[:, c0:c0 + 256].rearrange("(k p) c -> p k c", p=128), w=[t_w])
                    for t in range(T):
                        b = t % 2
                        pb = self.pick("proj", [2, 3])
                        for qk in range(2):
                            self.proj_tok(self.bank(pb)[:, qk * 256:(qk + 1) * 256], pb, wt, t_w, qk * 256, 256, hT, [hT_tok[t]],
                                          lambda k: hT[:, k, t * 128:(t + 1) * 128])
                        s.op("act", lambda e: e.activation(out=raw[b][:], in_=self.bank(pb).rearrange("p (h d) -> p h d", h=4),
                                                           func=AF.Copy), r=[self.ps_tok[pb]], w=[t_raw[b]])
                        self.rope(raw[b][:], t_raw[b], qk16[b][:], t_qk16[b], rtmp, t_rtmp, cos16[:, t, :], sin16[:, t, :], t_tab,
                                  4, 16, 128)
                        pt = self.pick("pt", [6, 7])
                        ptv = self.bank16(pt).rearrange("p (j c) -> p j c", j=8)
                        for c in range(4):
                            s.op("pe", lambda e, c=c: e.transpose(out=ptv[:, c, :], in_=qk16[b][:, c, :], identity=self.ident16[:]),
                                 r=[t_qk16[b], self.t_const], w=[self.ps_tok[pt]])
                        s.op("act", lambda e: e.activation(out=qT[:, :, t * 128:(t + 1) * 128], in_=ptv[:, 0:2, :], func=AF.Copy,
                                                           scale=scale), r=[self.ps_tok[pt]], w=[t_qk[t]])
                        s.op("dve", lambda e: e.tensor_copy(out=kT[:, :, t * 128:(t + 1) * 128], in_=ptv[:, 2:4, :]),
                             r=[self.ps_tok[pt]], w=[t_qk[t]])

                    def pslice(j):
                        seg, jj = j // tps, j % tps
                        start = seg + d * jj * 128
                        return start, start + d * 127 + 1

                    def ptoks(j):
                        a, bnd = pslice(j)
                        return [t_qk[tt] for tt in range(a // 128, (bnd - 1) // 128 + 1)], \
                               [hT_tok[tt] for tt in range(a // 128, (bnd - 1) // 128 + 1)]

                    for j in range(T):
                        a, bnd = pslice(j)
                        pb = self.pick("proj", [2, 3])
                        self.proj_tok(self.bank(pb)[:, 0:256], pb, wt, t_w, 512, 256, hT, ptoks(j)[1],
                                      lambda k: hT[:, k, a:bnd:d])
                        s.op("act", lambda e: e.activation(out=vP[:, j, :], in_=self.bank(pb)[:, 0:256], func=AF.Copy),
                             r=[self.ps_tok[pb]], w=[t_vP[j]])
                    for j in range(T):
                        pp = j % 2
                        a, bnd = pslice(j)
                        kts = [kt for kt in (j - 1, j, j + 1) if 0 <= kt < T and kt // tps == j // tps]
                        nkt = len(kts)
                        nk = 128 * nkt
                        m0 = (kts[0] - (j - 1)) * 128
                        ka = pslice(kts[0])[0]
                        kb_ = pslice(kts[-1])[1]
                        kdeps = []
                        for kt in kts:
                            kdeps += ptoks(kt)[0]
                        for jh in range(2):
                            s.op("pe", lambda e, jh=jh: e.matmul(self.bank(4 + jh)[:, 0:nk], qT[:, jh, a:bnd:d], kT[:, jh, ka:kb_:d],
                                                                 start=True, stop=True),
                                 r=ptoks(j)[0] + kdeps, w=[self.ps_tok[4 + jh]])
                        self.softmax_block([4, 5], 2, nk, self.mask_dil[:, m0:m0 + nk], self.t_const, Sm[pp], t_Sm[pp], Pe[pp], t_Pe[pp],
                                           stt[pp], t_stt[pp])
                        ptb = self.pick("pt", [6, 7])
                        self.transpose_P(Pe[pp], t_Pe[pp], 2, nkt, PT[pp], t_PT[pp], ptb)
                        ob = self.pick("o", [0, 1])
                        for jh in range(2):
                            for kc in range(nkt):
                                s.op("pe", lambda e, jh=jh, kc=kc: e.matmul(self.bank(ob)[:, jh * 128:(jh + 1) * 128],
                                                                            PT[pp][:, jh * nkt + kc, :],
                                                                            vP[:, kts[kc], jh * 128:(jh + 1) * 128],
                                                                            start=(kc == 0), stop=(kc == nkt - 1)),
                                     r=[t_PT[pp]] + [t_vP[kt] for kt in kts], w=[self.ps_tok[ob]])
                        s.op("dve", lambda e: e.reciprocal(out=stt[pp][:, 4:6], in_=stt[pp][:, 2:4]), r=[t_stt[pp]], w=[t_stt[pp]])
                        for jh in range(2):
                            s.op("dve", lambda e, jh=jh: e.tensor_scalar(out=og[pp][:, jh * 128:(jh + 1) * 128],
                                                                         in0=self.bank(ob)[:, jh * 128:(jh + 1) * 128],
                                                                         scalar1=stt[pp][:, 4 + jh:5 + jh], scalar2=None, op0=ALU.mult),
                                 r=[self.ps_tok[ob], t_stt[pp]], w=[t_og[pp]])
                        s.op("act", lambda e: e.activation(out=lse[pp][:, 0:2], in_=stt[pp][:, 2:4], func=AF.Ln), r=[t_stt[pp]], w=[t_lse[pp]])
                        s.op("dve", lambda e: e.tensor_tensor(out=lse[pp][:, 2:4], in0=lse[pp][:, 0:2], in1=stt[pp][:, 0:2], op=ALU.subtract),
                             r=[t_stt[pp], t_lse[pp]], w=[t_lse[pp]])
                        s.dma("sp", OG[g, a:bnd:d, hp * 256:(hp + 1) * 256], og[pp][:], r=[t_og[pp]], w=[self.t_og_dram])
                        s.dma("sp", LSE[g, a:bnd:d, hp * 2:(hp + 1) * 2], lse[pp][:, 2:4], r=[t_lse[pp]], w=[self.t_og_dram])
            s.barrier()
        with ExitStack() as es:
            wg = self.sb(es, "dwg", [128, 8, 512], BF16)
            t_wg = Tok()
            self.load_w(wg, t_wg, w_in, O_GC, 512)
            og3 = [self.sb(es, "og3%d" % i, [128, 3, 512], F32) for i in range(2)]
            l3 = [self.sb(es, "l3%d" % i, [128, 3, 4], F32) for i in range(2)]
            t_in = [Tok(), Tok()]
            wk = [self.sb(es, "mwk%d" % i, [128, 8, 4], F32) for i in range(2)]
            t_wk = [Tok(), Tok()]
            yacc = [self.sb(es, "yacc%d" % i, [128, 512], F32) for i in range(2)]
            t_ya = [Tok(), Tok()]
            sg = [self.sb(es, "dsg%d" % i, [128, 512], F32) for i in range(2)]
            t_sg = [Tok(), Tok()]
            yc = [self.sb(es, "dyc%d" % i, [128, 512], BF16) for i in range(2)]
            t_yc = [Tok(), Tok()]
            ystage = [self.sb(es, "dystage%d" % i, [128, 4, 512], BF16) for i in range(2)]
            t_ys = [Tok(), Tok()]
            for t in range(T):
                b = t % 2
                blk, tl = t // 4, t % 4
                yb = blk % 2
                s.dma("sp", og3[b][:], OG[:, t * 128:(t + 1) * 128, :].rearrange("g p c -> p g c"), r=[self.t_og_dram], w=[t_in[b]])
                s.dma("sp", l3[b][:], LSE[:, t * 128:(t + 1) * 128, :].rearrange("g p c -> p g c"), r=[self.t_og_dram], w=[t_in[b]])
                pb = self.pick("proj", [2, 3])
                self.proj_tok(self.bank(pb), pb, wg, t_wg, 0, 512, hT, [hT_tok[t]], lambda k: hT[:, k, t * 128:(t + 1) * 128])
                s.op("act", lambda e: e.activation(out=sg[b][:], in_=self.bank(pb), func=AF.Silu), r=[self.ps_tok[pb]], w=[t_sg[b]])
                W = wk[b]
                s.op("dve", lambda e: e.tensor_tensor(out=W[:, 3, :], in0=l3[b][:, 0, :], in1=l3[b][:, 1, :], op=ALU.max), r=[t_in[b]], w=[t_wk[b]])
                s.op("dve", lambda e: e.tensor_tensor(out=W[:, 3, :], in0=W[:, 3, :], in1=l3[b][:, 2, :], op=ALU.max), r=[t_in[b], t_wk[b]], w=[t_wk[b]])
                s.op("dve", lambda e: e.tensor_tensor(out=W[:, 0:3, :], in0=l3[b][:], in1=W[:, 3, :].unsqueeze(1).to_broadcast([128, 3, 4]),
                                                      op=ALU.subtract), r=[t_in[b], t_wk[b]], w=[t_wk[b]])
                s.op("act", lambda e: e.activation(out=W[:, 0:3, :], in_=W[:, 0:3, :], func=AF.Exp), r=[t_wk[b]], w=[t_wk[b]])
                s.op("dve", lambda e: e.tensor_tensor(out=W[:, 4, :], in0=W[:, 0, :], in1=W[:, 1, :], op=ALU.add), r=[t_wk[b]], w=[t_wk[b]])
                s.op("dve", lambda e: e.tensor_tensor(out=W[:, 4, :], in0=W[:, 4, :], in1=W[:, 2, :], op=ALU.add), r=[t_wk[b]], w=[t_wk[b]])
                s.op("dve", lambda e: e.reciprocal(out=W[:, 5, :], in_=W[:, 4, :]), r=[t_wk[b]], w=[t_wk[b]])
                s.op("dve", lambda e: e.tensor_tensor(out=W[:, 0:3, :], in0=W[:, 0:3, :], in1=W[:, 5, :].unsqueeze(1).to_broadcast([128, 3, 4]),
                                                      op=ALU.mult), r=[t_wk[b]], w=[t_wk[b]])
                for h in range(4):
                    hs = slice(h * 128, (h + 1) * 128)
                    s.op("dve", lambda e, h=h, hs=hs: e.tensor_scalar(out=yacc[b][:, hs], in0=og3[b][:, 0, hs], scalar1=W[:, 0, h:h + 1],
                                                                      scalar2=None, op0=ALU.mult), r=[t_in[b], t_wk[b]], w=[t_ya[b]])
                    for g in (1, 2):
                        s.op("dve", lambda e, h=h, hs=hs, g=g: e.scalar_tensor_tensor(out=yacc[b][:, hs], in0=og3[b][:, g, hs],
                                                                                      scalar=W[:, g, h:h + 1], in1=yacc[b][:, hs],
                                                                                      op0=ALU.mult, op1=ALU.add),
                             r=[t_in[b], t_wk[b], t_ya[b]], w=[t_ya[b]])
                s.op("pool", lambda e: e.tensor_tensor(out=yc[b][:], in0=yacc[b][:], in1=sg[b][:], op=ALU.mult),
                     r=[t_ya[b], t_sg[b]], w=[t_yc[b]])
                self.to_ystage(yc[b], t_yc[b], 4, ystage[yb], t_ys[yb], tl, banks=[6, 7])
                if tl == 3 or t == T - 1:
                    self.store_ystage(ystage[yb], t_ys[yb], 4, 0, blk * 512, (tl + 1) * 128)
            s.barrier()

    def zero_YT(self, L, chunks):
        s = self.s
        with ExitStack() as es:
            z = self.sb(es, "zeros", [128, L], BF16)
            t_z = Tok()
            s.op("dve", lambda e: e.memset(z[:], 0.0), w=[t_z])
            for c in chunks:
                s.dma("sp", self.YT[c * 128:(c + 1) * 128, 0:L], z[:], r=[t_z])
            s.barrier()

    def phase_out(self, r0, L, src, dst, pre, nch):
        nc, s = self.nc, self.s
        T = L // 128
        with ExitStack() as es:
            wo = self.sb(es, "wo", [128, nch, D], BF16)
            t_wo = Tok()
            w_out = self.w[pre + "w_out"]
            for c0 in range(0, nch, 4):
                s.dma("pool", wo[:, c0:c0 + 4, :], w_out[c0 * 128:(c0 + 4) * 128, :].rearrange("(c p) n -> p c n", p=128),
                      w=[t_wo])
            gpost = self.sb(es, "gpost", [128, D], F32)
            t_gp = Tok()
            s.dma("sp", gpost[:], self.w[pre + "post_g"].partition_broadcast(128), w=[t_gp])
            yb = [self.sb(es, "yb%d" % i, [128, nch, 512], BF16) for i in range(2)]
            t_yb = [Tok(), Tok()]
            xr = [self.sb(es, "xr%d" % i, [128, D], F32) for i in range(2)]
            t_xr = [Tok(), Tok()]
            st = [self.sb(es, "ost%d" % i, [128, 4], F32) for i in range(2)]
            t_st = [Tok(), Tok()]
            junk = self.sb(es, "ojunk", [128, D], BF16)
            t_junk = Tok()
            tmp = [self.sb(es, "otmp%d" % i, [128, D], F32) for i in range(2)]
            t_tmp = [Tok(), Tok()]
            nblk = (L + 511) // 512
            for blk in range(nblk):
                tok0 = blk * 512
                ntok = min(512, L - tok0)
                bb = blk % 2
                s.dma("sp", yb[bb][:, :, 0:ntok],
                      self.YT[0:nch * 128, tok0:tok0 + ntok].rearrange("(c p) n -> p c n", p=128), w=[t_yb[bb]])
                for tl in range(ntok // 128):
                    t = blk * 4 + tl
                    b = t % 2
                    s.dma("sp", xr[b][:], src[r0 + t * 128:r0 + (t + 1) * 128, :], w=[t_xr[b]])
                    p2 = self.next_bank(2)
                    for half in range(2):
                        for c in range(nch):
                            s.op("pe", lambda e, c=c, half=half: e.matmul(self.bank(p2 + half), yb[bb][:, c, tl * 128:(tl + 1) * 128],
                                                                          wo[:, c, half * 512:(half + 1) * 512],
                                                                          start=(c == 0), stop=(c == nch - 1)),
                                 r=[t_yb[bb], t_wo], w=[self.ps_tok[p2 + half]])
                    pv = self.ps[:, p2 * 512:(p2 + 2) * 512]
                    pr = [self.ps_tok[p2], self.ps_tok[p2 + 1]]
                    s.op("act", lambda e: e.activation(out=junk[:], in_=pv, func=AF.Square, accum_out=st[b][:, 0:1]),
                         r=pr, w=[t_junk, t_st[b]])
                    s.op("dve", lambda e: e.tensor_scalar(out=st[b][:, 1:2], in0=st[b][:, 0:1], scalar1=1.0 / D, scalar2=EPS,
                                                          op0=ALU.mult, op1=ALU.add), r=[t_st[b]], w=[t_st[b]])
                    s.op("act", lambda e: e.activation(out=st[b][:, 2:3], in_=st[b][:, 1:2], func=AF.Sqrt), r=[t_st[b]], w=[t_st[b]])
                    s.op("dve", lambda e: e.reciprocal(out=st[b][:, 3:4], in_=st[b][:, 2:3]), r=[t_st[b]], w=[t_st[b]])
                    s.op("dve", lambda e: e.scalar_tensor_tensor(out=tmp[b][:], in0=pv, scalar=st[b][:, 3:4], in1=gpost[:],
                                                                 op0=ALU.mult, op1=ALU.mult),
                         r=pr + [t_st[b], t_gp], w=[t_tmp[b]])
                    s.op("pool", lambda e: e.tensor_tensor(out=tmp[b][:], in0=tmp[b][:], in1=xr[b][:], op=ALU.add),
                         r=[t_xr[b], t_tmp[b]], w=[t_tmp[b]])
                    s.dma("sp", dst[r0 + t * 128:r0 + (t + 1) * 128, :], tmp[b][:], r=[t_tmp[b]])
            s.barrier()


def host_consts(seq_lens):
    c = {}
    c["c_ident"] = np.eye(128, dtype=np.float32)
    c["c_mask_dil"] = band_mask(64, 192)
    c["c_mask_swa"] = band_mask(0, 256)
    for L in sorted(set(seq_lens)):
        c16, s16 = rope_tables(L, 16)
        c8, s8 = rope_tables(L, 8)
        c["c_cos16_%d" % L] = tok_layout(c16)
        c["c_sin16_%d" % L] = tok_layout(s16)
        c["c_cos8_%d" % L] = tok_layout(c8)
        c["c_sin8_%d" % L] = tok_layout(s8)
    j = np.arange(128)[:, None]
    i = np.arange(128)[None, :]
    same = (j // 64) == (i // 64)
    fw = []
    fw.append(np.where(same & (i >= j), 0.0, NEG))
    fw.append(np.where(same & (i > j), 1.0, 0.0))
    for lv in range(6):
        bsz = 1 << lv
        fw.append(np.where(((j // (2 * bsz)) == (i // (2 * bsz))) & ((j % (2 * bsz)) < bsz) & ((i % (2 * bsz)) >= bsz), 1.0, 0.0))
    bw = [m.T for m in fw]
    tri_f = np.where(same & (j <= i), 1.0, 0.0)
    gm = np.stack(fw + bw + [tri_f, tri_f.T], 0).astype(np.float32)
    c["c_gmask"] = np.ascontiguousarray(gm.transpose(1, 0, 2))
    c["c_delta"] = np.abs(np.linspace(math.log(1e-2) / 1.5, math.log(1e-2) / 0.3, 1024, dtype=np.float32)).reshape(1, 1024)
    for L in sorted(set(seq_lens)):
        T = L // 128
        N = 2 * L
        i = np.arange(L, dtype=np.int64)
        prod = ((2 * i[:, None] + 1) * (2 * i[None, :] + 1)) % (4 * N)
        ang = prod.astype(np.float64) * (2.0 * np.pi / (4 * N))
        for nm, fn in (("c_gc_%d" % L, np.cos), ("c_gs_%d" % L, np.sin)):
            tab = fn(ang).astype(np.float32).astype(ml_dtypes.bfloat16)
            c[nm] = np.ascontiguousarray(tab.reshape(T, 128, T, 128).transpose(2, 1, 0, 3))
        t = np.linspace(0.0, 1.0, L, dtype=np.float32)[:, None]
        wv = (2.0 * math.pi / L) * np.arange(L, dtype=np.float32)[:, None]
        f = np.linspace(1e-4, 15.0, 16, dtype=np.float32)[None, :]
        emb = np.concatenate([t, np.cos(f * wv), -np.sin(f * wv)], axis=-1).astype(np.float32)
        c["c_embT_%d" % L] = np.ascontiguousarray(emb.T)
        c["c_tcol_%d" % L] = np.ascontiguousarray((-t[:, 0]).reshape(T, 128).T)
        k = np.arange(L, dtype=np.float64)
        psi = np.pi * (k + 0.5) / N
        ps = np.stack([(2.0 / N) * np.cos(psi), (2.0 / N) * np.sin(psi)], 0).astype(np.float32)
        c["c_psi_%d" % L] = np.ascontiguousarray(ps.reshape(2, T, 128).transpose(2, 0, 1))
    return c


def col_layout(v):
    return np.ascontiguousarray(np.asarray(v, np.float32).reshape(8, 128).T)


def shared_inputs(inp, seq_lens):
    m = {}
    for nm in ("e_w_in", "e_w_out", "e_w_mem_kv", "o_w_in", "o_w_out", "o_w_mem_kv"):
        m[nm] = np.ascontiguousarray(np.asarray(inp[nm], np.float32)[0])
    for nm in ("e_pre_g", "e_mem_g", "o_pre_g", "o_mem_g"):
        m[nm] = col_layout(np.asarray(inp[nm])[0])
    for nm in ("e_post_g", "o_post_g"):
        m[nm] = np.ascontiguousarray(np.asarray(inp[nm], np.float32).reshape(1, D))
    m["swa_sink"] = np.ascontiguousarray(np.asarray(inp["swa_sink"], np.float32).reshape(1, 16))
    f32 = lambda a: np.asarray(a, np.float32)
    m["hy_filt_w1"] = np.ascontiguousarray(f32(inp["hy_filt_w1"])[0])
    m["hy_filt_w2"] = np.ascontiguousarray(f32(inp["hy_filt_w2"])[0])
    m["hy_filt_w3"] = np.ascontiguousarray(f32(inp["hy_filt_w3"])[0])
    m["hy_fvec"] = np.ascontiguousarray(np.stack([f32(inp["hy_filt_b1"])[0], f32(inp["hy_filt_b2"])[0], f32(inp["hy_freq"])[0]], 1))
    m["hy_conv_w"] = np.ascontiguousarray(f32(inp["hy_conv_w"])[0].reshape(3, 24, 128).transpose(2, 1, 0))
    m["hy_conv_b"] = np.ascontiguousarray(f32(inp["hy_conv_b"])[0].reshape(24, 128).T)
    m["gdn_conv_w"] = np.ascontiguousarray(f32(inp["gdn_conv_w"])[0].reshape(5, 24, 128).transpose(2, 1, 0))
    m["gdn_A_log"] = np.ascontiguousarray(f32(inp["gdn_A_log"]).reshape(1, 16))
    m["gdn_dt_bias"] = np.ascontiguousarray(f32(inp["gdn_dt_bias"]).reshape(1, 16))
    m["gdn_norm_g"] = np.ascontiguousarray(f32(inp["gdn_norm_g"]).reshape(1, 128))
    m["hy_skip"] = np.ascontiguousarray(f32(inp["hy_skip"])[0].reshape(8, 128).T)
    m.update(host_consts(seq_lens))
    return m


_CACHE = {}


def kernel(**inp):
    xp = np.asarray(inp["x_prompt"], np.float32)
    xs = np.asarray(inp["x_sample"], np.float32)
    mp = np.asarray(inp["mem_prompt"], np.float32)
    ms = np.asarray(inp["mem_sample"], np.float32)
    n = 8
    seq_lens = [xs.shape[1], xs.shape[1], xp.shape[1]]
    shared = shared_inputs(inp, seq_lens)
    in_maps = []
    for c in range(n):
        m = dict(shared)
        m["x"] = np.ascontiguousarray(np.concatenate([xs[2 * c], xs[2 * c + 1], xp[c]], axis=0))
        m["mem"] = np.ascontiguousarray(np.concatenate([ms[2 * c], ms[2 * c + 1], mp[c]], axis=0))
        in_maps.append(m)
    nc = KB(seq_lens).build()
    res = run_bass_kernel_spmd(nc, in_maps, core_ids=list(range(n)))
    Ls = xs.shape[1]
    y_p = np.stack([res.results[c]["y"][2 * Ls:] for c in range(n)], axis=0)
    y_s = np.stack([res.results[c]["y"][j * Ls:(j + 1) * Ls] for c in range(n) for j in range(2)], axis=0)
    return (y_p.astype(np.float32), y_s.astype(np.float32))
```

```python
import math
from contextlib import ExitStack

import numpy as np
import ml_dtypes

import concourse.bass as bass
import concourse.mybir as mybir
from concourse.bass_utils import run_bass_kernel_spmd

F32 = mybir.dt.float32
BF16 = mybir.dt.bfloat16
F32R = mybir.dt.float32r
AF = mybir.ActivationFunctionType
ALU = mybir.AluOpType
AX = mybir.AxisListType

D = 1024
EPS = 1e-6
NEG = -30000.0
ROPE_THETA = 500000.0
MEM_TOKENS = 256
EVEN_IN = 9248
ODD_IN = 8448
E_HY, E_GHY, E_QKV, E_GG, E_BETA, E_A, E_XQ, E_GX = 0, 3072, 4096, 7168, 8192, 8208, 8224, 8736
O_CQKV, O_GC, O_DQ, O_DKV, O_GD, O_XQ, O_GX = 0, 4608, 5120, 6144, 6400, 7424, 7936
DIL = (1, 4, 16)


class Tok:
    __slots__ = ("w", "r")

    def __init__(self):
        self.w = None
        self.r = {}


class Sch:
    RING = 8

    def __init__(self, nc, es):
        self.nc = nc
        self.eng = {"pe": nc.tensor, "act": nc.scalar, "dve": nc.vector, "pool": nc.gpsimd, "sp": nc.sync}
        self.sem = {}
        self.cnt = {}
        self.known = {}
        for e in self.eng:
            self.sem[e] = es.enter_context(nc.semaphore("s_" + e))
            self.cnt[e] = 0
            self.known[e] = {}
        self.dcnt = {}
        for q in ("sp", "act", "pool"):
            self.dcnt[q] = 0
            for i in range(self.RING):
                key = ("d", q, i)
                self.sem[key] = es.enter_context(nc.semaphore("d_%s_%d" % (q, i)))
                self.cnt[key] = 0
        self.ninst = 0

    def _wait(self, e, deps):
        need = {}
        for key, val in deps:
            if key == e and e == "pe":
                continue
            if self.known[e].get(key, 0) >= val:
                continue
            if need.get(key, 0) < val:
                need[key] = val
        for key, val in need.items():
            self.eng[e].wait_ge(self.sem[key], val)
            self.known[e][key] = val

    @staticmethod
    def _deps(r, w):
        deps = []
        for t in r:
            if t.w is not None:
                deps.append(t.w)
        for t in w:
            if t.w is not None:
                deps.append(t.w)
            deps.extend(t.r.items())
        return deps

    @staticmethod
    def _mark(me, r, w):
        key, val = me
        for t in r:
            if t.r.get(key, 0) < val:
                t.r[key] = val
        for t in w:
            t.w = me
            t.r = {}

    def op(self, e, fn, r=(), w=()):
        self._wait(e, self._deps(r, w))
        inst = fn(self.eng[e])
        self.cnt[e] += 1
        inst.then_inc(self.sem[e], 1)
        self._mark((e, self.cnt[e]), r, w)
        self.ninst += 1

    def dma(self, q, out, in_, r=(), w=()):
        i = self.dcnt[q]
        self.dcnt[q] += 1
        slot = i % self.RING
        key = ("d", q, slot)
        val = 16 * (i // self.RING + 1)
        deps = self._deps(r, w)
        if val > 16:
            deps.append((key, val - 16))
        self._wait(q, deps)
        self.eng[q].dma_start(out=out, in_=in_).then_inc(self.sem[key], 16)
        self.cnt[key] = val
        self._mark((key, val), r, w)
        self.ninst += 1

    def barrier(self):
        allv = [(k, v) for k, v in self.cnt.items() if v > 0]
        for e in self.eng:
            self._wait(e, allv)

    def finish(self):
        allv = [(k, v) for k, v in self.cnt.items() if v > 0]
        self._wait("sp", allv)


def rope_tables(L, half):
    inv = ROPE_THETA ** (-np.arange(half, dtype=np.float32) / half)
    ang = np.arange(L, dtype=np.float32)[:, None] * inv[None, :]
    return np.cos(ang).astype(np.float32), np.sin(ang).astype(np.float32)


def tok_layout(a):
    L, Fd = a.shape
    return np.ascontiguousarray(a.reshape(L // 128, 128, Fd).transpose(1, 0, 2))


def band_mask(lo_off, hi_off):
    i = np.arange(128)[:, None]
    c = np.arange(384)[None, :]
    ok = (c >= i + lo_off) & (c <= i + hi_off)
    return np.where(ok, 0.0, NEG).astype(np.float32)


class KB:
    def __init__(self, seq_lens, dbg=False):
        self.seq_lens = list(seq_lens)
        self.NS = len(seq_lens)
        self.NTOK = sum(seq_lens)
        self.LMAX = max(seq_lens)
        self.dbg = dbg
        self.nc = bass.Bass("TRN2", target_bir_lowering=False)
        self.consts = {}

    def din(self, name, shape, dt=F32):
        return self.nc.dram_tensor(name, list(shape), dt, kind="ExternalInput").ap()

    def dscr(self, name, shape, dt=F32, out=False):
        kind = "ExternalOutput" if (out and self.dbg) else "Internal"
        return self.nc.dram_tensor(name, list(shape), dt, kind=kind).ap()

    def sb(self, es, name, shape, dt):
        self.uid = getattr(self, "uid", 0) + 1
        return es.enter_context(self.nc.sbuf_tensor("%s_%d" % (name, self.uid), list(shape), dt))

    def build(self):
        nc = self.nc
        NTOK, NS = self.NTOK, self.NS
        self.x = self.din("x", [NTOK, D])
        self.mem = self.din("mem", [NS * MEM_TOKENS, D])
        self.y = nc.dram_tensor("y", [NTOK, D], F32, kind="ExternalOutput").ap()
        self.x1 = self.dscr("x1", [NTOK, D])
        self.w = {}
        for nm, shp in (("e_w_in", [D, EVEN_IN]), ("e_w_out", [2560, D]), ("e_w_mem_kv", [D, 1024]),
                        ("o_w_in", [D, ODD_IN]), ("o_w_out", [2048, D]), ("o_w_mem_kv", [D, 1024])):
            self.w[nm] = self.din(nm, shp)
        for nm in ("e_pre_g", "e_mem_g", "o_pre_g", "o_mem_g"):
            self.w[nm] = self.din(nm, [128, 8])
        for nm in ("e_post_g", "o_post_g"):
            self.w[nm] = self.din(nm, [1, D])
        self.w["swa_sink"] = self.din("swa_sink", [1, 16])
        self.c_ident = self.din("c_ident", [128, 128])
        self.c_mask_dil = self.din("c_mask_dil", [128, 384])
        self.c_mask_swa = self.din("c_mask_swa", [128, 384])
        self.c_rope = {}
        for L in sorted(set(self.seq_lens)):
            T = L // 128
            self.c_rope[L] = (self.din("c_cos16_%d" % L, [128, T, 16]), self.din("c_sin16_%d" % L, [128, T, 16]),
                              self.din("c_cos8_%d" % L, [128, T, 8]), self.din("c_sin8_%d" % L, [128, T, 8]))
        self.YT = self.dscr("YT", [20 * 128, self.LMAX], BF16, out=True)
        self.OG = self.dscr("OG", [3, self.LMAX, 512], F32)
        TM = self.LMAX // 128
        for nm, shp in (("hy_filt_w1", [33, 64]), ("hy_filt_w2", [64, 64]), ("hy_filt_w3", [64, 2048]), ("hy_fvec", [64, 3]),
                        ("hy_conv_w", [128, 24, 3]), ("hy_conv_b", [128, 24]), ("hy_skip", [128, 8])):
            self.w[nm] = self.din(nm, shp)
        self.c_delta = self.din("c_delta", [1, 1024])
        self.c_dft, self.c_filt, self.H = {}, {}, {}
        for L in sorted(set(self.seq_lens)):
            T = L // 128
            self.c_dft[L] = (self.din("c_gc_%d" % L, [T, 128, T, 128], BF16), self.din("c_gs_%d" % L, [T, 128, T, 128], BF16))
            self.c_filt[L] = {"embT": self.din("c_embT_%d" % L, [33, L]), "tcol": self.din("c_tcol_%d" % L, [128, T]),
                              "psi": self.din("c_psi_%d" % L, [128, 2, T])}
            self.H[L] = (self.dscr("HR_%d" % L, [T, 128, 1024]), self.dscr("HI_%d" % L, [T, 128, 1024]))
        self.CFSF = (self.dscr("CF", [TM, 128, 1024]), self.dscr("SF", [TM, 128, 1024]))
        self.ZRS = (self.dscr("ZR", [TM, 128, 1024], BF16), self.dscr("ZS", [TM, 128, 1024], BF16))
        self.UT = self.dscr("UT", [1024, self.LMAX], BF16)
        self.AT = self.dscr("AT", [1024, self.LMAX], BF16)
        self.BT = self.dscr("BT", [1024, self.LMAX], BF16)
        self.t_cf, self.t_H, self.t_uab, self.t_Z = Tok(), Tok(), Tok(), Tok()
        for nm, shp in (("gdn_conv_w", [128, 24, 5]), ("gdn_A_log", [1, 16]), ("gdn_dt_bias", [1, 16]), ("gdn_norm_g", [1, 128])):
            self.w[nm] = self.din(nm, shp)
        self.c_gmask = self.din("c_gmask", [128, 18, 128])
        self.QT = self.dscr("QT", [1024, self.LMAX])
        self.KT = self.dscr("KT", [1024, self.LMAX])
        self.KTOK = self.dscr("KTOK", [self.LMAX, 1024])
        self.VTOK = self.dscr("VTOK", [self.LMAX, 1024])
        self.BG = self.dscr("BG", [self.LMAX, 32])
        self.SGG = self.dscr("SGG", [self.LMAX, 1024], BF16)
        self.OFB = self.dscr("OFB", [2, self.LMAX, 1024])
        self.G_UB = self.dscr("G_UB", [2, TM, 128, 1024])
        self.G_KD = self.dscr("G_KD", [2, TM, 128, 1024])
        self.G_WT = self.dscr("G_WT", [2, TM, 128, 8, 128])
        self.G_QKM = self.dscr("G_QKM", [2, TM, 128, 8, 128])
        self.G_SCC = self.dscr("G_SCC", [2, TM, 64, 2, 8])
        self.G_GLB = self.dscr("G_GLB", [2, TM, 128, 8, 2])
        self.t_gd, self.t_of, self.t_gp = Tok(), Tok(), Tok()
        self.LSE = self.dscr("LSE", [3, self.LMAX, 4], F32)

        with ExitStack() as es:
            self.es = es
            s = self.s = Sch(nc, es)
            self.ident32 = self.sb(es, "ident32", [128, 128], F32)
            self.ident16 = self.sb(es, "ident16", [128, 128], BF16)
            self.t_const = Tok()
            s.dma("sp", self.ident32[:], self.c_ident, w=[self.t_const])
            s.dma("pool", self.ident16[:], self.c_ident, w=[self.t_const])
            self.mask_dil = self.sb(es, "mask_dil", [128, 384], F32)
            self.mask_swa = self.sb(es, "mask_swa", [128, 384], F32)
            self.eps_col = self.sb(es, "eps_col", [128, 2], F32)
            s.op("dve", lambda e: e.memset(self.eps_col[:, 0:1], EPS), w=[self.t_const])
            s.op("dve", lambda e: e.memset(self.eps_col[:, 1:2], 1.0), w=[self.t_const])
            s.dma("sp", self.mask_dil[:], self.c_mask_dil, w=[self.t_const])
            s.dma("sp", self.mask_swa[:], self.c_mask_swa, w=[self.t_const])
            self.ps = es.enter_context(nc.psum_tensor("ps", [128, 4096], F32))
            self.ps_tok = [Tok() for _ in range(8)]
            self.ps_rr = 0
            self.t_og_dram = Tok()
            s.barrier()

            if getattr(self, "en_even", [1, 1, 1])[0]:
                for L in sorted(set(self.seq_lens)):
                    self.phase_filter(L)
            r0 = 0
            for si, L in enumerate(self.seq_lens):
                self.run_seq(si, r0, L)
                r0 += L
            s.finish()
        return nc

    def bank(self, b):
        return self.ps[:, b * 512:(b + 1) * 512]

    def bank16(self, b):
        return self.ps[:, b * 512:(b + 1) * 512].bitcast(BF16)

    def run_seq(self, si, r0, L):
        T = L // 128
        for layer in range(2):
            src = self.x if layer == 0 else self.x1
            dst = self.x1 if layer == 0 else self.y
            with ExitStack() as les:
                hT = self.sb(les, "hT", [128, 8, L], BF16)
                hT_tok = [Tok() for _ in range(T)]
                pre = "e_" if layer == 0 else "o_"
                self.phase_norm(lambda t: src[r0 + t * 128:r0 + (t + 1) * 128, :], T, self.w[pre + "pre_g"], hT, hT_tok)
                if layer == 0:
                    nch = 20
                    en = getattr(self, "en_even", [1, 1, 1])
                    if en[1]:
                        self.phase_gdn1(si, L, hT, hT_tok)
                    if en[2]:
                        self.phase_xattn(si, L, hT, hT_tok, "e_", E_XQ, E_GX, 16)
                    if en[0]:
                        self.phase_hy1(si, L, hT, hT_tok)
                    if not all(en):
                        self.zero_YT(L, [c for c in range(20) if not en[0 if c < 8 else (1 if c < 16 else 2)]])
                else:
                    nch = 16
                    en = getattr(self, "en_odd", [1, 1, 1])
                    if en[0]:
                        self.phase_dil(si, L, hT, hT_tok)
                    if en[1]:
                        self.phase_swa(si, L, hT, hT_tok)
                    if en[2]:
                        self.phase_xattn(si, L, hT, hT_tok, "o_", O_XQ, O_GX, 12)
                    if not all(en):
                        self.zero_YT(L, [c for c in range(16) if not en[0 if c < 4 else (1 if c < 12 else 2)]])
                self.s.barrier()
            if layer == 0 and getattr(self, "en_even", [1, 1, 1])[1]:
                self.phase_gdn2(si, L)
            if layer == 0 and getattr(self, "en_even", [1, 1, 1])[0]:
                self.phase_hy2(si, L)
            self.phase_out(r0, L, src, dst, pre, nch)

    def phase_norm(self, src, T, g_dram, hT, hT_tok, tag="n"):
        nc, s = self.nc, self.s
        with ExitStack() as es:
            gcol = self.sb(es, tag + "gcol", [128, 8], F32)
            gfull = self.sb(es, tag + "gfull", [128, 8, 128], F32)
            t_g = Tok()
            s.dma("sp", gcol[:], g_dram, w=[t_g])
            for k in range(8):
                s.op("dve", lambda e, k=k: e.tensor_scalar(out=gfull[:, k, :], in0=self.ident32[:], scalar1=0.0,
                                                             scalar2=gcol[:, k:k + 1], op0=ALU.mult, op1=ALU.add),
                     r=[t_g, self.t_const], w=[t_g])
            xt = [self.sb(es, tag + "xt%d" % i, [128, D], F32) for i in range(2)]
            xn = [self.sb(es, tag + "xn%d" % i, [128, D], BF16) for i in range(2)]
            st = [self.sb(es, tag + "st%d" % i, [128, 4], F32) for i in range(2)]
            junk = self.sb(es, tag + "junk", [128, D], BF16)
            t_xt = [Tok(), Tok()]
            t_xn = [Tok(), Tok()]
            t_st = [Tok(), Tok()]
            t_junk = Tok()
            for t in range(T):
                b = t % 2
                s.dma("sp", xt[b][:], src(t), w=[t_xt[b]])
                s.op("act", lambda e: e.activation(out=junk[:], in_=xt[b][:], func=AF.Square, accum_out=st[b][:, 0:1]),
                     r=[t_xt[b]], w=[t_junk, t_st[b]])
                s.op("dve", lambda e: e.tensor_scalar(out=st[b][:, 1:2], in0=st[b][:, 0:1], scalar1=1.0 / D, scalar2=EPS,
                                                      op0=ALU.mult, op1=ALU.add), r=[t_st[b]], w=[t_st[b]])
                s.op("act", lambda e: e.activation(out=st[b][:, 2:3], in_=st[b][:, 1:2], func=AF.Sqrt), r=[t_st[b]], w=[t_st[b]])
                s.op("dve", lambda e: e.reciprocal(out=st[b][:, 3:4], in_=st[b][:, 2:3]), r=[t_st[b]], w=[t_st[b]])
                s.op("act", lambda e: e.activation(out=xn[b][:], in_=xt[b][:], func=AF.Copy, scale=st[b][:, 3:4]),
                     r=[t_xt[b], t_st[b]], w=[t_xn[b]])
                pb = self.next_bank()
                pv = self.bank16(pb).rearrange("p (k c) -> p k c", k=8)
                for k in range(8):
                    s.op("pe", lambda e, k=k: e.transpose(out=pv[:, k, :], in_=xn[b][:, k * 128:(k + 1) * 128],
                                                          identity=self.ident16[:]),
                         r=[t_xn[b], self.t_const], w=[self.ps_tok[pb]])
                s.op("dve", lambda e: e.tensor_tensor(out=hT[:, :, t * 128:(t + 1) * 128], in0=pv, in1=gfull[:],
                                                      op=ALU.mult), r=[self.ps_tok[pb], t_g], w=[hT_tok[t]])
            s.barrier()

    def next_bank(self, n=1):
        b = self.ps_rr
        if n == 2 and b % 2:
            b += 1
        if b + n > 8:
            b = 0
        self.ps_rr = (b + n) % 8
        return b

    def pick(self, cls, banks):
        rr = self.__dict__.setdefault("_rr", {})
        i = rr.get(cls, 0)
        rr[cls] = i + 1
        return banks[i % len(banks)]

    def rope(self, src, t_src, dst, t_dst, tmp, t_tmp, cos, sin, t_tab, H, half, dh):
        s = self.s
        A = src[:, :, 0:half]
        B = src[:, :, half:2 * half]
        cb = cos.unsqueeze(1).to_broadcast([128, H, half])
        sb_ = sin.unsqueeze(1).to_broadcast([128, H, half])
        s.op("dve", lambda e: e.tensor_tensor(out=tmp[:, 0, 0:H, :], in0=A, in1=cb, op=ALU.mult), r=[t_src, t_tab], w=[t_tmp])
        s.op("dve", lambda e: e.tensor_tensor(out=tmp[:, 1, 0:H, :], in0=B, in1=sb_, op=ALU.mult), r=[t_src, t_tab], w=[t_tmp])
        s.op("dve", lambda e: e.tensor_tensor(out=tmp[:, 2, 0:H, :], in0=B, in1=cb, op=ALU.mult), r=[t_src, t_tab], w=[t_tmp])
        s.op("dve", lambda e: e.tensor_tensor(out=tmp[:, 3, 0:H, :], in0=A, in1=sb_, op=ALU.mult), r=[t_src, t_tab], w=[t_tmp])
        s.op("dve", lambda e: e.tensor_tensor(out=dst[:, :, 0:half], in0=tmp[:, 0, 0:H, :], in1=tmp[:, 1, 0:H, :], op=ALU.subtract),
             r=[t_tmp], w=[t_dst])
        s.op("dve", lambda e: e.tensor_tensor(out=dst[:, :, half:2 * half], in0=tmp[:, 2, 0:H, :], in1=tmp[:, 3, 0:H, :], op=ALU.add),
             r=[t_tmp], w=[t_dst])
        s.op("act", lambda e: e.activation(out=dst[:, :, 2 * half:dh], in_=src[:, :, 2 * half:dh], func=AF.Copy),
             r=[t_src], w=[t_dst])

    def softmax_gen(self, sbanks, nh, nk, mask_ap, t_mask, Sm, t_Sm, Pe, t_Pe, st, t_st, sink_ap=None, t_sink=None):
        s = self.s
        for j in range(nh):
            s.op("dve", lambda e, j=j: e.tensor_tensor(out=Sm[:, j, 0:nk], in0=self.bank(sbanks[j])[:, 0:nk], in1=mask_ap, op=ALU.add),
                 r=[self.ps_tok[sbanks[j]], t_mask], w=[t_Sm])
        yield
        if sink_ap is None:
            s.op("dve", lambda e: e.tensor_reduce(out=st[:, 0:nh], in_=Sm[:, 0:nh, 0:nk], op=ALU.max, axis=AX.X, negate=True),
                 r=[t_Sm], w=[t_st])
        else:
            s.op("dve", lambda e: e.tensor_reduce(out=st[:, 2 * nh:3 * nh], in_=Sm[:, 0:nh, 0:nk], op=ALU.max, axis=AX.X),
                 r=[t_Sm], w=[t_st])
            s.op("dve", lambda e: e.tensor_tensor(out=st[:, 2 * nh:3 * nh], in0=st[:, 2 * nh:3 * nh], in1=sink_ap, op=ALU.max),
                 r=[t_st, t_sink], w=[t_st])
            s.op("dve", lambda e: e.tensor_scalar(out=st[:, 0:nh], in0=st[:, 2 * nh:3 * nh], scalar1=-1.0, scalar2=None, op0=ALU.mult),
                 r=[t_st], w=[t_st])
            s.op("dve", lambda e: e.tensor_tensor(out=st[:, 3 * nh:4 * nh], in0=sink_ap, in1=st[:, 0:nh], op=ALU.add),
                 r=[t_st, t_sink], w=[t_st])
            s.op("act", lambda e: e.activation(out=st[:, 3 * nh:4 * nh], in_=st[:, 3 * nh:4 * nh], func=AF.Exp), r=[t_st], w=[t_st])
        yield
        for j in range(nh):
            s.op("act", lambda e, j=j: e.activation(out=Pe[:, j, 0:nk], in_=Sm[:, j, 0:nk], func=AF.Exp, bias=st[:, j:j + 1],
                                                    accum_out=st[:, nh + j:nh + j + 1]),
                 r=[t_Sm, t_st], w=[t_Pe, t_st])
        yield
        if sink_ap is not None:
            s.op("dve", lambda e: e.tensor_tensor(out=st[:, nh:2 * nh], in0=st[:, nh:2 * nh], in1=st[:, 3 * nh:4 * nh], op=ALU.add),
                 r=[t_st], w=[t_st])

    def softmax_block(self, *a, **k):
        for _ in self.softmax_gen(*a, **k):
            pass

    def transpose_P(self, Pe, t_Pe, nh, nkt, PT, t_PT, ptb):
        s = self.s
        ptv = self.bank16(ptb).rearrange("p (j c) -> p j c", j=8)
        n = 0
        for j in range(nh):
            for kc in range(nkt):
                s.op("pe", lambda e, j=j, kc=kc, n=n: e.transpose(out=ptv[:, n, :], in_=Pe[:, j, kc * 128:(kc + 1) * 128],
                                                                  identity=self.ident16[:]),
                     r=[t_Pe, self.t_const], w=[self.ps_tok[ptb]])
                n += 1
        s.op("act", lambda e: e.activation(out=PT[:, 0:n, :], in_=ptv[:, 0:n, :], func=AF.Copy), r=[self.ps_tok[ptb]], w=[t_PT])

    def load_w(self, wt, t_w, w_dram, c0, n):
        self.s.dma("pool", wt[:, :, 0:n], w_dram[:, c0:c0 + n].rearrange("(k p) c -> p k c", p=128), w=[t_w])

    def proj_feat(self, out_ap, pb, wt, t_w, wc0, hT, hT_tok, tok0, ntok, extra_r=()):
        s = self.s
        toks = [hT_tok[t] for t in range(tok0 // 128, (tok0 + ntok + 127) // 128)]
        for k in range(8):
            s.op("pe", lambda e, k=k: e.matmul(out_ap, wt[:, k, wc0:wc0 + 128], hT[:, k, tok0:tok0 + ntok],
                                               start=(k == 0), stop=(k == 7)),
                 r=[t_w] + toks + list(extra_r), w=[self.ps_tok[pb]])

    def proj_tok(self, out_ap, pb, wt, t_w, wc0, ncol, hT, hT_tok_list, tok_ap_fn):
        s = self.s
        for k in range(8):
            s.op("pe", lambda e, k=k: e.matmul(out_ap, tok_ap_fn(k), wt[:, k, wc0:wc0 + ncol],
                                               start=(k == 0), stop=(k == 7)),
                 r=[t_w] + list(hT_tok_list), w=[self.ps_tok[pb]])

    def phase_xattn(self, si, L, hT, hT_tok, pre, off_q, off_g, ch0):
        nc, s = self.nc, self.s
        T = L // 128
        w_in = self.w[pre + "w_in"]
        with ExitStack() as es:
            memT = self.sb(es, "memT", [128, 8, MEM_TOKENS], BF16)
            memT_tok = [Tok(), Tok()]
            self.phase_norm(lambda t: self.mem[si * MEM_TOKENS + t * 128: si * MEM_TOKENS + (t + 1) * 128, :], 2,
                            self.w[pre + "mem_g"], memT, memT_tok, tag="m")
            wkv = self.sb(es, "wkv", [128, 8, 1024], BF16)
            t_wkv = Tok()
            self.load_w(wkv, t_wkv, self.w[pre + "w_mem_kv"], 0, 1024)
            KmT = self.sb(es, "KmT", [128, 4, MEM_TOKENS], BF16)
            Vm = self.sb(es, "Vm", [128, 2, 512], BF16)
            t_km, t_vm = Tok(), Tok()
            for h in range(4):
                pb = self.next_bank()
                self.proj_feat(self.bank(pb)[:, 0:256], pb, wkv, t_wkv, h * 128, memT, memT_tok, 0, 256)
                s.op("act", lambda e: e.activation(out=KmT[:, h, :], in_=self.bank(pb)[:, 0:256], func=AF.Copy),
                     r=[self.ps_tok[pb]], w=[t_km])
            for mt in range(2):
                pb = self.next_bank()
                self.proj_tok(self.bank(pb), pb, wkv, t_wkv, 512, 512, memT, memT_tok,
                              lambda k: memT[:, k, mt * 128:(mt + 1) * 128])
                s.op("act", lambda e: e.activation(out=Vm[:, mt, :], in_=self.bank(pb), func=AF.Copy),
                     r=[self.ps_tok[pb]], w=[t_vm])
            wq = self.sb(es, "wq", [128, 8, 512], BF16)
            wg = self.sb(es, "wg", [128, 8, 512], BF16)
            t_wq, t_wg = Tok(), Tok()
            self.load_w(wq, t_wq, w_in, off_q, 512)
            self.load_w(wg, t_wg, w_in, off_g, 512)
            qT = self.sb(es, "qT", [128, 4, 512], BF16)
            t_qT = Tok()
            sg = [self.sb(es, "sg%d" % i, [128, 512], F32) for i in range(2)]
            t_sg = [Tok(), Tok()]
            Sm = [self.sb(es, "Sm%d" % i, [128, 4, 256], F32) for i in range(2)]
            Pe = [self.sb(es, "Pe%d" % i, [128, 4, 256], BF16) for i in range(2)]
            t_Pe = [Tok(), Tok()]
            stt = [self.sb(es, "stt%d" % i, [128, 16], F32) for i in range(2)]
            t_stt = [Tok(), Tok()]
            PT = [self.sb(es, "PT%d" % i, [128, 8, 128], BF16) for i in range(2)]
            t_PT = [Tok(), Tok()]
            yx = [self.sb(es, "yx%d" % i, [128, 512], BF16) for i in range(2)]
            t_yx = [Tok(), Tok()]
            ystage = [self.sb(es, "ystage%d" % i, [128, 4, 512], BF16) for i in range(2)]
            t_ys = [Tok(), Tok()]
            scale = 128.0 ** -0.5
            nblk = (L + 511) // 512
            for blk in range(nblk):
                tok0 = blk * 512
                ntok = min(512, L - tok0)
                yb = blk % 2
                for h in range(4):
                    pb = self.next_bank()
                    self.proj_feat(self.bank(pb)[:, 0:ntok], pb, wq, t_wq, h * 128, hT, hT_tok, tok0, ntok)
                    s.op("act", lambda e: e.activation(out=qT[:, h, 0:ntok], in_=self.bank(pb)[:, 0:ntok], func=AF.Copy,
                                                       scale=scale), r=[self.ps_tok[pb]], w=[t_qT])
                for tl in range(ntok // 128):
                    t = blk * 4 + tl
                    b = t % 2
                    pg = self.next_bank()
                    self.proj_tok(self.bank(pg), pg, wg, t_wg, 0, 512, hT, [hT_tok[t]],
                                  lambda k: hT[:, k, t * 128:(t + 1) * 128])
                    s.op("act", lambda e: e.activation(out=sg[b][:], in_=self.bank(pg), func=AF.Silu),
                         r=[self.ps_tok[pg]], w=[t_sg[b]])
                    p2 = self.next_bank(2)
                    sv = self.ps[:, p2 * 512:(p2 + 2) * 512].rearrange("p (h m) -> p h m", h=4)
                    for h in range(4):
                        pbh = p2 + h // 2
                        s.op("pe", lambda e, h=h: e.matmul(sv[:, h, :], qT[:, h, tl * 128:(tl + 1) * 128], KmT[:, h, :],
                                                           start=True, stop=True),
                             r=[t_qT, t_km], w=[self.ps_tok[pbh]])
                    s.op("dve", lambda e: e.tensor_reduce(out=stt[b][:, 0:4], in_=sv, op=ALU.max, axis=AX.X, negate=True),
                         r=[self.ps_tok[p2], self.ps_tok[p2 + 1]], w=[t_stt[b]])
                    for h in range(4):
                        s.op("act", lambda e, h=h: e.activation(out=Pe[b][:, h, :], in_=sv[:, h, :], func=AF.Exp,
                                                                bias=stt[b][:, h:h + 1], accum_out=stt[b][:, 4 + h:5 + h]),
                             r=[self.ps_tok[p2], self.ps_tok[p2 + 1], t_stt[b]], w=[t_Pe[b], t_stt[b]])
                    s.op("dve", lambda e: e.reciprocal(out=stt[b][:, 8:12], in_=stt[b][:, 4:8]), r=[t_stt[b]], w=[t_stt[b]])
                    pt = self.next_bank()
                    ptv = self.bank16(pt).rearrange("p (j c) -> p j c", j=8)
                    for h in range(4):
                        for mc in range(2):
                            s.op("pe", lambda e, h=h, mc=mc: e.transpose(out=ptv[:, h * 2 + mc, :],
                                                                         in_=Pe[b][:, h, mc * 128:(mc + 1) * 128],
                                                                         identity=self.ident16[:]),
                                 r=[t_Pe[b], self.t_const], w=[self.ps_tok[pt]])
                    s.op("act", lambda e: e.activation(out=PT[b][:], in_=ptv, func=AF.Copy), r=[self.ps_tok[pt]], w=[t_PT[b]])
                    po = self.next_bank()
                    for h in range(4):
                        for mc in range(2):
                            s.op("pe", lambda e, h=h, mc=mc: e.matmul(self.bank(po)[:, h * 128:(h + 1) * 128],
                                                                      PT[b][:, h * 2 + mc, :], Vm[:, mc, h * 128:(h + 1) * 128],
                                                                      start=(mc == 0), stop=(mc == 1)),
                                 r=[t_PT[b], t_vm], w=[self.ps_tok[po]])
                    for h in range(4):
                        s.op("dve", lambda e, h=h: e.scalar_tensor_tensor(out=yx[b][:, h * 128:(h + 1) * 128],
                                                                          in0=self.bank(po)[:, h * 128:(h + 1) * 128],
                                                                          scalar=stt[b][:, 8 + h:9 + h], in1=sg[b][:, h * 128:(h + 1) * 128],
                                                                          op0=ALU.mult, op1=ALU.mult),
                             r=[self.ps_tok[po], t_stt[b], t_sg[b]], w=[t_yx[b]])
                    self.to_ystage(yx[b], t_yx[b], 4, ystage[yb], t_ys[yb], tl)
                self.store_ystage(ystage[yb], t_ys[yb], 4, ch0, tok0, ntok)
            s.barrier()

    def to_ystage(self, ytile, t_y, nch, ystage, t_ys, tl, banks=None):
        s = self.s
        for c0 in range(0, nch, 8):
            n = min(8, nch - c0)
            pt = self.next_bank() if banks is None else self.pick("pt", banks)
            ptv = self.bank16(pt).rearrange("p (j c) -> p j c", j=8)
            for c in range(n):
                s.op("pe", lambda e, c=c: e.transpose(out=ptv[:, c, :], in_=ytile[:, (c0 + c) * 128:(c0 + c + 1) * 128],
                                                      identity=self.ident16[:]),
                     r=[t_y, self.t_const], w=[self.ps_tok[pt]])
            s.op("act", lambda e: e.activation(out=ystage[:, c0:c0 + n, tl * 128:(tl + 1) * 128], in_=ptv[:, 0:n, :],
                                               func=AF.Copy), r=[self.ps_tok[pt]], w=[t_ys])

    def store_ystage(self, ystage, t_ys, nch, ch0, tok0, ntok):
        dst = self.YT[ch0 * 128:(ch0 + nch) * 128, tok0:tok0 + ntok].rearrange("(c p) n -> p c n", p=128)
        self.s.dma("sp", dst, ystage[:, 0:nch, 0:ntok], r=[t_ys])

    def phase_swa(self, si, L, hT, hT_tok):
        nc, s = self.nc, self.s
        T = L // 128
        w_in = self.w["o_w_in"]
        with ExitStack() as es:
            wkv = self.sb(es, "swkv", [128, 8, 256], BF16)
            wq = self.sb(es, "swq", [128, 8, 1024], BF16)
            wg = self.sb(es, "swg", [128, 8, 1024], BF16)
            t_wkv, t_wq, t_wg = Tok(), Tok(), Tok()
            self.load_w(wkv, t_wkv, w_in, O_DKV, 256)
            self.load_w(wq, t_wq, w_in, O_DQ, 1024)
            self.load_w(wg, t_wg, w_in, O_GD, 1024)
            cos8 = self.sb(es, "cos8", [128, T, 8], F32)
            sin8 = self.sb(es, "sin8", [128, T, 8], F32)
            t_tab = Tok()
            s.dma("sp", cos8[:], self.c_rope[L][2], w=[t_tab])
            s.dma("sp", sin8[:], self.c_rope[L][3], w=[t_tab])
            sink = self.sb(es, "sink", [128, 16], F32)
            t_sink = Tok()
            s.dma("sp", sink[:], self.w["swa_sink"].partition_broadcast(128), w=[t_sink])
            kTd = self.sb(es, "kTd", [128, 2, L], BF16)
            t_kT = [Tok() for _ in range(T)]
            vtok = self.sb(es, "vtok", [128, T, 128], BF16)
            t_v = [Tok() for _ in range(T)]
            raw = [self.sb(es, "sraw%d" % i, [128, 16, 64], F32) for i in range(2)]
            t_raw = [Tok(), Tok()]
            rtmp = self.sb(es, "srtmp", [128, 4, 16, 8], F32)
            t_rtmp = Tok()
            kr = self.sb(es, "skr", [128, 2, 64], BF16)
            t_kr = Tok()
            kd = self.sb(es, "skd", [128, 2, 2, 64], BF16)
            t_kd = Tok()
            for t in range(T):
                b = t % 2
                pb = self.pick("proj", [2, 3])
                self.proj_tok(self.bank(pb)[:, 0:256], pb, wkv, t_wkv, 0, 256, hT, [hT_tok[t]],
                              lambda k: hT[:, k, t * 128:(t + 1) * 128])
                s.op("act", lambda e: e.activation(out=raw[b][:, 0:2, :], in_=self.bank(pb)[:, 0:128].rearrange("p (h d) -> p h d", h=2),
                                                   func=AF.Copy), r=[self.ps_tok[pb]], w=[t_raw[b]])
                s.op("act", lambda e: e.activation(out=vtok[:, t, :], in_=self.bank(pb)[:, 128:256], func=AF.Copy),
                     r=[self.ps_tok[pb]], w=[t_v[t]])
                self.rope(raw[b][:, 0:2, :], t_raw[b], kr[:], t_kr, rtmp, t_rtmp, cos8[:, t, :], sin8[:, t, :], t_tab, 2, 8, 64)
                for r_ in range(2):
                    s.op("dve", lambda e, r_=r_: e.tensor_copy(out=kd[:, :, r_, :], in_=kr[:]), r=[t_kr], w=[t_kd])
                pt = self.pick("pt", [6, 7])
                ptv = self.bank16(pt).rearrange("p (j c) -> p j c", j=8)
                for kv in range(2):
                    s.op("pe", lambda e, kv=kv: e.transpose(out=ptv[:, kv, :], in_=kd[:, kv, :, :].rearrange("p r d -> p (r d)"),
                                                            identity=self.ident16[:]), r=[t_kd, self.t_const], w=[self.ps_tok[pt]])
                s.op("act", lambda e: e.activation(out=kTd[:, :, t * 128:(t + 1) * 128], in_=ptv[:, 0:2, :], func=AF.Copy),
                     r=[self.ps_tok[pt]], w=[t_kT[t]])
            q16 = [self.sb(es, "sq16%d" % i, [128, 16, 64], BF16) for i in range(2)]
            t_q16 = [Tok(), Tok()]
            qT = [self.sb(es, "sqT%d" % i, [128, 8, 128], BF16) for i in range(2)]
            t_qT = [Tok(), Tok()]
            sgd = [self.sb(es, "sgd%d" % i, [128, 1024], F32) for i in range(2)]
            t_sgd = [Tok(), Tok()]
            Sm = [self.sb(es, "sSm%d" % i, [128, 2, 384], F32) for i in range(2)]
            t_Sm = [Tok(), Tok()]
            Pe = [self.sb(es, "sPe%d" % i, [128, 2, 384], BF16) for i in range(2)]
            t_Pe = [Tok(), Tok()]
            stt = [self.sb(es, "sst%d" % i, [128, 8], F32) for i in range(2)]
            t_stt = [Tok(), Tok()]
            PT = [self.sb(es, "sPT%d" % i, [128, 8, 128], BF16) for i in range(2)]
            t_PT = [Tok(), Tok()]
            dens = [self.sb(es, "sdens%d" % i, [128, 32], F32) for i in range(2)]
            t_dens = [Tok(), Tok()]
            densl = [[self.sb(es, "sdensl%d_%d" % (ln, i), [128, 8], F32) for i in range(2)] for ln in range(2)]
            t_densl = [[Tok(), Tok()], [Tok(), Tok()]]
            yd = [self.sb(es, "syd%d" % i, [128, 1024], BF16) for i in range(2)]
            t_yd = [Tok(), Tok()]
            ystage = [self.sb(es, "systage%d" % i, [128, 8, 512], BF16) for i in range(2)]
            t_ys = [Tok(), Tok()]
            for t in range(T):
                b = t % 2
                blk, tl = t // 4, t % 4
                yb = blk % 2
                for half in range(2):
                    pb = 2 + half
                    self.proj_tok(self.bank(pb), pb, wq, t_wq, half * 512, 512, hT, [hT_tok[t]],
                                  lambda k: hT[:, k, t * 128:(t + 1) * 128])
                    s.op("act", lambda e: e.activation(out=raw[b][:, half * 8:(half + 1) * 8, :],
                                                       in_=self.bank(pb).rearrange("p (h d) -> p h d", h=8), func=AF.Copy,
                                                       scale=0.125), r=[self.ps_tok[pb]], w=[t_raw[b]])
                self.rope(raw[b][:], t_raw[b], q16[b][:], t_q16[b], rtmp, t_rtmp, cos8[:, t, :], sin8[:, t, :], t_tab, 16, 8, 64)
                pt = self.pick("pt", [6, 7])
                ptv = self.bank16(pt).rearrange("p (j c) -> p j c", j=8)
                for c in range(8):
                    s.op("pe", lambda e, c=c: e.transpose(out=ptv[:, c, :], in_=q16[b][:, 2 * c:2 * c + 2, :].rearrange("p h d -> p (h d)"),
                                                          identity=self.ident16[:]), r=[t_q16[b], self.t_const], w=[self.ps_tok[pt]])
                s.op("act", lambda e: e.activation(out=qT[b][:], in_=ptv, func=AF.Copy), r=[self.ps_tok[pt]], w=[t_qT[b]])
                for half in range(2):
                    pb = 2 + half
                    self.proj_tok(self.bank(pb), pb, wg, t_wg, half * 512, 512, hT, [hT_tok[t]],
                                  lambda k: hT[:, k, t * 128:(t + 1) * 128])
                    s.op("act", lambda e: e.activation(out=sgd[b][:, half * 512:(half + 1) * 512], in_=self.bank(pb), func=AF.Silu),
                         r=[self.ps_tok[pb]], w=[t_sgd[b]])
                kts = [kt for kt in (t - 1, t, t + 1) if 0 <= kt < T]
                nkt = len(kts)
                nk = 128 * nkt
                m0 = (kts[0] - (t - 1)) * 128
                def swa_lane(ln):
                    pp = ln
                    sb0 = 4 if ln == 0 else 2
                    ptb = 6 + ln
                    for c in range(4 * ln, 4 * ln + 4):
                        kv = c // 4
                        for j in range(2):
                            s.op("pe", lambda e, j=j: e.matmul(self.bank(sb0 + j)[:, 0:nk], qT[b][j * 64:(j + 1) * 64, c, :],
                                                               kTd[j * 64:(j + 1) * 64, kv, kts[0] * 128:kts[0] * 128 + nk],
                                                               start=True, stop=True),
                                 r=[t_qT[b]] + [t_kT[kt] for kt in kts], w=[self.ps_tok[sb0 + j]])
                        yield
                        yield from self.softmax_gen([sb0, sb0 + 1], 2, nk, self.mask_swa[:, m0:m0 + nk], self.t_const, Sm[pp], t_Sm[pp], Pe[pp], t_Pe[pp],
                                                    stt[pp], t_stt[pp], sink_ap=sink[:, 2 * c:2 * c + 2], t_sink=t_sink)
                        s.op("dve", lambda e: e.tensor_copy(out=densl[ln][b][:, 2 * (c % 4):2 * (c % 4) + 2], in_=stt[pp][:, 2:4]), r=[t_stt[pp]], w=[t_densl[ln][b]])
                        yield
                        self.transpose_P(Pe[pp], t_Pe[pp], 2, nkt, PT[pp], t_PT[pp], ptb)
                        yield
                        for j in range(2):
                            hq = 2 * c + j
                            ob = hq // 8
                            for kc in range(nkt):
                                s.op("pe", lambda e, j=j, kc=kc: e.matmul(self.bank(ob)[:, (hq % 8) * 64:(hq % 8 + 1) * 64],
                                                                          PT[pp][:, j * nkt + kc, :], vtok[:, kts[kc], kv * 64:(kv + 1) * 64],
                                                                          start=(kc == 0), stop=(kc == nkt - 1)),
                                     r=[t_PT[pp]] + [t_v[kt] for kt in kts], w=[self.ps_tok[ob]])
                        yield

                self.lockstep([swa_lane(0), swa_lane(1)])
                for ln in range(2):
                    s.op("dve", lambda e, ln=ln: e.tensor_copy(out=dens[b][:, 8 * ln:8 * ln + 8], in_=densl[ln][b][:]), r=[t_densl[ln][b]], w=[t_dens[b]])
                s.op("dve", lambda e: e.reciprocal(out=dens[b][:, 16:32], in_=dens[b][:, 0:16]), r=[t_dens[b]], w=[t_dens[b]])
                for hq in range(16):
                    ob = hq // 8
                    s.op("dve", lambda e, hq=hq: e.scalar_tensor_tensor(out=yd[b][:, hq * 64:(hq + 1) * 64],
                                                                        in0=self.bank(ob)[:, (hq % 8) * 64:(hq % 8 + 1) * 64],
                                                                        scalar=dens[b][:, 16 + hq:17 + hq], in1=sgd[b][:, hq * 64:(hq + 1) * 64],
                                                                        op0=ALU.mult, op1=ALU.mult),
                         r=[self.ps_tok[ob], t_dens[b], t_sgd[b]], w=[t_yd[b]])
                self.to_ystage(yd[b], t_yd[b], 8, ystage[yb], t_ys[yb], tl, banks=[6, 7])
                if tl == 3 or t == T - 1:
                    self.store_ystage(ystage[yb], t_ys[yb], 8, 4, blk * 512, (tl + 1) * 128)
            s.barrier()

    def dft_fwd(self, L, x_tok, t_x, ncols, cb):
        s = self.s
        T = L // 128
        gc_d, gs_d = self.c_dft[L]
        with ExitStack() as es:
            tc = [self.sb(es, "tc%d" % i, [128, T, 128], BF16) for i in range(2)]
            ts = [self.sb(es, "ts%d" % i, [128, T, 128], BF16) for i in range(2)]
            t_tab = [Tok(), Tok()]
            for kc in range(T):
                b = kc % 2
                s.dma("sp", tc[b][:], gc_d[kc], w=[t_tab[b]])
                s.dma("sp", ts[b][:], gs_d[kc], w=[t_tab[b]])
                for half in range(ncols // 512):
                    bC = self.pick("dftC", [0, 2])
                    bS = bC + 1
                    for nci in range(T):
                        s.op("pe", lambda e, nci=nci: e.matmul(self.bank(bC), tc[b][:, nci, :], x_tok[:, nci, half * 512:(half + 1) * 512],
                                                               start=(nci == 0), stop=(nci == T - 1)),
                             r=[t_tab[b], t_x], w=[self.ps_tok[bC]])
                    for nci in range(T):
                        s.op("pe", lambda e, nci=nci: e.matmul(self.bank(bS), ts[b][:, nci, :], x_tok[:, nci, half * 512:(half + 1) * 512],
                                                               start=(nci == 0), stop=(nci == T - 1)),
                             r=[t_tab[b], t_x], w=[self.ps_tok[bS]])
                    cb(kc, half, bC, bS)
            s.barrier()

    def phase_filter(self, L):
        nc, s = self.nc, self.s
        T = L // 128
        HR, HI = self.H[L]
        CF, SF = self.CFSF
        c = self.c_filt[L]
        with ExitStack() as es:
            embT = self.sb(es, "embT", [33, L], F32)
            w1 = self.sb(es, "fw1", [33, 64], F32)
            w2 = self.sb(es, "fw2", [64, 64], F32)
            w3 = self.sb(es, "fw3", [64, 2048], F32)
            vec = self.sb(es, "fvec", [64, 3], F32)
            hid1 = self.sb(es, "hid1", [64, L], F32)
            hid2 = self.sb(es, "hid2", [64, L], F32)
            tcol = self.sb(es, "tcol", [128, T], F32)
            dbc = self.sb(es, "dbc", [128, 1024], F32)
            psi = self.sb(es, "psi", [128, 2, T], F32)
            t_c = Tok()
            s.dma("sp", embT[:], c["embT"], w=[t_c])
            s.dma("sp", w1[:], self.w["hy_filt_w1"], w=[t_c])
            s.dma("sp", w2[:], self.w["hy_filt_w2"], w=[t_c])
            s.dma("sp", w3[:], self.w["hy_filt_w3"], w=[t_c])
            s.dma("sp", vec[:], self.w["hy_fvec"], w=[t_c])
            s.dma("sp", tcol[:], c["tcol"], w=[t_c])
            s.dma("sp", dbc[:], self.c_delta.partition_broadcast(128), w=[t_c])
            s.dma("sp", psi[:], c["psi"], w=[t_c])
            arg = [self.sb(es, "farg%d" % i, [64, 512], F32) for i in range(2)]
            sn = [self.sb(es, "fsn%d" % i, [64, 512], F32) for i in range(2)]
            t_arg = [Tok(), Tok()]
            t_hid = Tok()
            for layer_i, (wm, kdim, src, dst, bcol) in enumerate(((w1, 33, embT, hid1, 0), (w2, 64, hid1, hid2, 1))):
                for blk in range(L // 512):
                    b = blk % 2
                    pb = self.pick("f", [4, 5])
                    s.op("pe", lambda e: e.matmul(self.bank(pb)[0:64, :], wm[0:kdim, :], src[0:kdim, blk * 512:(blk + 1) * 512],
                                                  start=True, stop=True), r=[t_c, t_hid], w=[self.ps_tok[pb]])
                    s.op("dve", lambda e: e.tensor_scalar(out=arg[b][:], in0=self.bank(pb)[0:64, :], scalar1=vec[:, bcol:bcol + 1],
                                                          scalar2=vec[:, 2:3], op0=ALU.add, op1=ALU.mult),
                         r=[self.ps_tok[pb], t_c], w=[t_arg[b]])
                    s.op("act", lambda e: e.activation(out=sn[b][:], in_=arg[b][:], func=AF.Sin, scale=1.0 / 3.0), r=[t_arg[b]], w=[t_arg[b]])
                    s.op("dve", lambda e: e.tensor_tensor(out=arg[b][:], in0=sn[b][:], in1=sn[b][:], op=ALU.mult), r=[t_arg[b]], w=[t_arg[b]])
                    s.op("dve", lambda e: e.tensor_scalar(out=arg[b][:], in0=arg[b][:], scalar1=-4.0, scalar2=3.0, op0=ALU.mult, op1=ALU.add),
                         r=[t_arg[b]], w=[t_arg[b]])
                    s.op("dve", lambda e: e.tensor_tensor(out=dst[:, blk * 512:(blk + 1) * 512], in0=sn[b][:], in1=arg[b][:], op=ALU.mult),
                         r=[t_arg[b]], w=[t_hid])
            filt = self.sb(es, "filt", [128, T, 1024], BF16)
            t_filt = Tok()
            dec = [self.sb(es, "fdec%d" % i, [128, 1024], F32) for i in range(2)]
            t_dec = [Tok(), Tok()]
            zc = [self.sb(es, "fzc%d" % i, [128, 4, 512], F32) for i in range(2)]
            t_zc = [Tok(), Tok()]
            ho = [self.sb(es, "fho%d" % i, [128, 2, 512], F32) for i in range(2)]
            t_ho = [Tok(), Tok()]
            for dirn in range(2):
                for t in range(T):
                    b = t % 2
                    s.op("act", lambda e: e.activation(out=dec[b][:], in_=dbc[:], func=AF.Exp, scale=tcol[:, t:t + 1]),
                         r=[t_c], w=[t_dec[b]])
                    for hb in range(2):
                        pb = self.pick("f", [4, 5])
                        s.op("pe", lambda e: e.matmul(self.bank(pb), hid2[:, t * 128:(t + 1) * 128],
                                                      w3[:, dirn * 1024 + hb * 512:dirn * 1024 + (hb + 1) * 512], start=True, stop=True),
                             r=[t_hid, t_c], w=[self.ps_tok[pb]])
                        s.op("dve", lambda e: e.tensor_tensor(out=filt[:, t, hb * 512:(hb + 1) * 512], in0=self.bank(pb),
                                                              in1=dec[b][:, hb * 512:(hb + 1) * 512], op=ALU.mult),
                             r=[self.ps_tok[pb], t_dec[b]], w=[t_filt])
                if dirn == 1:
                    s.op("dve", lambda e: e.memset(filt[0:1, 0, :], 0.0), w=[t_filt])

                def cb(kc, half, bC, bS, dirn=dirn):
                    b = (kc * 2 + half) % 2
                    hs = slice(half * 512, (half + 1) * 512)
                    if dirn == 0:
                        s.op("act", lambda e: e.activation(out=zc[b][:, 0, :], in_=self.bank(bC), func=AF.Copy), r=[self.ps_tok[bC]], w=[t_zc[b]])
                        s.op("act", lambda e: e.activation(out=zc[b][:, 1, :], in_=self.bank(bS), func=AF.Copy), r=[self.ps_tok[bS]], w=[t_zc[b]])
                        s.dma("sp", CF[kc, :, hs], zc[b][:, 0, :], r=[t_zc[b]], w=[self.t_cf])
                        s.dma("sp", SF[kc, :, hs], zc[b][:, 1, :], r=[t_zc[b]], w=[self.t_cf])
                    else:
                        s.dma("sp", zc[b][:, 0, :], CF[kc, :, hs], r=[self.t_cf], w=[t_zc[b]])
                        s.dma("sp", zc[b][:, 1, :], SF[kc, :, hs], r=[self.t_cf], w=[t_zc[b]])
                        Z = zc[b]
                        s.op("dve", lambda e: e.tensor_tensor(out=Z[:, 2, :], in0=Z[:, 0, :], in1=self.bank(bC), op=ALU.add),
                             r=[self.ps_tok[bC], t_zc[b]], w=[t_zc[b]])
                        s.op("dve", lambda e: e.tensor_tensor(out=Z[:, 0, :], in0=Z[:, 0, :], in1=self.bank(bC), op=ALU.subtract),
                             r=[self.ps_tok[bC], t_zc[b]], w=[t_zc[b]])
                        s.op("dve", lambda e: e.tensor_tensor(out=Z[:, 3, :], in0=Z[:, 1, :], in1=self.bank(bS), op=ALU.add),
                             r=[self.ps_tok[bS], t_zc[b]], w=[t_zc[b]])
                        s.op("dve", lambda e: e.tensor_tensor(out=Z[:, 1, :], in0=self.bank(bS), in1=Z[:, 1, :], op=ALU.subtract),
                             r=[self.ps_tok[bS], t_zc[b]], w=[t_zc[b]])
                        cps = psi[:, 0, kc:kc + 1]
                        sps = psi[:, 1, kc:kc + 1]
                        s.op("dve", lambda e: e.tensor_scalar(out=ho[b][:, 0, :], in0=Z[:, 2, :], scalar1=cps, scalar2=None, op0=ALU.mult),
                             r=[t_zc[b], t_c], w=[t_ho[b]])
                        s.op("dve", lambda e: e.scalar_tensor_tensor(out=ho[b][:, 0, :], in0=Z[:, 3, :], scalar=sps, in1=ho[b][:, 0, :],
                                                                     op0=ALU.mult, op1=ALU.add), r=[t_zc[b], t_c, t_ho[b]], w=[t_ho[b]])
                        s.op("dve", lambda e: e.tensor_scalar(out=ho[b][:, 1, :], in0=Z[:, 0, :], scalar1=sps, scalar2=None, op0=ALU.mult),
                             r=[t_zc[b], t_c], w=[t_ho[b]])
                        s.op("dve", lambda e: e.scalar_tensor_tensor(out=ho[b][:, 1, :], in0=Z[:, 1, :], scalar=cps, in1=ho[b][:, 1, :],
                                                                     op0=ALU.mult, op1=ALU.add), r=[t_zc[b], t_c, t_ho[b]], w=[t_ho[b]])
                        s.dma("sp", HR[kc, :, hs], ho[b][:, 0, :], r=[t_ho[b]], w=[self.t_H])
                        s.dma("sp", HI[kc, :, hs], ho[b][:, 1, :], r=[t_ho[b]], w=[self.t_H])

                self.dft_fwd(L, filt, t_filt, 1024, cb)
            s.barrier()

    def phase_hy1(self, si, L, hT, hT_tok):
        nc, s = self.nc, self.s
        w_in = self.w["e_w_in"]
        nblk = L // 512
        with ExitStack() as es:
            cw = self.sb(es, "hcw", [128, 24, 3], F32)
            cbias = self.sb(es, "hcb", [128, 24], F32)
            skip = self.sb(es, "hskip", [128, 8], F32)
            t_c = Tok()
            s.dma("sp", cw[:], self.w["hy_conv_w"], w=[t_c])
            s.dma("sp", cbias[:], self.w["hy_conv_b"], w=[t_c])
            s.dma("sp", skip[:], self.w["hy_skip"], w=[t_c])
            wts = [self.sb(es, "hw%d" % i, [128, 8, 4, 128], BF16) for i in range(2)]
            t_wts = [Tok(), Tok()]
            diag = [self.sb(es, "hdiag%d" % i, [128, 9, 128], BF16) for i in range(2)]
            t_diag = [Tok(), Tok()]
            zT = self.sb(es, "hzT", [128, 3, L + 2], BF16)
            t_zT = Tok()
            s.op("dve", lambda e: e.memset(zT[:, :, 0:1], 0.0), w=[t_zT])
            s.op("dve", lambda e: e.memset(zT[:, :, L + 1:L + 2], 0.0), w=[t_zT])
            xa = [self.sb(es, "hxa%d" % i, [128, 4, 512], F32) for i in range(2)]
            t_xa = [Tok(), Tok()]
            o16 = [self.sb(es, "ho16%d" % i, [128, 3, 512], BF16) for i in range(2)]
            t_o16 = [Tok(), Tok()]
            for c in range(8):
                wt, t_w = wts[c % 2], t_wts[c % 2]
                dg, t_dg = diag[c % 2], t_diag[c % 2]
                for a in range(4):
                    s.dma("pool", wt[:, :, a, :], w_in[:, a * 1024 + c * 128:a * 1024 + (c + 1) * 128].rearrange("(k p) c -> p k c", p=128),
                          w=[t_w])
                for a in range(3):
                    for j in range(3):
                        s.op("dve", lambda e, a=a, j=j: e.tensor_scalar(out=dg[:, a * 3 + j, :], in0=self.ident16[:],
                                                                        scalar1=cw[:, a * 8 + c, j:j + 1], scalar2=None, op0=ALU.mult),
                             r=[t_c, self.t_const], w=[t_dg])
                for blk in range(nblk):
                    for a in range(3):
                        pb = self.pick("proj", [2, 3])
                        self.proj_feat(self.bank(pb), pb, wt[:, :, a, :], t_w, 0, hT, hT_tok, blk * 512, 512)
                        s.op("act", lambda e, a=a: e.activation(out=zT[:, a, 1 + blk * 512:1 + (blk + 1) * 512], in_=self.bank(pb), func=AF.Copy),
                             r=[self.ps_tok[pb]], w=[t_zT])
                for blk in range(nblk):
                    b = blk % 2
                    X = xa[b]
                    for a in range(3):
                        pb = self.pick("conv", [4, 5])
                        for j in range(3):
                            s.op("pe", lambda e, a=a, j=j: e.matmul(self.bank(pb), dg[:, a * 3 + j, :], zT[:, a, blk * 512 + j:blk * 512 + j + 512],
                                                                    start=(j == 0), stop=(j == 2)), r=[t_dg, t_zT], w=[self.ps_tok[pb]])
                        s.op("act", lambda e, a=a: e.activation(out=X[:, a, :], in_=self.bank(pb), func=AF.Identity,
                                                                bias=cbias[:, a * 8 + c:a * 8 + c + 1]), r=[self.ps_tok[pb], t_c], w=[t_xa[b]])
                    pb = self.pick("proj", [2, 3])
                    self.proj_feat(self.bank(pb), pb, wt[:, :, 3, :], t_w, 0, hT, hT_tok, blk * 512, 512)
                    s.op("act", lambda e: e.activation(out=X[:, 3, :], in_=self.bank(pb), func=AF.Silu), r=[self.ps_tok[pb]], w=[t_xa[b]])
                    O = o16[b]
                    s.op("dve", lambda e: e.tensor_tensor(out=X[:, 2, :], in0=X[:, 2, :], in1=X[:, 1, :], op=ALU.mult), r=[t_xa[b]], w=[t_xa[b]])
                    s.op("pool", lambda e: e.tensor_tensor(out=X[:, 0, :], in0=X[:, 0, :], in1=X[:, 3, :], op=ALU.mult), r=[t_xa[b]], w=[t_xa[b]])
                    s.op("act", lambda e: e.activation(out=O[:, 0, :], in_=X[:, 2, :], func=AF.Copy), r=[t_xa[b]], w=[t_o16[b]])
                    s.op("act", lambda e: e.activation(out=O[:, 1, :], in_=X[:, 0, :], func=AF.Copy), r=[t_xa[b]], w=[t_o16[b]])
                    s.op("dve", lambda e: e.scalar_tensor_tensor(out=O[:, 2, :], in0=X[:, 2, :], scalar=skip[:, c:c + 1], in1=X[:, 0, :],
                                                                 op0=ALU.mult, op1=ALU.mult), r=[t_xa[b], t_c], w=[t_o16[b]])
                    for a, dr in enumerate((self.UT, self.AT, self.BT)):
                        s.dma("sp", dr[c * 128:(c + 1) * 128, blk * 512:(blk + 1) * 512], O[:, a, :], r=[t_o16[b]], w=[self.t_uab])
            s.barrier()

    def phase_hy2(self, si, L):
        nc, s = self.nc, self.s
        T = L // 128
        HR, HI = self.H[L]
        ZR, ZS = self.ZRS
        gc_d, gs_d = self.c_dft[L]
        with ExitStack() as es:
            u_tok = self.sb(es, "u_tok", [128, T, 1024], BF16)
            t_u = Tok()
            with ExitStack() as es2:
                ut = [self.sb(es2, "utl%d" % i, [128, L], BF16) for i in range(2)]
                t_ut = [Tok(), Tok()]
                for c in range(8):
                    b = c % 2
                    s.dma("sp", ut[b][:], self.UT[c * 128:(c + 1) * 128, 0:L], r=[self.t_uab], w=[t_ut[b]])
                    for t0 in range(0, T, 8):
                        n = min(8, T - t0)
                        pt = self.pick("pt", [6, 7])
                        ptv = self.bank16(pt).rearrange("p (j c) -> p j c", j=8)
                        for i in range(n):
                            s.op("pe", lambda e, i=i: e.transpose(out=ptv[:, i, :], in_=ut[b][:, (t0 + i) * 128:(t0 + i + 1) * 128],
                                                                  identity=self.ident16[:]), r=[t_ut[b], self.t_const], w=[self.ps_tok[pt]])
                        s.op("act", lambda e: e.activation(out=u_tok[:, t0:t0 + n, c * 128:(c + 1) * 128], in_=ptv[:, 0:n, :], func=AF.Copy),
                             r=[self.ps_tok[pt]], w=[t_u])
                s.barrier()
            hh = [self.sb(es, "hh%d" % i, [128, 2, 512], F32) for i in range(2)]
            t_hh = [Tok(), Tok()]
            cs = [self.sb(es, "cs%d" % i, [128, 2, 512], F32) for i in range(2)]
            t_cs = [Tok(), Tok()]
            tt = [self.sb(es, "tt%d" % i, [128, 4, 512], F32) for i in range(2)]
            t_tt = [Tok(), Tok()]
            zz = [self.sb(es, "zz%d" % i, [128, 2, 512], BF16) for i in range(2)]
            t_zz = [Tok(), Tok()]

            def cb(kc, half, bC, bS):
                b = (kc * 2 + half) % 2
                hs = slice(half * 512, (half + 1) * 512)
                s.dma("sp", hh[b][:, 0, :], HR[kc, :, hs], r=[self.t_H], w=[t_hh[b]])
                s.dma("sp", hh[b][:, 1, :], HI[kc, :, hs], r=[self.t_H], w=[t_hh[b]])
                s.op("act", lambda e: e.activation(out=cs[b][:, 0, :], in_=self.bank(bC), func=AF.Copy), r=[self.ps_tok[bC]], w=[t_cs[b]])
                s.op("act", lambda e: e.activation(out=cs[b][:, 1, :], in_=self.bank(bS), func=AF.Copy), r=[self.ps_tok[bS]], w=[t_cs[b]])
                TT = tt[b]
                s.op("dve", lambda e: e.tensor_tensor(out=TT[:, 0, :], in0=hh[b][:, 0, :], in1=cs[b][:, 0, :], op=ALU.mult), r=[t_hh[b], t_cs[b]], w=[t_tt[b]])
                s.op("pool", lambda e: e.tensor_tensor(out=TT[:, 1, :], in0=hh[b][:, 1, :], in1=cs[b][:, 1, :], op=ALU.mult), r=[t_hh[b], t_cs[b]], w=[t_tt[b]])
                s.op("pool", lambda e: e.tensor_tensor(out=TT[:, 2, :], in0=hh[b][:, 0, :], in1=cs[b][:, 1, :], op=ALU.mult), r=[t_hh[b], t_cs[b]], w=[t_tt[b]])
                s.op("dve", lambda e: e.tensor_tensor(out=TT[:, 3, :], in0=hh[b][:, 1, :], in1=cs[b][:, 0, :], op=ALU.mult), r=[t_hh[b], t_cs[b]], w=[t_tt[b]])
                s.op("dve", lambda e: e.tensor_tensor(out=zz[b][:, 0, :], in0=TT[:, 0, :], in1=TT[:, 1, :], op=ALU.add), r=[t_tt[b]], w=[t_zz[b]])
                s.op("pool", lambda e: e.tensor_tensor(out=zz[b][:, 1, :], in0=TT[:, 2, :], in1=TT[:, 3, :], op=ALU.subtract), r=[t_tt[b]], w=[t_zz[b]])
                s.dma("sp", ZR[kc, :, hs], zz[b][:, 0, :], r=[t_zz[b]], w=[self.t_Z])
                s.dma("sp", ZS[kc, :, hs], zz[b][:, 1, :], r=[t_zz[b]], w=[self.t_Z])

            self.dft_fwd(L, u_tok, t_u, 1024, cb)
        with ExitStack() as es:
            zr = self.sb(es, "zr", [128, T, 512], BF16)
            zs = self.sb(es, "zs", [128, T, 512], BF16)
            t_z = Tok()
            tc = [self.sb(es, "itc%d" % i, [128, T, 128], BF16) for i in range(2)]
            ts = [self.sb(es, "its%d" % i, [128, T, 128], BF16) for i in range(2)]
            t_tab = [Tok(), Tok()]
            y16 = [self.sb(es, "y16%d" % i, [128, 512], BF16) for i in range(2)]
            t_y16 = [Tok(), Tok()]
            ystage = [self.sb(es, "hystage%d" % i, [128, 4, 512], BF16) for i in range(2)]
            t_ys = [Tok(), Tok()]
            ab = [self.sb(es, "hab%d" % i, [128, 2, 4, 512], BF16) for i in range(2)]
            t_ab = [Tok(), Tok()]
            for ch in range(2):
                cs_ = slice(ch * 512, (ch + 1) * 512)
                s.dma("sp", zr[:], ZR[0:T, :, cs_].rearrange("k p c -> p k c"), r=[self.t_Z], w=[t_z])
                s.dma("sp", zs[:], ZS[0:T, :, cs_].rearrange("k p c -> p k c"), r=[self.t_Z], w=[t_z])
                for nci in range(T):
                    b = nci % 2
                    blk, tl = nci // 4, nci % 4
                    yb = blk % 2
                    if tl == 0:
                        rows = slice(ch * 512, (ch + 1) * 512)
                        s.dma("sp", ab[yb][:, 0, :, :], self.AT[rows, blk * 512:(blk + 1) * 512].rearrange("(c p) n -> p c n", p=128),
                              r=[self.t_uab], w=[t_ab[yb]])
                        s.dma("sp", ab[yb][:, 1, :, :], self.BT[rows, blk * 512:(blk + 1) * 512].rearrange("(c p) n -> p c n", p=128),
                              r=[self.t_uab], w=[t_ab[yb]])
                    s.dma("sp", tc[b][:], gc_d[nci], w=[t_tab[b]])
                    s.dma("sp", ts[b][:], gs_d[nci], w=[t_tab[b]])
                    pb = self.pick("inv", [0, 1, 2, 3])
                    for kc in range(T):
                        s.op("pe", lambda e, kc=kc: e.matmul(self.bank(pb), tc[b][:, kc, :], zr[:, kc, :], start=(kc == 0), stop=False),
                             r=[t_tab[b], t_z], w=[self.ps_tok[pb]])
                    for kc in range(T):
                        s.op("pe", lambda e, kc=kc: e.matmul(self.bank(pb), ts[b][:, kc, :], zs[:, kc, :], start=False, stop=(kc == T - 1)),
                             r=[t_tab[b], t_z], w=[self.ps_tok[pb]])
                    s.op("act", lambda e: e.activation(out=y16[b][:], in_=self.bank(pb), func=AF.Copy), r=[self.ps_tok[pb]], w=[t_y16[b]])
                    self.to_ystage(y16[b], t_y16[b], 4, ystage[yb], t_ys[yb], tl, banks=[6, 7])
                    if tl == 3:
                        s.op("dve", lambda e: e.tensor_tensor(out=ystage[yb][:], in0=ystage[yb][:], in1=ab[yb][:, 0, :, :], op=ALU.mult),
                             r=[t_ys[yb], t_ab[yb]], w=[t_ys[yb]])
                        s.op("dve", lambda e: e.tensor_tensor(out=ystage[yb][:], in0=ystage[yb][:], in1=ab[yb][:, 1, :, :], op=ALU.add),
                             r=[t_ys[yb], t_ab[yb]], w=[t_ys[yb]])
                        self.store_ystage(ystage[yb], t_ys[yb], 4, ch * 4, blk * 512, 512)
            s.barrier()

    def phase_gdn1(self, si, L, hT, hT_tok):
        nc, s = self.nc, self.s
        T = L // 128
        nblk = L // 512
        w_in = self.w["e_w_in"]
        with ExitStack() as es:
            cw = self.sb(es, "gcw", [128, 24, 5], F32)
            t_c = Tok()
            s.dma("sp", cw[:], self.w["gdn_conv_w"], w=[t_c])
            ones_r = self.sb(es, "ones_r", [128, 128], F32R)
            s.op("dve", lambda e: e.tensor_scalar(out=ones_r[:], in0=self.ident32[:], scalar1=0.0, scalar2=1.0, op0=ALU.mult, op1=ALU.add),
                 r=[self.t_const], w=[t_c])
            wts = [self.sb(es, "gw%d" % i, [128, 8, 3, 128], BF16) for i in range(2)]
            t_wts = [Tok(), Tok()]
            diag = [self.sb(es, "gdiag%d" % i, [128, 15, 128], BF16) for i in range(2)]
            t_diag = [Tok(), Tok()]
            zT = self.sb(es, "gzT", [128, 3, L + 4], BF16)
            t_zT = Tok()
            s.op("dve", lambda e: e.memset(zT[:, :, 0:2], 0.0), w=[t_zT])
            s.op("dve", lambda e: e.memset(zT[:, :, L + 2:L + 4], 0.0), w=[t_zT])
            xs = [self.sb(es, "gx%d" % i, [128, 3, 512], F32) for i in range(2)]
            t_xs = [Tok(), Tok()]
            sq = [self.sb(es, "gsq%d" % i, [128, 512], F32R) for i in range(2)]
            t_sq = [Tok(), Tok()]
            rs = [self.sb(es, "grs%d" % i, [128, 512], F32) for i in range(2)]
            t_rs = [Tok(), Tok()]
            kst = [self.sb(es, "gkst%d" % i, [128, 4, 128], F32) for i in range(2)]
            t_kst = [Tok(), Tok()]
            for h in range(8):
                wt, t_w = wts[h % 2], t_wts[h % 2]
                dg, t_dg = diag[h % 2], t_diag[h % 2]
                for a in range(3):
                    c0 = E_QKV + a * 1024 + h * 128
                    s.dma("pool", wt[:, :, a, :], w_in[:, c0:c0 + 128].rearrange("(k p) c -> p k c", p=128), w=[t_w])
                    for j in range(5):
                        s.op("dve", lambda e, a=a, j=j: e.tensor_scalar(out=dg[:, a * 5 + j, :], in0=self.ident16[:],
                                                                        scalar1=cw[:, a * 8 + h, j:j + 1], scalar2=None, op0=ALU.mult),
                             r=[t_c, self.t_const], w=[t_dg])
                for blk in range(nblk):
                    for a in range(3):
                        pb = self.pick("proj", [2, 3])
                        self.proj_feat(self.bank(pb), pb, wt[:, :, a, :], t_w, 0, hT, hT_tok, blk * 512, 512)
                        s.op("act", lambda e, a=a: e.activation(out=zT[:, a, 2 + blk * 512:2 + (blk + 1) * 512], in_=self.bank(pb), func=AF.Copy),
                             r=[self.ps_tok[pb]], w=[t_zT])
                for blk in range(nblk):
                    b = blk % 2
                    X = xs[b]
                    for a in range(3):
                        pb = self.pick("conv", [4, 5])
                        for j in range(5):
                            s.op("pe", lambda e, a=a, j=j: e.matmul(self.bank(pb), dg[:, a * 5 + j, :], zT[:, a, blk * 512 + j:blk * 512 + j + 512],
                                                                    start=(j == 0), stop=(j == 4)), r=[t_dg, t_zT], w=[self.ps_tok[pb]])
                        s.op("act", lambda e, a=a: e.activation(out=X[:, a, :], in_=self.bank(pb), func=AF.Silu), r=[self.ps_tok[pb]], w=[t_xs[b]])
                    for a in range(2):
                        bb = (blk * 2 + a) % 2
                        s.op("dve", lambda e, a=a: e.tensor_tensor(out=sq[bb][:], in0=X[:, a, :], in1=X[:, a, :], op=ALU.mult), r=[t_xs[b]], w=[t_sq[bb]])
                        pb = self.pick("ss", [0, 1])
                        s.op("pe", lambda e: e.matmul(self.bank(pb), ones_r[:], sq[bb][:], start=True, stop=True), r=[t_c, t_sq[bb]], w=[self.ps_tok[pb]])
                        s.op("act", lambda e: e.activation(out=rs[bb][:], in_=self.bank(pb), func=AF.Ln, bias=self.eps_col[:, 0:1]), r=[self.ps_tok[pb], self.t_const], w=[t_rs[bb]])
                        s.op("act", lambda e: e.activation(out=rs[bb][:], in_=rs[bb][:], func=AF.Exp, scale=-0.5), r=[t_rs[bb]], w=[t_rs[bb]])
                        sc = (128.0 ** -0.5) if a == 0 else 1.0
                        s.op("dve", lambda e, a=a: e.scalar_tensor_tensor(out=X[:, a, :], in0=X[:, a, :], scalar=sc, in1=rs[bb][:],
                                                                          op0=ALU.mult, op1=ALU.mult), r=[t_xs[b], t_rs[bb]], w=[t_xs[b]])
                        dr = self.QT if a == 0 else self.KT
                        s.dma("sp", dr[h * 128:(h + 1) * 128, blk * 512:(blk + 1) * 512], X[:, a, :], r=[t_xs[b]], w=[self.t_gd])
                    for a in (1, 2):
                        bb = (blk * 2 + a) % 2
                        pt = self.pick("pt", [6, 7])
                        ptv = self.bank(pt).rearrange("p (j c) -> p j c", j=4)
                        for i in range(4):
                            s.op("pe", lambda e, a=a, i=i: e.transpose(out=ptv[:, i, :], in_=X[:, a, i * 128:(i + 1) * 128], identity=self.ident32[:]),
                                 r=[t_xs[b], self.t_const], w=[self.ps_tok[pt]])
                        s.op("act", lambda e: e.activation(out=kst[bb][:], in_=ptv, func=AF.Copy), r=[self.ps_tok[pt]], w=[t_kst[bb]])
                        dr = self.KTOK if a == 1 else self.VTOK
                        s.dma("sp", dr[blk * 512:(blk + 1) * 512, h * 128:(h + 1) * 128].rearrange("(i p) d -> p i d", p=128), kst[bb][:],
                              r=[t_kst[bb]], w=[self.t_gd])
            wg = self.sb(es, "gwg", [128, 8, 1024], BF16)
            wbg = self.sb(es, "gwbg", [128, 8, 32], BF16)
            t_wg = Tok()
            self.load_w(wg, t_wg, w_in, E_GG, 1024)
            self.load_w(wbg, t_wg, w_in, E_BETA, 32)
            rows = self.sb(es, "grows", [128, 2, 16], F32)
            s.dma("sp", rows[:, 0, :], self.w["gdn_A_log"].partition_broadcast(128), w=[t_c])
            s.dma("sp", rows[:, 1, :], self.w["gdn_dt_bias"].partition_broadcast(128), w=[t_c])
            s.op("act", lambda e: e.activation(out=rows[:, 0, :], in_=rows[:, 0, :], func=AF.Exp), r=[t_c], w=[t_c])
            s.op("dve", lambda e: e.tensor_scalar(out=rows[:, 0, :], in0=rows[:, 0, :], scalar1=-1.0, scalar2=None, op0=ALU.mult), r=[t_c], w=[t_c])
            sg = [self.sb(es, "gsg%d" % i, [128, 1024], BF16) for i in range(2)]
            t_sg = [Tok(), Tok()]
            bgt = [self.sb(es, "gbgt%d" % i, [128, 4, 16], F32) for i in range(2)]
            t_bgt = [Tok(), Tok()]
            bgo = [self.sb(es, "gbgo%d" % i, [128, 32], F32) for i in range(2)]
            t_bgo = [Tok(), Tok()]
            for t in range(T):
                b = t % 2
                for half in range(2):
                    pb = 2 + half
                    self.proj_tok(self.bank(pb), pb, wg, t_wg, half * 512, 512, hT, [hT_tok[t]], lambda k: hT[:, k, t * 128:(t + 1) * 128])
                    s.op("act", lambda e: e.activation(out=sg[b][:, half * 512:(half + 1) * 512], in_=self.bank(pb), func=AF.Silu),
                         r=[self.ps_tok[pb]], w=[t_sg[b]])
                s.dma("sp", self.SGG[t * 128:(t + 1) * 128, :], sg[b][:], r=[t_sg[b]], w=[self.t_gd])
                pb = self.pick("conv", [4, 5])
                self.proj_tok(self.bank(pb)[:, 0:32], pb, wbg, t_wg, 0, 32, hT, [hT_tok[t]], lambda k: hT[:, k, t * 128:(t + 1) * 128])
                B = bgt[b]
                s.op("act", lambda e: e.activation(out=bgo[b][:, 0:16], in_=self.bank(pb)[:, 0:16], func=AF.Sigmoid), r=[self.ps_tok[pb]], w=[t_bgo[b]])
                s.op("dve", lambda e: e.tensor_tensor(out=B[:, 0, :], in0=self.bank(pb)[:, 16:32], in1=rows[:, 1, :], op=ALU.add), r=[self.ps_tok[pb], t_c], w=[t_bgt[b]])
                s.op("act", lambda e: e.activation(out=B[:, 1, :], in_=B[:, 0, :], func=AF.Abs), r=[t_bgt[b]], w=[t_bgt[b]])
                s.op("act", lambda e: e.activation(out=B[:, 1, :], in_=B[:, 1, :], func=AF.Exp, scale=-1.0), r=[t_bgt[b]], w=[t_bgt[b]])
                s.op("act", lambda e: e.activation(out=B[:, 1, :], in_=B[:, 1, :], func=AF.Ln, bias=self.eps_col[:, 1:2]), r=[t_bgt[b], self.t_const], w=[t_bgt[b]])
                s.op("dve", lambda e: e.tensor_scalar(out=B[:, 2, :], in0=B[:, 0, :], scalar1=0.0, scalar2=None, op0=ALU.max), r=[t_bgt[b]], w=[t_bgt[b]])
                s.op("dve", lambda e: e.tensor_tensor(out=B[:, 2, :], in0=B[:, 2, :], in1=B[:, 1, :], op=ALU.add), r=[t_bgt[b]], w=[t_bgt[b]])
                s.op("dve", lambda e: e.tensor_tensor(out=bgo[b][:, 16:32], in0=B[:, 2, :], in1=rows[:, 0, :], op=ALU.mult), r=[t_bgt[b], t_c], w=[t_bgo[b]])
                s.dma("sp", self.BG[t * 128:(t + 1) * 128, :], bgo[b][:], r=[t_bgo[b]], w=[self.t_gd])
            s.barrier()

    @staticmethod
    def lockstep(gens):
        gens = list(gens)
        while gens:
            for g in list(gens):
                try:
                    next(g)
                except StopIteration:
                    gens.remove(g)

    def phase_gdn2(self, si, L):
        self.phase_gdnP(L)
        self.phase_gdnR(L)
        self.phase_gdnC(L)

    def phase_gdnP(self, L):
        nc, s = self.nc, self.s
        T = L // 128
        QT3 = self.QT.rearrange("(h d) n -> d h n", d=128)
        KT3 = self.KT.rearrange("(h d) n -> d h n", d=128)
        with ExitStack() as es:
            cm = self.sb(es, "gmask", [128, 18, 128], F32)
            t_c = Tok()
            s.dma("sp", cm[:], self.c_gmask, w=[t_c])
            ones32 = self.sb(es, "ones32", [128, 128], F32)
            s.op("dve", lambda e: e.memset(ones32[:], 1.0), w=[t_c])
            ident_r = self.sb(es, "ident_r", [128, 128], F32R)
            s.op("dve", lambda e: e.tensor_copy(out=ident_r[:], in_=self.ident32[:]), r=[self.t_const], w=[t_c])
            idb = self.ident32[:].unsqueeze(1).to_broadcast([128, 4, 128])
            units = [(dirn, t, qd) for dirn in range(2) for t in range(T) for qd in range(2)]
            NL = 4

            def lane(li):
                def A(nm, dt=F32, shp=(128, 4, 128)):
                    return self.sb(es, "L%d%s" % (li, nm), list(shp), dt), Tok()
                lq, t_lq = A("lq")
                lk, t_lk = A("lk")
                qr, t_qr = A("qr", F32R)
                kr, t_kr = A("kr", F32R)
                lkt, t_lkt = A("lkt")
                lvt, t_lvt = A("lvt")
                ET, t_ET = A("ET")
                Bt, t_Bt = A("Bt")
                Ct, t_Ct = A("Ct")
                Bm, t_Bm = A("Bm", F32R)
                Cm, t_Cm = A("Cm", F32R)
                P, t_P = A("P", F32R)
                Q, t_Q = A("Q", F32R)
                Wn, t_Wn = A("Wn", F32R)
                Vn, t_Vn = A("Vn", F32R)
                QKm, t_QKm = A("QKm")
                ub, t_ub = A("ub")
                wT, t_wT = A("wT")
                kd, t_kd = A("kd")
                bg, t_bg = A("bg", F32, (128, 32))
                sc, t_sc = A("sc", F32, (128, 4, 4))
                scc, t_scc = A("scc", F32, (64, 2, 4))
                glb, t_glb = A("glb", F32, (128, 4, 2))
                Dm, t_Dm = lq, t_lq
                dgG, t_dgG = lk, t_lk
                vr, t_vr = qr, t_qr
                kg, t_kg = kr, t_kr
                bA, bB = 2 * li, 2 * li + 1
                pA, pB = self.ps_tok[bA], self.ps_tok[bB]
                v4 = lambda bk: self.bank(bk).rearrange("p (h c) -> p h c", h=4)
                for ui in range(li, len(units), NL):
                    dirn, t, qd = units[ui]
                    mo = 8 * dirn
                    mo2 = 8 * (1 - dirn)
                    hs = slice(qd * 4, qd * 4 + 4)
                    cs_ = slice(t * 128, (t + 1) * 128)
                    fs = slice(qd * 512, (qd + 1) * 512)
                    s.dma("sp", lq[:], QT3[:, hs, cs_], r=[self.t_gd], w=[t_lq])
                    s.dma("sp", lk[:], KT3[:, hs, cs_], r=[self.t_gd], w=[t_lk])
                    s.dma("sp", lkt[:], self.KTOK[cs_, fs].rearrange("p (h d) -> p h d", h=4), r=[self.t_gd], w=[t_lkt])
                    s.dma("sp", lvt[:], self.VTOK[cs_, fs].rearrange("p (h d) -> p h d", h=4), r=[self.t_gd], w=[t_lvt])
                    s.dma("sp", bg[:], self.BG[cs_, :], r=[self.t_gd], w=[t_bg])
                    yield
                    s.op("act", lambda e: e.activation(out=qr[:], in_=lq[:], func=AF.Copy), r=[t_lq], w=[t_qr])
                    s.op("act", lambda e: e.activation(out=kr[:], in_=lk[:], func=AF.Copy), r=[t_lk], w=[t_kr])
                    beta = bg[:, dirn * 8 + qd * 4:dirn * 8 + qd * 4 + 4]
                    gcol = bg[:, 16 + dirn * 8 + qd * 4:16 + dirn * 8 + qd * 4 + 4]
                    s.op("pe", lambda e: e.matmul(self.bank(bA)[:, 0:4], cm[:, 16 + dirn, :], gcol, start=True, stop=True),
                         r=[t_bg, t_c], w=[pA])
                    for cc in range(2):
                        s.op("pe", lambda e, cc=cc: e.matmul(self.bank(bA)[0:64, 16 + cc * 4:20 + cc * 4], cm[:, 16 + dirn, cc * 64:(cc + 1) * 64], gcol,
                                                             start=True, stop=True), r=[t_bg, t_c], w=[pA])
                    for hh in range(4):
                        s.op("pe", lambda e, hh=hh: e.matmul(self.bank(bB)[:, hh * 128:(hh + 1) * 128], kr[:, hh, :], kr[:, hh, :], start=True, stop=True),
                             r=[t_kr], w=[pB])
                    yield
                    s.op("dve", lambda e: e.tensor_copy(out=sc[:, :, 0], in_=self.bank(bA)[:, 0:4]), r=[pA], w=[t_sc])
                    s.op("act", lambda e: e.activation(out=sc[:, :, 1], in_=self.bank(bA)[:, 0:4], func=AF.Exp), r=[pA], w=[t_sc])
                    s.op("act", lambda e: e.activation(out=scc[:], in_=self.bank(bA)[0:64, 16:24].rearrange("p (c h) -> p c h", c=2), func=AF.Exp),
                         r=[pA], w=[t_scc])
                    yield
                    for hh in range(4):
                        s.op("dve", lambda e, hh=hh: e.tensor_scalar(out=dgG[:, hh, :], in0=self.ident32[:], scalar1=sc[:, hh, 0:1], scalar2=None, op0=ALU.mult),
                             r=[t_sc, self.t_const], w=[t_dgG])
                    s.op("pe", lambda e: e.matmul(self.bank(bA), ones32[:], dgG[:].rearrange("p h c -> p (h c)"), start=True, stop=True),
                         r=[t_c, t_dgG], w=[pA])
                    yield
                    gbc = v4(bA)
                    lastc = (63, 127) if dirn == 0 else (0, 64)
                    s.op("dve", lambda e: e.tensor_tensor(out=Dm[:], in0=gbc, in1=sc[:, :, 0:1].to_broadcast([128, 4, 128]), op=ALU.subtract),
                         r=[pA, t_sc], w=[t_Dm])
                    s.op("act", lambda e: e.activation(out=glb[:], in_=gbc[:, :, lastc[0]:lastc[1] + 1:64], func=AF.Exp), r=[pA], w=[t_glb])
                    for cc in range(2):
                        rr = slice(cc * 64, cc * 64 + 64)
                        s.op("dve", lambda e, cc=cc, rr=rr: e.tensor_tensor(out=sc[rr, :, 2], in0=gbc[rr, :, lastc[cc]], in1=sc[rr, :, 0], op=ALU.subtract),
                             r=[pA, t_sc], w=[t_sc])
                    yield
                    s.op("dve", lambda e: e.scalar_tensor_tensor(out=Dm[:], in0=Dm[:], scalar=0.0, in1=cm[:, mo + 0, :].unsqueeze(1).to_broadcast([128, 4, 128]),
                                                                 op0=ALU.min, op1=ALU.add), r=[t_Dm, t_c], w=[t_Dm])
                    s.op("act", lambda e: e.activation(out=sc[:, :, 2], in_=sc[:, :, 2], func=AF.Exp), r=[t_sc], w=[t_sc])
                    yield
                    s.op("act", lambda e: e.activation(out=ET[:], in_=Dm[:], func=AF.Exp), r=[t_Dm], w=[t_ET])
                    yield
                    s.op("dve", lambda e: e.tensor_tensor(out=Bt[:], in0=v4(bB), in1=ET[:], op=ALU.mult), r=[pB, t_ET], w=[t_Bt])
                    yield
                    for hh in range(4):
                        s.op("pe", lambda e, hh=hh: e.matmul(self.bank(bB)[:, hh * 128:(hh + 1) * 128], kr[:, hh, :], qr[:, hh, :], start=True, stop=True),
                             r=[t_kr, t_qr], w=[pB])
                    s.op("dve", lambda e: e.tensor_tensor(out=Bt[:], in0=Bt[:], in1=cm[:, mo + 1, :].unsqueeze(1).to_broadcast([128, 4, 128]), op=ALU.mult),
                         r=[t_Bt, t_c], w=[t_Bt])
                    yield
                    s.op("dve", lambda e: e.tensor_tensor(out=Bt[:], in0=Bt[:], in1=beta.unsqueeze(2).to_broadcast([128, 4, 128]), op=ALU.mult),
                         r=[t_Bt, t_bg], w=[t_Bt])
                    s.op("dve", lambda e: e.tensor_tensor(out=QKm[:], in0=v4(bB), in1=ET[:], op=ALU.mult), r=[pB, t_ET], w=[t_QKm])
                    s.dma("sp", self.G_QKM[dirn, t, :, hs, :], QKm[:], r=[t_QKm], w=[self.t_gp])
                    yield
                    for hh in range(4):
                        s.op("pe", lambda e, hh=hh: e.transpose(out=self.bank(bA)[:, hh * 128:(hh + 1) * 128], in_=Bt[:, hh, :], identity=self.ident32[:]),
                             r=[t_Bt, self.t_const], w=[pA])
                    s.op("pool", lambda e: e.tensor_tensor(out=kd[:], in0=lkt[:], in1=sc[:, :, 2:3].to_broadcast([128, 4, 128]), op=ALU.mult),
                         r=[t_lkt, t_sc], w=[t_kd])
                    s.dma("sp", self.G_KD[dirn, t, :, fs].rearrange("p (h d) -> p h d", h=4), kd[:], r=[t_kd], w=[self.t_gp])
                    s.dma("sp", self.G_SCC[dirn, t, :, :, hs], scc[:], r=[t_scc], w=[self.t_gp])
                    s.dma("sp", self.G_GLB[dirn, t, :, hs, :], glb[:], r=[t_glb], w=[self.t_gp])
                    yield
                    s.op("act", lambda e: e.activation(out=Ct[:], in_=v4(bA), func=AF.Copy), r=[pA], w=[t_Ct])
                    s.op("dve", lambda e: e.tensor_tensor(out=Bm[:], in0=Bt[:], in1=cm[:, mo + 2, :].unsqueeze(1).to_broadcast([128, 4, 128]), op=ALU.mult),
                         r=[t_Bt, t_c], w=[t_Bm])
                    yield
                    s.op("dve", lambda e: e.tensor_tensor(out=Cm[:], in0=Ct[:], in1=cm[:, mo2 + 2, :].unsqueeze(1).to_broadcast([128, 4, 128]), op=ALU.mult),
                         r=[t_Ct, t_c], w=[t_Cm])
                    s.op("dve", lambda e: e.scalar_tensor_tensor(out=P[:], in0=Bm[:], scalar=-1.0, in1=idb, op0=ALU.mult, op1=ALU.add),
                         r=[t_Bm, self.t_const], w=[t_P])
                    yield
                    s.op("dve", lambda e: e.scalar_tensor_tensor(out=Q[:], in0=Cm[:], scalar=-1.0, in1=idb, op0=ALU.mult, op1=ALU.add),
                         r=[t_Cm, self.t_const], w=[t_Q])
                    yield
                    for lv in range(1, 6):
                        last = (lv == 5)
                        s.op("dve", lambda e, lv=lv: e.tensor_tensor(out=Bm[:], in0=Bt[:], in1=cm[:, mo + 2 + lv, :].unsqueeze(1).to_broadcast([128, 4, 128]),
                                                                     op=ALU.mult), r=[t_Bt, t_c], w=[t_Bm])
                        if not last:
                            s.op("pool", lambda e, lv=lv: e.tensor_tensor(out=Cm[:], in0=Ct[:], in1=cm[:, mo2 + 2 + lv, :].unsqueeze(1).to_broadcast([128, 4, 128]),
                                                                          op=ALU.mult), r=[t_Ct, t_c], w=[t_Cm])
                        yield
                        for hh in range(4):
                            s.op("pe", lambda e, hh=hh: e.matmul(self.bank(bA)[:, hh * 128:(hh + 1) * 128], Bm[:, hh, :], Q[:, hh, :], start=True, stop=True),
                                 r=[t_Bm, t_Q], w=[pA])
                        if not last:
                            for hh in range(4):
                                s.op("pe", lambda e, hh=hh: e.matmul(self.bank(bB)[:, hh * 128:(hh + 1) * 128], Cm[:, hh, :], P[:, hh, :], start=True, stop=True),
                                     r=[t_Cm, t_P], w=[pB])
                        yield
                        s.op("act", lambda e: e.activation(out=Wn[:], in_=v4(bA), func=AF.Copy, scale=-1.0), r=[pA], w=[t_Wn])
                        if not last:
                            s.op("dve", lambda e: e.tensor_scalar(out=Vn[:], in0=v4(bB), scalar1=-1.0, scalar2=None, op0=ALU.mult), r=[pB], w=[t_Vn])
                        yield
                        for hh in range(4):
                            s.op("pe", lambda e, hh=hh: e.matmul(self.bank(bA)[:, hh * 128:(hh + 1) * 128], Wn[:, hh, :], P[:, hh, :], start=True, stop=True),
                                 r=[t_Wn, t_P], w=[pA])
                        if not last:
                            for hh in range(4):
                                s.op("pe", lambda e, hh=hh: e.matmul(self.bank(bB)[:, hh * 128:(hh + 1) * 128], Vn[:, hh, :], Q[:, hh, :], start=True, stop=True),
                                     r=[t_Vn, t_Q], w=[pB])
                        yield
                        s.op("dve", lambda e: e.tensor_tensor(out=P[:], in0=P[:].bitcast(F32), in1=v4(bA), op=ALU.add), r=[pA, t_P], w=[t_P])
                        if not last:
                            s.op("dve", lambda e: e.tensor_tensor(out=Q[:], in0=Q[:].bitcast(F32), in1=v4(bB), op=ALU.add), r=[pB, t_Q], w=[t_Q])
                        yield
                    s.op("dve", lambda e: e.tensor_tensor(out=kg[:], in0=lkt[:], in1=sc[:, :, 1:2].to_broadcast([128, 4, 128]), op=ALU.mult),
                         r=[t_lkt, t_sc], w=[t_kg])
                    s.op("act", lambda e: e.activation(out=vr[:], in_=lvt[:], func=AF.Copy), r=[t_lvt], w=[t_vr])
                    yield
                    for hh in range(4):
                        s.op("pe", lambda e, hh=hh: e.matmul(self.bank(bA)[:, hh * 128:(hh + 1) * 128], P[:, hh, :], vr[:, hh, :], start=True, stop=True),
                             r=[t_P, t_vr], w=[pA])
                        s.op("pe", lambda e, hh=hh: e.matmul(self.bank(bB)[:, hh * 128:(hh + 1) * 128], kg[:, hh, :], P[:, hh, :], start=True, stop=True),
                             r=[t_P, t_kg], w=[pB])
                    yield
                    s.op("dve", lambda e: e.tensor_tensor(out=ub[:], in0=v4(bA), in1=beta.unsqueeze(2).to_broadcast([128, 4, 128]), op=ALU.mult),
                         r=[pA, t_bg], w=[t_ub])
                    s.op("act", lambda e: e.activation(out=wT[:], in_=v4(bB), func=AF.Copy), r=[pB], w=[t_wT])
                    s.dma("sp", self.G_UB[dirn, t, :, fs].rearrange("p (h d) -> p h d", h=4), ub[:], r=[t_ub], w=[self.t_gp])
                    s.dma("sp", self.G_WT[dirn, t, :, hs, :], wT[:], r=[t_wT], w=[self.t_gp])
                    yield

            self.lockstep([lane(i) for i in range(NL)])
            s.barrier()

    def phase_gdnR(self, L):
        nc, s = self.nc, self.s
        T = L // 128
        QT3 = self.QT.rearrange("(h d) n -> d h n", d=128)
        with ExitStack() as es:
            def lane(li):
                dirn, qd = li // 2, li % 2
                hs = slice(qd * 4, qd * 4 + 4)
                fs = slice(qd * 512, (qd + 1) * 512)

                def A(nm, dt=F32, shp=(128, 4, 128)):
                    return self.sb(es, "R%d%s" % (li, nm), list(shp), dt), Tok()
                ldb = []
                for i in range(2):
                    ldb.append(dict(wT=A("lwT%d" % i), qk=A("lqk%d" % i), kd=A("lkd%d" % i), ub=A("lub%d" % i), q=A("lq%d" % i),
                                    bg=A("lbg%d" % i, F32, (128, 32)), scc=A("lscc%d" % i, F32, (64, 2, 4)), glb=A("lglb%d" % i, F32, (128, 4, 2))))
                wTr, t_wTr = A("wTr", F32R)
                qkr, t_qkr = A("qkr", F32R)
                kdr, t_kdr = A("kdr", F32R)
                qr, t_qr = A("qr", F32R)
                nb, t_nb = A("nb", F32, (128, 4))
                S, t_S = A("S", F32R)
                vn, t_vn = A("vn", F32R)
                o1s, t_o1s = A("o1s", F32, (64, 4, 128))
                o2s, t_o2s = A("o2s", F32, (64, 4, 128))
                OTs = [A("OT%d" % i, F32, (64, 2, 512)) for i in range(2)]
                bA, bB = 2 * li, 2 * li + 1
                pA, pB = self.ps_tok[bA], self.ps_tok[bB]
                v4 = lambda bk: self.bank(bk).rearrange("p (h c) -> p h c", h=4)
                s.op("dve", lambda e: e.tensor_scalar(out=S[:], in0=self.ident32[:].unsqueeze(1).to_broadcast([128, 4, 128]), scalar1=0.0, scalar2=None,
                                                      op0=ALU.mult), r=[self.t_const], w=[t_S])
                tiles = list(range(T)) if dirn == 0 else list(range(T - 1, -1, -1))

                def load(it):
                    t = tiles[it]
                    Ld = ldb[it % 2]
                    cs_ = slice(t * 128, (t + 1) * 128)
                    s.dma("sp", Ld["wT"][0][:], self.G_WT[dirn, t, :, hs, :], r=[self.t_gp], w=[Ld["wT"][1]])
                    s.dma("sp", Ld["qk"][0][:], self.G_QKM[dirn, t, :, hs, :], r=[self.t_gp], w=[Ld["qk"][1]])
                    s.dma("sp", Ld["kd"][0][:], self.G_KD[dirn, t, :, fs].rearrange("p (h d) -> p h d", h=4), r=[self.t_gp], w=[Ld["kd"][1]])
                    s.dma("sp", Ld["ub"][0][:], self.G_UB[dirn, t, :, fs].rearrange("p (h d) -> p h d", h=4), r=[self.t_gp], w=[Ld["ub"][1]])
                    s.dma("sp", Ld["q"][0][:], QT3[:, hs, cs_], r=[self.t_gd], w=[Ld["q"][1]])
                    s.dma("sp", Ld["bg"][0][:], self.BG[cs_, :], r=[self.t_gd], w=[Ld["bg"][1]])
                    s.dma("sp", Ld["scc"][0][:], self.G_SCC[dirn, t, :, :, hs], r=[self.t_gp], w=[Ld["scc"][1]])
                    s.dma("sp", Ld["glb"][0][:], self.G_GLB[dirn, t, :, hs, :], r=[self.t_gp], w=[Ld["glb"][1]])

                load(0)
                for it, t in enumerate(tiles):
                    Ld = ldb[it % 2]
                    if it + 1 < T:
                        load(it + 1)
                    OT, t_OT = OTs[it % 2]
                    s.op("pool", lambda e: e.tensor_copy(out=wTr[:], in_=Ld["wT"][0][:]), r=[Ld["wT"][1]], w=[t_wTr])
                    s.op("act", lambda e: e.activation(out=qr[:], in_=Ld["q"][0][:], func=AF.Copy), r=[Ld["q"][1]], w=[t_qr])
                    s.op("pool", lambda e: e.tensor_copy(out=qkr[:], in_=Ld["qk"][0][:]), r=[Ld["qk"][1]], w=[t_qkr])
                    s.op("pool", lambda e: e.tensor_copy(out=kdr[:], in_=Ld["kd"][0][:]), r=[Ld["kd"][1]], w=[t_kdr])
                    s.op("dve", lambda e: e.tensor_scalar(out=nb[:], in0=Ld["bg"][0][:, dirn * 8 + qd * 4:dirn * 8 + qd * 4 + 4], scalar1=-1.0, scalar2=None,
                                                          op0=ALU.mult), r=[Ld["bg"][1]], w=[t_nb])
                    ubv, t_ubv = Ld["ub"]
                    sccv, t_sccv = Ld["scc"]
                    glbv, t_glbv = Ld["glb"]
                    yield
                    for cc in ((0, 1) if dirn == 0 else (1, 0)):
                        rr = slice(cc * 64, cc * 64 + 64)
                        M = (cc + 1) * 64
                        for hh in range(4):
                            s.op("pe", lambda e, hh=hh: e.matmul(self.bank(bA)[0:M, hh * 128:(hh + 1) * 128], wTr[:, hh, 0:M], S[:, hh, :], start=True, stop=True),
                                 r=[t_wTr, t_S], w=[pA])
                        for hh in range(4):
                            s.op("pe", lambda e, hh=hh: e.matmul(self.bank(bB)[0:64, hh * 128:(hh + 1) * 128], qr[:, hh, rr], S[:, hh, :], start=True, stop=True),
                                 r=[t_qr, t_S], w=[pB])
                        yield
                        for hh in range(4):
                            s.op("dve", lambda e, hh=hh: e.scalar_tensor_tensor(out=vn[rr, hh, :], in0=self.bank(bA)[rr, hh * 128:(hh + 1) * 128],
                                                                                scalar=nb[rr, hh:hh + 1], in1=ubv[rr, hh, :], op0=ALU.mult, op1=ALU.add),
                                 r=[pA, t_nb, t_ubv], w=[t_vn])
                        s.op("act", lambda e: e.activation(out=o1s[:], in_=self.bank(bB)[0:64, :].rearrange("p (h c) -> p h c", h=4), func=AF.Copy),
                             r=[pB], w=[t_o1s])
                        yield
                        for hh in range(4):
                            s.op("pe", lambda e, hh=hh: e.matmul(self.bank(bB)[:, hh * 128:(hh + 1) * 128], kdr[rr, hh, :], vn[rr, hh, :], start=True, stop=True),
                                 r=[t_kdr, t_vn], w=[pB])
                        for hh in range(4):
                            s.op("pe", lambda e, hh=hh: e.matmul(self.bank(bA)[0:64, hh * 128:(hh + 1) * 128], qkr[rr, hh, rr], vn[rr, hh, :], start=True, stop=True),
                                 r=[t_qkr, t_vn], w=[pA])
                        yield
                        for hh in range(4):
                            s.op("dve", lambda e, hh=hh: e.scalar_tensor_tensor(out=S[:, hh, :], in0=S[:, hh, :].bitcast(F32), scalar=glbv[:, hh, cc:cc + 1],
                                                                                in1=self.bank(bB)[:, hh * 128:(hh + 1) * 128], op0=ALU.mult, op1=ALU.add),
                                 r=[pB, t_glbv, t_S], w=[t_S])
                        s.op("act", lambda e: e.activation(out=o2s[:], in_=self.bank(bA)[0:64, :].rearrange("p (h c) -> p h c", h=4), func=AF.Copy),
                             r=[pA], w=[t_o2s])
                        yield
                        for hh in range(4):
                            s.op("dve", lambda e, hh=hh: e.scalar_tensor_tensor(out=OT[:, cc, hh * 128:(hh + 1) * 128], in0=o1s[:, hh, :],
                                                                                 scalar=sccv[:, cc, hh:hh + 1], in1=o2s[:, hh, :], op0=ALU.mult, op1=ALU.add),
                                 r=[t_o1s, t_sccv, t_o2s], w=[t_OT])
                    s.dma("sp", self.OFB[dirn, t * 128:(t + 1) * 128, fs].rearrange("(c p) f -> p c f", p=64), OT[:], r=[t_OT], w=[self.t_of])
                    yield

            self.lockstep([lane(i) for i in range(4)])
            s.barrier()

    def phase_gdnC(self, L):
        nc, s = self.nc, self.s
        T = L // 128
        with ExitStack() as es:
            gnorm = self.sb(es, "gnormg", [128, 128], F32)
            t_c = Tok()
            s.dma("sp", gnorm[:], self.w["gdn_norm_g"].partition_broadcast(128), w=[t_c])
            of = [self.sb(es, "cof%d" % i, [128, 1024], F32) for i in range(2)]
            ob = [self.sb(es, "cob%d" % i, [128, 1024], F32) for i in range(2)]
            sg = [self.sb(es, "csg%d" % i, [128, 1024], BF16) for i in range(2)]
            t_in = [Tok(), Tok()]
            rst = [self.sb(es, "crst%d" % i, [128, 24], F32) for i in range(2)]
            t_rst = [Tok(), Tok()]
            junk = self.sb(es, "cjunk", [128, 128], F32)
            t_junk = Tok()
            yb16 = [self.sb(es, "cyb%d" % i, [128, 1024], BF16) for i in range(2)]
            t_yb = [Tok(), Tok()]
            ystage = [self.sb(es, "cystage%d" % i, [128, 8, 512], BF16) for i in range(2)]
            t_ys = [Tok(), Tok()]
            for t in range(T):
                b = t % 2
                blk, tl = t // 4, t % 4
                yb = blk % 2
                cs_ = slice(t * 128, (t + 1) * 128)
                s.dma("sp", of[b][:], self.OFB[0, cs_, :], r=[self.t_of], w=[t_in[b]])
                s.dma("sp", ob[b][:], self.OFB[1, cs_, :], r=[self.t_of], w=[t_in[b]])
                s.dma("sp", sg[b][:], self.SGG[cs_, :], r=[self.t_gd], w=[t_in[b]])
                O = of[b]
                s.op("pool", lambda e: e.tensor_tensor(out=O[:], in0=O[:], in1=ob[b][:], op=ALU.add), r=[t_in[b]], w=[t_in[b]])
                for h in range(8):
                    s.op("act", lambda e, h=h: e.activation(out=junk[:], in_=O[:, h * 128:(h + 1) * 128], func=AF.Square, accum_out=rst[b][:, h:h + 1]),
                         r=[t_in[b]], w=[t_junk, t_rst[b]])
                s.op("dve", lambda e: e.tensor_scalar(out=rst[b][:, 8:16], in0=rst[b][:, 0:8], scalar1=1.0 / 128.0, scalar2=EPS, op0=ALU.mult, op1=ALU.add),
                     r=[t_rst[b]], w=[t_rst[b]])
                s.op("act", lambda e: e.activation(out=rst[b][:, 8:16], in_=rst[b][:, 8:16], func=AF.Ln), r=[t_rst[b]], w=[t_rst[b]])
                s.op("act", lambda e: e.activation(out=rst[b][:, 16:24], in_=rst[b][:, 8:16], func=AF.Exp, scale=-0.5), r=[t_rst[b]], w=[t_rst[b]])
                O3 = O[:].rearrange("p (h d) -> p h d", h=8)
                s.op("dve", lambda e: e.tensor_tensor(out=O3, in0=O3, in1=rst[b][:, 16:24].unsqueeze(2).to_broadcast([128, 8, 128]), op=ALU.mult),
                     r=[t_in[b], t_rst[b]], w=[t_in[b]])
                s.op("pool", lambda e: e.tensor_tensor(out=O3, in0=O3, in1=gnorm[:].unsqueeze(1).to_broadcast([128, 8, 128]), op=ALU.mult),
                     r=[t_in[b], t_c], w=[t_in[b]])
                s.op("dve", lambda e: e.tensor_tensor(out=yb16[b][:], in0=O[:], in1=sg[b][:], op=ALU.mult), r=[t_in[b]], w=[t_yb[b]])
                self.to_ystage(yb16[b], t_yb[b], 8, ystage[yb], t_ys[yb], tl, banks=[6, 7])
                if tl == 3 or t == T - 1:
                    self.store_ystage(ystage[yb], t_ys[yb], 8, 8, blk * 512, (tl + 1) * 128)
            s.barrier()

    def phase_dil(self, si, L, hT, hT_tok):
        nc, s = self.nc, self.s
        T = L // 128
        w_in = self.w["o_w_in"]
        OG, LSE = self.OG, self.LSE
        with ExitStack() as es:
            wts = [self.sb(es, "dw%d" % i, [128, 8, 768], BF16) for i in range(2)]
            t_wts = [Tok(), Tok()]
            cos16 = self.sb(es, "cos16", [128, T, 16], F32)
            sin16 = self.sb(es, "sin16", [128, T, 16], F32)
            t_tab = Tok()
            s.dma("sp", cos16[:], self.c_rope[L][0], w=[t_tab])
            s.dma("sp", sin16[:], self.c_rope[L][1], w=[t_tab])
            qT = self.sb(es, "dqT", [128, 2, L], BF16)
            kT = self.sb(es, "dkT", [128, 2, L], BF16)
            t_qk = [Tok() for _ in range(T)]
            vP = self.sb(es, "dvP", [128, T, 256], BF16)
            t_vP = [Tok() for _ in range(T)]
            raw = [self.sb(es, "draw%d" % i, [128, 4, 128], F32) for i in range(2)]
            t_raw = [Tok(), Tok()]
            rtmp = self.sb(es, "drtmp", [128, 4, 4, 16], F32)
            t_rtmp = Tok()
            qk16 = [self.sb(es, "dqk16%d" % i, [128, 4, 128], BF16) for i in range(2)]
            t_qk16 = [Tok(), Tok()]
            Sm = [self.sb(es, "dSm%d" % i, [128, 2, 384], F32) for i in range(2)]
            t_Sm = [Tok(), Tok()]
            Pe = [self.sb(es, "dPe%d" % i, [128, 2, 384], BF16) for i in range(2)]
            t_Pe = [Tok(), Tok()]
            stt = [self.sb(es, "dst%d" % i, [128, 8], F32) for i in range(2)]
            t_stt = [Tok(), Tok()]
            PT = [self.sb(es, "dPT%d" % i, [128, 8, 128], BF16) for i in range(2)]
            t_PT = [Tok(), Tok()]
            og = [self.sb(es, "dog%d" % i, [128, 256], F32) for i in range(2)]
            t_og = [Tok(), Tok()]
            lse = [self.sb(es, "dlse%d" % i, [128, 4], F32) for i in range(2)]
            t_lse = [Tok(), Tok()]
            scale = 128.0 ** -0.5
            it = 0
            for g, d in enumerate(DIL):
                ls = L // d
                tps = ls // 128
                for hp in range(2):
                    wt, t_w = wts[it % 2], t_wts[it % 2]
                    it += 1
                    for qkv in range(3):
                        c0 = O_CQKV + ((qkv * 3 + g) * 4 + hp * 2) * 128
                        s.dma("pool", wt[:, :, qkv * 256:(qkv + 1) * 256],
                              w_in[:, c0:c0 + 256].rearrange("(k p) c -> p k c", p=128), w=[t_w])
                    for t in range(T):
                        b = t % 2
                        pb = self.pick("proj", [2, 3])
                        for qk in range(2):
                            self.proj_tok(self.bank(pb)[:, qk * 256:(qk + 1) * 256], pb, wt, t_w, qk * 256, 256, hT, [hT_tok[t]],
                                          lambda k: hT[:, k, t * 128:(t + 1) * 128])
                        s.op("act", lambda e: e.activation(out=raw[b][:], in_=self.bank(pb).rearrange("p (h d) -> p h d", h=4),
                                                           func=AF.Copy), r=[self.ps_tok[pb]], w=[t_raw[b]])
                        self.rope(raw[b][:], t_raw[b], qk16[b][:], t_qk16[b], rtmp, t_rtmp, cos16[:, t, :], sin16[:, t, :], t_tab,
                                  4, 16, 128)
                        pt = self.pick("pt", [6, 7])
                        ptv = self.bank16(pt).rearrange("p (j c) -> p j c", j=8)
                        for c in range(4):
                            s.op("pe", lambda e, c=c: e.transpose(out=ptv[:, c, :], in_=qk16[b][:, c, :], identity=self.ident16[:]),
                                 r=[t_qk16[b], self.t_const], w=[self.ps_tok[pt]])
                        s.op("act", lambda e: e.activation(out=qT[:, :, t * 128:(t + 1) * 128], in_=ptv[:, 0:2, :], func=AF.Copy,
                                                           scale=scale), r=[self.ps_tok[pt]], w=[t_qk[t]])
                        s.op("dve", lambda e: e.tensor_copy(out=kT[:, :, t * 128:(t + 1) * 128], in_=ptv[:, 2:4, :]),
                             r=[self.ps_tok[pt]], w=[t_qk[t]])

                    def pslice(j):
                        seg, jj = j // tps, j % tps
                        start = seg + d * jj * 128
                        return start, start + d * 127 + 1

                    def ptoks(j):
                        a, bnd = pslice(j)
                        return [t_qk[tt] for tt in range(a // 128, (bnd - 1) // 128 + 1)], \
                               [hT_tok[tt] for tt in range(a // 128, (bnd - 1) // 128 + 1)]

                    for j in range(T):
                        a, bnd = pslice(j)
                        pb = self.pick("proj", [2, 3])
                        self.proj_tok(self.bank(pb)[:, 0:256], pb, wt, t_w, 512, 256, hT, ptoks(j)[1],
                                      lambda k: hT[:, k, a:bnd:d])
                        s.op("act", lambda e: e.activation(out=vP[:, j, :], in_=self.bank(pb)[:, 0:256], func=AF.Copy),
                             r=[self.ps_tok[pb]], w=[t_vP[j]])
                    def dil_lane(ln):
                        pp = ln
                        sb0 = 4 if ln == 0 else 2
                        ptb = 6 + ln
                        ob = ln
                        for j in range(ln, T, 2):
                            a, bnd = pslice(j)
                            kts = [kt for kt in (j - 1, j, j + 1) if 0 <= kt < T and kt // tps == j // tps]
                            nkt = len(kts)
                            nk = 128 * nkt
                            m0 = (kts[0] - (j - 1)) * 128
                            ka = pslice(kts[0])[0]
                            kb_ = pslice(kts[-1])[1]
                            kdeps = []
                            for kt in kts:
                                kdeps += ptoks(kt)[0]
                            for jh in range(2):
                                s.op("pe", lambda e, jh=jh: e.matmul(self.bank(sb0 + jh)[:, 0:nk], qT[:, jh, a:bnd:d], kT[:, jh, ka:kb_:d],
                                                                     start=True, stop=True),
                                     r=ptoks(j)[0] + kdeps, w=[self.ps_tok[sb0 + jh]])
                            yield
                            yield from self.softmax_gen([sb0, sb0 + 1], 2, nk, self.mask_dil[:, m0:m0 + nk], self.t_const, Sm[pp], t_Sm[pp], Pe[pp], t_Pe[pp],
                                                        stt[pp], t_stt[pp])
                            yield
                            self.transpose_P(Pe[pp], t_Pe[pp], 2, nkt, PT[pp], t_PT[pp], ptb)
                            s.op("dve", lambda e: e.reciprocal(out=stt[pp][:, 4:6], in_=stt[pp][:, 2:4]), r=[t_stt[pp]], w=[t_stt[pp]])
                            s.op("act", lambda e: e.activation(out=lse[pp][:, 0:2], in_=stt[pp][:, 2:4], func=AF.Ln), r=[t_stt[pp]], w=[t_lse[pp]])
                            yield
                            for jh in range(2):
                                for kc in range(nkt):
                                    s.op("pe", lambda e, jh=jh, kc=kc: e.matmul(self.bank(ob)[:, jh * 128:(jh + 1) * 128],
                                                                                PT[pp][:, jh * nkt + kc, :],
                                                                                vP[:, kts[kc], jh * 128:(jh + 1) * 128],
                                                                                start=(kc == 0), stop=(kc == nkt - 1)),
                                         r=[t_PT[pp]] + [t_vP[kt] for kt in kts], w=[self.ps_tok[ob]])
                            s.op("dve", lambda e: e.tensor_tensor(out=lse[pp][:, 2:4], in0=lse[pp][:, 0:2], in1=stt[pp][:, 0:2], op=ALU.subtract),
                                 r=[t_stt[pp], t_lse[pp]], w=[t_lse[pp]])
                            yield
                            for jh in range(2):
                                s.op("dve", lambda e, jh=jh: e.tensor_scalar(out=og[pp][:, jh * 128:(jh + 1) * 128],
                                                                             in0=self.bank(ob)[:, jh * 128:(jh + 1) * 128],
                                                                             scalar1=stt[pp][:, 4 + jh:5 + jh], scalar2=None, op0=ALU.mult),
                                     r=[self.ps_tok[ob], t_stt[pp]], w=[t_og[pp]])
                            s.dma("sp", OG[g, a:bnd:d, hp * 256:(hp + 1) * 256], og[pp][:], r=[t_og[pp]], w=[self.t_og_dram])
                            s.dma("sp", LSE[g, a:bnd:d, hp * 2:(hp + 1) * 2], lse[pp][:, 2:4], r=[t_lse[pp]], w=[self.t_og_dram])
                            yield

                    self.lockstep([dil_lane(0), dil_lane(1)])
            s.barrier()
        with ExitStack() as es:
            wg = self.sb(es, "dwg", [128, 8, 512], BF16)
            t_wg = Tok()
            self.load_w(wg, t_wg, w_in, O_GC, 512)
            og3 = [self.sb(es, "og3%d" % i, [128, 3, 512], F32) for i in range(2)]
            l3 = [self.sb(es, "l3%d" % i, [128, 3, 4], F32) for i in range(2)]
            t_in = [Tok(), Tok()]
            wk = [self.sb(es, "mwk%d" % i, [128, 8, 4], F32) for i in range(2)]
            t_wk = [Tok(), Tok()]
            yacc = [self.sb(es, "yacc%d" % i, [128, 512], F32) for i in range(2)]
            t_ya = [Tok(), Tok()]
            sg = [self.sb(es, "dsg%d" % i, [128, 512], F32) for i in range(2)]
            t_sg = [Tok(), Tok()]
            yc = [self.sb(es, "dyc%d" % i, [128, 512], BF16) for i in range(2)]
            t_yc = [Tok(), Tok()]
            ystage = [self.sb(es, "dystage%d" % i, [128, 4, 512], BF16) for i in range(2)]
            t_ys = [Tok(), Tok()]
            for t in range(T):
                b = t % 2
                blk, tl = t // 4, t % 4
                yb = blk % 2
                s.dma("sp", og3[b][:], OG[:, t * 128:(t + 1) * 128, :].rearrange("g p c -> p g c"), r=[self.t_og_dram], w=[t_in[b]])
                s.dma("sp", l3[b][:], LSE[:, t * 128:(t + 1) * 128, :].rearrange("g p c -> p g c"), r=[self.t_og_dram], w=[t_in[b]])
                pb = self.pick("proj", [2, 3])
                self.proj_tok(self.bank(pb), pb, wg, t_wg, 0, 512, hT, [hT_tok[t]], lambda k: hT[:, k, t * 128:(t + 1) * 128])
                s.op("act", lambda e: e.activation(out=sg[b][:], in_=self.bank(pb), func=AF.Silu), r=[self.ps_tok[pb]], w=[t_sg[b]])
                W = wk[b]
                s.op("dve", lambda e: e.tensor_tensor(out=W[:, 3, :], in0=l3[b][:, 0, :], in1=l3[b][:, 1, :], op=ALU.max), r=[t_in[b]], w=[t_wk[b]])
                s.op("dve", lambda e: e.tensor_tensor(out=W[:, 3, :], in0=W[:, 3, :], in1=l3[b][:, 2, :], op=ALU.max), r=[t_in[b], t_wk[b]], w=[t_wk[b]])
                s.op("dve", lambda e: e.tensor_tensor(out=W[:, 0:3, :], in0=l3[b][:], in1=W[:, 3, :].unsqueeze(1).to_broadcast([128, 3, 4]),
                                                      op=ALU.subtract), r=[t_in[b], t_wk[b]], w=[t_wk[b]])
                s.op("act", lambda e: e.activation(out=W[:, 0:3, :], in_=W[:, 0:3, :], func=AF.Exp), r=[t_wk[b]], w=[t_wk[b]])
                s.op("dve", lambda e: e.tensor_tensor(out=W[:, 4, :], in0=W[:, 0, :], in1=W[:, 1, :], op=ALU.add), r=[t_wk[b]], w=[t_wk[b]])
                s.op("dve", lambda e: e.tensor_tensor(out=W[:, 4, :], in0=W[:, 4, :], in1=W[:, 2, :], op=ALU.add), r=[t_wk[b]], w=[t_wk[b]])
                s.op("dve", lambda e: e.reciprocal(out=W[:, 5, :], in_=W[:, 4, :]), r=[t_wk[b]], w=[t_wk[b]])
                s.op("dve", lambda e: e.tensor_tensor(out=W[:, 0:3, :], in0=W[:, 0:3, :], in1=W[:, 5, :].unsqueeze(1).to_broadcast([128, 3, 4]),
                                                      op=ALU.mult), r=[t_wk[b]], w=[t_wk[b]])
                for h in range(4):
                    hs = slice(h * 128, (h + 1) * 128)
                    s.op("dve", lambda e, h=h, hs=hs: e.tensor_scalar(out=yacc[b][:, hs], in0=og3[b][:, 0, hs], scalar1=W[:, 0, h:h + 1],
                                                                      scalar2=None, op0=ALU.mult), r=[t_in[b], t_wk[b]], w=[t_ya[b]])
                    for g in (1, 2):
                        s.op("dve", lambda e, h=h, hs=hs, g=g: e.scalar_tensor_tensor(out=yacc[b][:, hs], in0=og3[b][:, g, hs],
                                                                                      scalar=W[:, g, h:h + 1], in1=yacc[b][:, hs],
                                                                                      op0=ALU.mult, op1=ALU.add),
                             r=[t_in[b], t_wk[b], t_ya[b]], w=[t_ya[b]])
                s.op("pool", lambda e: e.tensor_tensor(out=yc[b][:], in0=yacc[b][:], in1=sg[b][:], op=ALU.mult),
                     r=[t_ya[b], t_sg[b]], w=[t_yc[b]])
                self.to_ystage(yc[b], t_yc[b], 4, ystage[yb], t_ys[yb], tl, banks=[6, 7])
                if tl == 3 or t == T - 1:
                    self.store_ystage(ystage[yb], t_ys[yb], 4, 0, blk * 512, (tl + 1) * 128)
            s.barrier()

    def zero_YT(self, L, chunks):
        s = self.s
        with ExitStack() as es:
            z = self.sb(es, "zeros", [128, L], BF16)
            t_z = Tok()
            s.op("dve", lambda e: e.memset(z[:], 0.0), w=[t_z])
            for c in chunks:
                s.dma("sp", self.YT[c * 128:(c + 1) * 128, 0:L], z[:], r=[t_z])
            s.barrier()

    def phase_out(self, r0, L, src, dst, pre, nch):
        nc, s = self.nc, self.s
        T = L // 128
        with ExitStack() as es:
            wo = self.sb(es, "wo", [128, nch, D], BF16)
            t_wo = Tok()
            w_out = self.w[pre + "w_out"]
            for c0 in range(0, nch, 4):
                s.dma("pool", wo[:, c0:c0 + 4, :], w_out[c0 * 128:(c0 + 4) * 128, :].rearrange("(c p) n -> p c n", p=128),
                      w=[t_wo])
            gpost = self.sb(es, "gpost", [128, D], F32)
            t_gp = Tok()
            s.dma("sp", gpost[:], self.w[pre + "post_g"].partition_broadcast(128), w=[t_gp])
            yb = [self.sb(es, "yb%d" % i, [128, nch, 512], BF16) for i in range(2)]
            t_yb = [Tok(), Tok()]
            xr = [self.sb(es, "xr%d" % i, [128, D], F32) for i in range(2)]
            t_xr = [Tok(), Tok()]
            st = [self.sb(es, "ost%d" % i, [128, 4], F32) for i in range(2)]
            t_st = [Tok(), Tok()]
            junk = self.sb(es, "ojunk", [128, D], BF16)
            t_junk = Tok()
            tmp = [self.sb(es, "otmp%d" % i, [128, D], F32) for i in range(2)]
            t_tmp = [Tok(), Tok()]
            nblk = (L + 511) // 512
            for blk in range(nblk):
                tok0 = blk * 512
                ntok = min(512, L - tok0)
                bb = blk % 2
                s.dma("sp", yb[bb][:, :, 0:ntok],
                      self.YT[0:nch * 128, tok0:tok0 + ntok].rearrange("(c p) n -> p c n", p=128), w=[t_yb[bb]])
                for tl in range(ntok // 128):
                    t = blk * 4 + tl
                    b = t % 2
                    s.dma("sp", xr[b][:], src[r0 + t * 128:r0 + (t + 1) * 128, :], w=[t_xr[b]])
                    p2 = self.next_bank(2)
                    for half in range(2):
                        for c in range(nch):
                            s.op("pe", lambda e, c=c, half=half: e.matmul(self.bank(p2 + half), yb[bb][:, c, tl * 128:(tl + 1) * 128],
                                                                          wo[:, c, half * 512:(half + 1) * 512],
                                                                          start=(c == 0), stop=(c == nch - 1)),
                                 r=[t_yb[bb], t_wo], w=[self.ps_tok[p2 + half]])
                    pv = self.ps[:, p2 * 512:(p2 + 2) * 512]
                    pr = [self.ps_tok[p2], self.ps_tok[p2 + 1]]
                    s.op("act", lambda e: e.activation(out=junk[:], in_=pv, func=AF.Square, accum_out=st[b][:, 0:1]),
                         r=pr, w=[t_junk, t_st[b]])
                    s.op("dve", lambda e: e.tensor_scalar(out=st[b][:, 1:2], in0=st[b][:, 0:1], scalar1=1.0 / D, scalar2=EPS,
                                                          op0=ALU.mult, op1=ALU.add), r=[t_st[b]], w=[t_st[b]])
                    s.op("act", lambda e: e.activation(out=st[b][:, 2:3], in_=st[b][:, 1:2], func=AF.Sqrt), r=[t_st[b]], w=[t_st[b]])
                    s.op("dve", lambda e: e.reciprocal(out=st[b][:, 3:4], in_=st[b][:, 2:3]), r=[t_st[b]], w=[t_st[b]])
                    s.op("dve", lambda e: e.scalar_tensor_tensor(out=tmp[b][:], in0=pv, scalar=st[b][:, 3:4], in1=gpost[:],
                                                                 op0=ALU.mult, op1=ALU.mult),
                         r=pr + [t_st[b], t_gp], w=[t_tmp[b]])
                    s.op("pool", lambda e: e.tensor_tensor(out=tmp[b][:], in0=tmp[b][:], in1=xr[b][:], op=ALU.add),
                         r=[t_xr[b], t_tmp[b]], w=[t_tmp[b]])
                    s.dma("sp", dst[r0 + t * 128:r0 + (t + 1) * 128, :], tmp[b][:], r=[t_tmp[b]])
            s.barrier()


def host_consts(seq_lens):
    c = {}
    c["c_ident"] = np.eye(128, dtype=np.float32)
    c["c_mask_dil"] = band_mask(64, 192)
    c["c_mask_swa"] = band_mask(0, 256)
    for L in sorted(set(seq_lens)):
        c16, s16 = rope_tables(L, 16)
        c8, s8 = rope_tables(L, 8)
        c["c_cos16_%d" % L] = tok_layout(c16)
        c["c_sin16_%d" % L] = tok_layout(s16)
        c["c_cos8_%d" % L] = tok_layout(c8)
        c["c_sin8_%d" % L] = tok_layout(s8)
    j = np.arange(128)[:, None]
    i = np.arange(128)[None, :]
    same = (j // 64) == (i // 64)
    fw = []
    fw.append(np.where(same & (i >= j), 0.0, NEG))
    fw.append(np.where(same & (i > j), 1.0, 0.0))
    for lv in range(6):
        bsz = 1 << lv
        fw.append(np.where(((j // (2 * bsz)) == (i // (2 * bsz))) & ((j % (2 * bsz)) < bsz) & ((i % (2 * bsz)) >= bsz), 1.0, 0.0))
    bw = [m.T for m in fw]
    tri_f = np.where(same & (j <= i), 1.0, 0.0)
    gm = np.stack(fw + bw + [tri_f, tri_f.T], 0).astype(np.float32)
    c["c_gmask"] = np.ascontiguousarray(gm.transpose(1, 0, 2))
    c["c_delta"] = np.abs(np.linspace(math.log(1e-2) / 1.5, math.log(1e-2) / 0.3, 1024, dtype=np.float32)).reshape(1, 1024)
    for L in sorted(set(seq_lens)):
        T = L // 128
        N = 2 * L
        i = np.arange(L, dtype=np.int64)
        prod = ((2 * i[:, None] + 1) * (2 * i[None, :] + 1)) % (4 * N)
        ang = prod.astype(np.float64) * (2.0 * np.pi / (4 * N))
        for nm, fn in (("c_gc_%d" % L, np.cos), ("c_gs_%d" % L, np.sin)):
            tab = fn(ang).astype(np.float32).astype(ml_dtypes.bfloat16)
            c[nm] = np.ascontiguousarray(tab.reshape(T, 128, T, 128).transpose(2, 1, 0, 3))
        t = np.linspace(0.0, 1.0, L, dtype=np.float32)[:, None]
        wv = (2.0 * math.pi / L) * np.arange(L, dtype=np.float32)[:, None]
        f = np.linspace(1e-4, 15.0, 16, dtype=np.float32)[None, :]
        emb = np.concatenate([t, np.cos(f * wv), -np.sin(f * wv)], axis=-1).astype(np.float32)
        c["c_embT_%d" % L] = np.ascontiguousarray(emb.T)
        c["c_tcol_%d" % L] = np.ascontiguousarray((-t[:, 0]).reshape(T, 128).T)
        k = np.arange(L, dtype=np.float64)
        psi = np.pi * (k + 0.5) / N
        ps = np.stack([(2.0 / N) * np.cos(psi), (2.0 / N) * np.sin(psi)], 0).astype(np.float32)
        c["c_psi_%d" % L] = np.ascontiguousarray(ps.reshape(2, T, 128).transpose(2, 0, 1))
    return c


def col_layout(v):
    return np.ascontiguousarray(np.asarray(v, np.float32).reshape(8, 128).T)


def shared_inputs(inp, seq_lens):
    m = {}
    for nm in ("e_w_in", "e_w_out", "e_w_mem_kv", "o_w_in", "o_w_out", "o_w_mem_kv"):
        m[nm] = np.ascontiguousarray(np.asarray(inp[nm], np.float32)[0])
    for nm in ("e_pre_g", "e_mem_g", "o_pre_g", "o_mem_g"):
        m[nm] = col_layout(np.asarray(inp[nm])[0])
    for nm in ("e_post_g", "o_post_g"):
        m[nm] = np.ascontiguousarray(np.asarray(inp[nm], np.float32).reshape(1, D))
    m["swa_sink"] = np.ascontiguousarray(np.asarray(inp["swa_sink"], np.float32).reshape(1, 16))
    f32 = lambda a: np.asarray(a, np.float32)
    m["hy_filt_w1"] = np.ascontiguousarray(f32(inp["hy_filt_w1"])[0])
    m["hy_filt_w2"] = np.ascontiguousarray(f32(inp["hy_filt_w2"])[0])
    m["hy_filt_w3"] = np.ascontiguousarray(f32(inp["hy_filt_w3"])[0])
    m["hy_fvec"] = np.ascontiguousarray(np.stack([f32(inp["hy_filt_b1"])[0], f32(inp["hy_filt_b2"])[0], f32(inp["hy_freq"])[0]], 1))
    m["hy_conv_w"] = np.ascontiguousarray(f32(inp["hy_conv_w"])[0].reshape(3, 24, 128).transpose(2, 1, 0))
    m["hy_conv_b"] = np.ascontiguousarray(f32(inp["hy_conv_b"])[0].reshape(24, 128).T)
    m["gdn_conv_w"] = np.ascontiguousarray(f32(inp["gdn_conv_w"])[0].reshape(5, 24, 128).transpose(2, 1, 0))
    m["gdn_A_log"] = np.ascontiguousarray(f32(inp["gdn_A_log"]).reshape(1, 16))
    m["gdn_dt_bias"] = np.ascontiguousarray(f32(inp["gdn_dt_bias"]).reshape(1, 16))
    m["gdn_norm_g"] = np.ascontiguousarray(f32(inp["gdn_norm_g"]).reshape(1, 128))
    m["hy_skip"] = np.ascontiguousarray(f32(inp["hy_skip"])[0].reshape(8, 128).T)
    m.update(host_consts(seq_lens))
    return m


_CACHE = {}


def kernel(**inp):
    xp = np.asarray(inp["x_prompt"], np.float32)
    xs = np.asarray(inp["x_sample"], np.float32)
    mp = np.asarray(inp["mem_prompt"], np.float32)
    ms = np.asarray(inp["mem_sample"], np.float32)
    n = 8
    seq_lens = [xs.shape[1], xs.shape[1], xp.shape[1]]
    shared = shared_inputs(inp, seq_lens)
    in_maps = []
    for c in range(n):
        m = dict(shared)
        m["x"] = np.ascontiguousarray(np.concatenate([xs[2 * c], xs[2 * c + 1], xp[c]], axis=0))
        m["mem"] = np.ascontiguousarray(np.concatenate([ms[2 * c], ms[2 * c + 1], mp[c]], axis=0))
        in_maps.append(m)
    nc = KB(seq_lens).build()
    res = run_bass_kernel_spmd(nc, in_maps, core_ids=list(range(n)))
    Ls = xs.shape[1]
    y_p = np.stack([res.results[c]["y"][2 * Ls:] for c in range(n)], axis=0)
    y_s = np.stack([res.results[c]["y"][j * Ls:(j + 1) * Ls] for c in range(n) for j in range(2)], axis=0)
    return (y_p.astype(np.float32), y_s.astype(np.float32))
```

```python
import math
from contextlib import ExitStack

import numpy as np
import ml_dtypes

import concourse.bass as bass
import concourse.mybir as mybir
from concourse.bass_utils import run_bass_kernel_spmd

F32 = mybir.dt.float32
BF16 = mybir.dt.bfloat16
F32R = mybir.dt.float32r
AF = mybir.ActivationFunctionType
ALU = mybir.AluOpType
AX = mybir.AxisListType

D = 1024
EPS = 1e-6
NEG = -30000.0
ROPE_THETA = 500000.0
MEM_TOKENS = 256
EVEN_IN = 9248
ODD_IN = 8448
E_HY, E_GHY, E_QKV, E_GG, E_BETA, E_A, E_XQ, E_GX = 0, 3072, 4096, 7168, 8192, 8208, 8224, 8736
O_CQKV, O_GC, O_DQ, O_DKV, O_GD, O_XQ, O_GX = 0, 4608, 5120, 6144, 6400, 7424, 7936
DIL = (1, 4, 16)


class Tok:
    __slots__ = ("w", "r")

    def __init__(self):
        self.w = None
        self.r = {}


class Sch:
    RING = 8

    def __init__(self, nc, es):
        self.nc = nc
        self.eng = {"pe": nc.tensor, "act": nc.scalar, "dve": nc.vector, "pool": nc.gpsimd, "sp": nc.sync}
        self.sem = {}
        self.cnt = {}
        self.known = {}
        for e in self.eng:
            self.sem[e] = es.enter_context(nc.semaphore("s_" + e))
            self.cnt[e] = 0
            self.known[e] = {}
        self.dcnt = {}
        for q in ("sp", "act", "pool"):
            self.dcnt[q] = 0
            for i in range(self.RING):
                key = ("d", q, i)
                self.sem[key] = es.enter_context(nc.semaphore("d_%s_%d" % (q, i)))
                self.cnt[key] = 0
        self.ninst = 0

    def _wait(self, e, deps):
        need = {}
        for key, val in deps:
            if key == e and e == "pe":
                continue
            if self.known[e].get(key, 0) >= val:
                continue
            if need.get(key, 0) < val:
                need[key] = val
        for key, val in need.items():
            self.eng[e].wait_ge(self.sem[key], val)
            self.known[e][key] = val

    @staticmethod
    def _deps(r, w):
        deps = []
        for t in r:
            if t.w is not None:
                deps.append(t.w)
        for t in w:
            if t.w is not None:
                deps.append(t.w)
            deps.extend(t.r.items())
        return deps

    @staticmethod
    def _mark(me, r, w):
        key, val = me
        for t in r:
            if t.r.get(key, 0) < val:
                t.r[key] = val
        for t in w:
            t.w = me
            t.r = {}

    def op(self, e, fn, r=(), w=()):
        self._wait(e, self._deps(r, w))
        inst = fn(self.eng[e])
        self.cnt[e] += 1
        inst.then_inc(self.sem[e], 1)
        self._mark((e, self.cnt[e]), r, w)
        self.ninst += 1

    def dma(self, q, out, in_, r=(), w=()):
        i = self.dcnt[q]
        self.dcnt[q] += 1
        slot = i % self.RING
        key = ("d", q, slot)
        val = 16 * (i // self.RING + 1)
        deps = self._deps(r, w)
        if val > 16:
            deps.append((key, val - 16))
        self._wait(q, deps)
        self.eng[q].dma_start(out=out, in_=in_).then_inc(self.sem[key], 16)
        self.cnt[key] = val
        self._mark((key, val), r, w)
        self.ninst += 1

    def barrier(self):
        allv = [(k, v) for k, v in self.cnt.items() if v > 0]
        for e in self.eng:
            self._wait(e, allv)

    def finish(self):
        allv = [(k, v) for k, v in self.cnt.items() if v > 0]
        self._wait("sp", allv)


def rope_tables(L, half):
    inv = ROPE_THETA ** (-np.arange(half, dtype=np.float32) / half)
    ang = np.arange(L, dtype=np.float32)[:, None] * inv[None, :]
    return np.cos(ang).astype(np.float32), np.sin(ang).astype(np.float32)


def tok_layout(a):
    L, Fd = a.shape
    return np.ascontiguousarray(a.reshape(L // 128, 128, Fd).transpose(1, 0, 2))


def band_mask(lo_off, hi_off):
    i = np.arange(128)[:, None]
    c = np.arange(384)[None, :]
    ok = (c >= i + lo_off) & (c <= i + hi_off)
    return np.where(ok, 0.0, NEG).astype(np.float32)


class KB:
    def __init__(self, seq_lens, dbg=False):
        self.seq_lens = list(seq_lens)
        self.NS = len(seq_lens)
        self.NTOK = sum(seq_lens)
        self.LMAX = max(seq_lens)
        self.dbg = dbg
        self.nc = bass.Bass("TRN2", target_bir_lowering=False)
        self.consts = {}

    def din(self, name, shape, dt=F32):
        return self.nc.dram_tensor(name, list(shape), dt, kind="ExternalInput").ap()

    def dscr(self, name, shape, dt=F32, out=False):
        kind = "ExternalOutput" if (out and self.dbg) else "Internal"
        return self.nc.dram_tensor(name, list(shape), dt, kind=kind).ap()

    def sb(self, es, name, shape, dt):
        self.uid = getattr(self, "uid", 0) + 1
        return es.enter_context(self.nc.sbuf_tensor("%s_%d" % (name, self.uid), list(shape), dt))

    def build(self):
        nc = self.nc
        NTOK, NS = self.NTOK, self.NS
        self.x = self.din("x", [NTOK, D])
        self.mem = self.din("mem", [NS * MEM_TOKENS, D])
        self.y = nc.dram_tensor("y", [NTOK, D], F32, kind="ExternalOutput").ap()
        self.x1 = self.dscr("x1", [NTOK, D])
        self.w = {}
        for nm, shp in (("e_w_in", [D, EVEN_IN]), ("e_w_out", [2560, D]), ("e_w_mem_kv", [D, 1024]),
                        ("o_w_in", [D, ODD_IN]), ("o_w_out", [2048, D]), ("o_w_mem_kv", [D, 1024])):
            self.w[nm] = self.din(nm, shp)
        for nm in ("e_pre_g", "e_mem_g", "o_pre_g", "o_mem_g"):
            self.w[nm] = self.din(nm, [128, 8])
        for nm in ("e_post_g", "o_post_g"):
            self.w[nm] = self.din(nm, [1, D])
        self.w["swa_sink"] = self.din("swa_sink", [1, 16])
        self.c_ident = self.din("c_ident", [128, 128])
        self.c_mask_dil = self.din("c_mask_dil", [128, 384])
        self.c_mask_swa = self.din("c_mask_swa", [128, 384])
        self.c_rope = {}
        for L in sorted(set(self.seq_lens)):
            T = L // 128
            self.c_rope[L] = (self.din("c_cos16_%d" % L, [128, T, 16]), self.din("c_sin16_%d" % L, [128, T, 16]),
                              self.din("c_cos8_%d" % L, [128, T, 8]), self.din("c_sin8_%d" % L, [128, T, 8]))
        self.YT = self.dscr("YT", [20 * 128, self.LMAX], BF16, out=True)
        self.OG = self.dscr("OG", [3, self.LMAX, 512], F32)
        TM = self.LMAX // 128
        for nm, shp in (("hy_filt_w1", [33, 64]), ("hy_filt_w2", [64, 64]), ("hy_filt_w3", [64, 2048]), ("hy_fvec", [64, 3]),
                        ("hy_conv_w", [128, 24, 3]), ("hy_conv_b", [128, 24]), ("hy_skip", [128, 8])):
            self.w[nm] = self.din(nm, shp)
        self.c_delta = self.din("c_delta", [1, 1024])
        self.c_dft, self.c_filt, self.H = {}, {}, {}
        for L in sorted(set(self.seq_lens)):
            T = L // 128
            self.c_dft[L] = (self.din("c_gc_%d" % L, [T, 128, T, 128], BF16), self.din("c_gs_%d" % L, [T, 128, T, 128], BF16))
            self.c_filt[L] = {"embT": self.din("c_embT_%d" % L, [33, L]), "tcol": self.din("c_tcol_%d" % L, [128, T]),
                              "psi": self.din("c_psi_%d" % L, [128, 2, T])}
            self.H[L] = (self.dscr("HR_%d" % L, [T, 128, 1024]), self.dscr("HI_%d" % L, [T, 128, 1024]))
        self.CFSF = (self.dscr("CF", [TM, 128, 1024]), self.dscr("SF", [TM, 128, 1024]))
        self.ZRS = (self.dscr("ZR", [TM, 128, 1024], BF16), self.dscr("ZS", [TM, 128, 1024], BF16))
        self.UT = self.dscr("UT", [1024, self.LMAX], BF16)
        self.AT = self.dscr("AT", [1024, self.LMAX], BF16)
        self.BT = self.dscr("BT", [1024, self.LMAX], BF16)
        self.t_cf, self.t_H, self.t_uab, self.t_Z = Tok(), Tok(), Tok(), Tok()
        for nm, shp in (("gdn_conv_w", [128, 24, 5]), ("gdn_A_log", [1, 16]), ("gdn_dt_bias", [1, 16]), ("gdn_norm_g", [1, 128])):
            self.w[nm] = self.din(nm, shp)
        self.c_gmask = self.din("c_gmask", [128, 18, 128])
        self.QT = self.dscr("QT", [1024, self.LMAX])
        self.KT = self.dscr("KT", [1024, self.LMAX])
        self.KTOK = self.dscr("KTOK", [self.LMAX, 1024])
        self.VTOK = self.dscr("VTOK", [self.LMAX, 1024])
        self.BG = self.dscr("BG", [self.LMAX, 32])
        self.SGG = self.dscr("SGG", [self.LMAX, 1024], BF16)
        self.OFB = self.dscr("OFB", [2, self.LMAX, 1024])
        self.G_UB = self.dscr("G_UB", [2, TM, 128, 1024])
        self.G_KD = self.dscr("G_KD", [2, TM, 128, 1024])
        self.G_WT = self.dscr("G_WT", [2, TM, 128, 8, 128])
        self.G_QKM = self.dscr("G_QKM", [2, TM, 128, 8, 128])
        self.G_SCC = self.dscr("G_SCC", [2, TM, 64, 2, 8])
        self.G_GLB = self.dscr("G_GLB", [2, TM, 128, 8, 2])
        self.t_gd, self.t_of, self.t_gp = Tok(), Tok(), Tok()
        self.LSE = self.dscr("LSE", [3, self.LMAX, 4], F32)

        with ExitStack() as es:
            self.es = es
            s = self.s = Sch(nc, es)
            self.ident32 = self.sb(es, "ident32", [128, 128], F32)
            self.ident16 = self.sb(es, "ident16", [128, 128], BF16)
            self.t_const = Tok()
            s.dma("sp", self.ident32[:], self.c_ident, w=[self.t_const])
            s.dma("pool", self.ident16[:], self.c_ident, w=[self.t_const])
            self.mask_dil = self.sb(es, "mask_dil", [128, 384], F32)
            self.mask_swa = self.sb(es, "mask_swa", [128, 384], F32)
            self.eps_col = self.sb(es, "eps_col", [128, 2], F32)
            s.op("dve", lambda e: e.memset(self.eps_col[:, 0:1], EPS), w=[self.t_const])
            s.op("dve", lambda e: e.memset(self.eps_col[:, 1:2], 1.0), w=[self.t_const])
            s.dma("sp", self.mask_dil[:], self.c_mask_dil, w=[self.t_const])
            s.dma("sp", self.mask_swa[:], self.c_mask_swa, w=[self.t_const])
            self.ps = es.enter_context(nc.psum_tensor("ps", [128, 4096], F32))
            self.ps_tok = [Tok() for _ in range(8)]
            self.ps_rr = 0
            self.t_og_dram = Tok()
            s.barrier()

            if getattr(self, "en_even", [1, 1, 1])[0]:
                for L in sorted(set(self.seq_lens)):
                    self.phase_filter(L)
            r0 = 0
            for si, L in enumerate(self.seq_lens):
                self.run_seq(si, r0, L)
                r0 += L
            s.finish()
        return nc

    def bank(self, b):
        return self.ps[:, b * 512:(b + 1) * 512]

    def bank16(self, b):
        return self.ps[:, b * 512:(b + 1) * 512].bitcast(BF16)

    def run_seq(self, si, r0, L):
        T = L // 128
        for layer in range(2):
            src = self.x if layer == 0 else self.x1
            dst = self.x1 if layer == 0 else self.y
            with ExitStack() as les:
                hT = self.sb(les, "hT", [128, 8, L], BF16)
                hT_tok = [Tok() for _ in range(T)]
                pre = "e_" if layer == 0 else "o_"
                self.phase_norm(lambda t: src[r0 + t * 128:r0 + (t + 1) * 128, :], T, self.w[pre + "pre_g"], hT, hT_tok)
                if layer == 0:
                    nch = 20
                    en = getattr(self, "en_even", [1, 1, 1])
                    if en[1]:
                        self.phase_gdn1(si, L, hT, hT_tok)
                    if en[2]:
                        self.phase_xattn(si, L, hT, hT_tok, "e_", E_XQ, E_GX, 16)
                    if en[0]:
                        self.phase_hy1(si, L, hT, hT_tok)
                    if not all(en):
                        self.zero_YT(L, [c for c in range(20) if not en[0 if c < 8 else (1 if c < 16 else 2)]])
                else:
                    nch = 16
                    en = getattr(self, "en_odd", [1, 1, 1])
                    if en[0]:
                        self.phase_dil(si, L, hT, hT_tok)
                    if en[1]:
                        self.phase_swa(si, L, hT, hT_tok)
                    if en[2]:
                        self.phase_xattn(si, L, hT, hT_tok, "o_", O_XQ, O_GX, 12)
                    if not all(en):
                        self.zero_YT(L, [c for c in range(16) if not en[0 if c < 4 else (1 if c < 12 else 2)]])
                self.s.barrier()
            if layer == 0 and getattr(self, "en_even", [1, 1, 1])[1]:
                self.phase_gdn2(si, L)
            if layer == 0 and getattr(self, "en_even", [1, 1, 1])[0]:
                self.phase_hy2(si, L)
            self.phase_out(r0, L, src, dst, pre, nch)

    def phase_norm(self, src, T, g_dram, hT, hT_tok, tag="n"):
        nc, s = self.nc, self.s
        with ExitStack() as es:
            gcol = self.sb(es, tag + "gcol", [128, 8], F32)
            gfull = self.sb(es, tag + "gfull", [128, 8, 128], F32)
            t_g = Tok()
            s.dma("sp", gcol[:], g_dram, w=[t_g])
            for k in range(8):
                s.op("dve", lambda e, k=k: e.tensor_scalar(out=gfull[:, k, :], in0=self.ident32[:], scalar1=0.0,
                                                             scalar2=gcol[:, k:k + 1], op0=ALU.mult, op1=ALU.add),
                     r=[t_g, self.t_const], w=[t_g])
            xt = [self.sb(es, tag + "xt%d" % i, [128, D], F32) for i in range(2)]
            xn = [self.sb(es, tag + "xn%d" % i, [128, D], BF16) for i in range(2)]
            st = [self.sb(es, tag + "st%d" % i, [128, 4], F32) for i in range(2)]
            junk = [self.sb(es, tag + "junk%d" % i, [128, D], BF16) for i in range(2)]
            t_xt = [Tok(), Tok()]
            t_xn = [Tok(), Tok()]
            t_st = [Tok(), Tok()]
            t_junk = [Tok(), Tok()]
            def n_lane(ln):
                b = ln
                pb = ln
                for t in range(ln, T, 2):
                    s.dma("sp", xt[b][:], src(t), w=[t_xt[b]])
                    yield
                    s.op("act", lambda e: e.activation(out=junk[b][:], in_=xt[b][:], func=AF.Square, accum_out=st[b][:, 0:1]),
                         r=[t_xt[b]], w=[t_junk[b], t_st[b]])
                    yield
                    s.op("dve", lambda e: e.tensor_scalar(out=st[b][:, 1:2], in0=st[b][:, 0:1], scalar1=1.0 / D, scalar2=EPS,
                                                          op0=ALU.mult, op1=ALU.add), r=[t_st[b]], w=[t_st[b]])
                    yield
                    s.op("act", lambda e: e.activation(out=st[b][:, 2:3], in_=st[b][:, 1:2], func=AF.Sqrt), r=[t_st[b]], w=[t_st[b]])
                    yield
                    s.op("dve", lambda e: e.reciprocal(out=st[b][:, 3:4], in_=st[b][:, 2:3]), r=[t_st[b]], w=[t_st[b]])
                    yield
                    s.op("act", lambda e: e.activation(out=xn[b][:], in_=xt[b][:], func=AF.Copy, scale=st[b][:, 3:4]),
                         r=[t_xt[b], t_st[b]], w=[t_xn[b]])
                    yield
                    pv = self.bank16(pb).rearrange("p (k c) -> p k c", k=8)
                    for k in range(8):
                        s.op("pe", lambda e, k=k: e.transpose(out=pv[:, k, :], in_=xn[b][:, k * 128:(k + 1) * 128],
                                                              identity=self.ident16[:]),
                             r=[t_xn[b], self.t_const], w=[self.ps_tok[pb]])
                    yield
                    s.op("dve", lambda e: e.tensor_tensor(out=hT[:, :, t * 128:(t + 1) * 128], in0=pv, in1=gfull[:],
                                                          op=ALU.mult), r=[self.ps_tok[pb], t_g], w=[hT_tok[t]])
                    yield

            self.lockstep([n_lane(0), n_lane(1)])
            s.barrier()

    def next_bank(self, n=1):
        b = self.ps_rr
        if n == 2 and b % 2:
            b += 1
        if b + n > 8:
            b = 0
        self.ps_rr = (b + n) % 8
        return b

    def pick(self, cls, banks):
        rr = self.__dict__.setdefault("_rr", {})
        i = rr.get(cls, 0)
        rr[cls] = i + 1
        return banks[i % len(banks)]

    def rope(self, src, t_src, dst, t_dst, tmp, t_tmp, cos, sin, t_tab, H, half, dh):
        s = self.s
        A = src[:, :, 0:half]
        B = src[:, :, half:2 * half]
        cb = cos.unsqueeze(1).to_broadcast([128, H, half])
        sb_ = sin.unsqueeze(1).to_broadcast([128, H, half])
        s.op("dve", lambda e: e.tensor_tensor(out=tmp[:, 0, 0:H, :], in0=A, in1=cb, op=ALU.mult), r=[t_src, t_tab], w=[t_tmp])
        s.op("dve", lambda e: e.tensor_tensor(out=tmp[:, 1, 0:H, :], in0=B, in1=sb_, op=ALU.mult), r=[t_src, t_tab], w=[t_tmp])
        s.op("dve", lambda e: e.tensor_tensor(out=tmp[:, 2, 0:H, :], in0=B, in1=cb, op=ALU.mult), r=[t_src, t_tab], w=[t_tmp])
        s.op("dve", lambda e: e.tensor_tensor(out=tmp[:, 3, 0:H, :], in0=A, in1=sb_, op=ALU.mult), r=[t_src, t_tab], w=[t_tmp])
        s.op("dve", lambda e: e.tensor_tensor(out=dst[:, :, 0:half], in0=tmp[:, 0, 0:H, :], in1=tmp[:, 1, 0:H, :], op=ALU.subtract),
             r=[t_tmp], w=[t_dst])
        s.op("dve", lambda e: e.tensor_tensor(out=dst[:, :, half:2 * half], in0=tmp[:, 2, 0:H, :], in1=tmp[:, 3, 0:H, :], op=ALU.add),
             r=[t_tmp], w=[t_dst])
        s.op("act", lambda e: e.activation(out=dst[:, :, 2 * half:dh], in_=src[:, :, 2 * half:dh], func=AF.Copy),
             r=[t_src], w=[t_dst])

    def softmax_gen(self, sbanks, nh, nk, mask_ap, t_mask, Sm, t_Sm, Pe, t_Pe, st, t_st, sink_ap=None, t_sink=None):
        s = self.s
        for j in range(nh):
            s.op("dve", lambda e, j=j: e.tensor_tensor(out=Sm[:, j, 0:nk], in0=self.bank(sbanks[j])[:, 0:nk], in1=mask_ap, op=ALU.add),
                 r=[self.ps_tok[sbanks[j]], t_mask], w=[t_Sm])
        yield
        if sink_ap is None:
            s.op("dve", lambda e: e.tensor_reduce(out=st[:, 0:nh], in_=Sm[:, 0:nh, 0:nk], op=ALU.max, axis=AX.X, negate=True),
                 r=[t_Sm], w=[t_st])
        else:
            s.op("dve", lambda e: e.tensor_reduce(out=st[:, 2 * nh:3 * nh], in_=Sm[:, 0:nh, 0:nk], op=ALU.max, axis=AX.X),
                 r=[t_Sm], w=[t_st])
            s.op("dve", lambda e: e.tensor_tensor(out=st[:, 2 * nh:3 * nh], in0=st[:, 2 * nh:3 * nh], in1=sink_ap, op=ALU.max),
                 r=[t_st, t_sink], w=[t_st])
            s.op("dve", lambda e: e.tensor_scalar(out=st[:, 0:nh], in0=st[:, 2 * nh:3 * nh], scalar1=-1.0, scalar2=None, op0=ALU.mult),
                 r=[t_st], w=[t_st])
            s.op("dve", lambda e: e.tensor_tensor(out=st[:, 3 * nh:4 * nh], in0=sink_ap, in1=st[:, 0:nh], op=ALU.add),
                 r=[t_st, t_sink], w=[t_st])
            s.op("act", lambda e: e.activation(out=st[:, 3 * nh:4 * nh], in_=st[:, 3 * nh:4 * nh], func=AF.Exp), r=[t_st], w=[t_st])
        yield
        for j in range(nh):
            s.op("act", lambda e, j=j: e.activation(out=Pe[:, j, 0:nk], in_=Sm[:, j, 0:nk], func=AF.Exp, bias=st[:, j:j + 1],
                                                    accum_out=st[:, nh + j:nh + j + 1]),
                 r=[t_Sm, t_st], w=[t_Pe, t_st])
        yield
        if sink_ap is not None:
            s.op("dve", lambda e: e.tensor_tensor(out=st[:, nh:2 * nh], in0=st[:, nh:2 * nh], in1=st[:, 3 * nh:4 * nh], op=ALU.add),
                 r=[t_st], w=[t_st])

    def softmax_block(self, *a, **k):
        for _ in self.softmax_gen(*a, **k):
            pass

    def transpose_P(self, Pe, t_Pe, nh, nkt, PT, t_PT, ptb):
        s = self.s
        ptv = self.bank16(ptb).rearrange("p (j c) -> p j c", j=8)
        n = 0
        for j in range(nh):
            for kc in range(nkt):
                s.op("pe", lambda e, j=j, kc=kc, n=n: e.transpose(out=ptv[:, n, :], in_=Pe[:, j, kc * 128:(kc + 1) * 128],
                                                                  identity=self.ident16[:]),
                     r=[t_Pe, self.t_const], w=[self.ps_tok[ptb]])
                n += 1
        s.op("act", lambda e: e.activation(out=PT[:, 0:n, :], in_=ptv[:, 0:n, :], func=AF.Copy), r=[self.ps_tok[ptb]], w=[t_PT])

    def load_w(self, wt, t_w, w_dram, c0, n):
        self.s.dma("pool", wt[:, :, 0:n], w_dram[:, c0:c0 + n].rearrange("(k p) c -> p k c", p=128), w=[t_w])

    def proj_feat(self, out_ap, pb, wt, t_w, wc0, hT, hT_tok, tok0, ntok, extra_r=()):
        s = self.s
        toks = [hT_tok[t] for t in range(tok0 // 128, (tok0 + ntok + 127) // 128)]
        for k in range(8):
            s.op("pe", lambda e, k=k: e.matmul(out_ap, wt[:, k, wc0:wc0 + 128], hT[:, k, tok0:tok0 + ntok],
                                               start=(k == 0), stop=(k == 7)),
                 r=[t_w] + toks + list(extra_r), w=[self.ps_tok[pb]])

    def proj_tok(self, out_ap, pb, wt, t_w, wc0, ncol, hT, hT_tok_list, tok_ap_fn):
        s = self.s
        for k in range(8):
            s.op("pe", lambda e, k=k: e.matmul(out_ap, tok_ap_fn(k), wt[:, k, wc0:wc0 + ncol],
                                               start=(k == 0), stop=(k == 7)),
                 r=[t_w] + list(hT_tok_list), w=[self.ps_tok[pb]])

    def phase_xattn(self, si, L, hT, hT_tok, pre, off_q, off_g, ch0):
        nc, s = self.nc, self.s
        T = L // 128
        w_in = self.w[pre + "w_in"]
        with ExitStack() as es:
            memT = self.sb(es, "memT", [128, 8, MEM_TOKENS], BF16)
            memT_tok = [Tok(), Tok()]
            self.phase_norm(lambda t: self.mem[si * MEM_TOKENS + t * 128: si * MEM_TOKENS + (t + 1) * 128, :], 2,
                            self.w[pre + "mem_g"], memT, memT_tok, tag="m")
            wkv = self.sb(es, "wkv", [128, 8, 1024], BF16)
            t_wkv = Tok()
            self.load_w(wkv, t_wkv, self.w[pre + "w_mem_kv"], 0, 1024)
            KmT = self.sb(es, "KmT", [128, 4, MEM_TOKENS], BF16)
            Vm = self.sb(es, "Vm", [128, 2, 512], BF16)
            t_km, t_vm = Tok(), Tok()
            for h in range(4):
                pb = self.next_bank()
                self.proj_feat(self.bank(pb)[:, 0:256], pb, wkv, t_wkv, h * 128, memT, memT_tok, 0, 256)
                s.op("act", lambda e: e.activation(out=KmT[:, h, :], in_=self.bank(pb)[:, 0:256], func=AF.Copy),
                     r=[self.ps_tok[pb]], w=[t_km])
            for mt in range(2):
                pb = self.next_bank()
                self.proj_tok(self.bank(pb), pb, wkv, t_wkv, 512, 512, memT, memT_tok,
                              lambda k: memT[:, k, mt * 128:(mt + 1) * 128])
                s.op("act", lambda e: e.activation(out=Vm[:, mt, :], in_=self.bank(pb), func=AF.Copy),
                     r=[self.ps_tok[pb]], w=[t_vm])
            wq = self.sb(es, "wq", [128, 8, 512], BF16)
            wg = self.sb(es, "wg", [128, 8, 512], BF16)
            t_wq, t_wg = Tok(), Tok()
            self.load_w(wq, t_wq, w_in, off_q, 512)
            self.load_w(wg, t_wg, w_in, off_g, 512)
            qT = self.sb(es, "qT", [128, 4, 512], BF16)
            t_qT = Tok()
            sg = [self.sb(es, "sg%d" % i, [128, 512], F32) for i in range(2)]
            t_sg = [Tok(), Tok()]
            Sm = [self.sb(es, "Sm%d" % i, [128, 4, 256], F32) for i in range(2)]
            Pe = [self.sb(es, "Pe%d" % i, [128, 4, 256], BF16) for i in range(2)]
            t_Pe = [Tok(), Tok()]
            stt = [self.sb(es, "stt%d" % i, [128, 16], F32) for i in range(2)]
            t_stt = [Tok(), Tok()]
            PT = [self.sb(es, "PT%d" % i, [128, 8, 128], BF16) for i in range(2)]
            t_PT = [Tok(), Tok()]
            yx = [self.sb(es, "yx%d" % i, [128, 512], BF16) for i in range(2)]
            t_yx = [Tok(), Tok()]
            ystage = [self.sb(es, "ystage%d" % i, [128, 4, 512], BF16) for i in range(2)]
            t_ys = [Tok(), Tok()]
            scale = 128.0 ** -0.5
            nblk = (L + 511) // 512
            for blk in range(nblk):
                tok0 = blk * 512
                ntok = min(512, L - tok0)
                yb = blk % 2
                for h in range(4):
                    pb = self.pick("xq", [2, 3])
                    self.proj_feat(self.bank(pb)[:, 0:ntok], pb, wq, t_wq, h * 128, hT, hT_tok, tok0, ntok)
                    s.op("act", lambda e: e.activation(out=qT[:, h, 0:ntok], in_=self.bank(pb)[:, 0:ntok], func=AF.Copy,
                                                       scale=scale), r=[self.ps_tok[pb]], w=[t_qT])
                def x_lane(ln):
                    gb = ln
                    s0 = 4 + 2 * ln
                    ptb = 2 + ln
                    for tl in range(ln, ntok // 128, 2):
                        t = blk * 4 + tl
                        b = ln
                        self.proj_tok(self.bank(gb), gb, wg, t_wg, 0, 512, hT, [hT_tok[t]],
                                      lambda k: hT[:, k, t * 128:(t + 1) * 128])
                        sv = self.ps[:, s0 * 512:(s0 + 2) * 512].rearrange("p (h m) -> p h m", h=4)
                        for h in range(4):
                            pbh = s0 + h // 2
                            s.op("pe", lambda e, h=h: e.matmul(sv[:, h, :], qT[:, h, tl * 128:(tl + 1) * 128], KmT[:, h, :],
                                                               start=True, stop=True),
                                 r=[t_qT, t_km], w=[self.ps_tok[pbh]])
                        yield
                        s.op("act", lambda e: e.activation(out=sg[b][:], in_=self.bank(gb), func=AF.Silu),
                             r=[self.ps_tok[gb]], w=[t_sg[b]])
                        s.op("dve", lambda e: e.tensor_reduce(out=stt[b][:, 0:4], in_=sv, op=ALU.max, axis=AX.X, negate=True),
                             r=[self.ps_tok[s0], self.ps_tok[s0 + 1]], w=[t_stt[b]])
                        yield
                        for h in range(4):
                            s.op("act", lambda e, h=h: e.activation(out=Pe[b][:, h, :], in_=sv[:, h, :], func=AF.Exp,
                                                                    bias=stt[b][:, h:h + 1], accum_out=stt[b][:, 4 + h:5 + h]),
                                 r=[self.ps_tok[s0], self.ps_tok[s0 + 1], t_stt[b]], w=[t_Pe[b], t_stt[b]])
                        yield
                        s.op("dve", lambda e: e.reciprocal(out=stt[b][:, 8:12], in_=stt[b][:, 4:8]), r=[t_stt[b]], w=[t_stt[b]])
                        ptv = self.bank16(ptb).rearrange("p (j c) -> p j c", j=8)
                        for h in range(4):
                            for mc in range(2):
                                s.op("pe", lambda e, h=h, mc=mc: e.transpose(out=ptv[:, h * 2 + mc, :],
                                                                             in_=Pe[b][:, h, mc * 128:(mc + 1) * 128],
                                                                             identity=self.ident16[:]),
                                     r=[t_Pe[b], self.t_const], w=[self.ps_tok[ptb]])
                        yield
                        s.op("act", lambda e: e.activation(out=PT[b][:], in_=ptv, func=AF.Copy), r=[self.ps_tok[ptb]], w=[t_PT[b]])
                        yield
                        po = gb
                        for h in range(4):
                            for mc in range(2):
                                s.op("pe", lambda e, h=h, mc=mc: e.matmul(self.bank(po)[:, h * 128:(h + 1) * 128],
                                                                          PT[b][:, h * 2 + mc, :], Vm[:, mc, h * 128:(h + 1) * 128],
                                                                          start=(mc == 0), stop=(mc == 1)),
                                     r=[t_PT[b], t_vm], w=[self.ps_tok[po]])
                        yield
                        for h in range(4):
                            s.op("dve", lambda e, h=h: e.scalar_tensor_tensor(out=yx[b][:, h * 128:(h + 1) * 128],
                                                                              in0=self.bank(po)[:, h * 128:(h + 1) * 128],
                                                                              scalar=stt[b][:, 8 + h:9 + h], in1=sg[b][:, h * 128:(h + 1) * 128],
                                                                              op0=ALU.mult, op1=ALU.mult),
                                 r=[self.ps_tok[po], t_stt[b], t_sg[b]], w=[t_yx[b]])
                        yield
                        self.to_ystage(yx[b], t_yx[b], 4, ystage[yb], t_ys[yb], tl, banks=[ptb])
                        yield

                self.lockstep([x_lane(0), x_lane(1)])
                self.store_ystage(ystage[yb], t_ys[yb], 4, ch0, tok0, ntok)
            s.barrier()

    def to_ystage(self, ytile, t_y, nch, ystage, t_ys, tl, banks=None):
        s = self.s
        for c0 in range(0, nch, 8):
            n = min(8, nch - c0)
            pt = self.next_bank() if banks is None else self.pick("pt", banks)
            ptv = self.bank16(pt).rearrange("p (j c) -> p j c", j=8)
            for c in range(n):
                s.op("pe", lambda e, c=c: e.transpose(out=ptv[:, c, :], in_=ytile[:, (c0 + c) * 128:(c0 + c + 1) * 128],
                                                      identity=self.ident16[:]),
                     r=[t_y, self.t_const], w=[self.ps_tok[pt]])
            s.op("act", lambda e: e.activation(out=ystage[:, c0:c0 + n, tl * 128:(tl + 1) * 128], in_=ptv[:, 0:n, :],
                                               func=AF.Copy), r=[self.ps_tok[pt]], w=[t_ys])

    def store_ystage(self, ystage, t_ys, nch, ch0, tok0, ntok):
        dst = self.YT[ch0 * 128:(ch0 + nch) * 128, tok0:tok0 + ntok].rearrange("(c p) n -> p c n", p=128)
        self.s.dma("sp", dst, ystage[:, 0:nch, 0:ntok], r=[t_ys])

    def phase_swa(self, si, L, hT, hT_tok):
        nc, s = self.nc, self.s
        T = L // 128
        w_in = self.w["o_w_in"]
        with ExitStack() as es:
            wkv = self.sb(es, "swkv", [128, 8, 256], BF16)
            wq = self.sb(es, "swq", [128, 8, 1024], BF16)
            wg = self.sb(es, "swg", [128, 8, 1024], BF16)
            t_wkv, t_wq, t_wg = Tok(), Tok(), Tok()
            self.load_w(wkv, t_wkv, w_in, O_DKV, 256)
            self.load_w(wq, t_wq, w_in, O_DQ, 1024)
            self.load_w(wg, t_wg, w_in, O_GD, 1024)
            cos8 = self.sb(es, "cos8", [128, T, 8], F32)
            sin8 = self.sb(es, "sin8", [128, T, 8], F32)
            t_tab = Tok()
            s.dma("sp", cos8[:], self.c_rope[L][2], w=[t_tab])
            s.dma("sp", sin8[:], self.c_rope[L][3], w=[t_tab])
            sink = self.sb(es, "sink", [128, 16], F32)
            t_sink = Tok()
            s.dma("sp", sink[:], self.w["swa_sink"].partition_broadcast(128), w=[t_sink])
            kTd = self.sb(es, "kTd", [128, 2, L], BF16)
            t_kT = [Tok() for _ in range(T)]
            vtok = self.sb(es, "vtok", [128, T, 128], BF16)
            t_v = [Tok() for _ in range(T)]
            raw = [self.sb(es, "sraw%d" % i, [128, 16, 64], F32) for i in range(2)]
            t_raw = [Tok(), Tok()]
            rtmps = [self.sb(es, "srtmp%d" % i, [128, 4, 16, 8], F32) for i in range(2)]
            t_rtmps = [Tok(), Tok()]
            rtmp, t_rtmp = rtmps[0], t_rtmps[0]
            krs = [self.sb(es, "skr%d" % i, [128, 2, 64], BF16) for i in range(2)]
            t_krs = [Tok(), Tok()]
            kds = [self.sb(es, "skd%d" % i, [128, 2, 2, 64], BF16) for i in range(2)]
            t_kds = [Tok(), Tok()]
            def kv_lane(ln):
                b = ln
                pb = 2 + ln
                pt = 6 + ln
                for t in range(ln, T, 2):
                    self.proj_tok(self.bank(pb)[:, 0:256], pb, wkv, t_wkv, 0, 256, hT, [hT_tok[t]],
                                  lambda k: hT[:, k, t * 128:(t + 1) * 128])
                    yield
                    s.op("act", lambda e: e.activation(out=raw[b][:, 0:2, :], in_=self.bank(pb)[:, 0:128].rearrange("p (h d) -> p h d", h=2),
                                                       func=AF.Copy), r=[self.ps_tok[pb]], w=[t_raw[b]])
                    s.op("act", lambda e: e.activation(out=vtok[:, t, :], in_=self.bank(pb)[:, 128:256], func=AF.Copy),
                         r=[self.ps_tok[pb]], w=[t_v[t]])
                    yield
                    self.rope(raw[b][:, 0:2, :], t_raw[b], krs[ln][:], t_krs[ln], rtmps[ln], t_rtmps[ln], cos8[:, t, :], sin8[:, t, :], t_tab, 2, 8, 64)
                    yield
                    for r_ in range(2):
                        s.op("dve", lambda e, r_=r_: e.tensor_copy(out=kds[ln][:, :, r_, :], in_=krs[ln][:]), r=[t_krs[ln]], w=[t_kds[ln]])
                    yield
                    ptv = self.bank16(pt).rearrange("p (j c) -> p j c", j=8)
                    for kv in range(2):
                        s.op("pe", lambda e, kv=kv: e.transpose(out=ptv[:, kv, :], in_=kds[ln][:, kv, :, :].rearrange("p r d -> p (r d)"),
                                                                identity=self.ident16[:]), r=[t_kds[ln], self.t_const], w=[self.ps_tok[pt]])
                    yield
                    s.op("act", lambda e: e.activation(out=kTd[:, :, t * 128:(t + 1) * 128], in_=ptv[:, 0:2, :], func=AF.Copy),
                         r=[self.ps_tok[pt]], w=[t_kT[t]])
                    yield

            self.lockstep([kv_lane(0), kv_lane(1)])
            q16 = [self.sb(es, "sq16%d" % i, [128, 16, 64], BF16) for i in range(2)]
            t_q16 = [Tok(), Tok()]
            qT = [self.sb(es, "sqT%d" % i, [128, 8, 128], BF16) for i in range(2)]
            t_qT = [Tok(), Tok()]
            sgd = [self.sb(es, "sgd%d" % i, [128, 1024], F32) for i in range(2)]
            t_sgd = [Tok(), Tok()]
            Sm = [self.sb(es, "sSm%d" % i, [128, 2, 384], F32) for i in range(2)]
            t_Sm = [Tok(), Tok()]
            Pe = [self.sb(es, "sPe%d" % i, [128, 2, 384], BF16) for i in range(2)]
            t_Pe = [Tok(), Tok()]
            stt = [self.sb(es, "sst%d" % i, [128, 8], F32) for i in range(2)]
            t_stt = [Tok(), Tok()]
            PT = [self.sb(es, "sPT%d" % i, [128, 8, 128], BF16) for i in range(2)]
            t_PT = [Tok(), Tok()]
            dens = [self.sb(es, "sdens%d" % i, [128, 32], F32) for i in range(2)]
            t_dens = [Tok(), Tok()]
            densl = [[self.sb(es, "sdensl%d_%d" % (ln, i), [128, 8], F32) for i in range(2)] for ln in range(2)]
            t_densl = [[Tok(), Tok()], [Tok(), Tok()]]
            yd = [self.sb(es, "syd%d" % i, [128, 1024], BF16) for i in range(2)]
            t_yd = [Tok(), Tok()]
            ystage = [self.sb(es, "systage%d" % i, [128, 8, 512], BF16) for i in range(2)]
            t_ys = [Tok(), Tok()]
            def prologue(tt):
                pbq = tt % 2
                for half in range(2):
                    pb = 2 + half
                    self.proj_tok(self.bank(pb), pb, wq, t_wq, half * 512, 512, hT, [hT_tok[tt]],
                                  lambda k: hT[:, k, tt * 128:(tt + 1) * 128])
                    yield
                    s.op("act", lambda e: e.activation(out=raw[pbq][:, half * 8:(half + 1) * 8, :],
                                                       in_=self.bank(pb).rearrange("p (h d) -> p h d", h=8), func=AF.Copy,
                                                       scale=0.125), r=[self.ps_tok[pb]], w=[t_raw[pbq]])
                    yield
                self.rope(raw[pbq][:], t_raw[pbq], q16[pbq][:], t_q16[pbq], rtmp, t_rtmp, cos8[:, tt, :], sin8[:, tt, :], t_tab, 16, 8, 64)
                yield
                pt = self.pick("pt", [6, 7])
                ptv = self.bank16(pt).rearrange("p (j c) -> p j c", j=8)
                for c in range(8):
                    s.op("pe", lambda e, c=c: e.transpose(out=ptv[:, c, :], in_=q16[pbq][:, 2 * c:2 * c + 2, :].rearrange("p h d -> p (h d)"),
                                                          identity=self.ident16[:]), r=[t_q16[pbq], self.t_const], w=[self.ps_tok[pt]])
                yield
                s.op("act", lambda e: e.activation(out=qT[pbq][:], in_=ptv, func=AF.Copy), r=[self.ps_tok[pt]], w=[t_qT[pbq]])
                yield
                for half in range(2):
                    pb = 2 + half
                    self.proj_tok(self.bank(pb), pb, wg, t_wg, half * 512, 512, hT, [hT_tok[tt]],
                                  lambda k: hT[:, k, tt * 128:(tt + 1) * 128])
                    yield
                    s.op("act", lambda e: e.activation(out=sgd[pbq][:, half * 512:(half + 1) * 512], in_=self.bank(pb), func=AF.Silu),
                         r=[self.ps_tok[pb]], w=[t_sgd[pbq]])
                    yield

            self.lockstep([prologue(0)])
            for t in range(T):
                b = t % 2
                blk, tl = t // 4, t % 4
                yb = blk % 2
                kts = [kt for kt in (t - 1, t, t + 1) if 0 <= kt < T]
                nkt = len(kts)
                nk = 128 * nkt
                m0 = (kts[0] - (t - 1)) * 128
                def swa_lane(ln):
                    pp = ln
                    sb0 = 4 if ln == 0 else 2
                    ptb = 6 + ln
                    for c in range(4 * ln, 4 * ln + 4):
                        kv = c // 4
                        for j in range(2):
                            s.op("pe", lambda e, j=j: e.matmul(self.bank(sb0 + j)[:, 0:nk], qT[b][j * 64:(j + 1) * 64, c, :],
                                                               kTd[j * 64:(j + 1) * 64, kv, kts[0] * 128:kts[0] * 128 + nk],
                                                               start=True, stop=True),
                                 r=[t_qT[b]] + [t_kT[kt] for kt in kts], w=[self.ps_tok[sb0 + j]])
                        yield
                        yield from self.softmax_gen([sb0, sb0 + 1], 2, nk, self.mask_swa[:, m0:m0 + nk], self.t_const, Sm[pp], t_Sm[pp], Pe[pp], t_Pe[pp],
                                                    stt[pp], t_stt[pp], sink_ap=sink[:, 2 * c:2 * c + 2], t_sink=t_sink)
                        s.op("dve", lambda e: e.tensor_copy(out=densl[ln][b][:, 2 * (c % 4):2 * (c % 4) + 2], in_=stt[pp][:, 2:4]), r=[t_stt[pp]], w=[t_densl[ln][b]])
                        yield
                        self.transpose_P(Pe[pp], t_Pe[pp], 2, nkt, PT[pp], t_PT[pp], ptb)
                        yield
                        for j in range(2):
                            hq = 2 * c + j
                            ob = hq // 8
                            for kc in range(nkt):
                                s.op("pe", lambda e, j=j, kc=kc: e.matmul(self.bank(ob)[:, (hq % 8) * 64:(hq % 8 + 1) * 64],
                                                                          PT[pp][:, j * nkt + kc, :], vtok[:, kts[kc], kv * 64:(kv + 1) * 64],
                                                                          start=(kc == 0), stop=(kc == nkt - 1)),
                                     r=[t_PT[pp]] + [t_v[kt] for kt in kts], w=[self.ps_tok[ob]])
                        yield

                self.lockstep([swa_lane(0), swa_lane(1)])
                if t + 1 < T:
                    self.lockstep([prologue(t + 1)])
                for ln in range(2):
                    s.op("dve", lambda e, ln=ln: e.tensor_copy(out=dens[b][:, 8 * ln:8 * ln + 8], in_=densl[ln][b][:]), r=[t_densl[ln][b]], w=[t_dens[b]])
                s.op("dve", lambda e: e.reciprocal(out=dens[b][:, 16:32], in_=dens[b][:, 0:16]), r=[t_dens[b]], w=[t_dens[b]])
                for hq in range(16):
                    ob = hq // 8
                    s.op("dve", lambda e, hq=hq: e.scalar_tensor_tensor(out=yd[b][:, hq * 64:(hq + 1) * 64],
                                                                        in0=self.bank(ob)[:, (hq % 8) * 64:(hq % 8 + 1) * 64],
                                                                        scalar=dens[b][:, 16 + hq:17 + hq], in1=sgd[b][:, hq * 64:(hq + 1) * 64],
                                                                        op0=ALU.mult, op1=ALU.mult),
                         r=[self.ps_tok[ob], t_dens[b], t_sgd[b]], w=[t_yd[b]])
                self.to_ystage(yd[b], t_yd[b], 8, ystage[yb], t_ys[yb], tl, banks=[6, 7])
                if tl == 3 or t == T - 1:
                    self.store_ystage(ystage[yb], t_ys[yb], 8, 4, blk * 512, (tl + 1) * 128)
            s.barrier()

    def dft_fwd(self, L, x_tok, t_x, ncols, cb):
        s = self.s
        T = L // 128
        gc_d, gs_d = self.c_dft[L]
        with ExitStack() as es:
            tc = [self.sb(es, "tc%d" % i, [128, T, 128], BF16) for i in range(2)]
            ts = [self.sb(es, "ts%d" % i, [128, T, 128], BF16) for i in range(2)]
            t_tab = [Tok(), Tok()]
            s.dma("sp", tc[0][:], gc_d[0], w=[t_tab[0]])
            s.dma("sp", ts[0][:], gs_d[0], w=[t_tab[0]])
            for kc in range(T):
                b = kc % 2
                if kc + 1 < T:
                    s.dma("sp", tc[1 - b][:], gc_d[kc + 1], w=[t_tab[1 - b]])
                    s.dma("sp", ts[1 - b][:], gs_d[kc + 1], w=[t_tab[1 - b]])
                for half in range(ncols // 512):
                    bC = self.pick("dftC", [0, 2])
                    bS = bC + 1
                    for nci in range(T):
                        s.op("pe", lambda e, nci=nci: e.matmul(self.bank(bC), tc[b][:, nci, :], x_tok[:, nci, half * 512:(half + 1) * 512],
                                                               start=(nci == 0), stop=(nci == T - 1)),
                             r=[t_tab[b], t_x], w=[self.ps_tok[bC]])
                    for nci in range(T):
                        s.op("pe", lambda e, nci=nci: e.matmul(self.bank(bS), ts[b][:, nci, :], x_tok[:, nci, half * 512:(half + 1) * 512],
                                                               start=(nci == 0), stop=(nci == T - 1)),
                             r=[t_tab[b], t_x], w=[self.ps_tok[bS]])
                    cb(kc, half, bC, bS)
            s.barrier()

    def phase_filter(self, L):
        nc, s = self.nc, self.s
        T = L // 128
        HR, HI = self.H[L]
        CF, SF = self.CFSF
        c = self.c_filt[L]
        with ExitStack() as es:
            embT = self.sb(es, "embT", [33, L], F32)
            w1 = self.sb(es, "fw1", [33, 64], F32)
            w2 = self.sb(es, "fw2", [64, 64], F32)
            w3 = self.sb(es, "fw3", [64, 2048], F32)
            vec = self.sb(es, "fvec", [64, 3], F32)
            hid1 = self.sb(es, "hid1", [64, L], F32)
            hid2 = self.sb(es, "hid2", [64, L], F32)
            tcol = self.sb(es, "tcol", [128, T], F32)
            dbc = self.sb(es, "dbc", [128, 1024], F32)
            psi = self.sb(es, "psi", [128, 2, T], F32)
            t_c = Tok()
            s.dma("sp", embT[:], c["embT"], w=[t_c])
            s.dma("sp", w1[:], self.w["hy_filt_w1"], w=[t_c])
            s.dma("sp", w2[:], self.w["hy_filt_w2"], w=[t_c])
            s.dma("sp", w3[:], self.w["hy_filt_w3"], w=[t_c])
            s.dma("sp", vec[:], self.w["hy_fvec"], w=[t_c])
            s.dma("sp", tcol[:], c["tcol"], w=[t_c])
            s.dma("sp", dbc[:], self.c_delta.partition_broadcast(128), w=[t_c])
            s.dma("sp", psi[:], c["psi"], w=[t_c])
            arg = [self.sb(es, "farg%d" % i, [64, 512], F32) for i in range(2)]
            sn = [self.sb(es, "fsn%d" % i, [64, 512], F32) for i in range(2)]
            t_arg = [Tok(), Tok()]
            t_hid = Tok()
            for layer_i, (wm, kdim, src, dst, bcol) in enumerate(((w1, 33, embT, hid1, 0), (w2, 64, hid1, hid2, 1))):
                for blk in range(L // 512):
                    b = blk % 2
                    pb = self.pick("f", [4, 5])
                    s.op("pe", lambda e: e.matmul(self.bank(pb)[0:64, :], wm[0:kdim, :], src[0:kdim, blk * 512:(blk + 1) * 512],
                                                  start=True, stop=True), r=[t_c, t_hid], w=[self.ps_tok[pb]])
                    s.op("dve", lambda e: e.tensor_scalar(out=arg[b][:], in0=self.bank(pb)[0:64, :], scalar1=vec[:, bcol:bcol + 1],
                                                          scalar2=vec[:, 2:3], op0=ALU.add, op1=ALU.mult),
                         r=[self.ps_tok[pb], t_c], w=[t_arg[b]])
                    s.op("act", lambda e: e.activation(out=sn[b][:], in_=arg[b][:], func=AF.Sin, scale=1.0 / 3.0), r=[t_arg[b]], w=[t_arg[b]])
                    s.op("dve", lambda e: e.tensor_tensor(out=arg[b][:], in0=sn[b][:], in1=sn[b][:], op=ALU.mult), r=[t_arg[b]], w=[t_arg[b]])
                    s.op("dve", lambda e: e.tensor_scalar(out=arg[b][:], in0=arg[b][:], scalar1=-4.0, scalar2=3.0, op0=ALU.mult, op1=ALU.add),
                         r=[t_arg[b]], w=[t_arg[b]])
                    s.op("dve", lambda e: e.tensor_tensor(out=dst[:, blk * 512:(blk + 1) * 512], in0=sn[b][:], in1=arg[b][:], op=ALU.mult),
                         r=[t_arg[b]], w=[t_hid])
            filt = self.sb(es, "filt", [128, T, 1024], BF16)
            t_filt = Tok()
            dec = [self.sb(es, "fdec%d" % i, [128, 1024], F32) for i in range(2)]
            t_dec = [Tok(), Tok()]
            zc = [self.sb(es, "fzc%d" % i, [128, 4, 512], F32) for i in range(2)]
            t_zc = [Tok(), Tok()]
            ho = [self.sb(es, "fho%d" % i, [128, 2, 512], F32) for i in range(2)]
            t_ho = [Tok(), Tok()]
            for dirn in range(2):
                for t in range(T):
                    b = t % 2
                    s.op("act", lambda e: e.activation(out=dec[b][:], in_=dbc[:], func=AF.Exp, scale=tcol[:, t:t + 1]),
                         r=[t_c], w=[t_dec[b]])
                    for hb in range(2):
                        pb = self.pick("f", [4, 5])
                        s.op("pe", lambda e: e.matmul(self.bank(pb), hid2[:, t * 128:(t + 1) * 128],
                                                      w3[:, dirn * 1024 + hb * 512:dirn * 1024 + (hb + 1) * 512], start=True, stop=True),
                             r=[t_hid, t_c], w=[self.ps_tok[pb]])
                        s.op("dve", lambda e: e.tensor_tensor(out=filt[:, t, hb * 512:(hb + 1) * 512], in0=self.bank(pb),
                                                              in1=dec[b][:, hb * 512:(hb + 1) * 512], op=ALU.mult),
                             r=[self.ps_tok[pb], t_dec[b]], w=[t_filt])
                if dirn == 1:
                    s.op("dve", lambda e: e.memset(filt[0:1, 0, :], 0.0), w=[t_filt])

                def cb(kc, half, bC, bS, dirn=dirn):
                    b = (kc * 2 + half) % 2
                    hs = slice(half * 512, (half + 1) * 512)
                    if dirn == 0:
                        s.op("act", lambda e: e.activation(out=zc[b][:, 0, :], in_=self.bank(bC), func=AF.Copy), r=[self.ps_tok[bC]], w=[t_zc[b]])
                        s.op("act", lambda e: e.activation(out=zc[b][:, 1, :], in_=self.bank(bS), func=AF.Copy), r=[self.ps_tok[bS]], w=[t_zc[b]])
                        s.dma("sp", CF[kc, :, hs], zc[b][:, 0, :], r=[t_zc[b]], w=[self.t_cf])
                        s.dma("sp", SF[kc, :, hs], zc[b][:, 1, :], r=[t_zc[b]], w=[self.t_cf])
                    else:
                        s.dma("sp", zc[b][:, 0, :], CF[kc, :, hs], r=[self.t_cf], w=[t_zc[b]])
                        s.dma("sp", zc[b][:, 1, :], SF[kc, :, hs], r=[self.t_cf], w=[t_zc[b]])
                        Z = zc[b]
                        s.op("dve", lambda e: e.tensor_tensor(out=Z[:, 2, :], in0=Z[:, 0, :], in1=self.bank(bC), op=ALU.add),
                             r=[self.ps_tok[bC], t_zc[b]], w=[t_zc[b]])
                        s.op("dve", lambda e: e.tensor_tensor(out=Z[:, 0, :], in0=Z[:, 0, :], in1=self.bank(bC), op=ALU.subtract),
                             r=[self.ps_tok[bC], t_zc[b]], w=[t_zc[b]])
                        s.op("dve", lambda e: e.tensor_tensor(out=Z[:, 3, :], in0=Z[:, 1, :], in1=self.bank(bS), op=ALU.add),
                             r=[self.ps_tok[bS], t_zc[b]], w=[t_zc[b]])
                        s.op("dve", lambda e: e.tensor_tensor(out=Z[:, 1, :], in0=self.bank(bS), in1=Z[:, 1, :], op=ALU.subtract),
                             r=[self.ps_tok[bS], t_zc[b]], w=[t_zc[b]])
                        cps = psi[:, 0, kc:kc + 1]
                        sps = psi[:, 1, kc:kc + 1]
                        s.op("dve", lambda e: e.tensor_scalar(out=ho[b][:, 0, :], in0=Z[:, 2, :], scalar1=cps, scalar2=None, op0=ALU.mult),
                             r=[t_zc[b], t_c], w=[t_ho[b]])
                        s.op("dve", lambda e: e.scalar_tensor_tensor(out=ho[b][:, 0, :], in0=Z[:, 3, :], scalar=sps, in1=ho[b][:, 0, :],
                                                                     op0=ALU.mult, op1=ALU.add), r=[t_zc[b], t_c, t_ho[b]], w=[t_ho[b]])
                        s.op("dve", lambda e: e.tensor_scalar(out=ho[b][:, 1, :], in0=Z[:, 0, :], scalar1=sps, scalar2=None, op0=ALU.mult),
                             r=[t_zc[b], t_c], w=[t_ho[b]])
                        s.op("dve", lambda e: e.scalar_tensor_tensor(out=ho[b][:, 1, :], in0=Z[:, 1, :], scalar=cps, in1=ho[b][:, 1, :],
                                                                     op0=ALU.mult, op1=ALU.add), r=[t_zc[b], t_c, t_ho[b]], w=[t_ho[b]])
                        s.dma("sp", HR[kc, :, hs], ho[b][:, 0, :], r=[t_ho[b]], w=[self.t_H])
                        s.dma("sp", HI[kc, :, hs], ho[b][:, 1, :], r=[t_ho[b]], w=[self.t_H])

                self.dft_fwd(L, filt, t_filt, 1024, cb)
            s.barrier()

    def phase_hy1(self, si, L, hT, hT_tok):
        nc, s = self.nc, self.s
        w_in = self.w["e_w_in"]
        nblk = L // 512
        with ExitStack() as es:
            cw = self.sb(es, "hcw", [128, 24, 3], F32)
            cbias = self.sb(es, "hcb", [128, 24], F32)
            skip = self.sb(es, "hskip", [128, 8], F32)
            t_c = Tok()
            s.dma("sp", cw[:], self.w["hy_conv_w"], w=[t_c])
            s.dma("sp", cbias[:], self.w["hy_conv_b"], w=[t_c])
            s.dma("sp", skip[:], self.w["hy_skip"], w=[t_c])
            wts = [self.sb(es, "hw%d" % i, [128, 8, 4, 128], BF16) for i in range(2)]
            t_wts = [Tok(), Tok()]
            diag = [self.sb(es, "hdiag%d" % i, [128, 9, 128], BF16) for i in range(2)]
            t_diag = [Tok(), Tok()]
            zT = self.sb(es, "hzT", [128, 3, L + 2], BF16)
            t_zT = Tok()
            s.op("dve", lambda e: e.memset(zT[:, :, 0:1], 0.0), w=[t_zT])
            s.op("dve", lambda e: e.memset(zT[:, :, L + 1:L + 2], 0.0), w=[t_zT])
            xa = [self.sb(es, "hxa%d" % i, [128, 4, 512], F32) for i in range(2)]
            t_xa = [Tok(), Tok()]
            o16 = [self.sb(es, "ho16%d" % i, [128, 3, 512], BF16) for i in range(2)]
            t_o16 = [Tok(), Tok()]
            for c in range(8):
                wt, t_w = wts[c % 2], t_wts[c % 2]
                dg, t_dg = diag[c % 2], t_diag[c % 2]
                for a in range(4):
                    s.dma("pool", wt[:, :, a, :], w_in[:, a * 1024 + c * 128:a * 1024 + (c + 1) * 128].rearrange("(k p) c -> p k c", p=128),
                          w=[t_w])
                for a in range(3):
                    for j in range(3):
                        s.op("dve", lambda e, a=a, j=j: e.tensor_scalar(out=dg[:, a * 3 + j, :], in0=self.ident16[:],
                                                                        scalar1=cw[:, a * 8 + c, j:j + 1], scalar2=None, op0=ALU.mult),
                             r=[t_c, self.t_const], w=[t_dg])
                for blk in range(nblk):
                    for a in range(3):
                        pb = self.pick("proj", [2, 3])
                        self.proj_feat(self.bank(pb), pb, wt[:, :, a, :], t_w, 0, hT, hT_tok, blk * 512, 512)
                        s.op("act", lambda e, a=a: e.activation(out=zT[:, a, 1 + blk * 512:1 + (blk + 1) * 512], in_=self.bank(pb), func=AF.Copy),
                             r=[self.ps_tok[pb]], w=[t_zT])
                for blk in range(nblk):
                    b = blk % 2
                    X = xa[b]
                    for a in range(3):
                        pb = self.pick("conv", [4, 5])
                        for j in range(3):
                            s.op("pe", lambda e, a=a, j=j: e.matmul(self.bank(pb), dg[:, a * 3 + j, :], zT[:, a, blk * 512 + j:blk * 512 + j + 512],
                                                                    start=(j == 0), stop=(j == 2)), r=[t_dg, t_zT], w=[self.ps_tok[pb]])
                        s.op("act", lambda e, a=a: e.activation(out=X[:, a, :], in_=self.bank(pb), func=AF.Identity,
                                                                bias=cbias[:, a * 8 + c:a * 8 + c + 1]), r=[self.ps_tok[pb], t_c], w=[t_xa[b]])
                    pb = self.pick("proj", [2, 3])
                    self.proj_feat(self.bank(pb), pb, wt[:, :, 3, :], t_w, 0, hT, hT_tok, blk * 512, 512)
                    s.op("act", lambda e: e.activation(out=X[:, 3, :], in_=self.bank(pb), func=AF.Silu), r=[self.ps_tok[pb]], w=[t_xa[b]])
                    O = o16[b]
                    s.op("dve", lambda e: e.tensor_tensor(out=X[:, 2, :], in0=X[:, 2, :], in1=X[:, 1, :], op=ALU.mult), r=[t_xa[b]], w=[t_xa[b]])
                    s.op("pool", lambda e: e.tensor_tensor(out=X[:, 0, :], in0=X[:, 0, :], in1=X[:, 3, :], op=ALU.mult), r=[t_xa[b]], w=[t_xa[b]])
                    s.op("act", lambda e: e.activation(out=O[:, 0, :], in_=X[:, 2, :], func=AF.Copy), r=[t_xa[b]], w=[t_o16[b]])
                    s.op("act", lambda e: e.activation(out=O[:, 1, :], in_=X[:, 0, :], func=AF.Copy), r=[t_xa[b]], w=[t_o16[b]])
                    s.op("dve", lambda e: e.scalar_tensor_tensor(out=O[:, 2, :], in0=X[:, 2, :], scalar=skip[:, c:c + 1], in1=X[:, 0, :],
                                                                 op0=ALU.mult, op1=ALU.mult), r=[t_xa[b], t_c], w=[t_o16[b]])
                    for a, dr in enumerate((self.UT, self.AT, self.BT)):
                        s.dma("sp", dr[c * 128:(c + 1) * 128, blk * 512:(blk + 1) * 512], O[:, a, :], r=[t_o16[b]], w=[self.t_uab])
            s.barrier()

    def phase_hy2(self, si, L):
        nc, s = self.nc, self.s
        T = L // 128
        HR, HI = self.H[L]
        ZR, ZS = self.ZRS
        gc_d, gs_d = self.c_dft[L]
        with ExitStack() as es:
            u_tok = self.sb(es, "u_tok", [128, T, 1024], BF16)
            t_u = Tok()
            with ExitStack() as es2:
                ut = [self.sb(es2, "utl%d" % i, [128, L], BF16) for i in range(2)]
                t_ut = [Tok(), Tok()]
                for c in range(8):
                    b = c % 2
                    s.dma("sp", ut[b][:], self.UT[c * 128:(c + 1) * 128, 0:L], r=[self.t_uab], w=[t_ut[b]])
                    for t0 in range(0, T, 8):
                        n = min(8, T - t0)
                        pt = self.pick("pt", [6, 7])
                        ptv = self.bank16(pt).rearrange("p (j c) -> p j c", j=8)
                        for i in range(n):
                            s.op("pe", lambda e, i=i: e.transpose(out=ptv[:, i, :], in_=ut[b][:, (t0 + i) * 128:(t0 + i + 1) * 128],
                                                                  identity=self.ident16[:]), r=[t_ut[b], self.t_const], w=[self.ps_tok[pt]])
                        s.op("act", lambda e: e.activation(out=u_tok[:, t0:t0 + n, c * 128:(c + 1) * 128], in_=ptv[:, 0:n, :], func=AF.Copy),
                             r=[self.ps_tok[pt]], w=[t_u])
                s.barrier()
            hh = [self.sb(es, "hh%d" % i, [128, 2, 512], F32) for i in range(2)]
            t_hh = [Tok(), Tok()]
            cs = [self.sb(es, "cs%d" % i, [128, 2, 512], F32) for i in range(2)]
            t_cs = [Tok(), Tok()]
            tt = [self.sb(es, "tt%d" % i, [128, 4, 512], F32) for i in range(2)]
            t_tt = [Tok(), Tok()]
            zz = [self.sb(es, "zz%d" % i, [128, 2, 512], BF16) for i in range(2)]
            t_zz = [Tok(), Tok()]

            def cb(kc, half, bC, bS):
                b = (kc * 2 + half) % 2
                hs = slice(half * 512, (half + 1) * 512)
                s.dma("sp", hh[b][:, 0, :], HR[kc, :, hs], r=[self.t_H], w=[t_hh[b]])
                s.dma("sp", hh[b][:, 1, :], HI[kc, :, hs], r=[self.t_H], w=[t_hh[b]])
                s.op("act", lambda e: e.activation(out=cs[b][:, 0, :], in_=self.bank(bC), func=AF.Copy), r=[self.ps_tok[bC]], w=[t_cs[b]])
                s.op("act", lambda e: e.activation(out=cs[b][:, 1, :], in_=self.bank(bS), func=AF.Copy), r=[self.ps_tok[bS]], w=[t_cs[b]])
                TT = tt[b]
                s.op("dve", lambda e: e.tensor_tensor(out=TT[:, 0, :], in0=hh[b][:, 0, :], in1=cs[b][:, 0, :], op=ALU.mult), r=[t_hh[b], t_cs[b]], w=[t_tt[b]])
                s.op("pool", lambda e: e.tensor_tensor(out=TT[:, 1, :], in0=hh[b][:, 1, :], in1=cs[b][:, 1, :], op=ALU.mult), r=[t_hh[b], t_cs[b]], w=[t_tt[b]])
                s.op("pool", lambda e: e.tensor_tensor(out=TT[:, 2, :], in0=hh[b][:, 0, :], in1=cs[b][:, 1, :], op=ALU.mult), r=[t_hh[b], t_cs[b]], w=[t_tt[b]])
                s.op("dve", lambda e: e.tensor_tensor(out=TT[:, 3, :], in0=hh[b][:, 1, :], in1=cs[b][:, 0, :], op=ALU.mult), r=[t_hh[b], t_cs[b]], w=[t_tt[b]])
                s.op("dve", lambda e: e.tensor_tensor(out=zz[b][:, 0, :], in0=TT[:, 0, :], in1=TT[:, 1, :], op=ALU.add), r=[t_tt[b]], w=[t_zz[b]])
                s.op("pool", lambda e: e.tensor_tensor(out=zz[b][:, 1, :], in0=TT[:, 2, :], in1=TT[:, 3, :], op=ALU.subtract), r=[t_tt[b]], w=[t_zz[b]])
                s.dma("sp", ZR[kc, :, hs], zz[b][:, 0, :], r=[t_zz[b]], w=[self.t_Z])
                s.dma("sp", ZS[kc, :, hs], zz[b][:, 1, :], r=[t_zz[b]], w=[self.t_Z])

            self.dft_fwd(L, u_tok, t_u, 1024, cb)
        with ExitStack() as es:
            zr = self.sb(es, "zr", [128, T, 512], BF16)
            zs = self.sb(es, "zs", [128, T, 512], BF16)
            t_z = Tok()
            tc = [self.sb(es, "itc%d" % i, [128, T, 128], BF16) for i in range(2)]
            ts = [self.sb(es, "its%d" % i, [128, T, 128], BF16) for i in range(2)]
            t_tab = [Tok(), Tok()]
            y16 = [self.sb(es, "y16%d" % i, [128, 512], BF16) for i in range(2)]
            t_y16 = [Tok(), Tok()]
            ystage = [self.sb(es, "hystage%d" % i, [128, 4, 512], BF16) for i in range(2)]
            t_ys = [Tok(), Tok()]
            ab = [self.sb(es, "hab%d" % i, [128, 2, 4, 512], BF16) for i in range(2)]
            t_ab = [Tok(), Tok()]
            for ch in range(2):
                cs_ = slice(ch * 512, (ch + 1) * 512)
                s.dma("sp", zr[:], ZR[0:T, :, cs_].rearrange("k p c -> p k c"), r=[self.t_Z], w=[t_z])
                s.dma("sp", zs[:], ZS[0:T, :, cs_].rearrange("k p c -> p k c"), r=[self.t_Z], w=[t_z])
                for nci in range(T):
                    b = nci % 2
                    blk, tl = nci // 4, nci % 4
                    yb = blk % 2
                    if tl == 0:
                        rows = slice(ch * 512, (ch + 1) * 512)
                        s.dma("sp", ab[yb][:, 0, :, :], self.AT[rows, blk * 512:(blk + 1) * 512].rearrange("(c p) n -> p c n", p=128),
                              r=[self.t_uab], w=[t_ab[yb]])
                        s.dma("sp", ab[yb][:, 1, :, :], self.BT[rows, blk * 512:(blk + 1) * 512].rearrange("(c p) n -> p c n", p=128),
                              r=[self.t_uab], w=[t_ab[yb]])
                    if nci == 0:
                        s.dma("sp", tc[0][:], gc_d[0], w=[t_tab[0]])
                        s.dma("sp", ts[0][:], gs_d[0], w=[t_tab[0]])
                    if nci + 1 < T:
                        s.dma("sp", tc[1 - b][:], gc_d[nci + 1], w=[t_tab[1 - b]])
                        s.dma("sp", ts[1 - b][:], gs_d[nci + 1], w=[t_tab[1 - b]])
                    pb = self.pick("inv", [0, 1, 2, 3])
                    for kc in range(T):
                        s.op("pe", lambda e, kc=kc: e.matmul(self.bank(pb), tc[b][:, kc, :], zr[:, kc, :], start=(kc == 0), stop=False),
                             r=[t_tab[b], t_z], w=[self.ps_tok[pb]])
                    for kc in range(T):
                        s.op("pe", lambda e, kc=kc: e.matmul(self.bank(pb), ts[b][:, kc, :], zs[:, kc, :], start=False, stop=(kc == T - 1)),
                             r=[t_tab[b], t_z], w=[self.ps_tok[pb]])
                    s.op("act", lambda e: e.activation(out=y16[b][:], in_=self.bank(pb), func=AF.Copy), r=[self.ps_tok[pb]], w=[t_y16[b]])
                    self.to_ystage(y16[b], t_y16[b], 4, ystage[yb], t_ys[yb], tl, banks=[6, 7])
                    if tl == 3:
                        s.op("dve", lambda e: e.tensor_tensor(out=ystage[yb][:], in0=ystage[yb][:], in1=ab[yb][:, 0, :, :], op=ALU.mult),
                             r=[t_ys[yb], t_ab[yb]], w=[t_ys[yb]])
                        s.op("dve", lambda e: e.tensor_tensor(out=ystage[yb][:], in0=ystage[yb][:], in1=ab[yb][:, 1, :, :], op=ALU.add),
                             r=[t_ys[yb], t_ab[yb]], w=[t_ys[yb]])
                        self.store_ystage(ystage[yb], t_ys[yb], 4, ch * 4, blk * 512, 512)
            s.barrier()

    def phase_gdn1(self, si, L, hT, hT_tok):
        nc, s = self.nc, self.s
        T = L // 128
        nblk = L // 512
        w_in = self.w["e_w_in"]
        with ExitStack() as es:
            cw = self.sb(es, "gcw", [128, 24, 5], F32)
            t_c = Tok()
            s.dma("sp", cw[:], self.w["gdn_conv_w"], w=[t_c])
            ones_r = self.sb(es, "ones_r", [128, 128], F32R)
            s.op("dve", lambda e: e.tensor_scalar(out=ones_r[:], in0=self.ident32[:], scalar1=0.0, scalar2=1.0, op0=ALU.mult, op1=ALU.add),
                 r=[self.t_const], w=[t_c])
            wts = [self.sb(es, "gw%d" % i, [128, 8, 3, 128], BF16) for i in range(2)]
            t_wts = [Tok(), Tok()]
            diag = [self.sb(es, "gdiag%d" % i, [128, 15, 128], BF16) for i in range(2)]
            t_diag = [Tok(), Tok()]
            zT = self.sb(es, "gzT", [128, 3, L + 4], BF16)
            t_zT = Tok()
            s.op("dve", lambda e: e.memset(zT[:, :, 0:2], 0.0), w=[t_zT])
            s.op("dve", lambda e: e.memset(zT[:, :, L + 2:L + 4], 0.0), w=[t_zT])
            xs = [self.sb(es, "gx%d" % i, [128, 3, 512], F32) for i in range(2)]
            t_xs = [Tok(), Tok()]
            sq = [self.sb(es, "gsq%d" % i, [128, 512], F32R) for i in range(2)]
            t_sq = [Tok(), Tok()]
            rs = [self.sb(es, "grs%d" % i, [128, 512], F32) for i in range(2)]
            t_rs = [Tok(), Tok()]
            kst = [self.sb(es, "gkst%d" % i, [128, 4, 128], F32) for i in range(2)]
            t_kst = [Tok(), Tok()]
            for h in range(8):
                wt, t_w = wts[h % 2], t_wts[h % 2]
                dg, t_dg = diag[h % 2], t_diag[h % 2]
                for a in range(3):
                    c0 = E_QKV + a * 1024 + h * 128
                    s.dma("pool", wt[:, :, a, :], w_in[:, c0:c0 + 128].rearrange("(k p) c -> p k c", p=128), w=[t_w])
                    for j in range(5):
                        s.op("dve", lambda e, a=a, j=j: e.tensor_scalar(out=dg[:, a * 5 + j, :], in0=self.ident16[:],
                                                                        scalar1=cw[:, a * 8 + h, j:j + 1], scalar2=None, op0=ALU.mult),
                             r=[t_c, self.t_const], w=[t_dg])
                for blk in range(nblk):
                    for a in range(3):
                        pb = self.pick("proj", [2, 3])
                        self.proj_feat(self.bank(pb), pb, wt[:, :, a, :], t_w, 0, hT, hT_tok, blk * 512, 512)
                        s.op("act", lambda e, a=a: e.activation(out=zT[:, a, 2 + blk * 512:2 + (blk + 1) * 512], in_=self.bank(pb), func=AF.Copy),
                             r=[self.ps_tok[pb]], w=[t_zT])
                for blk in range(nblk):
                    b = blk % 2
                    X = xs[b]
                    for a in range(3):
                        pb = self.pick("conv", [4, 5])
                        for j in range(5):
                            s.op("pe", lambda e, a=a, j=j: e.matmul(self.bank(pb), dg[:, a * 5 + j, :], zT[:, a, blk * 512 + j:blk * 512 + j + 512],
                                                                    start=(j == 0), stop=(j == 4)), r=[t_dg, t_zT], w=[self.ps_tok[pb]])
                        s.op("act", lambda e, a=a: e.activation(out=X[:, a, :], in_=self.bank(pb), func=AF.Silu), r=[self.ps_tok[pb]], w=[t_xs[b]])
                    for a in range(2):
                        bb = (blk * 2 + a) % 2
                        s.op("dve", lambda e, a=a: e.tensor_tensor(out=sq[bb][:], in0=X[:, a, :], in1=X[:, a, :], op=ALU.mult), r=[t_xs[b]], w=[t_sq[bb]])
                        pb = self.pick("ss", [0, 1])
                        s.op("pe", lambda e: e.matmul(self.bank(pb), ones_r[:], sq[bb][:], start=True, stop=True), r=[t_c, t_sq[bb]], w=[self.ps_tok[pb]])
                        s.op("act", lambda e: e.activation(out=rs[bb][:], in_=self.bank(pb), func=AF.Ln, bias=self.eps_col[:, 0:1]), r=[self.ps_tok[pb], self.t_const], w=[t_rs[bb]])
                        s.op("act", lambda e: e.activation(out=rs[bb][:], in_=rs[bb][:], func=AF.Exp, scale=-0.5), r=[t_rs[bb]], w=[t_rs[bb]])
                        sc = (128.0 ** -0.5) if a == 0 else 1.0
                        s.op("dve", lambda e, a=a: e.scalar_tensor_tensor(out=X[:, a, :], in0=X[:, a, :], scalar=sc, in1=rs[bb][:],
                                                                          op0=ALU.mult, op1=ALU.mult), r=[t_xs[b], t_rs[bb]], w=[t_xs[b]])
                        dr = self.QT if a == 0 else self.KT
                        s.dma("sp", dr[h * 128:(h + 1) * 128, blk * 512:(blk + 1) * 512], X[:, a, :], r=[t_xs[b]], w=[self.t_gd])
                    for a in (1, 2):
                        bb = (blk * 2 + a) % 2
                        pt = self.pick("pt", [6, 7])
                        ptv = self.bank(pt).rearrange("p (j c) -> p j c", j=4)
                        for i in range(4):
                            s.op("pe", lambda e, a=a, i=i: e.transpose(out=ptv[:, i, :], in_=X[:, a, i * 128:(i + 1) * 128], identity=self.ident32[:]),
                                 r=[t_xs[b], self.t_const], w=[self.ps_tok[pt]])
                        s.op("act", lambda e: e.activation(out=kst[bb][:], in_=ptv, func=AF.Copy), r=[self.ps_tok[pt]], w=[t_kst[bb]])
                        dr = self.KTOK if a == 1 else self.VTOK
                        s.dma("sp", dr[blk * 512:(blk + 1) * 512, h * 128:(h + 1) * 128].rearrange("(i p) d -> p i d", p=128), kst[bb][:],
                              r=[t_kst[bb]], w=[self.t_gd])
            wg = self.sb(es, "gwg", [128, 8, 1024], BF16)
            wbg = self.sb(es, "gwbg", [128, 8, 32], BF16)
            t_wg = Tok()
            self.load_w(wg, t_wg, w_in, E_GG, 1024)
            self.load_w(wbg, t_wg, w_in, E_BETA, 32)
            rows = self.sb(es, "grows", [128, 2, 16], F32)
            s.dma("sp", rows[:, 0, :], self.w["gdn_A_log"].partition_broadcast(128), w=[t_c])
            s.dma("sp", rows[:, 1, :], self.w["gdn_dt_bias"].partition_broadcast(128), w=[t_c])
            s.op("act", lambda e: e.activation(out=rows[:, 0, :], in_=rows[:, 0, :], func=AF.Exp), r=[t_c], w=[t_c])
            s.op("dve", lambda e: e.tensor_scalar(out=rows[:, 0, :], in0=rows[:, 0, :], scalar1=-1.0, scalar2=None, op0=ALU.mult), r=[t_c], w=[t_c])
            sg = [self.sb(es, "gsg%d" % i, [128, 1024], BF16) for i in range(2)]
            t_sg = [Tok(), Tok()]
            bgt = [self.sb(es, "gbgt%d" % i, [128, 4, 16], F32) for i in range(2)]
            t_bgt = [Tok(), Tok()]
            bgo = [self.sb(es, "gbgo%d" % i, [128, 32], F32) for i in range(2)]
            t_bgo = [Tok(), Tok()]
            for t in range(T):
                b = t % 2
                for half in range(2):
                    pb = 2 + half
                    self.proj_tok(self.bank(pb), pb, wg, t_wg, half * 512, 512, hT, [hT_tok[t]], lambda k: hT[:, k, t * 128:(t + 1) * 128])
                    s.op("act", lambda e: e.activation(out=sg[b][:, half * 512:(half + 1) * 512], in_=self.bank(pb), func=AF.Silu),
                         r=[self.ps_tok[pb]], w=[t_sg[b]])
                s.dma("sp", self.SGG[t * 128:(t + 1) * 128, :], sg[b][:], r=[t_sg[b]], w=[self.t_gd])
                pb = self.pick("conv", [4, 5])
                self.proj_tok(self.bank(pb)[:, 0:32], pb, wbg, t_wg, 0, 32, hT, [hT_tok[t]], lambda k: hT[:, k, t * 128:(t + 1) * 128])
                B = bgt[b]
                s.op("act", lambda e: e.activation(out=bgo[b][:, 0:16], in_=self.bank(pb)[:, 0:16], func=AF.Sigmoid), r=[self.ps_tok[pb]], w=[t_bgo[b]])
                s.op("dve", lambda e: e.tensor_tensor(out=B[:, 0, :], in0=self.bank(pb)[:, 16:32], in1=rows[:, 1, :], op=ALU.add), r=[self.ps_tok[pb], t_c], w=[t_bgt[b]])
                s.op("act", lambda e: e.activation(out=B[:, 1, :], in_=B[:, 0, :], func=AF.Abs), r=[t_bgt[b]], w=[t_bgt[b]])
                s.op("act", lambda e: e.activation(out=B[:, 1, :], in_=B[:, 1, :], func=AF.Exp, scale=-1.0), r=[t_bgt[b]], w=[t_bgt[b]])
                s.op("act", lambda e: e.activation(out=B[:, 1, :], in_=B[:, 1, :], func=AF.Ln, bias=self.eps_col[:, 1:2]), r=[t_bgt[b], self.t_const], w=[t_bgt[b]])
                s.op("dve", lambda e: e.tensor_scalar(out=B[:, 2, :], in0=B[:, 0, :], scalar1=0.0, scalar2=None, op0=ALU.max), r=[t_bgt[b]], w=[t_bgt[b]])
                s.op("dve", lambda e: e.tensor_tensor(out=B[:, 2, :], in0=B[:, 2, :], in1=B[:, 1, :], op=ALU.add), r=[t_bgt[b]], w=[t_bgt[b]])
                s.op("dve", lambda e: e.tensor_tensor(out=bgo[b][:, 16:32], in0=B[:, 2, :], in1=rows[:, 0, :], op=ALU.mult), r=[t_bgt[b], t_c], w=[t_bgo[b]])
                s.dma("sp", self.BG[t * 128:(t + 1) * 128, :], bgo[b][:], r=[t_bgo[b]], w=[self.t_gd])
            s.barrier()

    @staticmethod
    def lockstep(gens):
        gens = list(gens)
        while gens:
            for g in list(gens):
                try:
                    next(g)
                except StopIteration:
                    gens.remove(g)

    def phase_gdn2(self, si, L):
        self.phase_gdnP(L)
        self.phase_gdnR(L)
        self.phase_gdnC(L)

    def phase_gdnP(self, L):
        nc, s = self.nc, self.s
        T = L // 128
        QT3 = self.QT.rearrange("(h d) n -> d h n", d=128)
        KT3 = self.KT.rearrange("(h d) n -> d h n", d=128)
        with ExitStack() as es:
            cm = self.sb(es, "gmask", [128, 18, 128], F32)
            t_c = Tok()
            s.dma("sp", cm[:], self.c_gmask, w=[t_c])
            ones32 = self.sb(es, "ones32", [128, 128], F32)
            s.op("dve", lambda e: e.memset(ones32[:], 1.0), w=[t_c])
            ident_r = self.sb(es, "ident_r", [128, 128], F32R)
            s.op("dve", lambda e: e.tensor_copy(out=ident_r[:], in_=self.ident32[:]), r=[self.t_const], w=[t_c])
            idb = self.ident32[:].unsqueeze(1).to_broadcast([128, 4, 128])
            units = [(dirn, t, qd) for dirn in range(2) for t in range(T) for qd in range(2)]
            NL = 4

            def lane(li):
                def A(nm, dt=F32, shp=(128, 4, 128)):
                    return self.sb(es, "L%d%s" % (li, nm), list(shp), dt), Tok()
                lq, t_lq = A("lq")
                lk, t_lk = A("lk")
                qr, t_qr = A("qr", F32R)
                kr, t_kr = A("kr", F32R)
                lkt, t_lkt = A("lkt")
                lvt, t_lvt = A("lvt")
                ET, t_ET = A("ET")
                Bt, t_Bt = A("Bt")
                Ct, t_Ct = A("Ct")
                Bm, t_Bm = A("Bm", F32R)
                Cm, t_Cm = A("Cm", F32R)
                P, t_P = A("P", F32R)
                Q, t_Q = A("Q", F32R)
                Wn, t_Wn = A("Wn", F32R)
                Vn, t_Vn = A("Vn", F32R)
                QKm, t_QKm = A("QKm")
                ub, t_ub = A("ub")
                wT, t_wT = A("wT")
                kd, t_kd = A("kd")
                bg, t_bg = A("bg", F32, (128, 32))
                sc, t_sc = A("sc", F32, (128, 4, 4))
                scc, t_scc = A("scc", F32, (64, 2, 4))
                glb, t_glb = A("glb", F32, (128, 4, 2))
                Dm, t_Dm = lq, t_lq
                dgG, t_dgG = lk, t_lk
                vr, t_vr = qr, t_qr
                kg, t_kg = kr, t_kr
                bA, bB = 2 * li, 2 * li + 1
                pA, pB = self.ps_tok[bA], self.ps_tok[bB]
                v4 = lambda bk: self.bank(bk).rearrange("p (h c) -> p h c", h=4)
                for ui in range(li, len(units), NL):
                    dirn, t, qd = units[ui]
                    mo = 8 * dirn
                    mo2 = 8 * (1 - dirn)
                    hs = slice(qd * 4, qd * 4 + 4)
                    cs_ = slice(t * 128, (t + 1) * 128)
                    fs = slice(qd * 512, (qd + 1) * 512)
                    s.dma("sp", lq[:], QT3[:, hs, cs_], r=[self.t_gd], w=[t_lq])
                    s.dma("sp", lk[:], KT3[:, hs, cs_], r=[self.t_gd], w=[t_lk])
                    s.dma("sp", lkt[:], self.KTOK[cs_, fs].rearrange("p (h d) -> p h d", h=4), r=[self.t_gd], w=[t_lkt])
                    s.dma("sp", lvt[:], self.VTOK[cs_, fs].rearrange("p (h d) -> p h d", h=4), r=[self.t_gd], w=[t_lvt])
                    s.dma("sp", bg[:], self.BG[cs_, :], r=[self.t_gd], w=[t_bg])
                    yield
                    s.op("act", lambda e: e.activation(out=qr[:], in_=lq[:], func=AF.Copy), r=[t_lq], w=[t_qr])
                    s.op("act", lambda e: e.activation(out=kr[:], in_=lk[:], func=AF.Copy), r=[t_lk], w=[t_kr])
                    beta = bg[:, dirn * 8 + qd * 4:dirn * 8 + qd * 4 + 4]
                    gcol = bg[:, 16 + dirn * 8 + qd * 4:16 + dirn * 8 + qd * 4 + 4]
                    s.op("pe", lambda e: e.matmul(self.bank(bA)[:, 0:4], cm[:, 16 + dirn, :], gcol, start=True, stop=True),
                         r=[t_bg, t_c], w=[pA])
                    for cc in range(2):
                        s.op("pe", lambda e, cc=cc: e.matmul(self.bank(bA)[0:64, 16 + cc * 4:20 + cc * 4], cm[:, 16 + dirn, cc * 64:(cc + 1) * 64], gcol,
                                                             start=True, stop=True), r=[t_bg, t_c], w=[pA])
                    for hh in range(4):
                        s.op("pe", lambda e, hh=hh: e.matmul(self.bank(bB)[:, hh * 128:(hh + 1) * 128], kr[:, hh, :], kr[:, hh, :], start=True, stop=True),
                             r=[t_kr], w=[pB])
                    yield
                    s.op("dve", lambda e: e.tensor_copy(out=sc[:, :, 0], in_=self.bank(bA)[:, 0:4]), r=[pA], w=[t_sc])
                    s.op("act", lambda e: e.activation(out=sc[:, :, 1], in_=self.bank(bA)[:, 0:4], func=AF.Exp), r=[pA], w=[t_sc])
                    s.op("act", lambda e: e.activation(out=scc[:], in_=self.bank(bA)[0:64, 16:24].rearrange("p (c h) -> p c h", c=2), func=AF.Exp),
                         r=[pA], w=[t_scc])
                    yield
                    for hh in range(4):
                        s.op("dve", lambda e, hh=hh: e.tensor_scalar(out=dgG[:, hh, :], in0=self.ident32[:], scalar1=sc[:, hh, 0:1], scalar2=None, op0=ALU.mult),
                             r=[t_sc, self.t_const], w=[t_dgG])
                    s.op("pe", lambda e: e.matmul(self.bank(bA), ones32[:], dgG[:].rearrange("p h c -> p (h c)"), start=True, stop=True),
                         r=[t_c, t_dgG], w=[pA])
                    yield
                    gbc = v4(bA)
                    lastc = (63, 127) if dirn == 0 else (0, 64)
                    s.op("dve", lambda e: e.tensor_tensor(out=Dm[:], in0=gbc, in1=sc[:, :, 0:1].to_broadcast([128, 4, 128]), op=ALU.subtract),
                         r=[pA, t_sc], w=[t_Dm])
                    s.op("act", lambda e: e.activation(out=glb[:], in_=gbc[:, :, lastc[0]:lastc[1] + 1:64], func=AF.Exp), r=[pA], w=[t_glb])
                    for cc in range(2):
                        rr = slice(cc * 64, cc * 64 + 64)
                        s.op("dve", lambda e, cc=cc, rr=rr: e.tensor_tensor(out=sc[rr, :, 2], in0=gbc[rr, :, lastc[cc]], in1=sc[rr, :, 0], op=ALU.subtract),
                             r=[pA, t_sc], w=[t_sc])
                    yield
                    s.op("dve", lambda e: e.scalar_tensor_tensor(out=Dm[:], in0=Dm[:], scalar=0.0, in1=cm[:, mo + 0, :].unsqueeze(1).to_broadcast([128, 4, 128]),
                                                                 op0=ALU.min, op1=ALU.add), r=[t_Dm, t_c], w=[t_Dm])
                    s.op("act", lambda e: e.activation(out=sc[:, :, 2], in_=sc[:, :, 2], func=AF.Exp), r=[t_sc], w=[t_sc])
                    yield
                    s.op("act", lambda e: e.activation(out=ET[:], in_=Dm[:], func=AF.Exp), r=[t_Dm], w=[t_ET])
                    yield
                    s.op("dve", lambda e: e.tensor_tensor(out=Bt[:], in0=v4(bB), in1=ET[:], op=ALU.mult), r=[pB, t_ET], w=[t_Bt])
                    yield
                    for hh in range(4):
                        s.op("pe", lambda e, hh=hh: e.matmul(self.bank(bB)[:, hh * 128:(hh + 1) * 128], kr[:, hh, :], qr[:, hh, :], start=True, stop=True),
                             r=[t_kr, t_qr], w=[pB])
                    s.op("dve", lambda e: e.tensor_tensor(out=Bt[:], in0=Bt[:], in1=cm[:, mo + 1, :].unsqueeze(1).to_broadcast([128, 4, 128]), op=ALU.mult),
                         r=[t_Bt, t_c], w=[t_Bt])
                    yield
                    s.op("dve", lambda e: e.tensor_tensor(out=Bt[:], in0=Bt[:], in1=beta.unsqueeze(2).to_broadcast([128, 4, 128]), op=ALU.mult),
                         r=[t_Bt, t_bg], w=[t_Bt])
                    s.op("dve", lambda e: e.tensor_tensor(out=QKm[:], in0=v4(bB), in1=ET[:], op=ALU.mult), r=[pB, t_ET], w=[t_QKm])
                    s.dma("sp", self.G_QKM[dirn, t, :, hs, :], QKm[:], r=[t_QKm], w=[self.t_gp])
                    yield
                    for hh in range(4):
                        s.op("pe", lambda e, hh=hh: e.transpose(out=self.bank(bA)[:, hh * 128:(hh + 1) * 128], in_=Bt[:, hh, :], identity=self.ident32[:]),
                             r=[t_Bt, self.t_const], w=[pA])
                    s.op("pool", lambda e: e.tensor_tensor(out=kd[:], in0=lkt[:], in1=sc[:, :, 2:3].to_broadcast([128, 4, 128]), op=ALU.mult),
                         r=[t_lkt, t_sc], w=[t_kd])
                    s.dma("sp", self.G_KD[dirn, t, :, fs].rearrange("p (h d) -> p h d", h=4), kd[:], r=[t_kd], w=[self.t_gp])
                    s.dma("sp", self.G_SCC[dirn, t, :, :, hs], scc[:], r=[t_scc], w=[self.t_gp])
                    s.dma("sp", self.G_GLB[dirn, t, :, hs, :], glb[:], r=[t_glb], w=[self.t_gp])
                    yield
                    s.op("act", lambda e: e.activation(out=Ct[:], in_=v4(bA), func=AF.Copy), r=[pA], w=[t_Ct])
                    s.op("dve", lambda e: e.tensor_tensor(out=Bm[:], in0=Bt[:], in1=cm[:, mo + 2, :].unsqueeze(1).to_broadcast([128, 4, 128]), op=ALU.mult),
                         r=[t_Bt, t_c], w=[t_Bm])
                    yield
                    s.op("dve", lambda e: e.tensor_tensor(out=Cm[:], in0=Ct[:], in1=cm[:, mo2 + 2, :].unsqueeze(1).to_broadcast([128, 4, 128]), op=ALU.mult),
                         r=[t_Ct, t_c], w=[t_Cm])
                    s.op("dve", lambda e: e.scalar_tensor_tensor(out=P[:], in0=Bm[:], scalar=-1.0, in1=idb, op0=ALU.mult, op1=ALU.add),
                         r=[t_Bm, self.t_const], w=[t_P])
                    yield
                    s.op("dve", lambda e: e.scalar_tensor_tensor(out=Q[:], in0=Cm[:], scalar=-1.0, in1=idb, op0=ALU.mult, op1=ALU.add),
                         r=[t_Cm, self.t_const], w=[t_Q])
                    yield
                    for lv in range(1, 6):
                        last = (lv == 5)
                        s.op("dve", lambda e, lv=lv: e.tensor_tensor(out=Bm[:], in0=Bt[:], in1=cm[:, mo + 2 + lv, :].unsqueeze(1).to_broadcast([128, 4, 128]),
                                                                     op=ALU.mult), r=[t_Bt, t_c], w=[t_Bm])
                        if not last:
                            s.op("pool", lambda e, lv=lv: e.tensor_tensor(out=Cm[:], in0=Ct[:], in1=cm[:, mo2 + 2 + lv, :].unsqueeze(1).to_broadcast([128, 4, 128]),
                                                                          op=ALU.mult), r=[t_Ct, t_c], w=[t_Cm])
                        yield
                        for hh in range(4):
                            s.op("pe", lambda e, hh=hh: e.matmul(self.bank(bA)[:, hh * 128:(hh + 1) * 128], Bm[:, hh, :], Q[:, hh, :], start=True, stop=True),
                                 r=[t_Bm, t_Q], w=[pA])
                        if not last:
                            for hh in range(4):
                                s.op("pe", lambda e, hh=hh: e.matmul(self.bank(bB)[:, hh * 128:(hh + 1) * 128], Cm[:, hh, :], P[:, hh, :], start=True, stop=True),
                                     r=[t_Cm, t_P], w=[pB])
                        yield
                        s.op("act", lambda e: e.activation(out=Wn[:], in_=v4(bA), func=AF.Copy, scale=-1.0), r=[pA], w=[t_Wn])
                        if not last:
                            s.op("dve", lambda e: e.tensor_scalar(out=Vn[:], in0=v4(bB), scalar1=-1.0, scalar2=None, op0=ALU.mult), r=[pB], w=[t_Vn])
                        yield
                        for hh in range(4):
                            s.op("pe", lambda e, hh=hh: e.matmul(self.bank(bA)[:, hh * 128:(hh + 1) * 128], Wn[:, hh, :], P[:, hh, :], start=True, stop=True),
                                 r=[t_Wn, t_P], w=[pA])
                        if not last:
                            for hh in range(4):
                                s.op("pe", lambda e, hh=hh: e.matmul(self.bank(bB)[:, hh * 128:(hh + 1) * 128], Vn[:, hh, :], Q[:, hh, :], start=True, stop=True),
                                     r=[t_Vn, t_Q], w=[pB])
                        yield
                        s.op("dve", lambda e: e.tensor_tensor(out=P[:], in0=P[:].bitcast(F32), in1=v4(bA), op=ALU.add), r=[pA, t_P], w=[t_P])
                        if not last:
                            s.op("dve", lambda e: e.tensor_tensor(out=Q[:], in0=Q[:].bitcast(F32), in1=v4(bB), op=ALU.add), r=[pB, t_Q], w=[t_Q])
                        yield
                    s.op("dve", lambda e: e.tensor_tensor(out=kg[:], in0=lkt[:], in1=sc[:, :, 1:2].to_broadcast([128, 4, 128]), op=ALU.mult),
                         r=[t_lkt, t_sc], w=[t_kg])
                    s.op("act", lambda e: e.activation(out=vr[:], in_=lvt[:], func=AF.Copy), r=[t_lvt], w=[t_vr])
                    yield
                    for hh in range(4):
                        s.op("pe", lambda e, hh=hh: e.matmul(self.bank(bA)[:, hh * 128:(hh + 1) * 128], P[:, hh, :], vr[:, hh, :], start=True, stop=True),
                             r=[t_P, t_vr], w=[pA])
                        s.op("pe", lambda e, hh=hh: e.matmul(self.bank(bB)[:, hh * 128:(hh + 1) * 128], kg[:, hh, :], P[:, hh, :], start=True, stop=True),
                             r=[t_P, t_kg], w=[pB])
                    yield
                    s.op("dve", lambda e: e.tensor_tensor(out=ub[:], in0=v4(bA), in1=beta.unsqueeze(2).to_broadcast([128, 4, 128]), op=ALU.mult),
                         r=[pA, t_bg], w=[t_ub])
                    s.op("act", lambda e: e.activation(out=wT[:], in_=v4(bB), func=AF.Copy), r=[pB], w=[t_wT])
                    s.dma("sp", self.G_UB[dirn, t, :, fs].rearrange("p (h d) -> p h d", h=4), ub[:], r=[t_ub], w=[self.t_gp])
                    s.dma("sp", self.G_WT[dirn, t, :, hs, :], wT[:], r=[t_wT], w=[self.t_gp])
                    yield

            self.lockstep([lane(i) for i in range(NL)])
            s.barrier()

    def phase_gdnR(self, L):
        nc, s = self.nc, self.s
        T = L // 128
        QT3 = self.QT.rearrange("(h d) n -> d h n", d=128)
        with ExitStack() as es:
            def lane(li):
                dirn, qd = li // 2, li % 2
                hs = slice(qd * 4, qd * 4 + 4)
                fs = slice(qd * 512, (qd + 1) * 512)

                def A(nm, dt=F32, shp=(128, 4, 128)):
                    return self.sb(es, "R%d%s" % (li, nm), list(shp), dt), Tok()
                ldb = []
                for i in range(2):
                    ldb.append(dict(wT=A("lwT%d" % i), qk=A("lqk%d" % i), kd=A("lkd%d" % i), ub=A("lub%d" % i), q=A("lq%d" % i),
                                    bg=A("lbg%d" % i, F32, (128, 32)), scc=A("lscc%d" % i, F32, (64, 2, 4)), glb=A("lglb%d" % i, F32, (128, 4, 2))))
                wTr, t_wTr = A("wTr", F32R)
                qkr, t_qkr = A("qkr", F32R)
                kdr, t_kdr = A("kdr", F32R)
                qr, t_qr = A("qr", F32R)
                nb, t_nb = A("nb", F32, (128, 4))
                S, t_S = A("S", F32R)
                vn, t_vn = A("vn", F32R)
                o1s, t_o1s = A("o1s", F32, (64, 4, 128))
                o2s, t_o2s = A("o2s", F32, (64, 4, 128))
                OTs = [A("OT%d" % i, F32, (64, 2, 512)) for i in range(2)]
                bA, bB = 2 * li, 2 * li + 1
                pA, pB = self.ps_tok[bA], self.ps_tok[bB]
                v4 = lambda bk: self.bank(bk).rearrange("p (h c) -> p h c", h=4)
                s.op("dve", lambda e: e.tensor_scalar(out=S[:], in0=self.ident32[:].unsqueeze(1).to_broadcast([128, 4, 128]), scalar1=0.0, scalar2=None,
                                                      op0=ALU.mult), r=[self.t_const], w=[t_S])
                tiles = list(range(T)) if dirn == 0 else list(range(T - 1, -1, -1))

                def load(it):
                    t = tiles[it]
                    Ld = ldb[it % 2]
                    cs_ = slice(t * 128, (t + 1) * 128)
                    s.dma("sp", Ld["wT"][0][:], self.G_WT[dirn, t, :, hs, :], r=[self.t_gp], w=[Ld["wT"][1]])
                    s.dma("sp", Ld["qk"][0][:], self.G_QKM[dirn, t, :, hs, :], r=[self.t_gp], w=[Ld["qk"][1]])
                    s.dma("sp", Ld["kd"][0][:], self.G_KD[dirn, t, :, fs].rearrange("p (h d) -> p h d", h=4), r=[self.t_gp], w=[Ld["kd"][1]])
                    s.dma("sp", Ld["ub"][0][:], self.G_UB[dirn, t, :, fs].rearrange("p (h d) -> p h d", h=4), r=[self.t_gp], w=[Ld["ub"][1]])
                    s.dma("sp", Ld["q"][0][:], QT3[:, hs, cs_], r=[self.t_gd], w=[Ld["q"][1]])
                    s.dma("sp", Ld["bg"][0][:], self.BG[cs_, :], r=[self.t_gd], w=[Ld["bg"][1]])
                    s.dma("sp", Ld["scc"][0][:], self.G_SCC[dirn, t, :, :, hs], r=[self.t_gp], w=[Ld["scc"][1]])
                    s.dma("sp", Ld["glb"][0][:], self.G_GLB[dirn, t, :, hs, :], r=[self.t_gp], w=[Ld["glb"][1]])

                load(0)
                for it, t in enumerate(tiles):
                    Ld = ldb[it % 2]
                    if it + 1 < T:
                        load(it + 1)
                    OT, t_OT = OTs[it % 2]
                    s.op("pool", lambda e: e.tensor_copy(out=wTr[:], in_=Ld["wT"][0][:]), r=[Ld["wT"][1]], w=[t_wTr])
                    s.op("act", lambda e: e.activation(out=qr[:], in_=Ld["q"][0][:], func=AF.Copy), r=[Ld["q"][1]], w=[t_qr])
                    s.op("pool", lambda e: e.tensor_copy(out=qkr[:], in_=Ld["qk"][0][:]), r=[Ld["qk"][1]], w=[t_qkr])
                    s.op("pool", lambda e: e.tensor_copy(out=kdr[:], in_=Ld["kd"][0][:]), r=[Ld["kd"][1]], w=[t_kdr])
                    s.op("dve", lambda e: e.tensor_scalar(out=nb[:], in0=Ld["bg"][0][:, dirn * 8 + qd * 4:dirn * 8 + qd * 4 + 4], scalar1=-1.0, scalar2=None,
                                                          op0=ALU.mult), r=[Ld["bg"][1]], w=[t_nb])
                    ubv, t_ubv = Ld["ub"]
                    sccv, t_sccv = Ld["scc"]
                    glbv, t_glbv = Ld["glb"]
                    yield
                    for cc in ((0, 1) if dirn == 0 else (1, 0)):
                        rr = slice(cc * 64, cc * 64 + 64)
                        M = (cc + 1) * 64
                        for hh in range(4):
                            s.op("pe", lambda e, hh=hh: e.matmul(self.bank(bA)[0:M, hh * 128:(hh + 1) * 128], wTr[:, hh, 0:M], S[:, hh, :], start=True, stop=True),
                                 r=[t_wTr, t_S], w=[pA])
                        for hh in range(4):
                            s.op("pe", lambda e, hh=hh: e.matmul(self.bank(bB)[0:64, hh * 128:(hh + 1) * 128], qr[:, hh, rr], S[:, hh, :], start=True, stop=True),
                                 r=[t_qr, t_S], w=[pB])
                        yield
                        for hh in range(4):
                            s.op("dve", lambda e, hh=hh: e.scalar_tensor_tensor(out=vn[rr, hh, :], in0=self.bank(bA)[rr, hh * 128:(hh + 1) * 128],
                                                                                scalar=nb[rr, hh:hh + 1], in1=ubv[rr, hh, :], op0=ALU.mult, op1=ALU.add),
                                 r=[pA, t_nb, t_ubv], w=[t_vn])
                        s.op("act", lambda e: e.activation(out=o1s[:], in_=self.bank(bB)[0:64, :].rearrange("p (h c) -> p h c", h=4), func=AF.Copy),
                             r=[pB], w=[t_o1s])
                        yield
                        for hh in range(4):
                            s.op("pe", lambda e, hh=hh: e.matmul(self.bank(bB)[:, hh * 128:(hh + 1) * 128], kdr[rr, hh, :], vn[rr, hh, :], start=True, stop=True),
                                 r=[t_kdr, t_vn], w=[pB])
                        for hh in range(4):
                            s.op("pe", lambda e, hh=hh: e.matmul(self.bank(bA)[0:64, hh * 128:(hh + 1) * 128], qkr[rr, hh, rr], vn[rr, hh, :], start=True, stop=True),
                                 r=[t_qkr, t_vn], w=[pA])
                        yield
                        for hh in range(4):
                            s.op("dve", lambda e, hh=hh: e.scalar_tensor_tensor(out=S[:, hh, :], in0=S[:, hh, :].bitcast(F32), scalar=glbv[:, hh, cc:cc + 1],
                                                                                in1=self.bank(bB)[:, hh * 128:(hh + 1) * 128], op0=ALU.mult, op1=ALU.add),
                                 r=[pB, t_glbv, t_S], w=[t_S])
                        s.op("act", lambda e: e.activation(out=o2s[:], in_=self.bank(bA)[0:64, :].rearrange("p (h c) -> p h c", h=4), func=AF.Copy),
                             r=[pA], w=[t_o2s])
                        yield
                        for hh in range(4):
                            s.op("dve", lambda e, hh=hh: e.scalar_tensor_tensor(out=OT[:, cc, hh * 128:(hh + 1) * 128], in0=o1s[:, hh, :],
                                                                                 scalar=sccv[:, cc, hh:hh + 1], in1=o2s[:, hh, :], op0=ALU.mult, op1=ALU.add),
                                 r=[t_o1s, t_sccv, t_o2s], w=[t_OT])
                    s.dma("sp", self.OFB[dirn, t * 128:(t + 1) * 128, fs].rearrange("(c p) f -> p c f", p=64), OT[:], r=[t_OT], w=[self.t_of])
                    yield

            self.lockstep([lane(i) for i in range(4)])
            s.barrier()

    def phase_gdnC(self, L):
        nc, s = self.nc, self.s
        T = L // 128
        with ExitStack() as es:
            gnorm = self.sb(es, "gnormg", [128, 128], F32)
            t_c = Tok()
            s.dma("sp", gnorm[:], self.w["gdn_norm_g"].partition_broadcast(128), w=[t_c])
            of = [self.sb(es, "cof%d" % i, [128, 1024], F32) for i in range(2)]
            ob = [self.sb(es, "cob%d" % i, [128, 1024], F32) for i in range(2)]
            sg = [self.sb(es, "csg%d" % i, [128, 1024], BF16) for i in range(2)]
            t_in = [Tok(), Tok()]
            rst = [self.sb(es, "crst%d" % i, [128, 24], F32) for i in range(2)]
            t_rst = [Tok(), Tok()]
            junk = self.sb(es, "cjunk", [128, 128], F32)
            t_junk = Tok()
            yb16 = [self.sb(es, "cyb%d" % i, [128, 1024], BF16) for i in range(2)]
            t_yb = [Tok(), Tok()]
            ystage = [self.sb(es, "cystage%d" % i, [128, 8, 512], BF16) for i in range(2)]
            t_ys = [Tok(), Tok()]
            for t in range(T):
                b = t % 2
                blk, tl = t // 4, t % 4
                yb = blk % 2
                cs_ = slice(t * 128, (t + 1) * 128)
                s.dma("sp", of[b][:], self.OFB[0, cs_, :], r=[self.t_of], w=[t_in[b]])
                s.dma("sp", ob[b][:], self.OFB[1, cs_, :], r=[self.t_of], w=[t_in[b]])
                s.dma("sp", sg[b][:], self.SGG[cs_, :], r=[self.t_gd], w=[t_in[b]])
                O = of[b]
                s.op("pool", lambda e: e.tensor_tensor(out=O[:], in0=O[:], in1=ob[b][:], op=ALU.add), r=[t_in[b]], w=[t_in[b]])
                for h in range(8):
                    s.op("act", lambda e, h=h: e.activation(out=junk[:], in_=O[:, h * 128:(h + 1) * 128], func=AF.Square, accum_out=rst[b][:, h:h + 1]),
                         r=[t_in[b]], w=[t_junk, t_rst[b]])
                s.op("dve", lambda e: e.tensor_scalar(out=rst[b][:, 8:16], in0=rst[b][:, 0:8], scalar1=1.0 / 128.0, scalar2=EPS, op0=ALU.mult, op1=ALU.add),
                     r=[t_rst[b]], w=[t_rst[b]])
                s.op("act", lambda e: e.activation(out=rst[b][:, 8:16], in_=rst[b][:, 8:16], func=AF.Ln), r=[t_rst[b]], w=[t_rst[b]])
                s.op("act", lambda e: e.activation(out=rst[b][:, 16:24], in_=rst[b][:, 8:16], func=AF.Exp, scale=-0.5), r=[t_rst[b]], w=[t_rst[b]])
                O3 = O[:].rearrange("p (h d) -> p h d", h=8)
                s.op("dve", lambda e: e.tensor_tensor(out=O3, in0=O3, in1=rst[b][:, 16:24].unsqueeze(2).to_broadcast([128, 8, 128]), op=ALU.mult),
                     r=[t_in[b], t_rst[b]], w=[t_in[b]])
                s.op("pool", lambda e: e.tensor_tensor(out=O3, in0=O3, in1=gnorm[:].unsqueeze(1).to_broadcast([128, 8, 128]), op=ALU.mult),
                     r=[t_in[b], t_c], w=[t_in[b]])
                s.op("dve", lambda e: e.tensor_tensor(out=yb16[b][:], in0=O[:], in1=sg[b][:], op=ALU.mult), r=[t_in[b]], w=[t_yb[b]])
                self.to_ystage(yb16[b], t_yb[b], 8, ystage[yb], t_ys[yb], tl, banks=[6, 7])
                if tl == 3 or t == T - 1:
                    self.store_ystage(ystage[yb], t_ys[yb], 8, 8, blk * 512, (tl + 1) * 128)
            s.barrier()

    def phase_dil(self, si, L, hT, hT_tok):
        nc, s = self.nc, self.s
        T = L // 128
        w_in = self.w["o_w_in"]
        OG, LSE = self.OG, self.LSE
        with ExitStack() as es:
            wts = [self.sb(es, "dw%d" % i, [128, 8, 768], BF16) for i in range(2)]
            t_wts = [Tok(), Tok()]
            cos16 = self.sb(es, "cos16", [128, T, 16], F32)
            sin16 = self.sb(es, "sin16", [128, T, 16], F32)
            t_tab = Tok()
            s.dma("sp", cos16[:], self.c_rope[L][0], w=[t_tab])
            s.dma("sp", sin16[:], self.c_rope[L][1], w=[t_tab])
            qT = self.sb(es, "dqT", [128, 2, L], BF16)
            kT = self.sb(es, "dkT", [128, 2, L], BF16)
            t_qk = [Tok() for _ in range(T)]
            vP = self.sb(es, "dvP", [128, T, 256], BF16)
            t_vP = [Tok() for _ in range(T)]
            raw = [self.sb(es, "draw%d" % i, [128, 4, 128], F32) for i in range(2)]
            t_raw = [Tok(), Tok()]
            rtmps = [self.sb(es, "drtmp%d" % i, [128, 4, 4, 16], F32) for i in range(2)]
            t_rtmps = [Tok(), Tok()]
            qk16 = [self.sb(es, "dqk16%d" % i, [128, 4, 128], BF16) for i in range(2)]
            t_qk16 = [Tok(), Tok()]
            Sm = [self.sb(es, "dSm%d" % i, [128, 2, 384], F32) for i in range(2)]
            t_Sm = [Tok(), Tok()]
            Pe = [self.sb(es, "dPe%d" % i, [128, 2, 384], BF16) for i in range(2)]
            t_Pe = [Tok(), Tok()]
            stt = [self.sb(es, "dst%d" % i, [128, 8], F32) for i in range(2)]
            t_stt = [Tok(), Tok()]
            PT = [self.sb(es, "dPT%d" % i, [128, 8, 128], BF16) for i in range(2)]
            t_PT = [Tok(), Tok()]
            og = [self.sb(es, "dog%d" % i, [128, 256], F32) for i in range(2)]
            t_og = [Tok(), Tok()]
            lse = [self.sb(es, "dlse%d" % i, [128, 4], F32) for i in range(2)]
            t_lse = [Tok(), Tok()]
            scale = 128.0 ** -0.5
            it = 0
            for g, d in enumerate(DIL):
                ls = L // d
                tps = ls // 128
                for hp in range(2):
                    wt, t_w = wts[it % 2], t_wts[it % 2]
                    it += 1
                    for qkv in range(3):
                        c0 = O_CQKV + ((qkv * 3 + g) * 4 + hp * 2) * 128
                        s.dma("pool", wt[:, :, qkv * 256:(qkv + 1) * 256],
                              w_in[:, c0:c0 + 256].rearrange("(k p) c -> p k c", p=128), w=[t_w])
                    def qk_lane(ln):
                        b = ln
                        pb = 2 + ln
                        pt = 6 + ln
                        for t in range(ln, T, 2):
                            for qk in range(2):
                                self.proj_tok(self.bank(pb)[:, qk * 256:(qk + 1) * 256], pb, wt, t_w, qk * 256, 256, hT, [hT_tok[t]],
                                              lambda k: hT[:, k, t * 128:(t + 1) * 128])
                            yield
                            s.op("act", lambda e: e.activation(out=raw[b][:], in_=self.bank(pb).rearrange("p (h d) -> p h d", h=4),
                                                               func=AF.Copy), r=[self.ps_tok[pb]], w=[t_raw[b]])
                            yield
                            self.rope(raw[b][:], t_raw[b], qk16[b][:], t_qk16[b], rtmps[ln], t_rtmps[ln], cos16[:, t, :], sin16[:, t, :], t_tab,
                                      4, 16, 128)
                            yield
                            ptv = self.bank16(pt).rearrange("p (j c) -> p j c", j=8)
                            for c in range(4):
                                s.op("pe", lambda e, c=c: e.transpose(out=ptv[:, c, :], in_=qk16[b][:, c, :], identity=self.ident16[:]),
                                     r=[t_qk16[b], self.t_const], w=[self.ps_tok[pt]])
                            yield
                            s.op("act", lambda e: e.activation(out=qT[:, :, t * 128:(t + 1) * 128], in_=ptv[:, 0:2, :], func=AF.Copy,
                                                               scale=scale), r=[self.ps_tok[pt]], w=[t_qk[t]])
                            s.op("dve", lambda e: e.tensor_copy(out=kT[:, :, t * 128:(t + 1) * 128], in_=ptv[:, 2:4, :]),
                                 r=[self.ps_tok[pt]], w=[t_qk[t]])
                            yield

                    self.lockstep([qk_lane(0), qk_lane(1)])

                    def pslice(j):
                        seg, jj = j // tps, j % tps
                        start = seg + d * jj * 128
                        return start, start + d * 127 + 1

                    def ptoks(j):
                        a, bnd = pslice(j)
                        return [t_qk[tt] for tt in range(a // 128, (bnd - 1) // 128 + 1)], \
                               [hT_tok[tt] for tt in range(a // 128, (bnd - 1) // 128 + 1)]

                    def v_lane(ln):
                        pb = 2 + ln
                        for j in range(ln, T, 2):
                            a, bnd = pslice(j)
                            self.proj_tok(self.bank(pb)[:, 0:256], pb, wt, t_w, 512, 256, hT, ptoks(j)[1],
                                          lambda k: hT[:, k, a:bnd:d])
                            yield
                            s.op("act" if ln == 0 else "dve", (lambda e: e.activation(out=vP[:, j, :], in_=self.bank(pb)[:, 0:256], func=AF.Copy)) if ln == 0
                                 else (lambda e: e.tensor_copy(out=vP[:, j, :], in_=self.bank(pb)[:, 0:256])),
                                 r=[self.ps_tok[pb]], w=[t_vP[j]])
                            yield

                    self.lockstep([v_lane(0), v_lane(1)])
                    def dil_lane(ln):
                        pp = ln
                        sb0 = 4 if ln == 0 else 2
                        ptb = 6 + ln
                        ob = ln
                        for j in range(ln, T, 2):
                            a, bnd = pslice(j)
                            kts = [kt for kt in (j - 1, j, j + 1) if 0 <= kt < T and kt // tps == j // tps]
                            nkt = len(kts)
                            nk = 128 * nkt
                            m0 = (kts[0] - (j - 1)) * 128
                            ka = pslice(kts[0])[0]
                            kb_ = pslice(kts[-1])[1]
                            kdeps = []
                            for kt in kts:
                                kdeps += ptoks(kt)[0]
                            for jh in range(2):
                                s.op("pe", lambda e, jh=jh: e.matmul(self.bank(sb0 + jh)[:, 0:nk], qT[:, jh, a:bnd:d], kT[:, jh, ka:kb_:d],
                                                                     start=True, stop=True),
                                     r=ptoks(j)[0] + kdeps, w=[self.ps_tok[sb0 + jh]])
                            yield
                            yield from self.softmax_gen([sb0, sb0 + 1], 2, nk, self.mask_dil[:, m0:m0 + nk], self.t_const, Sm[pp], t_Sm[pp], Pe[pp], t_Pe[pp],
                                                        stt[pp], t_stt[pp])
                            yield
                            self.transpose_P(Pe[pp], t_Pe[pp], 2, nkt, PT[pp], t_PT[pp], ptb)
                            s.op("dve", lambda e: e.reciprocal(out=stt[pp][:, 4:6], in_=stt[pp][:, 2:4]), r=[t_stt[pp]], w=[t_stt[pp]])
                            s.op("act", lambda e: e.activation(out=lse[pp][:, 0:2], in_=stt[pp][:, 2:4], func=AF.Ln), r=[t_stt[pp]], w=[t_lse[pp]])
                            yield
                            for jh in range(2):
                                for kc in range(nkt):
                                    s.op("pe", lambda e, jh=jh, kc=kc: e.matmul(self.bank(ob)[:, jh * 128:(jh + 1) * 128],
                                                                                PT[pp][:, jh * nkt + kc, :],
                                                                                vP[:, kts[kc], jh * 128:(jh + 1) * 128],
                                                                                start=(kc == 0), stop=(kc == nkt - 1)),
                                         r=[t_PT[pp]] + [t_vP[kt] for kt in kts], w=[self.ps_tok[ob]])
                            s.op("dve", lambda e: e.tensor_tensor(out=lse[pp][:, 2:4], in0=lse[pp][:, 0:2], in1=stt[pp][:, 0:2], op=ALU.subtract),
                                 r=[t_stt[pp], t_lse[pp]], w=[t_lse[pp]])
                            yield
                            for jh in range(2):
                                s.op("dve", lambda e, jh=jh: e.tensor_scalar(out=og[pp][:, jh * 128:(jh + 1) * 128],
                                                                             in0=self.bank(ob)[:, jh * 128:(jh + 1) * 128],
                                                                             scalar1=stt[pp][:, 4 + jh:5 + jh], scalar2=None, op0=ALU.mult),
                                     r=[self.ps_tok[ob], t_stt[pp]], w=[t_og[pp]])
                            s.dma("sp", OG[g, a:bnd:d, hp * 256:(hp + 1) * 256], og[pp][:], r=[t_og[pp]], w=[self.t_og_dram])
                            s.dma("sp", LSE[g, a:bnd:d, hp * 2:(hp + 1) * 2], lse[pp][:, 2:4], r=[t_lse[pp]], w=[self.t_og_dram])
                            yield

                    self.lockstep([dil_lane(0), dil_lane(1)])
            s.barrier()
        with ExitStack() as es:
            wg = self.sb(es, "dwg", [128, 8, 512], BF16)
            t_wg = Tok()
            self.load_w(wg, t_wg, w_in, O_GC, 512)
            og3 = [self.sb(es, "og3%d" % i, [128, 3, 512], F32) for i in range(2)]
            l3 = [self.sb(es, "l3%d" % i, [128, 3, 4], F32) for i in range(2)]
            t_in = [Tok(), Tok()]
            wk = [self.sb(es, "mwk%d" % i, [128, 8, 4], F32) for i in range(2)]
            t_wk = [Tok(), Tok()]
            yacc = [self.sb(es, "yacc%d" % i, [128, 512], F32) for i in range(2)]
            t_ya = [Tok(), Tok()]
            sg = [self.sb(es, "dsg%d" % i, [128, 512], F32) for i in range(2)]
            t_sg = [Tok(), Tok()]
            yc = [self.sb(es, "dyc%d" % i, [128, 512], BF16) for i in range(2)]
            t_yc = [Tok(), Tok()]
            ystage = [self.sb(es, "dystage%d" % i, [128, 4, 512], BF16) for i in range(2)]
            t_ys = [Tok(), Tok()]
            for t in range(T):
                b = t % 2
                blk, tl = t // 4, t % 4
                yb = blk % 2
                s.dma("sp", og3[b][:], OG[:, t * 128:(t + 1) * 128, :].rearrange("g p c -> p g c"), r=[self.t_og_dram], w=[t_in[b]])
                s.dma("sp", l3[b][:], LSE[:, t * 128:(t + 1) * 128, :].rearrange("g p c -> p g c"), r=[self.t_og_dram], w=[t_in[b]])
                pb = self.pick("proj", [2, 3])
                self.proj_tok(self.bank(pb), pb, wg, t_wg, 0, 512, hT, [hT_tok[t]], lambda k: hT[:, k, t * 128:(t + 1) * 128])
                s.op("act", lambda e: e.activation(out=sg[b][:], in_=self.bank(pb), func=AF.Silu), r=[self.ps_tok[pb]], w=[t_sg[b]])
                W = wk[b]
                s.op("dve", lambda e: e.tensor_tensor(out=W[:, 3, :], in0=l3[b][:, 0, :], in1=l3[b][:, 1, :], op=ALU.max), r=[t_in[b]], w=[t_wk[b]])
                s.op("dve", lambda e: e.tensor_tensor(out=W[:, 3, :], in0=W[:, 3, :], in1=l3[b][:, 2, :], op=ALU.max), r=[t_in[b], t_wk[b]], w=[t_wk[b]])
                s.op("dve", lambda e: e.tensor_tensor(out=W[:, 0:3, :], in0=l3[b][:], in1=W[:, 3, :].unsqueeze(1).to_broadcast([128, 3, 4]),
                                                      op=ALU.subtract), r=[t_in[b], t_wk[b]], w=[t_wk[b]])
                s.op("act", lambda e: e.activation(out=W[:, 0:3, :], in_=W[:, 0:3, :], func=AF.Exp), r=[t_wk[b]], w=[t_wk[b]])
                s.op("dve", lambda e: e.tensor_tensor(out=W[:, 4, :], in0=W[:, 0, :], in1=W[:, 1, :], op=ALU.add), r=[t_wk[b]], w=[t_wk[b]])
                s.op("dve", lambda e: e.tensor_tensor(out=W[:, 4, :], in0=W[:, 4, :], in1=W[:, 2, :], op=ALU.add), r=[t_wk[b]], w=[t_wk[b]])
                s.op("dve", lambda e: e.reciprocal(out=W[:, 5, :], in_=W[:, 4, :]), r=[t_wk[b]], w=[t_wk[b]])
                s.op("dve", lambda e: e.tensor_tensor(out=W[:, 0:3, :], in0=W[:, 0:3, :], in1=W[:, 5, :].unsqueeze(1).to_broadcast([128, 3, 4]),
                                                      op=ALU.mult), r=[t_wk[b]], w=[t_wk[b]])
                for h in range(4):
                    hs = slice(h * 128, (h + 1) * 128)
                    s.op("dve", lambda e, h=h, hs=hs: e.tensor_scalar(out=yacc[b][:, hs], in0=og3[b][:, 0, hs], scalar1=W[:, 0, h:h + 1],
                                                                      scalar2=None, op0=ALU.mult), r=[t_in[b], t_wk[b]], w=[t_ya[b]])
                    for g in (1, 2):
                        s.op("dve", lambda e, h=h, hs=hs, g=g: e.scalar_tensor_tensor(out=yacc[b][:, hs], in0=og3[b][:, g, hs],
                                                                                      scalar=W[:, g, h:h + 1], in1=yacc[b][:, hs],
                                                                                      op0=ALU.mult, op1=ALU.add),
                             r=[t_in[b], t_wk[b], t_ya[b]], w=[t_ya[b]])
                s.op("pool", lambda e: e.tensor_tensor(out=yc[b][:], in0=yacc[b][:], in1=sg[b][:], op=ALU.mult),
                     r=[t_ya[b], t_sg[b]], w=[t_yc[b]])
                self.to_ystage(yc[b], t_yc[b], 4, ystage[yb], t_ys[yb], tl, banks=[6, 7])
                if tl == 3 or t == T - 1:
                    self.store_ystage(ystage[yb], t_ys[yb], 4, 0, blk * 512, (tl + 1) * 128)
            s.barrier()

    def zero_YT(self, L, chunks):
        s = self.s
        with ExitStack() as es:
            z = self.sb(es, "zeros", [128, L], BF16)
            t_z = Tok()
            s.op("dve", lambda e: e.memset(z[:], 0.0), w=[t_z])
            for c in chunks:
                s.dma("sp", self.YT[c * 128:(c + 1) * 128, 0:L], z[:], r=[t_z])
            s.barrier()

    def phase_out(self, r0, L, src, dst, pre, nch):
        nc, s = self.nc, self.s
        T = L // 128
        with ExitStack() as es:
            wo = self.sb(es, "wo", [128, nch, D], BF16)
            t_wo = Tok()
            w_out = self.w[pre + "w_out"]
            for c0 in range(0, nch, 4):
                s.dma("pool", wo[:, c0:c0 + 4, :], w_out[c0 * 128:(c0 + 4) * 128, :].rearrange("(c p) n -> p c n", p=128),
                      w=[t_wo])
            gpost = self.sb(es, "gpost", [128, D], F32)
            t_gp = Tok()
            s.dma("sp", gpost[:], self.w[pre + "post_g"].partition_broadcast(128), w=[t_gp])
            yb = [self.sb(es, "yb%d" % i, [128, nch, 512], BF16) for i in range(2)]
            t_yb = [Tok(), Tok()]
            xr = [self.sb(es, "xr%d" % i, [128, D], F32) for i in range(2)]
            t_xr = [Tok(), Tok()]
            st = [self.sb(es, "ost%d" % i, [128, 4], F32) for i in range(2)]
            t_st = [Tok(), Tok()]
            junk = [self.sb(es, "ojunk%d" % i, [128, D], BF16) for i in range(2)]
            t_junk = [Tok(), Tok()]
            tmp = [self.sb(es, "otmp%d" % i, [128, D], F32) for i in range(2)]
            t_tmp = [Tok(), Tok()]
            nblk = (L + 511) // 512
            for blk in range(nblk):
                tok0 = blk * 512
                ntok = min(512, L - tok0)
                bb = blk % 2
                s.dma("sp", yb[bb][:, :, 0:ntok],
                      self.YT[0:nch * 128, tok0:tok0 + ntok].rearrange("(c p) n -> p c n", p=128), w=[t_yb[bb]])
                def o_lane(ln):
                    b = ln
                    p2 = 2 * ln
                    for tl in range(ln, ntok // 128, 2):
                        t = blk * 4 + tl
                        s.dma("sp", xr[b][:], src[r0 + t * 128:r0 + (t + 1) * 128, :], w=[t_xr[b]])
                        for half in range(2):
                            for c in range(nch):
                                s.op("pe", lambda e, c=c, half=half: e.matmul(self.bank(p2 + half), yb[bb][:, c, tl * 128:(tl + 1) * 128],
                                                                              wo[:, c, half * 512:(half + 1) * 512],
                                                                              start=(c == 0), stop=(c == nch - 1)),
                                     r=[t_yb[bb], t_wo], w=[self.ps_tok[p2 + half]])
                        yield
                        pv = self.ps[:, p2 * 512:(p2 + 2) * 512]
                        pr = [self.ps_tok[p2], self.ps_tok[p2 + 1]]
                        s.op("act", lambda e: e.activation(out=junk[b][:], in_=pv, func=AF.Square, accum_out=st[b][:, 0:1]),
                             r=pr, w=[t_junk[b], t_st[b]])
                        yield
                        s.op("dve", lambda e: e.tensor_scalar(out=st[b][:, 1:2], in0=st[b][:, 0:1], scalar1=1.0 / D, scalar2=EPS,
                                                              op0=ALU.mult, op1=ALU.add), r=[t_st[b]], w=[t_st[b]])
                        yield
                        s.op("act", lambda e: e.activation(out=st[b][:, 2:3], in_=st[b][:, 1:2], func=AF.Sqrt), r=[t_st[b]], w=[t_st[b]])
                        yield
                        s.op("dve", lambda e: e.reciprocal(out=st[b][:, 3:4], in_=st[b][:, 2:3]), r=[t_st[b]], w=[t_st[b]])
                        yield
                        s.op("dve", lambda e: e.scalar_tensor_tensor(out=tmp[b][:], in0=pv, scalar=st[b][:, 3:4], in1=gpost[:],
                                                                     op0=ALU.mult, op1=ALU.mult),
                             r=pr + [t_st[b], t_gp], w=[t_tmp[b]])
                        yield
                        s.op("pool", lambda e: e.tensor_tensor(out=tmp[b][:], in0=tmp[b][:], in1=xr[b][:], op=ALU.add),
                             r=[t_xr[b], t_tmp[b]], w=[t_tmp[b]])
                        yield
                        s.dma("sp", dst[r0 + t * 128:r0 + (t + 1) * 128, :], tmp[b][:], r=[t_tmp[b]])
                        yield

                self.lockstep([o_lane(0), o_lane(1)])
            s.barrier()


def host_consts(seq_lens):
    c = {}
    c["c_ident"] = np.eye(128, dtype=np.float32)
    c["c_mask_dil"] = band_mask(64, 192)
    c["c_mask_swa"] = band_mask(0, 256)
    for L in sorted(set(seq_lens)):
        c16, s16 = rope_tables(L, 16)
        c8, s8 = rope_tables(L, 8)
        c["c_cos16_%d" % L] = tok_layout(c16)
        c["c_sin16_%d" % L] = tok_layout(s16)
        c["c_cos8_%d" % L] = tok_layout(c8)
        c["c_sin8_%d" % L] = tok_layout(s8)
    j = np.arange(128)[:, None]
    i = np.arange(128)[None, :]
    same = (j // 64) == (i // 64)
    fw = []
    fw.append(np.where(same & (i >= j), 0.0, NEG))
    fw.append(np.where(same & (i > j), 1.0, 0.0))
    for lv in range(6):
        bsz = 1 << lv
        fw.append(np.where(((j // (2 * bsz)) == (i // (2 * bsz))) & ((j % (2 * bsz)) < bsz) & ((i % (2 * bsz)) >= bsz), 1.0, 0.0))
    bw = [m.T for m in fw]
    tri_f = np.where(same & (j <= i), 1.0, 0.0)
    gm = np.stack(fw + bw + [tri_f, tri_f.T], 0).astype(np.float32)
    c["c_gmask"] = np.ascontiguousarray(gm.transpose(1, 0, 2))
    c["c_delta"] = np.abs(np.linspace(math.log(1e-2) / 1.5, math.log(1e-2) / 0.3, 1024, dtype=np.float32)).reshape(1, 1024)
    for L in sorted(set(seq_lens)):
        T = L // 128
        N = 2 * L
        i = np.arange(L, dtype=np.int64)
        prod = ((2 * i[:, None] + 1) * (2 * i[None, :] + 1)) % (4 * N)
        ang = prod.astype(np.float64) * (2.0 * np.pi / (4 * N))
        for nm, fn in (("c_gc_%d" % L, np.cos), ("c_gs_%d" % L, np.sin)):
            tab = fn(ang).astype(np.float32).astype(ml_dtypes.bfloat16)
            c[nm] = np.ascontiguousarray(tab.reshape(T, 128, T, 128).transpose(2, 1, 0, 3))
        t = np.linspace(0.0, 1.0, L, dtype=np.float32)[:, None]
        wv = (2.0 * math.pi / L) * np.arange(L, dtype=np.float32)[:, None]
        f = np.linspace(1e-4, 15.0, 16, dtype=np.float32)[None, :]
        emb = np.concatenate([t, np.cos(f * wv), -np.sin(f * wv)], axis=-1).astype(np.float32)
        c["c_embT_%d" % L] = np.ascontiguousarray(emb.T)
        c["c_tcol_%d" % L] = np.ascontiguousarray((-t[:, 0]).reshape(T, 128).T)
        k = np.arange(L, dtype=np.float64)
        psi = np.pi * (k + 0.5) / N
        ps = np.stack([(2.0 / N) * np.cos(psi), (2.0 / N) * np.sin(psi)], 0).astype(np.float32)
        c["c_psi_%d" % L] = np.ascontiguousarray(ps.reshape(2, T, 128).transpose(2, 0, 1))
    return c


def col_layout(v):
    return np.ascontiguousarray(np.asarray(v, np.float32).reshape(8, 128).T)


def shared_inputs(inp, seq_lens):
    m = {}
    for nm in ("e_w_in", "e_w_out", "e_w_mem_kv", "o_w_in", "o_w_out", "o_w_mem_kv"):
        m[nm] = np.ascontiguousarray(np.asarray(inp[nm], np.float32)[0])
    for nm in ("e_pre_g", "e_mem_g", "o_pre_g", "o_mem_g"):
        m[nm] = col_layout(np.asarray(inp[nm])[0])
    for nm in ("e_post_g", "o_post_g"):
        m[nm] = np.ascontiguousarray(np.asarray(inp[nm], np.float32).reshape(1, D))
    m["swa_sink"] = np.ascontiguousarray(np.asarray(inp["swa_sink"], np.float32).reshape(1, 16))
    f32 = lambda a: np.asarray(a, np.float32)
    m["hy_filt_w1"] = np.ascontiguousarray(f32(inp["hy_filt_w1"])[0])
    m["hy_filt_w2"] = np.ascontiguousarray(f32(inp["hy_filt_w2"])[0])
    m["hy_filt_w3"] = np.ascontiguousarray(f32(inp["hy_filt_w3"])[0])
    m["hy_fvec"] = np.ascontiguousarray(np.stack([f32(inp["hy_filt_b1"])[0], f32(inp["hy_filt_b2"])[0], f32(inp["hy_freq"])[0]], 1))
    m["hy_conv_w"] = np.ascontiguousarray(f32(inp["hy_conv_w"])[0].reshape(3, 24, 128).transpose(2, 1, 0))
    m["hy_conv_b"] = np.ascontiguousarray(f32(inp["hy_conv_b"])[0].reshape(24, 128).T)
    m["gdn_conv_w"] = np.ascontiguousarray(f32(inp["gdn_conv_w"])[0].reshape(5, 24, 128).transpose(2, 1, 0))
    m["gdn_A_log"] = np.ascontiguousarray(f32(inp["gdn_A_log"]).reshape(1, 16))
    m["gdn_dt_bias"] = np.ascontiguousarray(f32(inp["gdn_dt_bias"]).reshape(1, 16))
    m["gdn_norm_g"] = np.ascontiguousarray(f32(inp["gdn_norm_g"]).reshape(1, 128))
    m["hy_skip"] = np.ascontiguousarray(f32(inp["hy_skip"])[0].reshape(8, 128).T)
    m.update(host_consts(seq_lens))
    return m


_CACHE = {}


def kernel(**inp):
    xp = np.asarray(inp["x_prompt"], np.float32)
    xs = np.asarray(inp["x_sample"], np.float32)
    mp = np.asarray(inp["mem_prompt"], np.float32)
    ms = np.asarray(inp["mem_sample"], np.float32)
    n = 8
    seq_lens = [xs.shape[1], xs.shape[1], xp.shape[1]]
    shared = shared_inputs(inp, seq_lens)
    in_maps = []
    for c in range(n):
        m = dict(shared)
        m["x"] = np.ascontiguousarray(np.concatenate([xs[2 * c], xs[2 * c + 1], xp[c]], axis=0))
        m["mem"] = np.ascontiguousarray(np.concatenate([ms[2 * c], ms[2 * c + 1], mp[c]], axis=0))
        in_maps.append(m)
    nc = KB(seq_lens).build()
    res = run_bass_kernel_spmd(nc, in_maps, core_ids=list(range(n)))
    Ls = xs.shape[1]
    y_p = np.stack([res.results[c]["y"][2 * Ls:] for c in range(n)], axis=0)
    y_s = np.stack([res.results[c]["y"][j * Ls:(j + 1) * Ls] for c in range(n) for j in range(2)], axis=0)
    return (y_p.astype(np.float32), y_s.astype(np.float32))
```

```python
import math
from contextlib import ExitStack

import numpy as np
import ml_dtypes

import concourse.bass as bass
import concourse.mybir as mybir
from concourse.bass_utils import run_bass_kernel_spmd

F32 = mybir.dt.float32
BF16 = mybir.dt.bfloat16
F32R = mybir.dt.float32r
AF = mybir.ActivationFunctionType
ALU = mybir.AluOpType
AX = mybir.AxisListType

D = 1024
EPS = 1e-6
NEG = -30000.0
ROPE_THETA = 500000.0
MEM_TOKENS = 256
EVEN_IN = 9248
ODD_IN = 8448
E_HY, E_GHY, E_QKV, E_GG, E_BETA, E_A, E_XQ, E_GX = 0, 3072, 4096, 7168, 8192, 8208, 8224, 8736
O_CQKV, O_GC, O_DQ, O_DKV, O_GD, O_XQ, O_GX = 0, 4608, 5120, 6144, 6400, 7424, 7936
DIL = (1, 4, 16)


class Tok:
    __slots__ = ("w", "r")

    def __init__(self):
        self.w = None
        self.r = {}


class Sch:
    RING = 8

    def __init__(self, nc, es):
        self.nc = nc
        self.eng = {"pe": nc.tensor, "act": nc.scalar, "dve": nc.vector, "pool": nc.gpsimd, "sp": nc.sync}
        self.sem = {}
        self.cnt = {}
        self.known = {}
        for e in self.eng:
            self.sem[e] = es.enter_context(nc.semaphore("s_" + e))
            self.cnt[e] = 0
            self.known[e] = {}
        self.dcnt = {}
        for q in ("sp", "act", "pool"):
            self.dcnt[q] = 0
            for i in range(self.RING):
                key = ("d", q, i)
                self.sem[key] = es.enter_context(nc.semaphore("d_%s_%d" % (q, i)))
                self.cnt[key] = 0
        self.ninst = 0

    def _wait(self, e, deps):
        need = {}
        for key, val in deps:
            if key == e and e == "pe":
                continue
            if self.known[e].get(key, 0) >= val:
                continue
            if need.get(key, 0) < val:
                need[key] = val
        for key, val in need.items():
            self.eng[e].wait_ge(self.sem[key], val)
            self.known[e][key] = val

    @staticmethod
    def _deps(r, w):
        deps = []
        for t in r:
            if t.w is not None:
                deps.append(t.w)
        for t in w:
            if t.w is not None:
                deps.append(t.w)
            deps.extend(t.r.items())
        return deps

    @staticmethod
    def _mark(me, r, w):
        key, val = me
        for t in r:
            if t.r.get(key, 0) < val:
                t.r[key] = val
        for t in w:
            t.w = me
            t.r = {}

    def op(self, e, fn, r=(), w=()):
        self._wait(e, self._deps(r, w))
        inst = fn(self.eng[e])
        self.cnt[e] += 1
        inst.then_inc(self.sem[e], 1)
        self._mark((e, self.cnt[e]), r, w)
        self.ninst += 1

    def dma(self, q, out, in_, r=(), w=()):
        i = self.dcnt[q]
        self.dcnt[q] += 1
        slot = i % self.RING
        key = ("d", q, slot)
        val = 16 * (i // self.RING + 1)
        deps = self._deps(r, w)
        if val > 16:
            deps.append((key, val - 16))
        self._wait(q, deps)
        self.eng[q].dma_start(out=out, in_=in_).then_inc(self.sem[key], 16)
        self.cnt[key] = val
        self._mark((key, val), r, w)
        self.ninst += 1

    def barrier(self):
        allv = [(k, v) for k, v in self.cnt.items() if v > 0]
        for e in self.eng:
            self._wait(e, allv)

    def finish(self):
        allv = [(k, v) for k, v in self.cnt.items() if v > 0]
        self._wait("sp", allv)


def rope_tables(L, half):
    inv = ROPE_THETA ** (-np.arange(half, dtype=np.float32) / half)
    ang = np.arange(L, dtype=np.float32)[:, None] * inv[None, :]
    return np.cos(ang).astype(np.float32), np.sin(ang).astype(np.float32)


def tok_layout(a):
    L, Fd = a.shape
    return np.ascontiguousarray(a.reshape(L // 128, 128, Fd).transpose(1, 0, 2))


def band_mask(lo_off, hi_off):
    i = np.arange(128)[:, None]
    c = np.arange(384)[None, :]
    ok = (c >= i + lo_off) & (c <= i + hi_off)
    return np.where(ok, 0.0, NEG).astype(np.float32)


class KB:
    def __init__(self, seq_lens, dbg=False):
        self.seq_lens = list(seq_lens)
        self.NS = len(seq_lens)
        self.NTOK = sum(seq_lens)
        self.LMAX = max(seq_lens)
        self.dbg = dbg
        self.nc = bass.Bass("TRN2", target_bir_lowering=False)
        self.consts = {}

    def din(self, name, shape, dt=F32):
        return self.nc.dram_tensor(name, list(shape), dt, kind="ExternalInput").ap()

    def dscr(self, name, shape, dt=F32, out=False):
        kind = "ExternalOutput" if (out and self.dbg) else "Internal"
        return self.nc.dram_tensor(name, list(shape), dt, kind=kind).ap()

    def sb(self, es, name, shape, dt):
        self.uid = getattr(self, "uid", 0) + 1
        return es.enter_context(self.nc.sbuf_tensor("%s_%d" % (name, self.uid), list(shape), dt))

    def build(self):
        nc = self.nc
        NTOK, NS = self.NTOK, self.NS
        self.x = self.din("x", [NTOK, D])
        self.mem = self.din("mem", [NS * MEM_TOKENS, D])
        self.y = nc.dram_tensor("y", [NTOK, D], F32, kind="ExternalOutput").ap()
        self.x1 = self.dscr("x1", [NTOK, D])
        self.w = {}
        for nm, shp in (("e_w_in", [D, EVEN_IN]), ("e_w_out", [2560, D]), ("e_w_mem_kv", [D, 1024]),
                        ("o_w_in", [D, ODD_IN]), ("o_w_out", [2048, D]), ("o_w_mem_kv", [D, 1024])):
            self.w[nm] = self.din(nm, shp)
        for nm in ("e_pre_g", "e_mem_g", "o_pre_g", "o_mem_g"):
            self.w[nm] = self.din(nm, [128, 8])
        for nm in ("e_post_g", "o_post_g"):
            self.w[nm] = self.din(nm, [1, D])
        self.w["swa_sink"] = self.din("swa_sink", [1, 16])
        self.c_ident = self.din("c_ident", [128, 128])
        self.c_mask_dil = self.din("c_mask_dil", [128, 384])
        self.c_mask_swa = self.din("c_mask_swa", [128, 384])
        self.c_rope = {}
        for L in sorted(set(self.seq_lens)):
            T = L // 128
            self.c_rope[L] = (self.din("c_cos16_%d" % L, [128, T, 16]), self.din("c_sin16_%d" % L, [128, T, 16]),
                              self.din("c_cos8_%d" % L, [128, T, 8]), self.din("c_sin8_%d" % L, [128, T, 8]))
        self.YT = self.dscr("YT", [20 * 128, self.LMAX], BF16, out=True)
        self.OG = self.dscr("OG", [3, self.LMAX, 512], F32)
        TM = self.LMAX // 128
        for nm, shp in (("hy_filt_w1", [33, 64]), ("hy_filt_w2", [64, 64]), ("hy_filt_w3", [64, 2048]), ("hy_fvec", [64, 3]),
                        ("hy_conv_w", [128, 24, 3]), ("hy_conv_b", [128, 24]), ("hy_skip", [128, 8])):
            self.w[nm] = self.din(nm, shp)
        self.c_delta = self.din("c_delta", [1, 1024])
        self.c_dft, self.c_filt, self.H = {}, {}, {}
        for L in sorted(set(self.seq_lens)):
            T = L // 128
            self.c_dft[L] = (self.din("c_gc_%d" % L, [T, 128, T, 128], BF16), self.din("c_gs_%d" % L, [T, 128, T, 128], BF16))
            self.c_filt[L] = {"embT": self.din("c_embT_%d" % L, [33, L]), "tcol": self.din("c_tcol_%d" % L, [128, T]),
                              "psi": self.din("c_psi_%d" % L, [128, 2, T])}
            self.H[L] = (self.dscr("HR_%d" % L, [T, 128, 1024]), self.dscr("HI_%d" % L, [T, 128, 1024]))
        self.CFSF = (self.dscr("CF", [TM, 128, 1024]), self.dscr("SF", [TM, 128, 1024]))
        self.ZRS = (self.dscr("ZR", [TM, 128, 1024], BF16), self.dscr("ZS", [TM, 128, 1024], BF16))
        self.UT = self.dscr("UT", [1024, self.LMAX], BF16)
        self.AT = self.dscr("AT", [1024, self.LMAX], BF16)
        self.BT = self.dscr("BT", [1024, self.LMAX], BF16)
        self.t_cf, self.t_H, self.t_uab, self.t_Z = Tok(), Tok(), Tok(), Tok()
        for nm, shp in (("gdn_conv_w", [128, 24, 5]), ("gdn_A_log", [1, 16]), ("gdn_dt_bias", [1, 16]), ("gdn_norm_g", [1, 128])):
            self.w[nm] = self.din(nm, shp)
        self.c_gmask = self.din("c_gmask", [128, 18, 128])
        self.QT = self.dscr("QT", [1024, self.LMAX])
        self.KT = self.dscr("KT", [1024, self.LMAX])
        self.KTOK = self.dscr("KTOK", [self.LMAX, 1024])
        self.VTOK = self.dscr("VTOK", [self.LMAX, 1024])
        self.BG = self.dscr("BG", [self.LMAX, 32])
        self.SGG = self.dscr("SGG", [self.LMAX, 1024], BF16)
        self.OFB = self.dscr("OFB", [2, self.LMAX, 1024])
        self.G_UB = self.dscr("G_UB", [2, TM, 128, 1024])
        self.G_KD = self.dscr("G_KD", [2, TM, 128, 1024])
        self.G_WT = self.dscr("G_WT", [2, TM, 128, 8, 128])
        self.G_QKM = self.dscr("G_QKM", [2, TM, 128, 8, 128])
        self.G_SCC = self.dscr("G_SCC", [2, TM, 64, 2, 8])
        self.G_GLB = self.dscr("G_GLB", [2, TM, 128, 8, 2])
        self.t_gd, self.t_of, self.t_gp = Tok(), Tok(), Tok()
        self.LSE = self.dscr("LSE", [3, self.LMAX, 4], F32)

        with ExitStack() as es:
            self.es = es
            s = self.s = Sch(nc, es)
            self.ident32 = self.sb(es, "ident32", [128, 128], F32)
            self.ident16 = self.sb(es, "ident16", [128, 128], BF16)
            self.t_const = Tok()
            s.dma("sp", self.ident32[:], self.c_ident, w=[self.t_const])
            s.dma("pool", self.ident16[:], self.c_ident, w=[self.t_const])
            self.mask_dil = self.sb(es, "mask_dil", [128, 384], F32)
            self.mask_swa = self.sb(es, "mask_swa", [128, 384], F32)
            self.eps_col = self.sb(es, "eps_col", [128, 2], F32)
            s.op("dve", lambda e: e.memset(self.eps_col[:, 0:1], EPS), w=[self.t_const])
            s.op("dve", lambda e: e.memset(self.eps_col[:, 1:2], 1.0), w=[self.t_const])
            s.dma("sp", self.mask_dil[:], self.c_mask_dil, w=[self.t_const])
            s.dma("sp", self.mask_swa[:], self.c_mask_swa, w=[self.t_const])
            self.ps = es.enter_context(nc.psum_tensor("ps", [128, 4096], F32))
            self.ps_tok = [Tok() for _ in range(8)]
            self.ps_rr = 0
            self.t_og_dram = Tok()
            s.barrier()

            if getattr(self, "en_even", [1, 1, 1])[0]:
                for L in sorted(set(self.seq_lens)):
                    self.phase_filter(L)
            r0 = 0
            for si, L in enumerate(self.seq_lens):
                self.run_seq(si, r0, L)
                r0 += L
            s.finish()
        return nc

    def bank(self, b):
        return self.ps[:, b * 512:(b + 1) * 512]

    def bank16(self, b):
        return self.ps[:, b * 512:(b + 1) * 512].bitcast(BF16)

    def run_seq(self, si, r0, L):
        T = L // 128
        for layer in range(2):
            src = self.x if layer == 0 else self.x1
            dst = self.x1 if layer == 0 else self.y
            with ExitStack() as les:
                hT = self.sb(les, "hT", [128, 8, L], BF16)
                hT_tok = [Tok() for _ in range(T)]
                pre = "e_" if layer == 0 else "o_"
                self.phase_norm(lambda t: src[r0 + t * 128:r0 + (t + 1) * 128, :], T, self.w[pre + "pre_g"], hT, hT_tok)
                if layer == 0:
                    nch = 20
                    en = getattr(self, "en_even", [1, 1, 1])
                    if en[1]:
                        self.phase_gdn1(si, L, hT, hT_tok)
                    if en[2]:
                        self.phase_xattn(si, L, hT, hT_tok, "e_", E_XQ, E_GX, 16)
                    if en[0]:
                        self.phase_hy1(si, L, hT, hT_tok)
                    if not all(en):
                        self.zero_YT(L, [c for c in range(20) if not en[0 if c < 8 else (1 if c < 16 else 2)]])
                else:
                    nch = 16
                    en = getattr(self, "en_odd", [1, 1, 1])
                    if en[0]:
                        self.phase_dil(si, L, hT, hT_tok)
                    if en[1]:
                        self.phase_swa(si, L, hT, hT_tok)
                    if en[2]:
                        self.phase_xattn(si, L, hT, hT_tok, "o_", O_XQ, O_GX, 12)
                    if not all(en):
                        self.zero_YT(L, [c for c in range(16) if not en[0 if c < 4 else (1 if c < 12 else 2)]])
                self.s.barrier()
            if layer == 0 and getattr(self, "en_even", [1, 1, 1])[1]:
                self.phase_gdn2(si, L)
            if layer == 0 and getattr(self, "en_even", [1, 1, 1])[0]:
                self.phase_hy2(si, L)
            self.phase_out(r0, L, src, dst, pre, nch)

    def phase_norm(self, src, T, g_dram, hT, hT_tok, tag="n"):
        nc, s = self.nc, self.s
        with ExitStack() as es:
            gcol = self.sb(es, tag + "gcol", [128, 8], F32)
            gfull = self.sb(es, tag + "gfull", [128, 8, 128], F32)
            t_g = Tok()
            s.dma("sp", gcol[:], g_dram, w=[t_g])
            for k in range(8):
                s.op("dve", lambda e, k=k: e.tensor_scalar(out=gfull[:, k, :], in0=self.ident32[:], scalar1=0.0,
                                                             scalar2=gcol[:, k:k + 1], op0=ALU.mult, op1=ALU.add),
                     r=[t_g, self.t_const], w=[t_g])
            xt = [self.sb(es, tag + "xt%d" % i, [128, D], F32) for i in range(2)]
            xn = [self.sb(es, tag + "xn%d" % i, [128, D], BF16) for i in range(2)]
            st = [self.sb(es, tag + "st%d" % i, [128, 4], F32) for i in range(2)]
            junk = [self.sb(es, tag + "junk%d" % i, [128, D], BF16) for i in range(2)]
            t_xt = [Tok(), Tok()]
            t_xn = [Tok(), Tok()]
            t_st = [Tok(), Tok()]
            t_junk = [Tok(), Tok()]
            def n_lane(ln):
                b = ln
                pb = ln
                for t in range(ln, T, 2):
                    s.dma("sp", xt[b][:], src(t), w=[t_xt[b]])
                    yield
                    s.op("act", lambda e: e.activation(out=junk[b][:], in_=xt[b][:], func=AF.Square, accum_out=st[b][:, 0:1]),
                         r=[t_xt[b]], w=[t_junk[b], t_st[b]])
                    yield
                    s.op("dve", lambda e: e.tensor_scalar(out=st[b][:, 1:2], in0=st[b][:, 0:1], scalar1=1.0 / D, scalar2=EPS,
                                                          op0=ALU.mult, op1=ALU.add), r=[t_st[b]], w=[t_st[b]])
                    yield
                    s.op("act", lambda e: e.activation(out=st[b][:, 2:3], in_=st[b][:, 1:2], func=AF.Sqrt), r=[t_st[b]], w=[t_st[b]])
                    yield
                    s.op("dve", lambda e: e.reciprocal(out=st[b][:, 3:4], in_=st[b][:, 2:3]), r=[t_st[b]], w=[t_st[b]])
                    yield
                    s.op("act", lambda e: e.activation(out=xn[b][:], in_=xt[b][:], func=AF.Copy, scale=st[b][:, 3:4]),
                         r=[t_xt[b], t_st[b]], w=[t_xn[b]])
                    yield
                    pv = self.bank16(pb).rearrange("p (k c) -> p k c", k=8)
                    for k in range(8):
                        s.op("pe", lambda e, k=k: e.transpose(out=pv[:, k, :], in_=xn[b][:, k * 128:(k + 1) * 128],
                                                              identity=self.ident16[:]),
                             r=[t_xn[b], self.t_const], w=[self.ps_tok[pb]])
                    yield
                    s.op("dve", lambda e: e.tensor_tensor(out=hT[:, :, t * 128:(t + 1) * 128], in0=pv, in1=gfull[:],
                                                          op=ALU.mult), r=[self.ps_tok[pb], t_g], w=[hT_tok[t]])
                    yield

            self.lockstep([n_lane(0), n_lane(1)])
            s.barrier()

    def next_bank(self, n=1):
        b = self.ps_rr
        if n == 2 and b % 2:
            b += 1
        if b + n > 8:
            b = 0
        self.ps_rr = (b + n) % 8
        return b

    def pick(self, cls, banks):
        rr = self.__dict__.setdefault("_rr", {})
        i = rr.get(cls, 0)
        rr[cls] = i + 1
        return banks[i % len(banks)]

    def rope(self, src, t_src, dst, t_dst, tmp, t_tmp, cos, sin, t_tab, H, half, dh):
        s = self.s
        A = src[:, :, 0:half]
        B = src[:, :, half:2 * half]
        cb = cos.unsqueeze(1).to_broadcast([128, H, half])
        sb_ = sin.unsqueeze(1).to_broadcast([128, H, half])
        s.op("dve", lambda e: e.tensor_tensor(out=tmp[:, 0, 0:H, :], in0=A, in1=cb, op=ALU.mult), r=[t_src, t_tab], w=[t_tmp])
        s.op("dve", lambda e: e.tensor_tensor(out=tmp[:, 1, 0:H, :], in0=B, in1=sb_, op=ALU.mult), r=[t_src, t_tab], w=[t_tmp])
        s.op("dve", lambda e: e.tensor_tensor(out=tmp[:, 2, 0:H, :], in0=B, in1=cb, op=ALU.mult), r=[t_src, t_tab], w=[t_tmp])
        s.op("dve", lambda e: e.tensor_tensor(out=tmp[:, 3, 0:H, :], in0=A, in1=sb_, op=ALU.mult), r=[t_src, t_tab], w=[t_tmp])
        s.op("dve", lambda e: e.tensor_tensor(out=dst[:, :, 0:half], in0=tmp[:, 0, 0:H, :], in1=tmp[:, 1, 0:H, :], op=ALU.subtract),
             r=[t_tmp], w=[t_dst])
        s.op("dve", lambda e: e.tensor_tensor(out=dst[:, :, half:2 * half], in0=tmp[:, 2, 0:H, :], in1=tmp[:, 3, 0:H, :], op=ALU.add),
             r=[t_tmp], w=[t_dst])
        s.op("act", lambda e: e.activation(out=dst[:, :, 2 * half:dh], in_=src[:, :, 2 * half:dh], func=AF.Copy),
             r=[t_src], w=[t_dst])

    def softmax_gen(self, sbanks, nh, nk, mask_ap, t_mask, Sm, t_Sm, Pe, t_Pe, st, t_st, sink_ap=None, t_sink=None):
        s = self.s
        for j in range(nh):
            s.op("dve", lambda e, j=j: e.tensor_tensor(out=Sm[:, j, 0:nk], in0=self.bank(sbanks[j])[:, 0:nk], in1=mask_ap, op=ALU.add),
                 r=[self.ps_tok[sbanks[j]], t_mask], w=[t_Sm])
        yield
        if sink_ap is None:
            s.op("dve", lambda e: e.tensor_reduce(out=st[:, 0:nh], in_=Sm[:, 0:nh, 0:nk], op=ALU.max, axis=AX.X, negate=True),
                 r=[t_Sm], w=[t_st])
        else:
            s.op("dve", lambda e: e.tensor_reduce(out=st[:, 2 * nh:3 * nh], in_=Sm[:, 0:nh, 0:nk], op=ALU.max, axis=AX.X),
                 r=[t_Sm], w=[t_st])
            s.op("dve", lambda e: e.tensor_tensor(out=st[:, 2 * nh:3 * nh], in0=st[:, 2 * nh:3 * nh], in1=sink_ap, op=ALU.max),
                 r=[t_st, t_sink], w=[t_st])
            s.op("dve", lambda e: e.tensor_scalar(out=st[:, 0:nh], in0=st[:, 2 * nh:3 * nh], scalar1=-1.0, scalar2=None, op0=ALU.mult),
                 r=[t_st], w=[t_st])
            s.op("dve", lambda e: e.tensor_tensor(out=st[:, 3 * nh:4 * nh], in0=sink_ap, in1=st[:, 0:nh], op=ALU.add),
                 r=[t_st, t_sink], w=[t_st])
            s.op("act", lambda e: e.activation(out=st[:, 3 * nh:4 * nh], in_=st[:, 3 * nh:4 * nh], func=AF.Exp), r=[t_st], w=[t_st])
        yield
        for j in range(nh):
            s.op("act", lambda e, j=j: e.activation(out=Pe[:, j, 0:nk], in_=Sm[:, j, 0:nk], func=AF.Exp, bias=st[:, j:j + 1],
                                                    accum_out=st[:, nh + j:nh + j + 1]),
                 r=[t_Sm, t_st], w=[t_Pe, t_st])
        yield
        if sink_ap is not None:
            s.op("dve", lambda e: e.tensor_tensor(out=st[:, nh:2 * nh], in0=st[:, nh:2 * nh], in1=st[:, 3 * nh:4 * nh], op=ALU.add),
                 r=[t_st], w=[t_st])

    def softmax_block(self, *a, **k):
        for _ in self.softmax_gen(*a, **k):
            pass

    def transpose_P(self, Pe, t_Pe, nh, nkt, PT, t_PT, ptb):
        s = self.s
        ptv = self.bank16(ptb).rearrange("p (j c) -> p j c", j=8)
        n = 0
        for j in range(nh):
            for kc in range(nkt):
                s.op("pe", lambda e, j=j, kc=kc, n=n: e.transpose(out=ptv[:, n, :], in_=Pe[:, j, kc * 128:(kc + 1) * 128],
                                                                  identity=self.ident16[:]),
                     r=[t_Pe, self.t_const], w=[self.ps_tok[ptb]])
                n += 1
        s.op("act", lambda e: e.activation(out=PT[:, 0:n, :], in_=ptv[:, 0:n, :], func=AF.Copy), r=[self.ps_tok[ptb]], w=[t_PT])

    def load_w(self, wt, t_w, w_dram, c0, n):
        self.s.dma("pool", wt[:, :, 0:n], w_dram[:, c0:c0 + n].rearrange("(k p) c -> p k c", p=128), w=[t_w])

    def proj_feat(self, out_ap, pb, wt, t_w, wc0, hT, hT_tok, tok0, ntok, extra_r=()):
        s = self.s
        toks = [hT_tok[t] for t in range(tok0 // 128, (tok0 + ntok + 127) // 128)]
        for k in range(8):
            s.op("pe", lambda e, k=k: e.matmul(out_ap, wt[:, k, wc0:wc0 + 128], hT[:, k, tok0:tok0 + ntok],
                                               start=(k == 0), stop=(k == 7)),
                 r=[t_w] + toks + list(extra_r), w=[self.ps_tok[pb]])

    def proj_tok(self, out_ap, pb, wt, t_w, wc0, ncol, hT, hT_tok_list, tok_ap_fn):
        s = self.s
        for k in range(8):
            s.op("pe", lambda e, k=k: e.matmul(out_ap, tok_ap_fn(k), wt[:, k, wc0:wc0 + ncol],
                                               start=(k == 0), stop=(k == 7)),
                 r=[t_w] + list(hT_tok_list), w=[self.ps_tok[pb]])

    def phase_xattn(self, si, L, hT, hT_tok, pre, off_q, off_g, ch0):
        nc, s = self.nc, self.s
        T = L // 128
        w_in = self.w[pre + "w_in"]
        with ExitStack() as es:
            memT = self.sb(es, "memT", [128, 8, MEM_TOKENS], BF16)
            memT_tok = [Tok(), Tok()]
            self.phase_norm(lambda t: self.mem[si * MEM_TOKENS + t * 128: si * MEM_TOKENS + (t + 1) * 128, :], 2,
                            self.w[pre + "mem_g"], memT, memT_tok, tag="m")
            wkv = self.sb(es, "wkv", [128, 8, 1024], BF16)
            t_wkv = Tok()
            self.load_w(wkv, t_wkv, self.w[pre + "w_mem_kv"], 0, 1024)
            KmT = self.sb(es, "KmT", [128, 4, MEM_TOKENS], BF16)
            Vm = self.sb(es, "Vm", [128, 2, 512], BF16)
            t_km, t_vm = Tok(), Tok()
            for h in range(4):
                pb = self.next_bank()
                self.proj_feat(self.bank(pb)[:, 0:256], pb, wkv, t_wkv, h * 128, memT, memT_tok, 0, 256)
                s.op("act", lambda e: e.activation(out=KmT[:, h, :], in_=self.bank(pb)[:, 0:256], func=AF.Copy),
                     r=[self.ps_tok[pb]], w=[t_km])
            for mt in range(2):
                pb = self.next_bank()
                self.proj_tok(self.bank(pb), pb, wkv, t_wkv, 512, 512, memT, memT_tok,
                              lambda k: memT[:, k, mt * 128:(mt + 1) * 128])
                s.op("act", lambda e: e.activation(out=Vm[:, mt, :], in_=self.bank(pb), func=AF.Copy),
                     r=[self.ps_tok[pb]], w=[t_vm])
            wq = self.sb(es, "wq", [128, 8, 512], BF16)
            wg = self.sb(es, "wg", [128, 8, 512], BF16)
            t_wq, t_wg = Tok(), Tok()
            self.load_w(wq, t_wq, w_in, off_q, 512)
            self.load_w(wg, t_wg, w_in, off_g, 512)
            qT = self.sb(es, "qT", [128, 4, 512], BF16)
            t_qT = Tok()
            sg = [self.sb(es, "sg%d" % i, [128, 512], F32) for i in range(2)]
            t_sg = [Tok(), Tok()]
            Sm = [self.sb(es, "Sm%d" % i, [128, 4, 256], F32) for i in range(2)]
            Pe = [self.sb(es, "Pe%d" % i, [128, 4, 256], BF16) for i in range(2)]
            t_Pe = [Tok(), Tok()]
            stt = [self.sb(es, "stt%d" % i, [128, 16], F32) for i in range(2)]
            t_stt = [Tok(), Tok()]
            PT = [self.sb(es, "PT%d" % i, [128, 8, 128], BF16) for i in range(2)]
            t_PT = [Tok(), Tok()]
            yx = [self.sb(es, "yx%d" % i, [128, 512], BF16) for i in range(2)]
            t_yx = [Tok(), Tok()]
            ystage = [self.sb(es, "ystage%d" % i, [128, 4, 512], BF16) for i in range(2)]
            t_ys = [Tok(), Tok()]
            scale = 128.0 ** -0.5
            nblk = (L + 511) // 512
            for blk in range(nblk):
                tok0 = blk * 512
                ntok = min(512, L - tok0)
                yb = blk % 2
                for h in range(4):
                    pb = self.pick("xq", [2, 3])
                    self.proj_feat(self.bank(pb)[:, 0:ntok], pb, wq, t_wq, h * 128, hT, hT_tok, tok0, ntok)
                    s.op("act", lambda e: e.activation(out=qT[:, h, 0:ntok], in_=self.bank(pb)[:, 0:ntok], func=AF.Copy,
                                                       scale=scale), r=[self.ps_tok[pb]], w=[t_qT])
                def x_lane(ln):
                    gb = ln
                    s0 = 4 + 2 * ln
                    ptb = 2 + ln
                    for tl in range(ln, ntok // 128, 2):
                        t = blk * 4 + tl
                        b = ln
                        self.proj_tok(self.bank(gb), gb, wg, t_wg, 0, 512, hT, [hT_tok[t]],
                                      lambda k: hT[:, k, t * 128:(t + 1) * 128])
                        sv = self.ps[:, s0 * 512:(s0 + 2) * 512].rearrange("p (h m) -> p h m", h=4)
                        for h in range(4):
                            pbh = s0 + h // 2
                            s.op("pe", lambda e, h=h: e.matmul(sv[:, h, :], qT[:, h, tl * 128:(tl + 1) * 128], KmT[:, h, :],
                                                               start=True, stop=True),
                                 r=[t_qT, t_km], w=[self.ps_tok[pbh]])
                        yield
                        s.op("act", lambda e: e.activation(out=sg[b][:], in_=self.bank(gb), func=AF.Silu),
                             r=[self.ps_tok[gb]], w=[t_sg[b]])
                        s.op("dve", lambda e: e.tensor_reduce(out=stt[b][:, 0:4], in_=sv, op=ALU.max, axis=AX.X, negate=True),
                             r=[self.ps_tok[s0], self.ps_tok[s0 + 1]], w=[t_stt[b]])
                        yield
                        for h in range(4):
                            s.op("act", lambda e, h=h: e.activation(out=Pe[b][:, h, :], in_=sv[:, h, :], func=AF.Exp,
                                                                    bias=stt[b][:, h:h + 1], accum_out=stt[b][:, 4 + h:5 + h]),
                                 r=[self.ps_tok[s0], self.ps_tok[s0 + 1], t_stt[b]], w=[t_Pe[b], t_stt[b]])
                        yield
                        s.op("dve", lambda e: e.reciprocal(out=stt[b][:, 8:12], in_=stt[b][:, 4:8]), r=[t_stt[b]], w=[t_stt[b]])
                        ptv = self.bank16(ptb).rearrange("p (j c) -> p j c", j=8)
                        for h in range(4):
                            for mc in range(2):
                                s.op("pe", lambda e, h=h, mc=mc: e.transpose(out=ptv[:, h * 2 + mc, :],
                                                                             in_=Pe[b][:, h, mc * 128:(mc + 1) * 128],
                                                                             identity=self.ident16[:]),
                                     r=[t_Pe[b], self.t_const], w=[self.ps_tok[ptb]])
                        yield
                        s.op("act", lambda e: e.activation(out=PT[b][:], in_=ptv, func=AF.Copy), r=[self.ps_tok[ptb]], w=[t_PT[b]])
                        yield
                        po = gb
                        for h in range(4):
                            for mc in range(2):
                                s.op("pe", lambda e, h=h, mc=mc: e.matmul(self.bank(po)[:, h * 128:(h + 1) * 128],
                                                                          PT[b][:, h * 2 + mc, :], Vm[:, mc, h * 128:(h + 1) * 128],
                                                                          start=(mc == 0), stop=(mc == 1)),
                                     r=[t_PT[b], t_vm], w=[self.ps_tok[po]])
                        yield
                        for h in range(4):
                            s.op("dve", lambda e, h=h: e.scalar_tensor_tensor(out=yx[b][:, h * 128:(h + 1) * 128],
                                                                              in0=self.bank(po)[:, h * 128:(h + 1) * 128],
                                                                              scalar=stt[b][:, 8 + h:9 + h], in1=sg[b][:, h * 128:(h + 1) * 128],
                                                                              op0=ALU.mult, op1=ALU.mult),
                                 r=[self.ps_tok[po], t_stt[b], t_sg[b]], w=[t_yx[b]])
                        yield
                        self.to_ystage(yx[b], t_yx[b], 4, ystage[yb], t_ys[yb], tl, banks=[ptb])
                        yield

                self.lockstep([x_lane(0), x_lane(1)])
                self.store_ystage(ystage[yb], t_ys[yb], 4, ch0, tok0, ntok)
            s.barrier()

    def to_ystage(self, ytile, t_y, nch, ystage, t_ys, tl, banks=None):
        s = self.s
        for c0 in range(0, nch, 8):
            n = min(8, nch - c0)
            pt = self.next_bank() if banks is None else self.pick("pt", banks)
            ptv = self.bank16(pt).rearrange("p (j c) -> p j c", j=8)
            for c in range(n):
                s.op("pe", lambda e, c=c: e.transpose(out=ptv[:, c, :], in_=ytile[:, (c0 + c) * 128:(c0 + c + 1) * 128],
                                                      identity=self.ident16[:]),
                     r=[t_y, self.t_const], w=[self.ps_tok[pt]])
            s.op("act", lambda e: e.activation(out=ystage[:, c0:c0 + n, tl * 128:(tl + 1) * 128], in_=ptv[:, 0:n, :],
                                               func=AF.Copy), r=[self.ps_tok[pt]], w=[t_ys])

    def store_ystage(self, ystage, t_ys, nch, ch0, tok0, ntok):
        dst = self.YT[ch0 * 128:(ch0 + nch) * 128, tok0:tok0 + ntok].rearrange("(c p) n -> p c n", p=128)
        self.s.dma("sp", dst, ystage[:, 0:nch, 0:ntok], r=[t_ys])

    def phase_swa(self, si, L, hT, hT_tok):
        nc, s = self.nc, self.s
        T = L // 128
        w_in = self.w["o_w_in"]
        with ExitStack() as es:
            wkv = self.sb(es, "swkv", [128, 8, 256], BF16)
            wq = self.sb(es, "swq", [128, 8, 1024], BF16)
            wg = self.sb(es, "swg", [128, 8, 1024], BF16)
            t_wkv, t_wq, t_wg = Tok(), Tok(), Tok()
            self.load_w(wkv, t_wkv, w_in, O_DKV, 256)
            self.load_w(wq, t_wq, w_in, O_DQ, 1024)
            self.load_w(wg, t_wg, w_in, O_GD, 1024)
            cos8 = self.sb(es, "cos8", [128, T, 8], F32)
            sin8 = self.sb(es, "sin8", [128, T, 8], F32)
            t_tab = Tok()
            s.dma("sp", cos8[:], self.c_rope[L][2], w=[t_tab])
            s.dma("sp", sin8[:], self.c_rope[L][3], w=[t_tab])
            sink = self.sb(es, "sink", [128, 16], F32)
            t_sink = Tok()
            s.dma("sp", sink[:], self.w["swa_sink"].partition_broadcast(128), w=[t_sink])
            kTd = self.sb(es, "kTd", [128, 2, L], BF16)
            t_kT = [Tok() for _ in range(T)]
            vtok = self.sb(es, "vtok", [128, T, 128], BF16)
            t_v = [Tok() for _ in range(T)]
            raw = [self.sb(es, "sraw%d" % i, [128, 16, 64], F32) for i in range(2)]
            t_raw = [Tok(), Tok()]
            rtmps = [self.sb(es, "srtmp%d" % i, [128, 4, 16, 8], F32) for i in range(2)]
            t_rtmps = [Tok(), Tok()]
            rtmp, t_rtmp = rtmps[0], t_rtmps[0]
            krs = [self.sb(es, "skr%d" % i, [128, 2, 64], BF16) for i in range(2)]
            t_krs = [Tok(), Tok()]
            kds = [self.sb(es, "skd%d" % i, [128, 2, 2, 64], BF16) for i in range(2)]
            t_kds = [Tok(), Tok()]
            def kv_lane(ln):
                b = ln
                pb = 2 + ln
                pt = 6 + ln
                for t in range(ln, T, 2):
                    self.proj_tok(self.bank(pb)[:, 0:256], pb, wkv, t_wkv, 0, 256, hT, [hT_tok[t]],
                                  lambda k: hT[:, k, t * 128:(t + 1) * 128])
                    yield
                    s.op("act", lambda e: e.activation(out=raw[b][:, 0:2, :], in_=self.bank(pb)[:, 0:128].rearrange("p (h d) -> p h d", h=2),
                                                       func=AF.Copy), r=[self.ps_tok[pb]], w=[t_raw[b]])
                    s.op("act", lambda e: e.activation(out=vtok[:, t, :], in_=self.bank(pb)[:, 128:256], func=AF.Copy),
                         r=[self.ps_tok[pb]], w=[t_v[t]])
                    yield
                    self.rope(raw[b][:, 0:2, :], t_raw[b], krs[ln][:], t_krs[ln], rtmps[ln], t_rtmps[ln], cos8[:, t, :], sin8[:, t, :], t_tab, 2, 8, 64)
                    yield
                    for r_ in range(2):
                        s.op("dve", lambda e, r_=r_: e.tensor_copy(out=kds[ln][:, :, r_, :], in_=krs[ln][:]), r=[t_krs[ln]], w=[t_kds[ln]])
                    yield
                    ptv = self.bank16(pt).rearrange("p (j c) -> p j c", j=8)
                    for kv in range(2):
                        s.op("pe", lambda e, kv=kv: e.transpose(out=ptv[:, kv, :], in_=kds[ln][:, kv, :, :].rearrange("p r d -> p (r d)"),
                                                                identity=self.ident16[:]), r=[t_kds[ln], self.t_const], w=[self.ps_tok[pt]])
                    yield
                    s.op("act", lambda e: e.activation(out=kTd[:, :, t * 128:(t + 1) * 128], in_=ptv[:, 0:2, :], func=AF.Copy),
                         r=[self.ps_tok[pt]], w=[t_kT[t]])
                    yield

            self.lockstep([kv_lane(0), kv_lane(1)])
            q16 = [self.sb(es, "sq16%d" % i, [128, 16, 64], BF16) for i in range(2)]
            t_q16 = [Tok(), Tok()]
            qT = [self.sb(es, "sqT%d" % i, [128, 8, 128], BF16) for i in range(2)]
            t_qT = [Tok(), Tok()]
            sgd = [self.sb(es, "sgd%d" % i, [128, 1024], F32) for i in range(2)]
            t_sgd = [Tok(), Tok()]
            Sm = [self.sb(es, "sSm%d" % i, [128, 2, 384], F32) for i in range(2)]
            t_Sm = [Tok(), Tok()]
            Pe = [self.sb(es, "sPe%d" % i, [128, 2, 384], BF16) for i in range(2)]
            t_Pe = [Tok(), Tok()]
            stt = [self.sb(es, "sst%d" % i, [128, 8], F32) for i in range(2)]
            t_stt = [Tok(), Tok()]
            PT = [self.sb(es, "sPT%d" % i, [128, 8, 128], BF16) for i in range(2)]
            t_PT = [Tok(), Tok()]
            dens = [self.sb(es, "sdens%d" % i, [128, 32], F32) for i in range(2)]
            t_dens = [Tok(), Tok()]
            densl = [[self.sb(es, "sdensl%d_%d" % (ln, i), [128, 8], F32) for i in range(2)] for ln in range(2)]
            t_densl = [[Tok(), Tok()], [Tok(), Tok()]]
            yd = [self.sb(es, "syd%d" % i, [128, 1024], BF16) for i in range(2)]
            t_yd = [Tok(), Tok()]
            ystage = [self.sb(es, "systage%d" % i, [128, 8, 512], BF16) for i in range(2)]
            t_ys = [Tok(), Tok()]
            def prologue(tt):
                pbq = tt % 2
                for half in range(2):
                    pb = 2 + half
                    self.proj_tok(self.bank(pb), pb, wq, t_wq, half * 512, 512, hT, [hT_tok[tt]],
                                  lambda k: hT[:, k, tt * 128:(tt + 1) * 128])
                    yield
                    s.op("act", lambda e: e.activation(out=raw[pbq][:, half * 8:(half + 1) * 8, :],
                                                       in_=self.bank(pb).rearrange("p (h d) -> p h d", h=8), func=AF.Copy,
                                                       scale=0.125), r=[self.ps_tok[pb]], w=[t_raw[pbq]])
                    yield
                self.rope(raw[pbq][:], t_raw[pbq], q16[pbq][:], t_q16[pbq], rtmp, t_rtmp, cos8[:, tt, :], sin8[:, tt, :], t_tab, 16, 8, 64)
                yield
                pt = self.pick("pt", [6, 7])
                ptv = self.bank16(pt).rearrange("p (j c) -> p j c", j=8)
                for c in range(8):
                    s.op("pe", lambda e, c=c: e.transpose(out=ptv[:, c, :], in_=q16[pbq][:, 2 * c:2 * c + 2, :].rearrange("p h d -> p (h d)"),
                                                          identity=self.ident16[:]), r=[t_q16[pbq], self.t_const], w=[self.ps_tok[pt]])
                yield
                s.op("act", lambda e: e.activation(out=qT[pbq][:], in_=ptv, func=AF.Copy), r=[self.ps_tok[pt]], w=[t_qT[pbq]])
                yield
                for half in range(2):
                    pb = 2 + half
                    self.proj_tok(self.bank(pb), pb, wg, t_wg, half * 512, 512, hT, [hT_tok[tt]],
                                  lambda k: hT[:, k, tt * 128:(tt + 1) * 128])
                    yield
                    s.op("act", lambda e: e.activation(out=sgd[pbq][:, half * 512:(half + 1) * 512], in_=self.bank(pb), func=AF.Silu),
                         r=[self.ps_tok[pb]], w=[t_sgd[pbq]])
                    yield

            self.lockstep([prologue(0)])
            for t in range(T):
                b = t % 2
                blk, tl = t // 4, t % 4
                yb = blk % 2
                kts = [kt for kt in (t - 1, t, t + 1) if 0 <= kt < T]
                nkt = len(kts)
                nk = 128 * nkt
                m0 = (kts[0] - (t - 1)) * 128
                def swa_lane(ln):
                    pp = ln
                    sb0 = 4 if ln == 0 else 2
                    ptb = 6 + ln
                    for c in range(4 * ln, 4 * ln + 4):
                        kv = c // 4
                        for j in range(2):
                            s.op("pe", lambda e, j=j: e.matmul(self.bank(sb0 + j)[:, 0:nk], qT[b][j * 64:(j + 1) * 64, c, :],
                                                               kTd[j * 64:(j + 1) * 64, kv, kts[0] * 128:kts[0] * 128 + nk],
                                                               start=True, stop=True),
                                 r=[t_qT[b]] + [t_kT[kt] for kt in kts], w=[self.ps_tok[sb0 + j]])
                        yield
                        yield from self.softmax_gen([sb0, sb0 + 1], 2, nk, self.mask_swa[:, m0:m0 + nk], self.t_const, Sm[pp], t_Sm[pp], Pe[pp], t_Pe[pp],
                                                    stt[pp], t_stt[pp], sink_ap=sink[:, 2 * c:2 * c + 2], t_sink=t_sink)
                        s.op("dve", lambda e: e.tensor_copy(out=densl[ln][b][:, 2 * (c % 4):2 * (c % 4) + 2], in_=stt[pp][:, 2:4]), r=[t_stt[pp]], w=[t_densl[ln][b]])
                        yield
                        self.transpose_P(Pe[pp], t_Pe[pp], 2, nkt, PT[pp], t_PT[pp], ptb)
                        yield
                        for j in range(2):
                            hq = 2 * c + j
                            ob = hq // 8
                            for kc in range(nkt):
                                s.op("pe", lambda e, j=j, kc=kc: e.matmul(self.bank(ob)[:, (hq % 8) * 64:(hq % 8 + 1) * 64],
                                                                          PT[pp][:, j * nkt + kc, :], vtok[:, kts[kc], kv * 64:(kv + 1) * 64],
                                                                          start=(kc == 0), stop=(kc == nkt - 1)),
                                     r=[t_PT[pp]] + [t_v[kt] for kt in kts], w=[self.ps_tok[ob]])
                        yield

                self.lockstep([swa_lane(0), swa_lane(1)])
                if t + 1 < T:
                    self.lockstep([prologue(t + 1)])
                for ln in range(2):
                    s.op("dve", lambda e, ln=ln: e.tensor_copy(out=dens[b][:, 8 * ln:8 * ln + 8], in_=densl[ln][b][:]), r=[t_densl[ln][b]], w=[t_dens[b]])
                s.op("dve", lambda e: e.reciprocal(out=dens[b][:, 16:32], in_=dens[b][:, 0:16]), r=[t_dens[b]], w=[t_dens[b]])
                for hq in range(16):
                    ob = hq // 8
                    s.op("dve", lambda e, hq=hq: e.scalar_tensor_tensor(out=yd[b][:, hq * 64:(hq + 1) * 64],
                                                                        in0=self.bank(ob)[:, (hq % 8) * 64:(hq % 8 + 1) * 64],
                                                                        scalar=dens[b][:, 16 + hq:17 + hq], in1=sgd[b][:, hq * 64:(hq + 1) * 64],
                                                                        op0=ALU.mult, op1=ALU.mult),
                         r=[self.ps_tok[ob], t_dens[b], t_sgd[b]], w=[t_yd[b]])
                self.to_ystage(yd[b], t_yd[b], 8, ystage[yb], t_ys[yb], tl, banks=[6, 7])
                if tl == 3 or t == T - 1:
                    self.store_ystage(ystage[yb], t_ys[yb], 8, 4, blk * 512, (tl + 1) * 128)
            s.barrier()

    def dft_fwd(self, L, x_tok, t_x, ncols, cb):
        s = self.s
        T = L // 128
        gc_d, gs_d = self.c_dft[L]
        with ExitStack() as es:
            tc = [self.sb(es, "tc%d" % i, [128, T, 128], BF16) for i in range(2)]
            ts = [self.sb(es, "ts%d" % i, [128, T, 128], BF16) for i in range(2)]
            t_tab = [Tok(), Tok()]
            s.dma("sp", tc[0][:], gc_d[0], w=[t_tab[0]])
            s.dma("sp", ts[0][:], gs_d[0], w=[t_tab[0]])
            for kc in range(T):
                b = kc % 2
                if kc + 1 < T:
                    s.dma("sp", tc[1 - b][:], gc_d[kc + 1], w=[t_tab[1 - b]])
                    s.dma("sp", ts[1 - b][:], gs_d[kc + 1], w=[t_tab[1 - b]])
                for half in range(ncols // 512):
                    bC = self.pick("dftC", [0, 2])
                    bS = bC + 1
                    for nci in range(T):
                        s.op("pe", lambda e, nci=nci: e.matmul(self.bank(bC), tc[b][:, nci, :], x_tok[:, nci, half * 512:(half + 1) * 512],
                                                               start=(nci == 0), stop=(nci == T - 1)),
                             r=[t_tab[b], t_x], w=[self.ps_tok[bC]])
                    for nci in range(T):
                        s.op("pe", lambda e, nci=nci: e.matmul(self.bank(bS), ts[b][:, nci, :], x_tok[:, nci, half * 512:(half + 1) * 512],
                                                               start=(nci == 0), stop=(nci == T - 1)),
                             r=[t_tab[b], t_x], w=[self.ps_tok[bS]])
                    cb(kc, half, bC, bS)
            s.barrier()

    def phase_filter(self, L):
        nc, s = self.nc, self.s
        T = L // 128
        HR, HI = self.H[L]
        CF, SF = self.CFSF
        c = self.c_filt[L]
        with ExitStack() as es:
            embT = self.sb(es, "embT", [33, L], F32)
            w1 = self.sb(es, "fw1", [33, 64], F32)
            w2 = self.sb(es, "fw2", [64, 64], F32)
            w3 = self.sb(es, "fw3", [64, 2048], F32)
            vec = self.sb(es, "fvec", [64, 3], F32)
            hid1 = self.sb(es, "hid1", [64, L], F32)
            hid2 = self.sb(es, "hid2", [64, L], F32)
            tcol = self.sb(es, "tcol", [128, T], F32)
            dbc = self.sb(es, "dbc", [128, 1024], F32)
            psi = self.sb(es, "psi", [128, 2, T], F32)
            t_c = Tok()
            s.dma("sp", embT[:], c["embT"], w=[t_c])
            s.dma("sp", w1[:], self.w["hy_filt_w1"], w=[t_c])
            s.dma("sp", w2[:], self.w["hy_filt_w2"], w=[t_c])
            s.dma("sp", w3[:], self.w["hy_filt_w3"], w=[t_c])
            s.dma("sp", vec[:], self.w["hy_fvec"], w=[t_c])
            s.dma("sp", tcol[:], c["tcol"], w=[t_c])
            s.dma("sp", dbc[:], self.c_delta.partition_broadcast(128), w=[t_c])
            s.dma("sp", psi[:], c["psi"], w=[t_c])
            arg = [self.sb(es, "farg%d" % i, [64, 512], F32) for i in range(2)]
            sn = [self.sb(es, "fsn%d" % i, [64, 512], F32) for i in range(2)]
            t_arg = [Tok(), Tok()]
            t_hid = Tok()
            for layer_i, (wm, kdim, src, dst, bcol) in enumerate(((w1, 33, embT, hid1, 0), (w2, 64, hid1, hid2, 1))):
                for blk in range(L // 512):
                    b = blk % 2
                    pb = self.pick("f", [4, 5])
                    s.op("pe", lambda e: e.matmul(self.bank(pb)[0:64, :], wm[0:kdim, :], src[0:kdim, blk * 512:(blk + 1) * 512],
                                                  start=True, stop=True), r=[t_c, t_hid], w=[self.ps_tok[pb]])
                    s.op("dve", lambda e: e.tensor_scalar(out=arg[b][:], in0=self.bank(pb)[0:64, :], scalar1=vec[:, bcol:bcol + 1],
                                                          scalar2=vec[:, 2:3], op0=ALU.add, op1=ALU.mult),
                         r=[self.ps_tok[pb], t_c], w=[t_arg[b]])
                    s.op("act", lambda e: e.activation(out=sn[b][:], in_=arg[b][:], func=AF.Sin, scale=1.0 / 3.0), r=[t_arg[b]], w=[t_arg[b]])
                    s.op("dve", lambda e: e.tensor_tensor(out=arg[b][:], in0=sn[b][:], in1=sn[b][:], op=ALU.mult), r=[t_arg[b]], w=[t_arg[b]])
                    s.op("dve", lambda e: e.tensor_scalar(out=arg[b][:], in0=arg[b][:], scalar1=-4.0, scalar2=3.0, op0=ALU.mult, op1=ALU.add),
                         r=[t_arg[b]], w=[t_arg[b]])
                    s.op("dve", lambda e: e.tensor_tensor(out=dst[:, blk * 512:(blk + 1) * 512], in0=sn[b][:], in1=arg[b][:], op=ALU.mult),
                         r=[t_arg[b]], w=[t_hid])
            filt = self.sb(es, "filt", [128, T, 1024], BF16)
            t_filt = Tok()
            dec = [self.sb(es, "fdec%d" % i, [128, 1024], F32) for i in range(2)]
            t_dec = [Tok(), Tok()]
            zc = [self.sb(es, "fzc%d" % i, [128, 4, 512], F32) for i in range(2)]
            t_zc = [Tok(), Tok()]
            ho = [self.sb(es, "fho%d" % i, [128, 2, 512], F32) for i in range(2)]
            t_ho = [Tok(), Tok()]
            for dirn in range(2):
                for t in range(T):
                    b = t % 2
                    s.op("act", lambda e: e.activation(out=dec[b][:], in_=dbc[:], func=AF.Exp, scale=tcol[:, t:t + 1]),
                         r=[t_c], w=[t_dec[b]])
                    for hb in range(2):
                        pb = self.pick("f", [4, 5])
                        s.op("pe", lambda e: e.matmul(self.bank(pb), hid2[:, t * 128:(t + 1) * 128],
                                                      w3[:, dirn * 1024 + hb * 512:dirn * 1024 + (hb + 1) * 512], start=True, stop=True),
                             r=[t_hid, t_c], w=[self.ps_tok[pb]])
                        s.op("dve", lambda e: e.tensor_tensor(out=filt[:, t, hb * 512:(hb + 1) * 512], in0=self.bank(pb),
                                                              in1=dec[b][:, hb * 512:(hb + 1) * 512], op=ALU.mult),
                             r=[self.ps_tok[pb], t_dec[b]], w=[t_filt])
                if dirn == 1:
                    s.op("dve", lambda e: e.memset(filt[0:1, 0, :], 0.0), w=[t_filt])

                def cb(kc, half, bC, bS, dirn=dirn):
                    b = (kc * 2 + half) % 2
                    hs = slice(half * 512, (half + 1) * 512)
                    if dirn == 0:
                        s.op("act", lambda e: e.activation(out=zc[b][:, 0, :], in_=self.bank(bC), func=AF.Copy), r=[self.ps_tok[bC]], w=[t_zc[b]])
                        s.op("act", lambda e: e.activation(out=zc[b][:, 1, :], in_=self.bank(bS), func=AF.Copy), r=[self.ps_tok[bS]], w=[t_zc[b]])
                        s.dma("sp", CF[kc, :, hs], zc[b][:, 0, :], r=[t_zc[b]], w=[self.t_cf])
                        s.dma("sp", SF[kc, :, hs], zc[b][:, 1, :], r=[t_zc[b]], w=[self.t_cf])
                    else:
                        s.dma("sp", zc[b][:, 0, :], CF[kc, :, hs], r=[self.t_cf], w=[t_zc[b]])
                        s.dma("sp", zc[b][:, 1, :], SF[kc, :, hs], r=[self.t_cf], w=[t_zc[b]])
                        Z = zc[b]
                        s.op("dve", lambda e: e.tensor_tensor(out=Z[:, 2, :], in0=Z[:, 0, :], in1=self.bank(bC), op=ALU.add),
                             r=[self.ps_tok[bC], t_zc[b]], w=[t_zc[b]])
                        s.op("dve", lambda e: e.tensor_tensor(out=Z[:, 0, :], in0=Z[:, 0, :], in1=self.bank(bC), op=ALU.subtract),
                             r=[self.ps_tok[bC], t_zc[b]], w=[t_zc[b]])
                        s.op("dve", lambda e: e.tensor_tensor(out=Z[:, 3, :], in0=Z[:, 1, :], in1=self.bank(bS), op=ALU.add),
                             r=[self.ps_tok[bS], t_zc[b]], w=[t_zc[b]])
                        s.op("dve", lambda e: e.tensor_tensor(out=Z[:, 1, :], in0=self.bank(bS), in1=Z[:, 1, :], op=ALU.subtract),
                             r=[self.ps_tok[bS], t_zc[b]], w=[t_zc[b]])
                        cps = psi[:, 0, kc:kc + 1]
                        sps = psi[:, 1, kc:kc + 1]
                        s.op("dve", lambda e: e.tensor_scalar(out=ho[b][:, 0, :], in0=Z[:, 2, :], scalar1=cps, scalar2=None, op0=ALU.mult),
                             r=[t_zc[b], t_c], w=[t_ho[b]])
                        s.op("dve", lambda e: e.scalar_tensor_tensor(out=ho[b][:, 0, :], in0=Z[:, 3, :], scalar=sps, in1=ho[b][:, 0, :],
                                                                     op0=ALU.mult, op1=ALU.add), r=[t_zc[b], t_c, t_ho[b]], w=[t_ho[b]])
                        s.op("dve", lambda e: e.tensor_scalar(out=ho[b][:, 1, :], in0=Z[:, 0, :], scalar1=sps, scalar2=None, op0=ALU.mult),
                             r=[t_zc[b], t_c], w=[t_ho[b]])
                        s.op("dve", lambda e: e.scalar_tensor_tensor(out=ho[b][:, 1, :], in0=Z[:, 1, :], scalar=cps, in1=ho[b][:, 1, :],
                                                                     op0=ALU.mult, op1=ALU.add), r=[t_zc[b], t_c, t_ho[b]], w=[t_ho[b]])
                        s.dma("sp", HR[kc, :, hs], ho[b][:, 0, :], r=[t_ho[b]], w=[self.t_H])
                        s.dma("sp", HI[kc, :, hs], ho[b][:, 1, :], r=[t_ho[b]], w=[self.t_H])

                self.dft_fwd(L, filt, t_filt, 1024, cb)
            s.barrier()

    def phase_hy1(self, si, L, hT, hT_tok):
        nc, s = self.nc, self.s
        w_in = self.w["e_w_in"]
        nblk = L // 512
        with ExitStack() as es:
            cw = self.sb(es, "hcw", [128, 24, 3], F32)
            cbias = self.sb(es, "hcb", [128, 24], F32)
            skip = self.sb(es, "hskip", [128, 8], F32)
            t_c = Tok()
            s.dma("sp", cw[:], self.w["hy_conv_w"], w=[t_c])
            s.dma("sp", cbias[:], self.w["hy_conv_b"], w=[t_c])
            s.dma("sp", skip[:], self.w["hy_skip"], w=[t_c])
            wts = [self.sb(es, "hw%d" % i, [128, 8, 4, 128], BF16) for i in range(2)]
            t_wts = [Tok(), Tok()]
            diag = [self.sb(es, "hdiag%d" % i, [128, 9, 128], BF16) for i in range(2)]
            t_diag = [Tok(), Tok()]
            zT = self.sb(es, "hzT", [128, 3, L + 2], BF16)
            t_zT = Tok()
            s.op("dve", lambda e: e.memset(zT[:, :, 0:1], 0.0), w=[t_zT])
            s.op("dve", lambda e: e.memset(zT[:, :, L + 1:L + 2], 0.0), w=[t_zT])
            t_zTb = [Tok() for _ in range(nblk)]
            xa = [self.sb(es, "hxa%d" % i, [128, 4, 512], F32) for i in range(2)]
            t_xa = [Tok(), Tok()]
            o16 = [self.sb(es, "ho16%d" % i, [128, 3, 512], BF16) for i in range(2)]
            t_o16 = [Tok(), Tok()]
            for c in range(8):
                wt, t_w = wts[c % 2], t_wts[c % 2]
                dg, t_dg = diag[c % 2], t_diag[c % 2]
                for a in range(4):
                    s.dma("pool", wt[:, :, a, :], w_in[:, a * 1024 + c * 128:a * 1024 + (c + 1) * 128].rearrange("(k p) c -> p k c", p=128),
                          w=[t_w])
                for a in range(3):
                    for j in range(3):
                        s.op("dve", lambda e, a=a, j=j: e.tensor_scalar(out=dg[:, a * 3 + j, :], in0=self.ident16[:],
                                                                        scalar1=cw[:, a * 8 + c, j:j + 1], scalar2=None, op0=ALU.mult),
                             r=[t_c, self.t_const], w=[t_dg])
                def h1a_lane(ln):
                    pb = 2 + ln
                    for blk in range(ln, nblk, 2):
                        for a in range(3):
                            self.proj_feat(self.bank(pb), pb, wt[:, :, a, :], t_w, 0, hT, hT_tok, blk * 512, 512)
                            yield
                            s.op("act", lambda e, a=a: e.activation(out=zT[:, a, 1 + blk * 512:1 + (blk + 1) * 512], in_=self.bank(pb), func=AF.Copy),
                                 r=[self.ps_tok[pb]], w=[t_zTb[blk]])
                            yield

                self.lockstep([h1a_lane(0), h1a_lane(1)])

                def h1b_lane(ln):
                    b = ln
                    X = xa[b]
                    O = o16[b]
                    pc = 4 + ln
                    pp = 2 + ln
                    for blk in range(ln, nblk, 2):
                        zdeps = [t_zTb[bk] for bk in (blk - 1, blk, blk + 1) if 0 <= bk < nblk] + [t_zT]
                        self.proj_feat(self.bank(pp), pp, wt[:, :, 3, :], t_w, 0, hT, hT_tok, blk * 512, 512)
                        for a in range(3):
                            for j in range(3):
                                s.op("pe", lambda e, a=a, j=j: e.matmul(self.bank(pc), dg[:, a * 3 + j, :], zT[:, a, blk * 512 + j:blk * 512 + j + 512],
                                                                        start=(j == 0), stop=(j == 2)), r=[t_dg] + zdeps, w=[self.ps_tok[pc]])
                            yield
                            s.op("act", lambda e, a=a: e.activation(out=X[:, a, :], in_=self.bank(pc), func=AF.Identity,
                                                                    bias=cbias[:, a * 8 + c:a * 8 + c + 1]), r=[self.ps_tok[pc], t_c], w=[t_xa[b]])
                            yield
                        s.op("act", lambda e: e.activation(out=X[:, 3, :], in_=self.bank(pp), func=AF.Silu), r=[self.ps_tok[pp]], w=[t_xa[b]])
                        yield
                        s.op("dve", lambda e: e.tensor_tensor(out=X[:, 2, :], in0=X[:, 2, :], in1=X[:, 1, :], op=ALU.mult), r=[t_xa[b]], w=[t_xa[b]])
                        s.op("pool", lambda e: e.tensor_tensor(out=X[:, 0, :], in0=X[:, 0, :], in1=X[:, 3, :], op=ALU.mult), r=[t_xa[b]], w=[t_xa[b]])
                        yield
                        s.op("act", lambda e: e.activation(out=O[:, 0, :], in_=X[:, 2, :], func=AF.Copy), r=[t_xa[b]], w=[t_o16[b]])
                        s.op("act", lambda e: e.activation(out=O[:, 1, :], in_=X[:, 0, :], func=AF.Copy), r=[t_xa[b]], w=[t_o16[b]])
                        s.op("dve", lambda e: e.scalar_tensor_tensor(out=O[:, 2, :], in0=X[:, 2, :], scalar=skip[:, c:c + 1], in1=X[:, 0, :],
                                                                     op0=ALU.mult, op1=ALU.mult), r=[t_xa[b], t_c], w=[t_o16[b]])
                        yield
                        for a, dr in enumerate((self.UT, self.AT, self.BT)):
                            s.dma("sp", dr[c * 128:(c + 1) * 128, blk * 512:(blk + 1) * 512], O[:, a, :], r=[t_o16[b]], w=[self.t_uab])
                        yield

                self.lockstep([h1b_lane(0), h1b_lane(1)])
            s.barrier()

    def phase_hy2(self, si, L):
        nc, s = self.nc, self.s
        T = L // 128
        HR, HI = self.H[L]
        ZR, ZS = self.ZRS
        gc_d, gs_d = self.c_dft[L]
        with ExitStack() as es:
            u_tok = self.sb(es, "u_tok", [128, T, 1024], BF16)
            t_u = Tok()
            with ExitStack() as es2:
                ut = [self.sb(es2, "utl%d" % i, [128, L], BF16) for i in range(2)]
                t_ut = [Tok(), Tok()]
                for c in range(8):
                    b = c % 2
                    s.dma("sp", ut[b][:], self.UT[c * 128:(c + 1) * 128, 0:L], r=[self.t_uab], w=[t_ut[b]])
                    for t0 in range(0, T, 8):
                        n = min(8, T - t0)
                        pt = self.pick("pt", [6, 7])
                        ptv = self.bank16(pt).rearrange("p (j c) -> p j c", j=8)
                        for i in range(n):
                            s.op("pe", lambda e, i=i: e.transpose(out=ptv[:, i, :], in_=ut[b][:, (t0 + i) * 128:(t0 + i + 1) * 128],
                                                                  identity=self.ident16[:]), r=[t_ut[b], self.t_const], w=[self.ps_tok[pt]])
                        s.op("act", lambda e: e.activation(out=u_tok[:, t0:t0 + n, c * 128:(c + 1) * 128], in_=ptv[:, 0:n, :], func=AF.Copy),
                             r=[self.ps_tok[pt]], w=[t_u])
                s.barrier()
            hh = [self.sb(es, "hh%d" % i, [128, 2, 512], F32) for i in range(2)]
            t_hh = [Tok(), Tok()]
            cs = [self.sb(es, "cs%d" % i, [128, 2, 512], F32) for i in range(2)]
            t_cs = [Tok(), Tok()]
            tt = [self.sb(es, "tt%d" % i, [128, 4, 512], F32) for i in range(2)]
            t_tt = [Tok(), Tok()]
            zz = [self.sb(es, "zz%d" % i, [128, 2, 512], BF16) for i in range(2)]
            t_zz = [Tok(), Tok()]

            def cb(kc, half, bC, bS):
                b = (kc * 2 + half) % 2
                hs = slice(half * 512, (half + 1) * 512)
                s.dma("sp", hh[b][:, 0, :], HR[kc, :, hs], r=[self.t_H], w=[t_hh[b]])
                s.dma("sp", hh[b][:, 1, :], HI[kc, :, hs], r=[self.t_H], w=[t_hh[b]])
                s.op("act", lambda e: e.activation(out=cs[b][:, 0, :], in_=self.bank(bC), func=AF.Copy), r=[self.ps_tok[bC]], w=[t_cs[b]])
                s.op("act", lambda e: e.activation(out=cs[b][:, 1, :], in_=self.bank(bS), func=AF.Copy), r=[self.ps_tok[bS]], w=[t_cs[b]])
                TT = tt[b]
                s.op("dve", lambda e: e.tensor_tensor(out=TT[:, 0, :], in0=hh[b][:, 0, :], in1=cs[b][:, 0, :], op=ALU.mult), r=[t_hh[b], t_cs[b]], w=[t_tt[b]])
                s.op("pool", lambda e: e.tensor_tensor(out=TT[:, 1, :], in0=hh[b][:, 1, :], in1=cs[b][:, 1, :], op=ALU.mult), r=[t_hh[b], t_cs[b]], w=[t_tt[b]])
                s.op("pool", lambda e: e.tensor_tensor(out=TT[:, 2, :], in0=hh[b][:, 0, :], in1=cs[b][:, 1, :], op=ALU.mult), r=[t_hh[b], t_cs[b]], w=[t_tt[b]])
                s.op("dve", lambda e: e.tensor_tensor(out=TT[:, 3, :], in0=hh[b][:, 1, :], in1=cs[b][:, 0, :], op=ALU.mult), r=[t_hh[b], t_cs[b]], w=[t_tt[b]])
                s.op("dve", lambda e: e.tensor_tensor(out=zz[b][:, 0, :], in0=TT[:, 0, :], in1=TT[:, 1, :], op=ALU.add), r=[t_tt[b]], w=[t_zz[b]])
                s.op("pool", lambda e: e.tensor_tensor(out=zz[b][:, 1, :], in0=TT[:, 2, :], in1=TT[:, 3, :], op=ALU.subtract), r=[t_tt[b]], w=[t_zz[b]])
                s.dma("sp", ZR[kc, :, hs], zz[b][:, 0, :], r=[t_zz[b]], w=[self.t_Z])
                s.dma("sp", ZS[kc, :, hs], zz[b][:, 1, :], r=[t_zz[b]], w=[self.t_Z])

            self.dft_fwd(L, u_tok, t_u, 1024, cb)
        with ExitStack() as es:
            zr = self.sb(es, "zr", [128, T, 512], BF16)
            zs = self.sb(es, "zs", [128, T, 512], BF16)
            t_z = Tok()
            tc = [self.sb(es, "itc%d" % i, [128, T, 128], BF16) for i in range(2)]
            ts = [self.sb(es, "its%d" % i, [128, T, 128], BF16) for i in range(2)]
            t_tab = [Tok(), Tok()]
            y16 = [self.sb(es, "y16%d" % i, [128, 512], BF16) for i in range(2)]
            t_y16 = [Tok(), Tok()]
            ystage = [self.sb(es, "hystage%d" % i, [128, 4, 512], BF16) for i in range(2)]
            t_ys = [Tok(), Tok()]
            ab = [self.sb(es, "hab%d" % i, [128, 2, 4, 512], BF16) for i in range(2)]
            t_ab = [Tok(), Tok()]
            for ch in range(2):
                cs_ = slice(ch * 512, (ch + 1) * 512)
                s.dma("sp", zr[:], ZR[0:T, :, cs_].rearrange("k p c -> p k c"), r=[self.t_Z], w=[t_z])
                s.dma("sp", zs[:], ZS[0:T, :, cs_].rearrange("k p c -> p k c"), r=[self.t_Z], w=[t_z])
                for nci in range(T):
                    b = nci % 2
                    blk, tl = nci // 4, nci % 4
                    yb = blk % 2
                    if tl == 0:
                        rows = slice(ch * 512, (ch + 1) * 512)
                        s.dma("sp", ab[yb][:, 0, :, :], self.AT[rows, blk * 512:(blk + 1) * 512].rearrange("(c p) n -> p c n", p=128),
                              r=[self.t_uab], w=[t_ab[yb]])
                        s.dma("sp", ab[yb][:, 1, :, :], self.BT[rows, blk * 512:(blk + 1) * 512].rearrange("(c p) n -> p c n", p=128),
                              r=[self.t_uab], w=[t_ab[yb]])
                    if nci == 0:
                        s.dma("sp", tc[0][:], gc_d[0], w=[t_tab[0]])
                        s.dma("sp", ts[0][:], gs_d[0], w=[t_tab[0]])
                    if nci + 1 < T:
                        s.dma("sp", tc[1 - b][:], gc_d[nci + 1], w=[t_tab[1 - b]])
                        s.dma("sp", ts[1 - b][:], gs_d[nci + 1], w=[t_tab[1 - b]])
                    pb = self.pick("inv", [0, 1, 2, 3])
                    for kc in range(T):
                        s.op("pe", lambda e, kc=kc: e.matmul(self.bank(pb), tc[b][:, kc, :], zr[:, kc, :], start=(kc == 0), stop=False),
                             r=[t_tab[b], t_z], w=[self.ps_tok[pb]])
                    for kc in range(T):
                        s.op("pe", lambda e, kc=kc: e.matmul(self.bank(pb), ts[b][:, kc, :], zs[:, kc, :], start=False, stop=(kc == T - 1)),
                             r=[t_tab[b], t_z], w=[self.ps_tok[pb]])
                    s.op("act", lambda e: e.activation(out=y16[b][:], in_=self.bank(pb), func=AF.Copy), r=[self.ps_tok[pb]], w=[t_y16[b]])
                    self.to_ystage(y16[b], t_y16[b], 4, ystage[yb], t_ys[yb], tl, banks=[6, 7])
                    if tl == 3:
                        s.op("dve", lambda e: e.tensor_tensor(out=ystage[yb][:], in0=ystage[yb][:], in1=ab[yb][:, 0, :, :], op=ALU.mult),
                             r=[t_ys[yb], t_ab[yb]], w=[t_ys[yb]])
                        s.op("dve", lambda e: e.tensor_tensor(out=ystage[yb][:], in0=ystage[yb][:], in1=ab[yb][:, 1, :, :], op=ALU.add),
                             r=[t_ys[yb], t_ab[yb]], w=[t_ys[yb]])
                        self.store_ystage(ystage[yb], t_ys[yb], 4, ch * 4, blk * 512, 512)
            s.barrier()

    def phase_gdn1(self, si, L, hT, hT_tok):
        nc, s = self.nc, self.s
        T = L // 128
        nblk = L // 512
        w_in = self.w["e_w_in"]
        with ExitStack() as es:
            cw = self.sb(es, "gcw", [128, 24, 5], F32)
            t_c = Tok()
            s.dma("sp", cw[:], self.w["gdn_conv_w"], w=[t_c])
            ones_r = self.sb(es, "ones_r", [128, 128], F32R)
            s.op("dve", lambda e: e.tensor_scalar(out=ones_r[:], in0=self.ident32[:], scalar1=0.0, scalar2=1.0, op0=ALU.mult, op1=ALU.add),
                 r=[self.t_const], w=[t_c])
            wts = [self.sb(es, "gw%d" % i, [128, 8, 3, 128], BF16) for i in range(2)]
            t_wts = [Tok(), Tok()]
            diag = [self.sb(es, "gdiag%d" % i, [128, 15, 128], BF16) for i in range(2)]
            t_diag = [Tok(), Tok()]
            zT = self.sb(es, "gzT", [128, 3, L + 4], BF16)
            t_zT = Tok()
            s.op("dve", lambda e: e.memset(zT[:, :, 0:2], 0.0), w=[t_zT])
            s.op("dve", lambda e: e.memset(zT[:, :, L + 2:L + 4], 0.0), w=[t_zT])
            t_zTb = [Tok() for _ in range(nblk)]
            xs = [self.sb(es, "gx%d" % i, [128, 3, 512], F32) for i in range(2)]
            t_xs = [Tok(), Tok()]
            sq = [self.sb(es, "gsq%d" % i, [128, 512], F32R) for i in range(2)]
            t_sq = [Tok(), Tok()]
            rs = [self.sb(es, "grs%d" % i, [128, 512], F32) for i in range(2)]
            t_rs = [Tok(), Tok()]
            kst = [self.sb(es, "gkst%d" % i, [128, 4, 128], F32) for i in range(2)]
            t_kst = [Tok(), Tok()]
            for h in range(8):
                wt, t_w = wts[h % 2], t_wts[h % 2]
                dg, t_dg = diag[h % 2], t_diag[h % 2]
                for a in range(3):
                    c0 = E_QKV + a * 1024 + h * 128
                    s.dma("pool", wt[:, :, a, :], w_in[:, c0:c0 + 128].rearrange("(k p) c -> p k c", p=128), w=[t_w])
                    for j in range(5):
                        s.op("dve", lambda e, a=a, j=j: e.tensor_scalar(out=dg[:, a * 5 + j, :], in0=self.ident16[:],
                                                                        scalar1=cw[:, a * 8 + h, j:j + 1], scalar2=None, op0=ALU.mult),
                             r=[t_c, self.t_const], w=[t_dg])
                def g1a_lane(ln):
                    pb = 2 + ln
                    for blk in range(ln, nblk, 2):
                        for a in range(3):
                            self.proj_feat(self.bank(pb), pb, wt[:, :, a, :], t_w, 0, hT, hT_tok, blk * 512, 512)
                            yield
                            s.op("act", lambda e, a=a: e.activation(out=zT[:, a, 2 + blk * 512:2 + (blk + 1) * 512], in_=self.bank(pb), func=AF.Copy),
                                 r=[self.ps_tok[pb]], w=[t_zTb[blk]])
                            yield

                self.lockstep([g1a_lane(0), g1a_lane(1)])

                def g1b_lane(ln):
                    b = ln
                    X = xs[b]
                    pc = 4 + ln
                    pss = ln
                    pt = 6 + ln
                    for blk in range(ln, nblk, 2):
                        zdeps = [t_zTb[bk] for bk in (blk - 1, blk, blk + 1) if 0 <= bk < nblk] + [t_zT]
                        for a in range(3):
                            for j in range(5):
                                s.op("pe", lambda e, a=a, j=j: e.matmul(self.bank(pc), dg[:, a * 5 + j, :], zT[:, a, blk * 512 + j:blk * 512 + j + 512],
                                                                        start=(j == 0), stop=(j == 4)), r=[t_dg] + zdeps, w=[self.ps_tok[pc]])
                            yield
                            s.op("act", lambda e, a=a: e.activation(out=X[:, a, :], in_=self.bank(pc), func=AF.Silu), r=[self.ps_tok[pc]], w=[t_xs[b]])
                            yield
                        for a in range(2):
                            s.op("dve", lambda e, a=a: e.tensor_tensor(out=sq[b][:], in0=X[:, a, :], in1=X[:, a, :], op=ALU.mult), r=[t_xs[b]], w=[t_sq[b]])
                            yield
                            s.op("pe", lambda e: e.matmul(self.bank(pss), ones_r[:], sq[b][:], start=True, stop=True), r=[t_c, t_sq[b]], w=[self.ps_tok[pss]])
                            yield
                            s.op("act", lambda e: e.activation(out=rs[b][:], in_=self.bank(pss), func=AF.Ln, bias=self.eps_col[:, 0:1]), r=[self.ps_tok[pss], self.t_const], w=[t_rs[b]])
                            yield
                            s.op("act", lambda e: e.activation(out=rs[b][:], in_=rs[b][:], func=AF.Exp, scale=-0.5), r=[t_rs[b]], w=[t_rs[b]])
                            yield
                            sc = (128.0 ** -0.5) if a == 0 else 1.0
                            s.op("dve", lambda e, a=a: e.scalar_tensor_tensor(out=X[:, a, :], in0=X[:, a, :], scalar=sc, in1=rs[b][:],
                                                                              op0=ALU.mult, op1=ALU.mult), r=[t_xs[b], t_rs[b]], w=[t_xs[b]])
                            dr = self.QT if a == 0 else self.KT
                            s.dma("sp", dr[h * 128:(h + 1) * 128, blk * 512:(blk + 1) * 512], X[:, a, :], r=[t_xs[b]], w=[self.t_gd])
                            yield
                        for a in (1, 2):
                            ptv = self.bank(pt).rearrange("p (j c) -> p j c", j=4)
                            for i in range(4):
                                s.op("pe", lambda e, a=a, i=i: e.transpose(out=ptv[:, i, :], in_=X[:, a, i * 128:(i + 1) * 128], identity=self.ident32[:]),
                                     r=[t_xs[b], self.t_const], w=[self.ps_tok[pt]])
                            yield
                            s.op("act", lambda e: e.activation(out=kst[b][:], in_=ptv, func=AF.Copy), r=[self.ps_tok[pt]], w=[t_kst[b]])
                            dr = self.KTOK if a == 1 else self.VTOK
                            s.dma("sp", dr[blk * 512:(blk + 1) * 512, h * 128:(h + 1) * 128].rearrange("(i p) d -> p i d", p=128), kst[b][:],
                                  r=[t_kst[b]], w=[self.t_gd])
                            yield

                self.lockstep([g1b_lane(0), g1b_lane(1)])
            wg = self.sb(es, "gwg", [128, 8, 1024], BF16)
            wbg = self.sb(es, "gwbg", [128, 8, 32], BF16)
            t_wg = Tok()
            self.load_w(wg, t_wg, w_in, E_GG, 1024)
            self.load_w(wbg, t_wg, w_in, E_BETA, 32)
            rows = self.sb(es, "grows", [128, 2, 16], F32)
            s.dma("sp", rows[:, 0, :], self.w["gdn_A_log"].partition_broadcast(128), w=[t_c])
            s.dma("sp", rows[:, 1, :], self.w["gdn_dt_bias"].partition_broadcast(128), w=[t_c])
            s.op("act", lambda e: e.activation(out=rows[:, 0, :], in_=rows[:, 0, :], func=AF.Exp), r=[t_c], w=[t_c])
            s.op("dve", lambda e: e.tensor_scalar(out=rows[:, 0, :], in0=rows[:, 0, :], scalar1=-1.0, scalar2=None, op0=ALU.mult), r=[t_c], w=[t_c])
            sg = [self.sb(es, "gsg%d" % i, [128, 1024], BF16) for i in range(2)]
            t_sg = [Tok(), Tok()]
            bgt = [self.sb(es, "gbgt%d" % i, [128, 4, 16], F32) for i in range(2)]
            t_bgt = [Tok(), Tok()]
            bgo = [self.sb(es, "gbgo%d" % i, [128, 32], F32) for i in range(2)]
            t_bgo = [Tok(), Tok()]
            for t in range(T):
                b = t % 2
                for half in range(2):
                    pb = 2 + half
                    self.proj_tok(self.bank(pb), pb, wg, t_wg, half * 512, 512, hT, [hT_tok[t]], lambda k: hT[:, k, t * 128:(t + 1) * 128])
                    s.op("act", lambda e: e.activation(out=sg[b][:, half * 512:(half + 1) * 512], in_=self.bank(pb), func=AF.Silu),
                         r=[self.ps_tok[pb]], w=[t_sg[b]])
                s.dma("sp", self.SGG[t * 128:(t + 1) * 128, :], sg[b][:], r=[t_sg[b]], w=[self.t_gd])
                pb = self.pick("conv", [4, 5])
                self.proj_tok(self.bank(pb)[:, 0:32], pb, wbg, t_wg, 0, 32, hT, [hT_tok[t]], lambda k: hT[:, k, t * 128:(t + 1) * 128])
                B = bgt[b]
                s.op("act", lambda e: e.activation(out=bgo[b][:, 0:16], in_=self.bank(pb)[:, 0:16], func=AF.Sigmoid), r=[self.ps_tok[pb]], w=[t_bgo[b]])
                s.op("dve", lambda e: e.tensor_tensor(out=B[:, 0, :], in0=self.bank(pb)[:, 16:32], in1=rows[:, 1, :], op=ALU.add), r=[self.ps_tok[pb], t_c], w=[t_bgt[b]])
                s.op("act", lambda e: e.activation(out=B[:, 1, :], in_=B[:, 0, :], func=AF.Abs), r=[t_bgt[b]], w=[t_bgt[b]])
                s.op("act", lambda e: e.activation(out=B[:, 1, :], in_=B[:, 1, :], func=AF.Exp, scale=-1.0), r=[t_bgt[b]], w=[t_bgt[b]])
                s.op("act", lambda e: e.activation(out=B[:, 1, :], in_=B[:, 1, :], func=AF.Ln, bias=self.eps_col[:, 1:2]), r=[t_bgt[b], self.t_const], w=[t_bgt[b]])
                s.op("dve", lambda e: e.tensor_scalar(out=B[:, 2, :], in0=B[:, 0, :], scalar1=0.0, scalar2=None, op0=ALU.max), r=[t_bgt[b]], w=[t_bgt[b]])
                s.op("dve", lambda e: e.tensor_tensor(out=B[:, 2, :], in0=B[:, 2, :], in1=B[:, 1, :], op=ALU.add), r=[t_bgt[b]], w=[t_bgt[b]])
                s.op("dve", lambda e: e.tensor_tensor(out=bgo[b][:, 16:32], in0=B[:, 2, :], in1=rows[:, 0, :], op=ALU.mult), r=[t_bgt[b], t_c], w=[t_bgo[b]])
                s.dma("sp", self.BG[t * 128:(t + 1) * 128, :], bgo[b][:], r=[t_bgo[b]], w=[self.t_gd])
            s.barrier()

    @staticmethod
    def lockstep(gens):
        gens = list(gens)
        while gens:
            for g in list(gens):
                try:
                    next(g)
                except StopIteration:
                    gens.remove(g)

    def phase_gdn2(self, si, L):
        self.phase_gdnP(L)
        self.phase_gdnR(L)
        self.phase_gdnC(L)

    def phase_gdnP(self, L):
        nc, s = self.nc, self.s
        T = L // 128
        QT3 = self.QT.rearrange("(h d) n -> d h n", d=128)
        KT3 = self.KT.rearrange("(h d) n -> d h n", d=128)
        with ExitStack() as es:
            cm = self.sb(es, "gmask", [128, 18, 128], F32)
            t_c = Tok()
            s.dma("sp", cm[:], self.c_gmask, w=[t_c])
            ones32 = self.sb(es, "ones32", [128, 128], F32)
            s.op("dve", lambda e: e.memset(ones32[:], 1.0), w=[t_c])
            ident_r = self.sb(es, "ident_r", [128, 128], F32R)
            s.op("dve", lambda e: e.tensor_copy(out=ident_r[:], in_=self.ident32[:]), r=[self.t_const], w=[t_c])
            idb = self.ident32[:].unsqueeze(1).to_broadcast([128, 4, 128])
            units = [(dirn, t, qd) for dirn in range(2) for t in range(T) for qd in range(2)]
            NL = 4

            def lane(li):
                def A(nm, dt=F32, shp=(128, 4, 128)):
                    return self.sb(es, "L%d%s" % (li, nm), list(shp), dt), Tok()
                lq, t_lq = A("lq")
                lk, t_lk = A("lk")
                qr, t_qr = A("qr", F32R)
                kr, t_kr = A("kr", F32R)
                lkt, t_lkt = A("lkt")
                lvt, t_lvt = A("lvt")
                ET, t_ET = A("ET")
                Bt, t_Bt = A("Bt")
                Ct, t_Ct = A("Ct")
                Bm, t_Bm = A("Bm", F32R)
                Cm, t_Cm = A("Cm", F32R)
                P, t_P = A("P", F32R)
                Q, t_Q = A("Q", F32R)
                Wn, t_Wn = A("Wn", F32R)
                Vn, t_Vn = A("Vn", F32R)
                QKm, t_QKm = A("QKm")
                ub, t_ub = A("ub")
                wT, t_wT = A("wT")
                kd, t_kd = A("kd")
                bg, t_bg = A("bg", F32, (128, 32))
                sc, t_sc = A("sc", F32, (128, 4, 4))
                scc, t_scc = A("scc", F32, (64, 2, 4))
                glb, t_glb = A("glb", F32, (128, 4, 2))
                Dm, t_Dm = lq, t_lq
                dgG, t_dgG = lk, t_lk
                vr, t_vr = qr, t_qr
                kg, t_kg = kr, t_kr
                bA, bB = 2 * li, 2 * li + 1
                pA, pB = self.ps_tok[bA], self.ps_tok[bB]
                v4 = lambda bk: self.bank(bk).rearrange("p (h c) -> p h c", h=4)
                for ui in range(li, len(units), NL):
                    dirn, t, qd = units[ui]
                    mo = 8 * dirn
                    mo2 = 8 * (1 - dirn)
                    hs = slice(qd * 4, qd * 4 + 4)
                    cs_ = slice(t * 128, (t + 1) * 128)
                    fs = slice(qd * 512, (qd + 1) * 512)
                    s.dma("sp", lq[:], QT3[:, hs, cs_], r=[self.t_gd], w=[t_lq])
                    s.dma("sp", lk[:], KT3[:, hs, cs_], r=[self.t_gd], w=[t_lk])
                    s.dma("sp", lkt[:], self.KTOK[cs_, fs].rearrange("p (h d) -> p h d", h=4), r=[self.t_gd], w=[t_lkt])
                    s.dma("sp", lvt[:], self.VTOK[cs_, fs].rearrange("p (h d) -> p h d", h=4), r=[self.t_gd], w=[t_lvt])
                    s.dma("sp", bg[:], self.BG[cs_, :], r=[self.t_gd], w=[t_bg])
                    yield
                    s.op("act", lambda e: e.activation(out=qr[:], in_=lq[:], func=AF.Copy), r=[t_lq], w=[t_qr])
                    s.op("act", lambda e: e.activation(out=kr[:], in_=lk[:], func=AF.Copy), r=[t_lk], w=[t_kr])
                    beta = bg[:, dirn * 8 + qd * 4:dirn * 8 + qd * 4 + 4]
                    gcol = bg[:, 16 + dirn * 8 + qd * 4:16 + dirn * 8 + qd * 4 + 4]
                    s.op("pe", lambda e: e.matmul(self.bank(bA)[:, 0:4], cm[:, 16 + dirn, :], gcol, start=True, stop=True),
                         r=[t_bg, t_c], w=[pA])
                    for cc in range(2):
                        s.op("pe", lambda e, cc=cc: e.matmul(self.bank(bA)[0:64, 16 + cc * 4:20 + cc * 4], cm[:, 16 + dirn, cc * 64:(cc + 1) * 64], gcol,
                                                             start=True, stop=True), r=[t_bg, t_c], w=[pA])
                    for hh in range(4):
                        s.op("pe", lambda e, hh=hh: e.matmul(self.bank(bB)[:, hh * 128:(hh + 1) * 128], kr[:, hh, :], kr[:, hh, :], start=True, stop=True),
                             r=[t_kr], w=[pB])
                    yield
                    s.op("dve", lambda e: e.tensor_copy(out=sc[:, :, 0], in_=self.bank(bA)[:, 0:4]), r=[pA], w=[t_sc])
                    s.op("act", lambda e: e.activation(out=sc[:, :, 1], in_=self.bank(bA)[:, 0:4], func=AF.Exp), r=[pA], w=[t_sc])
                    s.op("act", lambda e: e.activation(out=scc[:], in_=self.bank(bA)[0:64, 16:24].rearrange("p (c h) -> p c h", c=2), func=AF.Exp),
                         r=[pA], w=[t_scc])
                    yield
                    for hh in range(4):
                        s.op("dve", lambda e, hh=hh: e.tensor_scalar(out=dgG[:, hh, :], in0=self.ident32[:], scalar1=sc[:, hh, 0:1], scalar2=None, op0=ALU.mult),
                             r=[t_sc, self.t_const], w=[t_dgG])
                    s.op("pe", lambda e: e.matmul(self.bank(bA), ones32[:], dgG[:].rearrange("p h c -> p (h c)"), start=True, stop=True),
                         r=[t_c, t_dgG], w=[pA])
                    yield
                    gbc = v4(bA)
                    lastc = (63, 127) if dirn == 0 else (0, 64)
                    s.op("dve", lambda e: e.tensor_tensor(out=Dm[:], in0=gbc, in1=sc[:, :, 0:1].to_broadcast([128, 4, 128]), op=ALU.subtract),
                         r=[pA, t_sc], w=[t_Dm])
                    s.op("act", lambda e: e.activation(out=glb[:], in_=gbc[:, :, lastc[0]:lastc[1] + 1:64], func=AF.Exp), r=[pA], w=[t_glb])
                    for cc in range(2):
                        rr = slice(cc * 64, cc * 64 + 64)
                        s.op("dve", lambda e, cc=cc, rr=rr: e.tensor_tensor(out=sc[rr, :, 2], in0=gbc[rr, :, lastc[cc]], in1=sc[rr, :, 0], op=ALU.subtract),
                             r=[pA, t_sc], w=[t_sc])
                    yield
                    s.op("dve", lambda e: e.scalar_tensor_tensor(out=Dm[:], in0=Dm[:], scalar=0.0, in1=cm[:, mo + 0, :].unsqueeze(1).to_broadcast([128, 4, 128]),
                                                                 op0=ALU.min, op1=ALU.add), r=[t_Dm, t_c], w=[t_Dm])
                    s.op("act", lambda e: e.activation(out=sc[:, :, 2], in_=sc[:, :, 2], func=AF.Exp), r=[t_sc], w=[t_sc])
                    yield
                    s.op("act", lambda e: e.activation(out=ET[:], in_=Dm[:], func=AF.Exp), r=[t_Dm], w=[t_ET])
                    yield
                    s.op("dve", lambda e: e.tensor_tensor(out=Bt[:], in0=v4(bB), in1=ET[:], op=ALU.mult), r=[pB, t_ET], w=[t_Bt])
                    yield
                    for hh in range(4):
                        s.op("pe", lambda e, hh=hh: e.matmul(self.bank(bB)[:, hh * 128:(hh + 1) * 128], kr[:, hh, :], qr[:, hh, :], start=True, stop=True),
                             r=[t_kr, t_qr], w=[pB])
                    s.op("dve", lambda e: e.tensor_tensor(out=Bt[:], in0=Bt[:], in1=cm[:, mo + 1, :].unsqueeze(1).to_broadcast([128, 4, 128]), op=ALU.mult),
                         r=[t_Bt, t_c], w=[t_Bt])
                    yield
                    s.op("dve", lambda e: e.tensor_tensor(out=Bt[:], in0=Bt[:], in1=beta.unsqueeze(2).to_broadcast([128, 4, 128]), op=ALU.mult),
                         r=[t_Bt, t_bg], w=[t_Bt])
                    s.op("dve", lambda e: e.tensor_tensor(out=QKm[:], in0=v4(bB), in1=ET[:], op=ALU.mult), r=[pB, t_ET], w=[t_QKm])
                    s.dma("sp", self.G_QKM[dirn, t, :, hs, :], QKm[:], r=[t_QKm], w=[self.t_gp])
                    yield
                    for hh in range(4):
                        s.op("pe", lambda e, hh=hh: e.transpose(out=self.bank(bA)[:, hh * 128:(hh + 1) * 128], in_=Bt[:, hh, :], identity=self.ident32[:]),
                             r=[t_Bt, self.t_const], w=[pA])
                    s.op("pool", lambda e: e.tensor_tensor(out=kd[:], in0=lkt[:], in1=sc[:, :, 2:3].to_broadcast([128, 4, 128]), op=ALU.mult),
                         r=[t_lkt, t_sc], w=[t_kd])
                    s.dma("sp", self.G_KD[dirn, t, :, fs].rearrange("p (h d) -> p h d", h=4), kd[:], r=[t_kd], w=[self.t_gp])
                    s.dma("sp", self.G_SCC[dirn, t, :, :, hs], scc[:], r=[t_scc], w=[self.t_gp])
                    s.dma("sp", self.G_GLB[dirn, t, :, hs, :], glb[:], r=[t_glb], w=[self.t_gp])
                    yield
                    s.op("act", lambda e: e.activation(out=Ct[:], in_=v4(bA), func=AF.Copy), r=[pA], w=[t_Ct])
                    s.op("dve", lambda e: e.tensor_tensor(out=Bm[:], in0=Bt[:], in1=cm[:, mo + 2, :].unsqueeze(1).to_broadcast([128, 4, 128]), op=ALU.mult),
                         r=[t_Bt, t_c], w=[t_Bm])
                    yield
                    s.op("dve", lambda e: e.tensor_tensor(out=Cm[:], in0=Ct[:], in1=cm[:, mo2 + 2, :].unsqueeze(1).to_broadcast([128, 4, 128]), op=ALU.mult),
                         r=[t_Ct, t_c], w=[t_Cm])
                    s.op("dve", lambda e: e.scalar_tensor_tensor(out=P[:], in0=Bm[:], scalar=-1.0, in1=idb, op0=ALU.mult, op1=ALU.add),
                         r=[t_Bm, self.t_const], w=[t_P])
                    yield
                    s.op("dve", lambda e: e.scalar_tensor_tensor(out=Q[:], in0=Cm[:], scalar=-1.0, in1=idb, op0=ALU.mult, op1=ALU.add),
                         r=[t_Cm, self.t_const], w=[t_Q])
                    yield
                    for lv in range(1, 6):
                        last = (lv == 5)
                        s.op("dve", lambda e, lv=lv: e.tensor_tensor(out=Bm[:], in0=Bt[:], in1=cm[:, mo + 2 + lv, :].unsqueeze(1).to_broadcast([128, 4, 128]),
                                                                     op=ALU.mult), r=[t_Bt, t_c], w=[t_Bm])
                        if not last:
                            s.op("pool", lambda e, lv=lv: e.tensor_tensor(out=Cm[:], in0=Ct[:], in1=cm[:, mo2 + 2 + lv, :].unsqueeze(1).to_broadcast([128, 4, 128]),
                                                                          op=ALU.mult), r=[t_Ct, t_c], w=[t_Cm])
                        yield
                        for hh in range(4):
                            s.op("pe", lambda e, hh=hh: e.matmul(self.bank(bA)[:, hh * 128:(hh + 1) * 128], Bm[:, hh, :], Q[:, hh, :], start=True, stop=True),
                                 r=[t_Bm, t_Q], w=[pA])
                        if not last:
                            for hh in range(4):
                                s.op("pe", lambda e, hh=hh: e.matmul(self.bank(bB)[:, hh * 128:(hh + 1) * 128], Cm[:, hh, :], P[:, hh, :], start=True, stop=True),
                                     r=[t_Cm, t_P], w=[pB])
                        yield
                        s.op("act", lambda e: e.activation(out=Wn[:], in_=v4(bA), func=AF.Copy, scale=-1.0), r=[pA], w=[t_Wn])
                        if not last:
                            s.op("act", lambda e: e.activation(out=Vn[:], in_=v4(bB), func=AF.Copy, scale=-1.0), r=[pB], w=[t_Vn])
                        yield
                        for hh in range(4):
                            s.op("pe", lambda e, hh=hh: e.matmul(self.bank(bA)[:, hh * 128:(hh + 1) * 128], Wn[:, hh, :], P[:, hh, :], start=True, stop=True),
                                 r=[t_Wn, t_P], w=[pA])
                        if not last:
                            for hh in range(4):
                                s.op("pe", lambda e, hh=hh: e.matmul(self.bank(bB)[:, hh * 128:(hh + 1) * 128], Vn[:, hh, :], Q[:, hh, :], start=True, stop=True),
                                     r=[t_Vn, t_Q], w=[pB])
                        yield
                        s.op("dve", lambda e: e.tensor_tensor(out=P[:], in0=P[:].bitcast(F32), in1=v4(bA), op=ALU.add), r=[pA, t_P], w=[t_P])
                        if not last:
                            s.op("dve", lambda e: e.tensor_tensor(out=Q[:], in0=Q[:].bitcast(F32), in1=v4(bB), op=ALU.add), r=[pB, t_Q], w=[t_Q])
                        yield
                    s.op("dve", lambda e: e.tensor_tensor(out=kg[:], in0=lkt[:], in1=sc[:, :, 1:2].to_broadcast([128, 4, 128]), op=ALU.mult),
                         r=[t_lkt, t_sc], w=[t_kg])
                    s.op("act", lambda e: e.activation(out=vr[:], in_=lvt[:], func=AF.Copy), r=[t_lvt], w=[t_vr])
                    yield
                    for hh in range(4):
                        s.op("pe", lambda e, hh=hh: e.matmul(self.bank(bA)[:, hh * 128:(hh + 1) * 128], P[:, hh, :], vr[:, hh, :], start=True, stop=True),
                             r=[t_P, t_vr], w=[pA])
                        s.op("pe", lambda e, hh=hh: e.matmul(self.bank(bB)[:, hh * 128:(hh + 1) * 128], kg[:, hh, :], P[:, hh, :], start=True, stop=True),
                             r=[t_P, t_kg], w=[pB])
                    yield
                    s.op("dve", lambda e: e.tensor_tensor(out=ub[:], in0=v4(bA), in1=beta.unsqueeze(2).to_broadcast([128, 4, 128]), op=ALU.mult),
                         r=[pA, t_bg], w=[t_ub])
                    s.op("act", lambda e: e.activation(out=wT[:], in_=v4(bB), func=AF.Copy), r=[pB], w=[t_wT])
                    s.dma("sp", self.G_UB[dirn, t, :, fs].rearrange("p (h d) -> p h d", h=4), ub[:], r=[t_ub], w=[self.t_gp])
                    s.dma("sp", self.G_WT[dirn, t, :, hs, :], wT[:], r=[t_wT], w=[self.t_gp])
                    yield

            self.lockstep([lane(i) for i in range(NL)])
            s.barrier()

    def phase_gdnR(self, L):
        nc, s = self.nc, self.s
        T = L // 128
        QT3 = self.QT.rearrange("(h d) n -> d h n", d=128)
        with ExitStack() as es:
            def lane(li):
                dirn, qd = li // 2, li % 2
                hs = slice(qd * 4, qd * 4 + 4)
                fs = slice(qd * 512, (qd + 1) * 512)

                def A(nm, dt=F32, shp=(128, 4, 128)):
                    return self.sb(es, "R%d%s" % (li, nm), list(shp), dt), Tok()
                ldb = []
                for i in range(2):
                    ldb.append(dict(wT=A("lwT%d" % i), qk=A("lqk%d" % i), kd=A("lkd%d" % i), ub=A("lub%d" % i), q=A("lq%d" % i),
                                    bg=A("lbg%d" % i, F32, (128, 32)), scc=A("lscc%d" % i, F32, (64, 2, 4)), glb=A("lglb%d" % i, F32, (128, 4, 2))))
                wTr, t_wTr = A("wTr", F32R)
                qkr, t_qkr = A("qkr", F32R)
                kdr, t_kdr = A("kdr", F32R)
                qr, t_qr = A("qr", F32R)
                nb, t_nb = A("nb", F32, (128, 4))
                S, t_S = A("S", F32R)
                vn, t_vn = A("vn", F32R)
                o1s, t_o1s = A("o1s", F32, (64, 4, 128))
                o2s, t_o2s = A("o2s", F32, (64, 4, 128))
                OTs = [A("OT%d" % i, F32, (64, 2, 512)) for i in range(2)]
                bA, bB = 2 * li, 2 * li + 1
                pA, pB = self.ps_tok[bA], self.ps_tok[bB]
                v4 = lambda bk: self.bank(bk).rearrange("p (h c) -> p h c", h=4)
                s.op("dve", lambda e: e.tensor_scalar(out=S[:], in0=self.ident32[:].unsqueeze(1).to_broadcast([128, 4, 128]), scalar1=0.0, scalar2=None,
                                                      op0=ALU.mult), r=[self.t_const], w=[t_S])
                tiles = list(range(T)) if dirn == 0 else list(range(T - 1, -1, -1))

                def load(it):
                    t = tiles[it]
                    Ld = ldb[it % 2]
                    cs_ = slice(t * 128, (t + 1) * 128)
                    s.dma("sp", Ld["wT"][0][:], self.G_WT[dirn, t, :, hs, :], r=[self.t_gp], w=[Ld["wT"][1]])
                    s.dma("sp", Ld["qk"][0][:], self.G_QKM[dirn, t, :, hs, :], r=[self.t_gp], w=[Ld["qk"][1]])
                    s.dma("sp", Ld["kd"][0][:], self.G_KD[dirn, t, :, fs].rearrange("p (h d) -> p h d", h=4), r=[self.t_gp], w=[Ld["kd"][1]])
                    s.dma("sp", Ld["ub"][0][:], self.G_UB[dirn, t, :, fs].rearrange("p (h d) -> p h d", h=4), r=[self.t_gp], w=[Ld["ub"][1]])
                    s.dma("sp", Ld["q"][0][:], QT3[:, hs, cs_], r=[self.t_gd], w=[Ld["q"][1]])
                    s.dma("sp", Ld["bg"][0][:], self.BG[cs_, :], r=[self.t_gd], w=[Ld["bg"][1]])
                    s.dma("sp", Ld["scc"][0][:], self.G_SCC[dirn, t, :, :, hs], r=[self.t_gp], w=[Ld["scc"][1]])
                    s.dma("sp", Ld["glb"][0][:], self.G_GLB[dirn, t, :, hs, :], r=[self.t_gp], w=[Ld["glb"][1]])

                load(0)
                for it, t in enumerate(tiles):
                    Ld = ldb[it % 2]
                    if it + 1 < T:
                        load(it + 1)
                    OT, t_OT = OTs[it % 2]
                    s.op("pool", lambda e: e.tensor_copy(out=wTr[:], in_=Ld["wT"][0][:]), r=[Ld["wT"][1]], w=[t_wTr])
                    s.op("act", lambda e: e.activation(out=qr[:], in_=Ld["q"][0][:], func=AF.Copy), r=[Ld["q"][1]], w=[t_qr])
                    s.op("pool", lambda e: e.tensor_copy(out=qkr[:], in_=Ld["qk"][0][:]), r=[Ld["qk"][1]], w=[t_qkr])
                    s.op("pool", lambda e: e.tensor_copy(out=kdr[:], in_=Ld["kd"][0][:]), r=[Ld["kd"][1]], w=[t_kdr])
                    s.op("dve", lambda e: e.tensor_scalar(out=nb[:], in0=Ld["bg"][0][:, dirn * 8 + qd * 4:dirn * 8 + qd * 4 + 4], scalar1=-1.0, scalar2=None,
                                                          op0=ALU.mult), r=[Ld["bg"][1]], w=[t_nb])
                    ubv, t_ubv = Ld["ub"]
                    sccv, t_sccv = Ld["scc"]
                    glbv, t_glbv = Ld["glb"]
                    yield
                    for cc in ((0, 1) if dirn == 0 else (1, 0)):
                        rr = slice(cc * 64, cc * 64 + 64)
                        M = (cc + 1) * 64
                        for hh in range(4):
                            s.op("pe", lambda e, hh=hh: e.matmul(self.bank(bA)[0:M, hh * 128:(hh + 1) * 128], wTr[:, hh, 0:M], S[:, hh, :], start=True, stop=True),
                                 r=[t_wTr, t_S], w=[pA])
                        for hh in range(4):
                            s.op("pe", lambda e, hh=hh: e.matmul(self.bank(bB)[0:64, hh * 128:(hh + 1) * 128], qr[:, hh, rr], S[:, hh, :], start=True, stop=True),
                                 r=[t_qr, t_S], w=[pB])
                        yield
                        for hh in range(4):
                            s.op("dve", lambda e, hh=hh: e.scalar_tensor_tensor(out=vn[rr, hh, :], in0=self.bank(bA)[rr, hh * 128:(hh + 1) * 128],
                                                                                scalar=nb[rr, hh:hh + 1], in1=ubv[rr, hh, :], op0=ALU.mult, op1=ALU.add),
                                 r=[pA, t_nb, t_ubv], w=[t_vn])
                        s.op("act", lambda e: e.activation(out=o1s[:], in_=self.bank(bB)[0:64, :].rearrange("p (h c) -> p h c", h=4), func=AF.Copy),
                             r=[pB], w=[t_o1s])
                        yield
                        for hh in range(4):
                            s.op("pe", lambda e, hh=hh: e.matmul(self.bank(bB)[:, hh * 128:(hh + 1) * 128], kdr[rr, hh, :], vn[rr, hh, :], start=True, stop=True),
                                 r=[t_kdr, t_vn], w=[pB])
                        for hh in range(4):
                            s.op("pe", lambda e, hh=hh: e.matmul(self.bank(bA)[0:64, hh * 128:(hh + 1) * 128], qkr[rr, hh, rr], vn[rr, hh, :], start=True, stop=True),
                                 r=[t_qkr, t_vn], w=[pA])
                        yield
                        for hh in range(4):
                            s.op("dve", lambda e, hh=hh: e.scalar_tensor_tensor(out=S[:, hh, :], in0=S[:, hh, :].bitcast(F32), scalar=glbv[:, hh, cc:cc + 1],
                                                                                in1=self.bank(bB)[:, hh * 128:(hh + 1) * 128], op0=ALU.mult, op1=ALU.add),
                                 r=[pB, t_glbv, t_S], w=[t_S])
                        s.op("act", lambda e: e.activation(out=o2s[:], in_=self.bank(bA)[0:64, :].rearrange("p (h c) -> p h c", h=4), func=AF.Copy),
                             r=[pA], w=[t_o2s])
                        yield
                        for hh in range(4):
                            s.op("dve", lambda e, hh=hh: e.scalar_tensor_tensor(out=OT[:, cc, hh * 128:(hh + 1) * 128], in0=o1s[:, hh, :],
                                                                                 scalar=sccv[:, cc, hh:hh + 1], in1=o2s[:, hh, :], op0=ALU.mult, op1=ALU.add),
                                 r=[t_o1s, t_sccv, t_o2s], w=[t_OT])
                    s.dma("sp", self.OFB[dirn, t * 128:(t + 1) * 128, fs].rearrange("(c p) f -> p c f", p=64), OT[:], r=[t_OT], w=[self.t_of])
                    yield

            self.lockstep([lane(i) for i in range(4)])
            s.barrier()

    def phase_gdnC(self, L):
        nc, s = self.nc, self.s
        T = L // 128
        with ExitStack() as es:
            gnorm = self.sb(es, "gnormg", [128, 128], F32)
            t_c = Tok()
            s.dma("sp", gnorm[:], self.w["gdn_norm_g"].partition_broadcast(128), w=[t_c])
            of = [self.sb(es, "cof%d" % i, [128, 1024], F32) for i in range(2)]
            ob = [self.sb(es, "cob%d" % i, [128, 1024], F32) for i in range(2)]
            sg = [self.sb(es, "csg%d" % i, [128, 1024], BF16) for i in range(2)]
            t_in = [Tok(), Tok()]
            rst = [self.sb(es, "crst%d" % i, [128, 24], F32) for i in range(2)]
            t_rst = [Tok(), Tok()]
            junk = self.sb(es, "cjunk", [128, 128], F32)
            t_junk = Tok()
            yb16 = [self.sb(es, "cyb%d" % i, [128, 1024], BF16) for i in range(2)]
            t_yb = [Tok(), Tok()]
            ystage = [self.sb(es, "cystage%d" % i, [128, 8, 512], BF16) for i in range(2)]
            t_ys = [Tok(), Tok()]
            for t in range(T):
                b = t % 2
                blk, tl = t // 4, t % 4
                yb = blk % 2
                cs_ = slice(t * 128, (t + 1) * 128)
                s.dma("sp", of[b][:], self.OFB[0, cs_, :], r=[self.t_of], w=[t_in[b]])
                s.dma("sp", ob[b][:], self.OFB[1, cs_, :], r=[self.t_of], w=[t_in[b]])
                s.dma("sp", sg[b][:], self.SGG[cs_, :], r=[self.t_gd], w=[t_in[b]])
                O = of[b]
                s.op("pool", lambda e: e.tensor_tensor(out=O[:], in0=O[:], in1=ob[b][:], op=ALU.add), r=[t_in[b]], w=[t_in[b]])
                for h in range(8):
                    s.op("act", lambda e, h=h: e.activation(out=junk[:], in_=O[:, h * 128:(h + 1) * 128], func=AF.Square, accum_out=rst[b][:, h:h + 1]),
                         r=[t_in[b]], w=[t_junk, t_rst[b]])
                s.op("dve", lambda e: e.tensor_scalar(out=rst[b][:, 8:16], in0=rst[b][:, 0:8], scalar1=1.0 / 128.0, scalar2=EPS, op0=ALU.mult, op1=ALU.add),
                     r=[t_rst[b]], w=[t_rst[b]])
                s.op("act", lambda e: e.activation(out=rst[b][:, 8:16], in_=rst[b][:, 8:16], func=AF.Ln), r=[t_rst[b]], w=[t_rst[b]])
                s.op("act", lambda e: e.activation(out=rst[b][:, 16:24], in_=rst[b][:, 8:16], func=AF.Exp, scale=-0.5), r=[t_rst[b]], w=[t_rst[b]])
                O3 = O[:].rearrange("p (h d) -> p h d", h=8)
                s.op("dve", lambda e: e.tensor_tensor(out=O3, in0=O3, in1=rst[b][:, 16:24].unsqueeze(2).to_broadcast([128, 8, 128]), op=ALU.mult),
                     r=[t_in[b], t_rst[b]], w=[t_in[b]])
                s.op("pool", lambda e: e.tensor_tensor(out=O3, in0=O3, in1=gnorm[:].unsqueeze(1).to_broadcast([128, 8, 128]), op=ALU.mult),
                     r=[t_in[b], t_c], w=[t_in[b]])
                s.op("dve", lambda e: e.tensor_tensor(out=yb16[b][:], in0=O[:], in1=sg[b][:], op=ALU.mult), r=[t_in[b]], w=[t_yb[b]])
                self.to_ystage(yb16[b], t_yb[b], 8, ystage[yb], t_ys[yb], tl, banks=[6, 7])
                if tl == 3 or t == T - 1:
                    self.store_ystage(ystage[yb], t_ys[yb], 8, 8, blk * 512, (tl + 1) * 128)
            s.barrier()

    def phase_dil(self, si, L, hT, hT_tok):
        nc, s = self.nc, self.s
        T = L // 128
        w_in = self.w["o_w_in"]
        OG, LSE = self.OG, self.LSE
        with ExitStack() as es:
            wts = [self.sb(es, "dw%d" % i, [128, 8, 768], BF16) for i in range(2)]
            t_wts = [Tok(), Tok()]
            cos16 = self.sb(es, "cos16", [128, T, 16], F32)
            sin16 = self.sb(es, "sin16", [128, T, 16], F32)
            t_tab = Tok()
            s.dma("sp", cos16[:], self.c_rope[L][0], w=[t_tab])
            s.dma("sp", sin16[:], self.c_rope[L][1], w=[t_tab])
            qT = self.sb(es, "dqT", [128, 2, L], BF16)
            kT = self.sb(es, "dkT", [128, 2, L], BF16)
            t_qk = [Tok() for _ in range(T)]
            vP = self.sb(es, "dvP", [128, T, 256], BF16)
            t_vP = [Tok() for _ in range(T)]
            raw = [self.sb(es, "draw%d" % i, [128, 4, 128], F32) for i in range(2)]
            t_raw = [Tok(), Tok()]
            rtmps = [self.sb(es, "drtmp%d" % i, [128, 4, 4, 16], F32) for i in range(2)]
            t_rtmps = [Tok(), Tok()]
            qk16 = [self.sb(es, "dqk16%d" % i, [128, 4, 128], BF16) for i in range(2)]
            t_qk16 = [Tok(), Tok()]
            Sm = [self.sb(es, "dSm%d" % i, [128, 2, 384], F32) for i in range(2)]
            t_Sm = [Tok(), Tok()]
            Pe = [self.sb(es, "dPe%d" % i, [128, 2, 384], BF16) for i in range(2)]
            t_Pe = [Tok(), Tok()]
            stt = [self.sb(es, "dst%d" % i, [128, 8], F32) for i in range(2)]
            t_stt = [Tok(), Tok()]
            PT = [self.sb(es, "dPT%d" % i, [128, 8, 128], BF16) for i in range(2)]
            t_PT = [Tok(), Tok()]
            og = [self.sb(es, "dog%d" % i, [128, 256], F32) for i in range(2)]
            t_og = [Tok(), Tok()]
            lse = [self.sb(es, "dlse%d" % i, [128, 4], F32) for i in range(2)]
            t_lse = [Tok(), Tok()]
            scale = 128.0 ** -0.5
            it = 0
            for g, d in enumerate(DIL):
                ls = L // d
                tps = ls // 128
                for hp in range(2):
                    wt, t_w = wts[it % 2], t_wts[it % 2]
                    it += 1
                    for qkv in range(3):
                        c0 = O_CQKV + ((qkv * 3 + g) * 4 + hp * 2) * 128
                        s.dma("pool", wt[:, :, qkv * 256:(qkv + 1) * 256],
                              w_in[:, c0:c0 + 256].rearrange("(k p) c -> p k c", p=128), w=[t_w])
                    def qk_lane(ln):
                        b = ln
                        pb = 2 + ln
                        pt = 6 + ln
                        for t in range(ln, T, 2):
                            for qk in range(2):
                                self.proj_tok(self.bank(pb)[:, qk * 256:(qk + 1) * 256], pb, wt, t_w, qk * 256, 256, hT, [hT_tok[t]],
                                              lambda k: hT[:, k, t * 128:(t + 1) * 128])
                            yield
                            s.op("act", lambda e: e.activation(out=raw[b][:], in_=self.bank(pb).rearrange("p (h d) -> p h d", h=4),
                                                               func=AF.Copy), r=[self.ps_tok[pb]], w=[t_raw[b]])
                            yield
                            self.rope(raw[b][:], t_raw[b], qk16[b][:], t_qk16[b], rtmps[ln], t_rtmps[ln], cos16[:, t, :], sin16[:, t, :], t_tab,
                                      4, 16, 128)
                            yield
                            ptv = self.bank16(pt).rearrange("p (j c) -> p j c", j=8)
                            for c in range(4):
                                s.op("pe", lambda e, c=c: e.transpose(out=ptv[:, c, :], in_=qk16[b][:, c, :], identity=self.ident16[:]),
                                     r=[t_qk16[b], self.t_const], w=[self.ps_tok[pt]])
                            yield
                            s.op("act", lambda e: e.activation(out=qT[:, :, t * 128:(t + 1) * 128], in_=ptv[:, 0:2, :], func=AF.Copy,
                                                               scale=scale), r=[self.ps_tok[pt]], w=[t_qk[t]])
                            s.op("dve", lambda e: e.tensor_copy(out=kT[:, :, t * 128:(t + 1) * 128], in_=ptv[:, 2:4, :]),
                                 r=[self.ps_tok[pt]], w=[t_qk[t]])
                            yield

                    self.lockstep([qk_lane(0), qk_lane(1)])

                    def pslice(j):
                        seg, jj = j // tps, j % tps
                        start = seg + d * jj * 128
                        return start, start + d * 127 + 1

                    def ptoks(j):
                        a, bnd = pslice(j)
                        return [t_qk[tt] for tt in range(a // 128, (bnd - 1) // 128 + 1)], \
                               [hT_tok[tt] for tt in range(a // 128, (bnd - 1) // 128 + 1)]

                    def v_lane(ln):
                        pb = 2 + ln
                        for j in range(ln, T, 2):
                            a, bnd = pslice(j)
                            self.proj_tok(self.bank(pb)[:, 0:256], pb, wt, t_w, 512, 256, hT, ptoks(j)[1],
                                          lambda k: hT[:, k, a:bnd:d])
                            yield
                            s.op("act" if ln == 0 else "dve", (lambda e: e.activation(out=vP[:, j, :], in_=self.bank(pb)[:, 0:256], func=AF.Copy)) if ln == 0
                                 else (lambda e: e.tensor_copy(out=vP[:, j, :], in_=self.bank(pb)[:, 0:256])),
                                 r=[self.ps_tok[pb]], w=[t_vP[j]])
                            yield

                    self.lockstep([v_lane(0), v_lane(1)])
                    def dil_lane(ln):
                        pp = ln
                        sb0 = 4 if ln == 0 else 2
                        ptb = 6 + ln
                        ob = ln
                        for j in range(ln, T, 2):
                            a, bnd = pslice(j)
                            kts = [kt for kt in (j - 1, j, j + 1) if 0 <= kt < T and kt // tps == j // tps]
                            nkt = len(kts)
                            nk = 128 * nkt
                            m0 = (kts[0] - (j - 1)) * 128
                            ka = pslice(kts[0])[0]
                            kb_ = pslice(kts[-1])[1]
                            kdeps = []
                            for kt in kts:
                                kdeps += ptoks(kt)[0]
                            for jh in range(2):
                                s.op("pe", lambda e, jh=jh: e.matmul(self.bank(sb0 + jh)[:, 0:nk], qT[:, jh, a:bnd:d], kT[:, jh, ka:kb_:d],
                                                                     start=True, stop=True),
                                     r=ptoks(j)[0] + kdeps, w=[self.ps_tok[sb0 + jh]])
                            yield
                            yield from self.softmax_gen([sb0, sb0 + 1], 2, nk, self.mask_dil[:, m0:m0 + nk], self.t_const, Sm[pp], t_Sm[pp], Pe[pp], t_Pe[pp],
                                                        stt[pp], t_stt[pp])
                            yield
                            self.transpose_P(Pe[pp], t_Pe[pp], 2, nkt, PT[pp], t_PT[pp], ptb)
                            s.op("dve", lambda e: e.reciprocal(out=stt[pp][:, 4:6], in_=stt[pp][:, 2:4]), r=[t_stt[pp]], w=[t_stt[pp]])
                            s.op("act", lambda e: e.activation(out=lse[pp][:, 0:2], in_=stt[pp][:, 2:4], func=AF.Ln), r=[t_stt[pp]], w=[t_lse[pp]])
                            yield
                            for jh in range(2):
                                for kc in range(nkt):
                                    s.op("pe", lambda e, jh=jh, kc=kc: e.matmul(self.bank(ob)[:, jh * 128:(jh + 1) * 128],
                                                                                PT[pp][:, jh * nkt + kc, :],
                                                                                vP[:, kts[kc], jh * 128:(jh + 1) * 128],
                                                                                start=(kc == 0), stop=(kc == nkt - 1)),
                                         r=[t_PT[pp]] + [t_vP[kt] for kt in kts], w=[self.ps_tok[ob]])
                            s.op("dve", lambda e: e.tensor_tensor(out=lse[pp][:, 2:4], in0=lse[pp][:, 0:2], in1=stt[pp][:, 0:2], op=ALU.subtract),
                                 r=[t_stt[pp], t_lse[pp]], w=[t_lse[pp]])
                            yield
                            for jh in range(2):
                                s.op("dve", lambda e, jh=jh: e.tensor_scalar(out=og[pp][:, jh * 128:(jh + 1) * 128],
                                                                             in0=self.bank(ob)[:, jh * 128:(jh + 1) * 128],
                                                                             scalar1=stt[pp][:, 4 + jh:5 + jh], scalar2=None, op0=ALU.mult),
                                     r=[self.ps_tok[ob], t_stt[pp]], w=[t_og[pp]])
                            s.dma("sp", OG[g, a:bnd:d, hp * 256:(hp + 1) * 256], og[pp][:], r=[t_og[pp]], w=[self.t_og_dram])
                            s.dma("sp", LSE[g, a:bnd:d, hp * 2:(hp + 1) * 2], lse[pp][:, 2:4], r=[t_lse[pp]], w=[self.t_og_dram])
                            yield

                    self.lockstep([dil_lane(0), dil_lane(1)])
            s.barrier()
        with ExitStack() as es:
            wg = self.sb(es, "dwg", [128, 8, 512], BF16)
            t_wg = Tok()
            self.load_w(wg, t_wg, w_in, O_GC, 512)
            og3 = [self.sb(es, "og3%d" % i, [128, 3, 512], F32) for i in range(2)]
            l3 = [self.sb(es, "l3%d" % i, [128, 3, 4], F32) for i in range(2)]
            t_in = [Tok(), Tok()]
            wk = [self.sb(es, "mwk%d" % i, [128, 8, 4], F32) for i in range(2)]
            t_wk = [Tok(), Tok()]
            yacc = [self.sb(es, "yacc%d" % i, [128, 512], F32) for i in range(2)]
            t_ya = [Tok(), Tok()]
            sg = [self.sb(es, "dsg%d" % i, [128, 512], F32) for i in range(2)]
            t_sg = [Tok(), Tok()]
            yc = [self.sb(es, "dyc%d" % i, [128, 512], BF16) for i in range(2)]
            t_yc = [Tok(), Tok()]
            ystage = [self.sb(es, "dystage%d" % i, [128, 4, 512], BF16) for i in range(2)]
            t_ys = [Tok(), Tok()]
            for t in range(T):
                b = t % 2
                blk, tl = t // 4, t % 4
                yb = blk % 2
                s.dma("sp", og3[b][:], OG[:, t * 128:(t + 1) * 128, :].rearrange("g p c -> p g c"), r=[self.t_og_dram], w=[t_in[b]])
                s.dma("sp", l3[b][:], LSE[:, t * 128:(t + 1) * 128, :].rearrange("g p c -> p g c"), r=[self.t_og_dram], w=[t_in[b]])
                pb = self.pick("proj", [2, 3])
                self.proj_tok(self.bank(pb), pb, wg, t_wg, 0, 512, hT, [hT_tok[t]], lambda k: hT[:, k, t * 128:(t + 1) * 128])
                s.op("act", lambda e: e.activation(out=sg[b][:], in_=self.bank(pb), func=AF.Silu), r=[self.ps_tok[pb]], w=[t_sg[b]])
                W = wk[b]
                s.op("dve", lambda e: e.tensor_tensor(out=W[:, 3, :], in0=l3[b][:, 0, :], in1=l3[b][:, 1, :], op=ALU.max), r=[t_in[b]], w=[t_wk[b]])
                s.op("dve", lambda e: e.tensor_tensor(out=W[:, 3, :], in0=W[:, 3, :], in1=l3[b][:, 2, :], op=ALU.max), r=[t_in[b], t_wk[b]], w=[t_wk[b]])
                s.op("dve", lambda e: e.tensor_tensor(out=W[:, 0:3, :], in0=l3[b][:], in1=W[:, 3, :].unsqueeze(1).to_broadcast([128, 3, 4]),
                                                      op=ALU.subtract), r=[t_in[b], t_wk[b]], w=[t_wk[b]])
                s.op("act", lambda e: e.activation(out=W[:, 0:3, :], in_=W[:, 0:3, :], func=AF.Exp), r=[t_wk[b]], w=[t_wk[b]])
                s.op("dve", lambda e: e.tensor_tensor(out=W[:, 4, :], in0=W[:, 0, :], in1=W[:, 1, :], op=ALU.add), r=[t_wk[b]], w=[t_wk[b]])
                s.op("dve", lambda e: e.tensor_tensor(out=W[:, 4, :], in0=W[:, 4, :], in1=W[:, 2, :], op=ALU.add), r=[t_wk[b]], w=[t_wk[b]])
                s.op("dve", lambda e: e.reciprocal(out=W[:, 5, :], in_=W[:, 4, :]), r=[t_wk[b]], w=[t_wk[b]])
                s.op("dve", lambda e: e.tensor_tensor(out=W[:, 0:3, :], in0=W[:, 0:3, :], in1=W[:, 5, :].unsqueeze(1).to_broadcast([128, 3, 4]),
                                                      op=ALU.mult), r=[t_wk[b]], w=[t_wk[b]])
                for h in range(4):
                    hs = slice(h * 128, (h + 1) * 128)
                    s.op("dve", lambda e, h=h, hs=hs: e.tensor_scalar(out=yacc[b][:, hs], in0=og3[b][:, 0, hs], scalar1=W[:, 0, h:h + 1],
                                                                      scalar2=None, op0=ALU.mult), r=[t_in[b], t_wk[b]], w=[t_ya[b]])
                    for g in (1, 2):
                        s.op("dve", lambda e, h=h, hs=hs, g=g: e.scalar_tensor_tensor(out=yacc[b][:, hs], in0=og3[b][:, g, hs],
                                                                                      scalar=W[:, g, h:h + 1], in1=yacc[b][:, hs],
                                                                                      op0=ALU.mult, op1=ALU.add),
                             r=[t_in[b], t_wk[b], t_ya[b]], w=[t_ya[b]])
                s.op("pool", lambda e: e.tensor_tensor(out=yc[b][:], in0=yacc[b][:], in1=sg[b][:], op=ALU.mult),
                     r=[t_ya[b], t_sg[b]], w=[t_yc[b]])
                self.to_ystage(yc[b], t_yc[b], 4, ystage[yb], t_ys[yb], tl, banks=[6, 7])
                if tl == 3 or t == T - 1:
                    self.store_ystage(ystage[yb], t_ys[yb], 4, 0, blk * 512, (tl + 1) * 128)
            s.barrier()

    def zero_YT(self, L, chunks):
        s = self.s
        with ExitStack() as es:
            z = self.sb(es, "zeros", [128, L], BF16)
            t_z = Tok()
            s.op("dve", lambda e: e.memset(z[:], 0.0), w=[t_z])
            for c in chunks:
                s.dma("sp", self.YT[c * 128:(c + 1) * 128, 0:L], z[:], r=[t_z])
            s.barrier()

    def phase_out(self, r0, L, src, dst, pre, nch):
        nc, s = self.nc, self.s
        T = L // 128
        with ExitStack() as es:
            wo = self.sb(es, "wo", [128, nch, D], BF16)
            t_wo = Tok()
            w_out = self.w[pre + "w_out"]
            for c0 in range(0, nch, 4):
                s.dma("pool", wo[:, c0:c0 + 4, :], w_out[c0 * 128:(c0 + 4) * 128, :].rearrange("(c p) n -> p c n", p=128),
                      w=[t_wo])
            gpost = self.sb(es, "gpost", [128, D], F32)
            t_gp = Tok()
            s.dma("sp", gpost[:], self.w[pre + "post_g"].partition_broadcast(128), w=[t_gp])
            yb = [self.sb(es, "yb%d" % i, [128, nch, 512], BF16) for i in range(2)]
            t_yb = [Tok(), Tok()]
            xr = [self.sb(es, "xr%d" % i, [128, D], F32) for i in range(2)]
            t_xr = [Tok(), Tok()]
            st = [self.sb(es, "ost%d" % i, [128, 4], F32) for i in range(2)]
            t_st = [Tok(), Tok()]
            junk = [self.sb(es, "ojunk%d" % i, [128, D], BF16) for i in range(2)]
            t_junk = [Tok(), Tok()]
            tmp = [self.sb(es, "otmp%d" % i, [128, D], F32) for i in range(2)]
            t_tmp = [Tok(), Tok()]
            nblk = (L + 511) // 512
            for blk in range(nblk):
                tok0 = blk * 512
                ntok = min(512, L - tok0)
                bb = blk % 2
                s.dma("sp", yb[bb][:, :, 0:ntok],
                      self.YT[0:nch * 128, tok0:tok0 + ntok].rearrange("(c p) n -> p c n", p=128), w=[t_yb[bb]])
                def o_lane(ln):
                    b = ln
                    p2 = 2 * ln
                    for tl in range(ln, ntok // 128, 2):
                        t = blk * 4 + tl
                        s.dma("sp", xr[b][:], src[r0 + t * 128:r0 + (t + 1) * 128, :], w=[t_xr[b]])
                        for half in range(2):
                            for c in range(nch):
                                s.op("pe", lambda e, c=c, half=half: e.matmul(self.bank(p2 + half), yb[bb][:, c, tl * 128:(tl + 1) * 128],
                                                                              wo[:, c, half * 512:(half + 1) * 512],
                                                                              start=(c == 0), stop=(c == nch - 1)),
                                     r=[t_yb[bb], t_wo], w=[self.ps_tok[p2 + half]])
                        yield
                        pv = self.ps[:, p2 * 512:(p2 + 2) * 512]
                        pr = [self.ps_tok[p2], self.ps_tok[p2 + 1]]
                        s.op("act", lambda e: e.activation(out=junk[b][:], in_=pv, func=AF.Square, accum_out=st[b][:, 0:1]),
                             r=pr, w=[t_junk[b], t_st[b]])
                        yield
                        s.op("dve", lambda e: e.tensor_scalar(out=st[b][:, 1:2], in0=st[b][:, 0:1], scalar1=1.0 / D, scalar2=EPS,
                                                              op0=ALU.mult, op1=ALU.add), r=[t_st[b]], w=[t_st[b]])
                        yield
                        s.op("act", lambda e: e.activation(out=st[b][:, 2:3], in_=st[b][:, 1:2], func=AF.Sqrt), r=[t_st[b]], w=[t_st[b]])
                        yield
                        s.op("dve", lambda e: e.reciprocal(out=st[b][:, 3:4], in_=st[b][:, 2:3]), r=[t_st[b]], w=[t_st[b]])
                        yield
                        s.op("dve", lambda e: e.scalar_tensor_tensor(out=tmp[b][:], in0=pv, scalar=st[b][:, 3:4], in1=gpost[:],
                                                                     op0=ALU.mult, op1=ALU.mult),
                             r=pr + [t_st[b], t_gp], w=[t_tmp[b]])
                        yield
                        s.op("pool", lambda e: e.tensor_tensor(out=tmp[b][:], in0=tmp[b][:], in1=xr[b][:], op=ALU.add),
                             r=[t_xr[b], t_tmp[b]], w=[t_tmp[b]])
                        yield
                        s.dma("sp", dst[r0 + t * 128:r0 + (t + 1) * 128, :], tmp[b][:], r=[t_tmp[b]])
                        yield

                self.lockstep([o_lane(0), o_lane(1)])
            s.barrier()


def host_consts(seq_lens):
    c = {}
    c["c_ident"] = np.eye(128, dtype=np.float32)
    c["c_mask_dil"] = band_mask(64, 192)
    c["c_mask_swa"] = band_mask(0, 256)
    for L in sorted(set(seq_lens)):
        c16, s16 = rope_tables(L, 16)
        c8, s8 = rope_tables(L, 8)
        c["c_cos16_%d" % L] = tok_layout(c16)
        c["c_sin16_%d" % L] = tok_layout(s16)
        c["c_cos8_%d" % L] = tok_layout(c8)
        c["c_sin8_%d" % L] = tok_layout(s8)
    j = np.arange(128)[:, None]
    i = np.arange(128)[None, :]
    same = (j // 64) == (i // 64)
    fw = []
    fw.append(np.where(same & (i >= j), 0.0, NEG))
    fw.append(np.where(same & (i > j), 1.0, 0.0))
    for lv in range(6):
        bsz = 1 << lv
        fw.append(np.where(((j // (2 * bsz)) == (i // (2 * bsz))) & ((j % (2 * bsz)) < bsz) & ((i % (2 * bsz)) >= bsz), 1.0, 0.0))
    bw = [m.T for m in fw]
    tri_f = np.where(same & (j <= i), 1.0, 0.0)
    gm = np.stack(fw + bw + [tri_f, tri_f.T], 0).astype(np.float32)
    c["c_gmask"] = np.ascontiguousarray(gm.transpose(1, 0, 2))
    c["c_delta"] = np.abs(np.linspace(math.log(1e-2) / 1.5, math.log(1e-2) / 0.3, 1024, dtype=np.float32)).reshape(1, 1024)
    for L in sorted(set(seq_lens)):
        T = L // 128
        N = 2 * L
        i = np.arange(L, dtype=np.int64)
        prod = ((2 * i[:, None] + 1) * (2 * i[None, :] + 1)) % (4 * N)
        ang = prod.astype(np.float64) * (2.0 * np.pi / (4 * N))
        for nm, fn in (("c_gc_%d" % L, np.cos), ("c_gs_%d" % L, np.sin)):
            tab = fn(ang).astype(np.float32).astype(ml_dtypes.bfloat16)
            c[nm] = np.ascontiguousarray(tab.reshape(T, 128, T, 128).transpose(2, 1, 0, 3))
        t = np.linspace(0.0, 1.0, L, dtype=np.float32)[:, None]
        wv = (2.0 * math.pi / L) * np.arange(L, dtype=np.float32)[:, None]
        f = np.linspace(1e-4, 15.0, 16, dtype=np.float32)[None, :]
        emb = np.concatenate([t, np.cos(f * wv), -np.sin(f * wv)], axis=-1).astype(np.float32)
        c["c_embT_%d" % L] = np.ascontiguousarray(emb.T)
        c["c_tcol_%d" % L] = np.ascontiguousarray((-t[:, 0]).reshape(T, 128).T)
        k = np.arange(L, dtype=np.float64)
        psi = np.pi * (k + 0.5) / N
        ps = np.stack([(2.0 / N) * np.cos(psi), (2.0 / N) * np.sin(psi)], 0).astype(np.float32)
        c["c_psi_%d" % L] = np.ascontiguousarray(ps.reshape(2, T, 128).transpose(2, 0, 1))
    return c


def col_layout(v):
    return np.ascontiguousarray(np.asarray(v, np.float32).reshape(8, 128).T)


def shared_inputs(inp, seq_lens):
    m = {}
    for nm in ("e_w_in", "e_w_out", "e_w_mem_kv", "o_w_in", "o_w_out", "o_w_mem_kv"):
        m[nm] = np.ascontiguousarray(np.asarray(inp[nm], np.float32)[0])
    for nm in ("e_pre_g", "e_mem_g", "o_pre_g", "o_mem_g"):
        m[nm] = col_layout(np.asarray(inp[nm])[0])
    for nm in ("e_post_g", "o_post_g"):
        m[nm] = np.ascontiguousarray(np.asarray(inp[nm], np.float32).reshape(1, D))
    m["swa_sink"] = np.ascontiguousarray(np.asarray(inp["swa_sink"], np.float32).reshape(1, 16))
    f32 = lambda a: np.asarray(a, np.float32)
    m["hy_filt_w1"] = np.ascontiguousarray(f32(inp["hy_filt_w1"])[0])
    m["hy_filt_w2"] = np.ascontiguousarray(f32(inp["hy_filt_w2"])[0])
    m["hy_filt_w3"] = np.ascontiguousarray(f32(inp["hy_filt_w3"])[0])
    m["hy_fvec"] = np.ascontiguousarray(np.stack([f32(inp["hy_filt_b1"])[0], f32(inp["hy_filt_b2"])[0], f32(inp["hy_freq"])[0]], 1))
    m["hy_conv_w"] = np.ascontiguousarray(f32(inp["hy_conv_w"])[0].reshape(3, 24, 128).transpose(2, 1, 0))
    m["hy_conv_b"] = np.ascontiguousarray(f32(inp["hy_conv_b"])[0].reshape(24, 128).T)
    m["gdn_conv_w"] = np.ascontiguousarray(f32(inp["gdn_conv_w"])[0].reshape(5, 24, 128).transpose(2, 1, 0))
    m["gdn_A_log"] = np.ascontiguousarray(f32(inp["gdn_A_log"]).reshape(1, 16))
    m["gdn_dt_bias"] = np.ascontiguousarray(f32(inp["gdn_dt_bias"]).reshape(1, 16))
    m["gdn_norm_g"] = np.ascontiguousarray(f32(inp["gdn_norm_g"]).reshape(1, 128))
    m["hy_skip"] = np.ascontiguousarray(f32(inp["hy_skip"])[0].reshape(8, 128).T)
    m.update(host_consts(seq_lens))
    return m


_CACHE = {}


def kernel(**inp):
    xp = np.asarray(inp["x_prompt"], np.float32)
    xs = np.asarray(inp["x_sample"], np.float32)
    mp = np.asarray(inp["mem_prompt"], np.float32)
    ms = np.asarray(inp["mem_sample"], np.float32)
    n = 8
    seq_lens = [xs.shape[1], xs.shape[1], xp.shape[1]]
    shared = shared_inputs(inp, seq_lens)
    in_maps = []
    for c in range(n):
        m = dict(shared)
        m["x"] = np.ascontiguousarray(np.concatenate([xs[2 * c], xs[2 * c + 1], xp[c]], axis=0))
        m["mem"] = np.ascontiguousarray(np.concatenate([ms[2 * c], ms[2 * c + 1], mp[c]], axis=0))
        in_maps.append(m)
    nc = KB(seq_lens).build()
    res = run_bass_kernel_spmd(nc, in_maps, core_ids=list(range(n)))
    Ls = xs.shape[1]
    y_p = np.stack([res.results[c]["y"][2 * Ls:] for c in range(n)], axis=0)
    y_s = np.stack([res.results[c]["y"][j * Ls:(j + 1) * Ls] for c in range(n) for j in range(2)], axis=0)
    return (y_p.astype(np.float32), y_s.astype(np.float32))
```

```python
import math
from contextlib import ExitStack

import numpy as np
import ml_dtypes

import concourse.bass as bass
import concourse.mybir as mybir
from concourse.bass_utils import run_bass_kernel_spmd

F32 = mybir.dt.float32
BF16 = mybir.dt.bfloat16
F32R = mybir.dt.float32r
AF = mybir.ActivationFunctionType
ALU = mybir.AluOpType
AX = mybir.AxisListType

D = 1024
EPS = 1e-6
NEG = -30000.0
ROPE_THETA = 500000.0
MEM_TOKENS = 256
EVEN_IN = 9248
ODD_IN = 8448
E_HY, E_GHY, E_QKV, E_GG, E_BETA, E_A, E_XQ, E_GX = 0, 3072, 4096, 7168, 8192, 8208, 8224, 8736
O_CQKV, O_GC, O_DQ, O_DKV, O_GD, O_XQ, O_GX = 0, 4608, 5120, 6144, 6400, 7424, 7936
DIL = (1, 4, 16)


class Tok:
    __slots__ = ("w", "r")

    def __init__(self):
        self.w = None
        self.r = {}


class Sch:
    RING = 8

    def __init__(self, nc, es):
        self.nc = nc
        self.eng = {"pe": nc.tensor, "act": nc.scalar, "dve": nc.vector, "pool": nc.gpsimd, "sp": nc.sync}
        self.sem = {}
        self.cnt = {}
        self.known = {}
        for e in self.eng:
            self.sem[e] = es.enter_context(nc.semaphore("s_" + e))
            self.cnt[e] = 0
            self.known[e] = {}
        self.dcnt = {}
        for q in ("sp", "act", "pool"):
            self.dcnt[q] = 0
            for i in range(self.RING):
                key = ("d", q, i)
                self.sem[key] = es.enter_context(nc.semaphore("d_%s_%d" % (q, i)))
                self.cnt[key] = 0
        self.ninst = 0

    def _wait(self, e, deps):
        need = {}
        for key, val in deps:
            if key == e and e == "pe":
                continue
            if self.known[e].get(key, 0) >= val:
                continue
            if need.get(key, 0) < val:
                need[key] = val
        for key, val in need.items():
            self.eng[e].wait_ge(self.sem[key], val)
            self.known[e][key] = val

    @staticmethod
    def _deps(r, w):
        deps = []
        for t in r:
            if t.w is not None:
                deps.append(t.w)
        for t in w:
            if t.w is not None:
                deps.append(t.w)
            deps.extend(t.r.items())
        return deps

    @staticmethod
    def _mark(me, r, w):
        key, val = me
        for t in r:
            if t.r.get(key, 0) < val:
                t.r[key] = val
        for t in w:
            t.w = me
            t.r = {}

    def op(self, e, fn, r=(), w=(), ww=()):
        deps = self._deps(r, w)
        for t in ww:
            if t.w is not None and t.w[0] != e:
                deps.append(t.w)
            deps.extend(t.r.items())
        self._wait(e, deps)
        inst = fn(self.eng[e])
        self.cnt[e] += 1
        inst.then_inc(self.sem[e], 1)
        self._mark((e, self.cnt[e]), r, list(w) + list(ww))
        self.ninst += 1

    def dma(self, q, out, in_, r=(), w=()):
        i = self.dcnt[q]
        self.dcnt[q] += 1
        slot = i % self.RING
        key = ("d", q, slot)
        val = 16 * (i // self.RING + 1)
        deps = self._deps(r, w)
        if val > 16:
            deps.append((key, val - 16))
        self._wait(q, deps)
        self.eng[q].dma_start(out=out, in_=in_).then_inc(self.sem[key], 16)
        self.cnt[key] = val
        self._mark((key, val), r, w)
        self.ninst += 1

    def barrier(self):
        allv = [(k, v) for k, v in self.cnt.items() if v > 0]
        for e in self.eng:
            self._wait(e, allv)

    def finish(self):
        allv = [(k, v) for k, v in self.cnt.items() if v > 0]
        self._wait("sp", allv)


def rope_tables(L, half):
    inv = ROPE_THETA ** (-np.arange(half, dtype=np.float32) / half)
    ang = np.arange(L, dtype=np.float32)[:, None] * inv[None, :]
    return np.cos(ang).astype(np.float32), np.sin(ang).astype(np.float32)


def tok_layout(a):
    L, Fd = a.shape
    return np.ascontiguousarray(a.reshape(L // 128, 128, Fd).transpose(1, 0, 2))


def band_mask(lo_off, hi_off):
    i = np.arange(128)[:, None]
    c = np.arange(384)[None, :]
    ok = (c >= i + lo_off) & (c <= i + hi_off)
    return np.where(ok, 0.0, NEG).astype(np.float32)


class KB:
    def __init__(self, seq_lens, dbg=False):
        self.seq_lens = list(seq_lens)
        self.NS = len(seq_lens)
        self.NTOK = sum(seq_lens)
        self.LMAX = max(seq_lens)
        self.dbg = dbg
        self.nc = bass.Bass("TRN2", target_bir_lowering=False)
        self.consts = {}

    def din(self, name, shape, dt=F32):
        return self.nc.dram_tensor(name, list(shape), dt, kind="ExternalInput").ap()

    def dscr(self, name, shape, dt=F32, out=False):
        kind = "ExternalOutput" if (out and self.dbg) else "Internal"
        return self.nc.dram_tensor(name, list(shape), dt, kind=kind).ap()

    def sb(self, es, name, shape, dt):
        self.uid = getattr(self, "uid", 0) + 1
        return es.enter_context(self.nc.sbuf_tensor("%s_%d" % (name, self.uid), list(shape), dt))

    def build(self):
        nc = self.nc
        NTOK, NS = self.NTOK, self.NS
        self.x = self.din("x", [NTOK, D])
        self.mem = self.din("mem", [NS * MEM_TOKENS, D])
        self.y = nc.dram_tensor("y", [NTOK, D], F32, kind="ExternalOutput").ap()
        self.x1 = self.dscr("x1", [NTOK, D])
        self.w = {}
        for nm, shp in (("e_w_in", [D, EVEN_IN]), ("e_w_out", [2560, D]), ("e_w_mem_kv", [D, 1024]),
                        ("o_w_in", [D, ODD_IN]), ("o_w_out", [2048, D]), ("o_w_mem_kv", [D, 1024])):
            self.w[nm] = self.din(nm, shp)
        for nm in ("e_pre_g", "e_mem_g", "o_pre_g", "o_mem_g"):
            self.w[nm] = self.din(nm, [128, 8])
        for nm in ("e_post_g", "o_post_g"):
            self.w[nm] = self.din(nm, [1, D])
        self.w["swa_sink"] = self.din("swa_sink", [1, 16])
        self.c_ident = self.din("c_ident", [128, 128])
        self.c_mask_dil = self.din("c_mask_dil", [128, 384])
        self.c_mask_swa = self.din("c_mask_swa", [128, 384])
        self.c_rope = {}
        for L in sorted(set(self.seq_lens)):
            T = L // 128
            self.c_rope[L] = (self.din("c_cos16_%d" % L, [128, T, 16]), self.din("c_sin16_%d" % L, [128, T, 16]),
                              self.din("c_cos8_%d" % L, [128, T, 8]), self.din("c_sin8_%d" % L, [128, T, 8]))
        self.YT = self.dscr("YT", [20 * 128, self.LMAX], BF16, out=True)
        self.OG = self.dscr("OG", [3, self.LMAX, 512], F32)
        TM = self.LMAX // 128
        for nm, shp in (("hy_filt_w1", [33, 64]), ("hy_filt_w2", [64, 64]), ("hy_filt_w3", [64, 2048]), ("hy_fvec", [64, 3]),
                        ("hy_conv_w", [128, 24, 3]), ("hy_conv_b", [128, 24]), ("hy_skip", [128, 8])):
            self.w[nm] = self.din(nm, shp)
        self.c_delta = self.din("c_delta", [1, 1024])
        self.c_dft, self.c_filt, self.H = {}, {}, {}
        for L in sorted(set(self.seq_lens)):
            T = L // 128
            self.c_dft[L] = (self.din("c_gc_%d" % L, [T, 128, T, 128], BF16), self.din("c_gs_%d" % L, [T, 128, T, 128], BF16))
            self.c_filt[L] = {"embT": self.din("c_embT_%d" % L, [33, L]), "tcol": self.din("c_tcol_%d" % L, [128, T]),
                              "psi": self.din("c_psi_%d" % L, [128, 2, T])}
            self.H[L] = (self.dscr("HR_%d" % L, [T, 128, 1024]), self.dscr("HI_%d" % L, [T, 128, 1024]))
        self.CFSF = (self.dscr("CF", [TM, 128, 1024]), self.dscr("SF", [TM, 128, 1024]))
        self.ZRS = (self.dscr("ZR", [TM, 128, 1024], BF16), self.dscr("ZS", [TM, 128, 1024], BF16))
        self.UT = self.dscr("UT", [1024, self.LMAX], BF16)
        self.AT = self.dscr("AT", [1024, self.LMAX], BF16)
        self.BT = self.dscr("BT", [1024, self.LMAX], BF16)
        self.t_cf, self.t_H, self.t_uab, self.t_Z = Tok(), Tok(), Tok(), Tok()
        for nm, shp in (("gdn_conv_w", [128, 24, 5]), ("gdn_A_log", [1, 16]), ("gdn_dt_bias", [1, 16]), ("gdn_norm_g", [1, 128])):
            self.w[nm] = self.din(nm, shp)
        self.c_gmask = self.din("c_gmask", [128, 18, 128])
        self.QT = self.dscr("QT", [1024, self.LMAX])
        self.KT = self.dscr("KT", [1024, self.LMAX])
        self.KTOK = self.dscr("KTOK", [self.LMAX, 1024])
        self.VTOK = self.dscr("VTOK", [self.LMAX, 1024])
        self.BG = self.dscr("BG", [self.LMAX, 32])
        self.SGG = self.dscr("SGG", [self.LMAX, 1024], BF16)
        self.OFB = self.dscr("OFB", [2, self.LMAX, 1024])
        self.G_UB = self.dscr("G_UB", [2, TM, 128, 1024])
        self.G_KD = self.dscr("G_KD", [2, TM, 128, 1024])
        self.G_WT = self.dscr("G_WT", [2, TM, 128, 8, 128])
        self.G_QKM = self.dscr("G_QKM", [2, TM, 128, 8, 128])
        self.G_SCC = self.dscr("G_SCC", [2, TM, 64, 2, 8])
        self.G_GLB = self.dscr("G_GLB", [2, TM, 128, 8, 2])
        self.t_gd, self.t_of, self.t_gp = Tok(), Tok(), Tok()
        self.LSE = self.dscr("LSE", [3, self.LMAX, 4], F32)

        with ExitStack() as es:
            self.es = es
            s = self.s = Sch(nc, es)
            self.ident32 = self.sb(es, "ident32", [128, 128], F32)
            self.ident16 = self.sb(es, "ident16", [128, 128], BF16)
            self.t_const = Tok()
            s.dma("sp", self.ident32[:], self.c_ident, w=[self.t_const])
            s.dma("pool", self.ident16[:], self.c_ident, w=[self.t_const])
            self.mask_dil = self.sb(es, "mask_dil", [128, 384], F32)
            self.mask_swa = self.sb(es, "mask_swa", [128, 384], F32)
            self.eps_col = self.sb(es, "eps_col", [128, 2], F32)
            s.op("dve", lambda e: e.memset(self.eps_col[:, 0:1], EPS), w=[self.t_const])
            s.op("dve", lambda e: e.memset(self.eps_col[:, 1:2], 1.0), w=[self.t_const])
            s.dma("sp", self.mask_dil[:], self.c_mask_dil, w=[self.t_const])
            s.dma("sp", self.mask_swa[:], self.c_mask_swa, w=[self.t_const])
            self.ps = es.enter_context(nc.psum_tensor("ps", [128, 4096], F32))
            self.ps_tok = [Tok() for _ in range(8)]
            self.ps_rr = 0
            self.t_og_dram = Tok()
            s.barrier()

            if getattr(self, "en_even", [1, 1, 1])[0]:
                for L in sorted(set(self.seq_lens)):
                    self.phase_filter(L)
            r0 = 0
            for si, L in enumerate(self.seq_lens):
                self.run_seq(si, r0, L)
                r0 += L
            s.finish()
        return nc

    def bank(self, b):
        return self.ps[:, b * 512:(b + 1) * 512]

    def bank16(self, b):
        return self.ps[:, b * 512:(b + 1) * 512].bitcast(BF16)

    def run_seq(self, si, r0, L):
        T = L // 128
        for layer in range(2):
            src = self.x if layer == 0 else self.x1
            dst = self.x1 if layer == 0 else self.y
            with ExitStack() as les:
                hT = self.sb(les, "hT", [128, 8, L], BF16)
                hT_tok = [Tok() for _ in range(T)]
                pre = "e_" if layer == 0 else "o_"
                self.phase_norm(lambda t: src[r0 + t * 128:r0 + (t + 1) * 128, :], T, self.w[pre + "pre_g"], hT, hT_tok)
                if layer == 0:
                    nch = 20
                    en = getattr(self, "en_even", [1, 1, 1])
                    if en[1]:
                        self.phase_gdn1(si, L, hT, hT_tok)
                    if en[2]:
                        self.phase_xattn(si, L, hT, hT_tok, "e_", E_XQ, E_GX, 16)
                    if en[0]:
                        self.phase_hy1(si, L, hT, hT_tok)
                    if not all(en):
                        self.zero_YT(L, [c for c in range(20) if not en[0 if c < 8 else (1 if c < 16 else 2)]])
                else:
                    nch = 16
                    en = getattr(self, "en_odd", [1, 1, 1])
                    if en[0]:
                        self.phase_dil(si, L, hT, hT_tok)
                    if en[1]:
                        self.phase_swa(si, L, hT, hT_tok)
                    if en[2]:
                        self.phase_xattn(si, L, hT, hT_tok, "o_", O_XQ, O_GX, 12)
                    if not all(en):
                        self.zero_YT(L, [c for c in range(16) if not en[0 if c < 4 else (1 if c < 12 else 2)]])
                self.s.barrier()
            if layer == 0 and getattr(self, "en_even", [1, 1, 1])[1]:
                self.phase_gdn2(si, L)
            if layer == 0 and getattr(self, "en_even", [1, 1, 1])[0]:
                self.phase_hy2(si, L)
            self.phase_out(r0, L, src, dst, pre, nch)

    def phase_norm(self, src, T, g_dram, hT, hT_tok, tag="n"):
        nc, s = self.nc, self.s
        with ExitStack() as es:
            gcol = self.sb(es, tag + "gcol", [128, 8], F32)
            gfull = self.sb(es, tag + "gfull", [128, 8, 128], F32)
            t_g = Tok()
            s.dma("sp", gcol[:], g_dram, w=[t_g])
            for k in range(8):
                s.op("dve", lambda e, k=k: e.tensor_scalar(out=gfull[:, k, :], in0=self.ident32[:], scalar1=0.0,
                                                             scalar2=gcol[:, k:k + 1], op0=ALU.mult, op1=ALU.add),
                     r=[t_g, self.t_const], w=[t_g])
            xt = [self.sb(es, tag + "xt%d" % i, [128, D], F32) for i in range(2)]
            xn = [self.sb(es, tag + "xn%d" % i, [128, D], BF16) for i in range(2)]
            st = [self.sb(es, tag + "st%d" % i, [128, 4], F32) for i in range(2)]
            junk = [self.sb(es, tag + "junk%d" % i, [128, D], BF16) for i in range(2)]
            t_xt = [Tok(), Tok()]
            t_xn = [Tok(), Tok()]
            t_st = [Tok(), Tok()]
            t_junk = [Tok(), Tok()]
            def n_lane(ln):
                b = ln
                pb = ln
                for t in range(ln, T, 2):
                    s.dma("sp", xt[b][:], src(t), w=[t_xt[b]])
                    yield
                    s.op("act", lambda e: e.activation(out=junk[b][:], in_=xt[b][:], func=AF.Square, accum_out=st[b][:, 0:1]),
                         r=[t_xt[b]], w=[t_junk[b], t_st[b]])
                    yield
                    s.op("dve", lambda e: e.tensor_scalar(out=st[b][:, 1:2], in0=st[b][:, 0:1], scalar1=1.0 / D, scalar2=EPS,
                                                          op0=ALU.mult, op1=ALU.add), r=[t_st[b]], w=[t_st[b]])
                    yield
                    s.op("act", lambda e: e.activation(out=st[b][:, 2:3], in_=st[b][:, 1:2], func=AF.Sqrt), r=[t_st[b]], w=[t_st[b]])
                    yield
                    s.op("dve", lambda e: e.reciprocal(out=st[b][:, 3:4], in_=st[b][:, 2:3]), r=[t_st[b]], w=[t_st[b]])
                    yield
                    s.op("act", lambda e: e.activation(out=xn[b][:], in_=xt[b][:], func=AF.Copy, scale=st[b][:, 3:4]),
                         r=[t_xt[b], t_st[b]], w=[t_xn[b]])
                    yield
                    pv = self.bank16(pb).rearrange("p (k c) -> p k c", k=8)
                    for k in range(8):
                        s.op("pe", lambda e, k=k: e.transpose(out=pv[:, k, :], in_=xn[b][:, k * 128:(k + 1) * 128],
                                                              identity=self.ident16[:]),
                             r=[t_xn[b], self.t_const], w=[self.ps_tok[pb]])
                    yield
                    s.op("dve", lambda e: e.tensor_tensor(out=hT[:, :, t * 128:(t + 1) * 128], in0=pv, in1=gfull[:],
                                                          op=ALU.mult), r=[self.ps_tok[pb], t_g], w=[hT_tok[t]])
                    yield

            self.lockstep([n_lane(0), n_lane(1)])
            s.barrier()

    def next_bank(self, n=1):
        b = self.ps_rr
        if n == 2 and b % 2:
            b += 1
        if b + n > 8:
            b = 0
        self.ps_rr = (b + n) % 8
        return b

    def pick(self, cls, banks):
        rr = self.__dict__.setdefault("_rr", {})
        i = rr.get(cls, 0)
        rr[cls] = i + 1
        return banks[i % len(banks)]

    def rope(self, src, t_src, dst, t_dst, tmp, t_tmp, cos, sin, t_tab, H, half, dh):
        s = self.s
        A = src[:, :, 0:half]
        B = src[:, :, half:2 * half]
        cb = cos.unsqueeze(1).to_broadcast([128, H, half])
        sb_ = sin.unsqueeze(1).to_broadcast([128, H, half])
        s.op("dve", lambda e: e.tensor_tensor(out=tmp[:, 0, 0:H, :], in0=A, in1=cb, op=ALU.mult), r=[t_src, t_tab], ww=[t_tmp])
        s.op("dve", lambda e: e.tensor_tensor(out=tmp[:, 1, 0:H, :], in0=B, in1=sb_, op=ALU.mult), r=[t_src, t_tab], ww=[t_tmp])
        s.op("dve", lambda e: e.tensor_tensor(out=tmp[:, 2, 0:H, :], in0=B, in1=cb, op=ALU.mult), r=[t_src, t_tab], ww=[t_tmp])
        s.op("dve", lambda e: e.tensor_tensor(out=tmp[:, 3, 0:H, :], in0=A, in1=sb_, op=ALU.mult), r=[t_src, t_tab], ww=[t_tmp])
        s.op("dve", lambda e: e.tensor_tensor(out=dst[:, :, 0:half], in0=tmp[:, 0, 0:H, :], in1=tmp[:, 1, 0:H, :], op=ALU.subtract),
             r=[t_tmp], ww=[t_dst])
        s.op("dve", lambda e: e.tensor_tensor(out=dst[:, :, half:2 * half], in0=tmp[:, 2, 0:H, :], in1=tmp[:, 3, 0:H, :], op=ALU.add),
             r=[t_tmp], ww=[t_dst])
        s.op("act", lambda e: e.activation(out=dst[:, :, 2 * half:dh], in_=src[:, :, 2 * half:dh], func=AF.Copy),
             r=[t_src], w=[t_dst])

    def softmax_gen(self, sbanks, nh, nk, mask_ap, t_mask, Sm, t_Sm, Pe, t_Pe, st, t_st, sink_ap=None, t_sink=None):
        s = self.s
        for j in range(nh):
            s.op("dve", lambda e, j=j: e.tensor_tensor(out=Sm[:, j, 0:nk], in0=self.bank(sbanks[j])[:, 0:nk], in1=mask_ap, op=ALU.add),
                 r=[self.ps_tok[sbanks[j]], t_mask], ww=[t_Sm])
        yield
        if sink_ap is None:
            s.op("dve", lambda e: e.tensor_reduce(out=st[:, 0:nh], in_=Sm[:, 0:nh, 0:nk], op=ALU.max, axis=AX.X, negate=True),
                 r=[t_Sm], w=[t_st])
        else:
            s.op("dve", lambda e: e.tensor_reduce(out=st[:, 2 * nh:3 * nh], in_=Sm[:, 0:nh, 0:nk], op=ALU.max, axis=AX.X),
                 r=[t_Sm], w=[t_st])
            s.op("dve", lambda e: e.tensor_tensor(out=st[:, 2 * nh:3 * nh], in0=st[:, 2 * nh:3 * nh], in1=sink_ap, op=ALU.max),
                 r=[t_st, t_sink], w=[t_st])
            s.op("dve", lambda e: e.tensor_scalar(out=st[:, 0:nh], in0=st[:, 2 * nh:3 * nh], scalar1=-1.0, scalar2=None, op0=ALU.mult),
                 r=[t_st], w=[t_st])
            s.op("dve", lambda e: e.tensor_tensor(out=st[:, 3 * nh:4 * nh], in0=sink_ap, in1=st[:, 0:nh], op=ALU.add),
                 r=[t_st, t_sink], w=[t_st])
            s.op("act", lambda e: e.activation(out=st[:, 3 * nh:4 * nh], in_=st[:, 3 * nh:4 * nh], func=AF.Exp), r=[t_st], w=[t_st])
        yield
        for j in range(nh):
            s.op("act", lambda e, j=j: e.activation(out=Pe[:, j, 0:nk], in_=Sm[:, j, 0:nk], func=AF.Exp, bias=st[:, j:j + 1],
                                                    accum_out=st[:, nh + j:nh + j + 1]),
                 r=[t_Sm, t_st], w=[t_Pe, t_st])
        yield
        if sink_ap is not None:
            s.op("dve", lambda e: e.tensor_tensor(out=st[:, nh:2 * nh], in0=st[:, nh:2 * nh], in1=st[:, 3 * nh:4 * nh], op=ALU.add),
                 r=[t_st], w=[t_st])

    def softmax_block(self, *a, **k):
        for _ in self.softmax_gen(*a, **k):
            pass

    def transpose_P(self, Pe, t_Pe, nh, nkt, PT, t_PT, ptb):
        s = self.s
        ptv = self.bank16(ptb).rearrange("p (j c) -> p j c", j=8)
        n = 0
        for j in range(nh):
            for kc in range(nkt):
                s.op("pe", lambda e, j=j, kc=kc, n=n: e.transpose(out=ptv[:, n, :], in_=Pe[:, j, kc * 128:(kc + 1) * 128],
                                                                  identity=self.ident16[:]),
                     r=[t_Pe, self.t_const], w=[self.ps_tok[ptb]])
                n += 1
        s.op("act", lambda e: e.activation(out=PT[:, 0:n, :], in_=ptv[:, 0:n, :], func=AF.Copy), r=[self.ps_tok[ptb]], w=[t_PT])

    def load_w(self, wt, t_w, w_dram, c0, n):
        self.s.dma("pool", wt[:, :, 0:n], w_dram[:, c0:c0 + n].rearrange("(k p) c -> p k c", p=128), w=[t_w])

    def proj_feat(self, out_ap, pb, wt, t_w, wc0, hT, hT_tok, tok0, ntok, extra_r=()):
        s = self.s
        toks = [hT_tok[t] for t in range(tok0 // 128, (tok0 + ntok + 127) // 128)]
        for k in range(8):
            s.op("pe", lambda e, k=k: e.matmul(out_ap, wt[:, k, wc0:wc0 + 128], hT[:, k, tok0:tok0 + ntok],
                                               start=(k == 0), stop=(k == 7)),
                 r=[t_w] + toks + list(extra_r), w=[self.ps_tok[pb]])

    def proj_tok(self, out_ap, pb, wt, t_w, wc0, ncol, hT, hT_tok_list, tok_ap_fn):
        s = self.s
        for k in range(8):
            s.op("pe", lambda e, k=k: e.matmul(out_ap, tok_ap_fn(k), wt[:, k, wc0:wc0 + ncol],
                                               start=(k == 0), stop=(k == 7)),
                 r=[t_w] + list(hT_tok_list), w=[self.ps_tok[pb]])

    def phase_xattn(self, si, L, hT, hT_tok, pre, off_q, off_g, ch0):
        nc, s = self.nc, self.s
        T = L // 128
        w_in = self.w[pre + "w_in"]
        with ExitStack() as es:
            memT = self.sb(es, "memT", [128, 8, MEM_TOKENS], BF16)
            memT_tok = [Tok(), Tok()]
            self.phase_norm(lambda t: self.mem[si * MEM_TOKENS + t * 128: si * MEM_TOKENS + (t + 1) * 128, :], 2,
                            self.w[pre + "mem_g"], memT, memT_tok, tag="m")
            wkv = self.sb(es, "wkv", [128, 8, 1024], BF16)
            t_wkv = Tok()
            self.load_w(wkv, t_wkv, self.w[pre + "w_mem_kv"], 0, 1024)
            KmT = self.sb(es, "KmT", [128, 4, MEM_TOKENS], BF16)
            Vm = self.sb(es, "Vm", [128, 2, 512], BF16)
            t_km, t_vm = Tok(), Tok()
            for h in range(4):
                pb = self.next_bank()
                self.proj_feat(self.bank(pb)[:, 0:256], pb, wkv, t_wkv, h * 128, memT, memT_tok, 0, 256)
                s.op("act", lambda e: e.activation(out=KmT[:, h, :], in_=self.bank(pb)[:, 0:256], func=AF.Copy),
                     r=[self.ps_tok[pb]], w=[t_km])
            for mt in range(2):
                pb = self.next_bank()
                self.proj_tok(self.bank(pb), pb, wkv, t_wkv, 512, 512, memT, memT_tok,
                              lambda k: memT[:, k, mt * 128:(mt + 1) * 128])
                s.op("act", lambda e: e.activation(out=Vm[:, mt, :], in_=self.bank(pb), func=AF.Copy),
                     r=[self.ps_tok[pb]], w=[t_vm])
            wq = self.sb(es, "wq", [128, 8, 512], BF16)
            wg = self.sb(es, "wg", [128, 8, 512], BF16)
            t_wq, t_wg = Tok(), Tok()
            self.load_w(wq, t_wq, w_in, off_q, 512)
            self.load_w(wg, t_wg, w_in, off_g, 512)
            qT = self.sb(es, "qT", [128, 4, 512], BF16)
            t_qT = Tok()
            sg = [self.sb(es, "sg%d" % i, [128, 512], F32) for i in range(2)]
            t_sg = [Tok(), Tok()]
            Sm = [self.sb(es, "Sm%d" % i, [128, 4, 256], F32) for i in range(2)]
            Pe = [self.sb(es, "Pe%d" % i, [128, 4, 256], BF16) for i in range(2)]
            t_Pe = [Tok(), Tok()]
            stt = [self.sb(es, "stt%d" % i, [128, 16], F32) for i in range(2)]
            t_stt = [Tok(), Tok()]
            PT = [self.sb(es, "PT%d" % i, [128, 8, 128], BF16) for i in range(2)]
            t_PT = [Tok(), Tok()]
            yx = [self.sb(es, "yx%d" % i, [128, 512], BF16) for i in range(2)]
            t_yx = [Tok(), Tok()]
            ystage = [self.sb(es, "ystage%d" % i, [128, 4, 512], BF16) for i in range(2)]
            t_ys = [Tok(), Tok()]
            scale = 128.0 ** -0.5
            nblk = (L + 511) // 512
            for blk in range(nblk):
                tok0 = blk * 512
                ntok = min(512, L - tok0)
                yb = blk % 2
                for h in range(4):
                    pb = self.pick("xq", [2, 3])
                    self.proj_feat(self.bank(pb)[:, 0:ntok], pb, wq, t_wq, h * 128, hT, hT_tok, tok0, ntok)
                    s.op("act", lambda e: e.activation(out=qT[:, h, 0:ntok], in_=self.bank(pb)[:, 0:ntok], func=AF.Copy,
                                                       scale=scale), r=[self.ps_tok[pb]], w=[t_qT])
                def x_lane(ln):
                    gb = ln
                    s0 = 4 + 2 * ln
                    ptb = 2 + ln
                    for tl in range(ln, ntok // 128, 2):
                        t = blk * 4 + tl
                        b = ln
                        self.proj_tok(self.bank(gb), gb, wg, t_wg, 0, 512, hT, [hT_tok[t]],
                                      lambda k: hT[:, k, t * 128:(t + 1) * 128])
                        sv = self.ps[:, s0 * 512:(s0 + 2) * 512].rearrange("p (h m) -> p h m", h=4)
                        for h in range(4):
                            pbh = s0 + h // 2
                            s.op("pe", lambda e, h=h: e.matmul(sv[:, h, :], qT[:, h, tl * 128:(tl + 1) * 128], KmT[:, h, :],
                                                               start=True, stop=True),
                                 r=[t_qT, t_km], w=[self.ps_tok[pbh]])
                        yield
                        s.op("act", lambda e: e.activation(out=sg[b][:], in_=self.bank(gb), func=AF.Silu),
                             r=[self.ps_tok[gb]], w=[t_sg[b]])
                        s.op("dve", lambda e: e.tensor_reduce(out=stt[b][:, 0:4], in_=sv, op=ALU.max, axis=AX.X, negate=True),
                             r=[self.ps_tok[s0], self.ps_tok[s0 + 1]], w=[t_stt[b]])
                        yield
                        for h in range(4):
                            s.op("act", lambda e, h=h: e.activation(out=Pe[b][:, h, :], in_=sv[:, h, :], func=AF.Exp,
                                                                    bias=stt[b][:, h:h + 1], accum_out=stt[b][:, 4 + h:5 + h]),
                                 r=[self.ps_tok[s0], self.ps_tok[s0 + 1], t_stt[b]], w=[t_Pe[b], t_stt[b]])
                        yield
                        s.op("dve", lambda e: e.reciprocal(out=stt[b][:, 8:12], in_=stt[b][:, 4:8]), r=[t_stt[b]], w=[t_stt[b]])
                        ptv = self.bank16(ptb).rearrange("p (j c) -> p j c", j=8)
                        for h in range(4):
                            for mc in range(2):
                                s.op("pe", lambda e, h=h, mc=mc: e.transpose(out=ptv[:, h * 2 + mc, :],
                                                                             in_=Pe[b][:, h, mc * 128:(mc + 1) * 128],
                                                                             identity=self.ident16[:]),
                                     r=[t_Pe[b], self.t_const], w=[self.ps_tok[ptb]])
                        yield
                        s.op("act", lambda e: e.activation(out=PT[b][:], in_=ptv, func=AF.Copy), r=[self.ps_tok[ptb]], w=[t_PT[b]])
                        yield
                        po = gb
                        for h in range(4):
                            for mc in range(2):
                                s.op("pe", lambda e, h=h, mc=mc: e.matmul(self.bank(po)[:, h * 128:(h + 1) * 128],
                                                                          PT[b][:, h * 2 + mc, :], Vm[:, mc, h * 128:(h + 1) * 128],
                                                                          start=(mc == 0), stop=(mc == 1)),
                                     r=[t_PT[b], t_vm], w=[self.ps_tok[po]])
                        yield
                        for h in range(4):
                            s.op("dve", lambda e, h=h: e.scalar_tensor_tensor(out=yx[b][:, h * 128:(h + 1) * 128],
                                                                              in0=self.bank(po)[:, h * 128:(h + 1) * 128],
                                                                              scalar=stt[b][:, 8 + h:9 + h], in1=sg[b][:, h * 128:(h + 1) * 128],
                                                                              op0=ALU.mult, op1=ALU.mult),
                                 r=[self.ps_tok[po], t_stt[b], t_sg[b]], w=[t_yx[b]])
                        yield
                        self.to_ystage(yx[b], t_yx[b], 4, ystage[yb], t_ys[yb], tl, banks=[ptb])
                        yield

                self.lockstep([x_lane(0), x_lane(1)])
                self.store_ystage(ystage[yb], t_ys[yb], 4, ch0, tok0, ntok)
            s.barrier()

    def to_ystage(self, ytile, t_y, nch, ystage, t_ys, tl, banks=None):
        s = self.s
        for c0 in range(0, nch, 8):
            n = min(8, nch - c0)
            pt = self.next_bank() if banks is None else self.pick("pt", banks)
            ptv = self.bank16(pt).rearrange("p (j c) -> p j c", j=8)
            for c in range(n):
                s.op("pe", lambda e, c=c: e.transpose(out=ptv[:, c, :], in_=ytile[:, (c0 + c) * 128:(c0 + c + 1) * 128],
                                                      identity=self.ident16[:]),
                     r=[t_y, self.t_const], w=[self.ps_tok[pt]])
            s.op("act", lambda e: e.activation(out=ystage[:, c0:c0 + n, tl * 128:(tl + 1) * 128], in_=ptv[:, 0:n, :],
                                               func=AF.Copy), r=[self.ps_tok[pt]], w=[t_ys])

    def store_ystage(self, ystage, t_ys, nch, ch0, tok0, ntok):
        dst = self.YT[ch0 * 128:(ch0 + nch) * 128, tok0:tok0 + ntok].rearrange("(c p) n -> p c n", p=128)
        self.s.dma("sp", dst, ystage[:, 0:nch, 0:ntok], r=[t_ys])

    def phase_swa(self, si, L, hT, hT_tok):
        nc, s = self.nc, self.s
        T = L // 128
        w_in = self.w["o_w_in"]
        with ExitStack() as es:
            wkv = self.sb(es, "swkv", [128, 8, 256], BF16)
            wq = self.sb(es, "swq", [128, 8, 1024], BF16)
            wg = self.sb(es, "swg", [128, 8, 1024], BF16)
            t_wkv, t_wq, t_wg = Tok(), Tok(), Tok()
            self.load_w(wkv, t_wkv, w_in, O_DKV, 256)
            self.load_w(wq, t_wq, w_in, O_DQ, 1024)
            self.load_w(wg, t_wg, w_in, O_GD, 1024)
            cos8 = self.sb(es, "cos8", [128, T, 8], F32)
            sin8 = self.sb(es, "sin8", [128, T, 8], F32)
            t_tab = Tok()
            s.dma("sp", cos8[:], self.c_rope[L][2], w=[t_tab])
            s.dma("sp", sin8[:], self.c_rope[L][3], w=[t_tab])
            sink = self.sb(es, "sink", [128, 16], F32)
            t_sink = Tok()
            s.dma("sp", sink[:], self.w["swa_sink"].partition_broadcast(128), w=[t_sink])
            kTd = self.sb(es, "kTd", [128, 2, L], BF16)
            t_kT = [Tok() for _ in range(T)]
            vtok = self.sb(es, "vtok", [128, T, 128], BF16)
            t_v = [Tok() for _ in range(T)]
            raw = [self.sb(es, "sraw%d" % i, [128, 16, 64], F32) for i in range(2)]
            t_raw = [Tok(), Tok()]
            rtmps = [self.sb(es, "srtmp%d" % i, [128, 4, 16, 8], F32) for i in range(2)]
            t_rtmps = [Tok(), Tok()]
            rtmp, t_rtmp = rtmps[0], t_rtmps[0]
            krs = [self.sb(es, "skr%d" % i, [128, 2, 64], BF16) for i in range(2)]
            t_krs = [Tok(), Tok()]
            kds = [self.sb(es, "skd%d" % i, [128, 2, 2, 64], BF16) for i in range(2)]
            t_kds = [Tok(), Tok()]
            def kv_lane(ln):
                b = ln
                pb = 2 + ln
                pt = 6 + ln
                for t in range(ln, T, 2):
                    self.proj_tok(self.bank(pb)[:, 0:256], pb, wkv, t_wkv, 0, 256, hT, [hT_tok[t]],
                                  lambda k: hT[:, k, t * 128:(t + 1) * 128])
                    yield
                    s.op("act", lambda e: e.activation(out=raw[b][:, 0:2, :], in_=self.bank(pb)[:, 0:128].rearrange("p (h d) -> p h d", h=2),
                                                       func=AF.Copy), r=[self.ps_tok[pb]], w=[t_raw[b]])
                    s.op("act", lambda e: e.activation(out=vtok[:, t, :], in_=self.bank(pb)[:, 128:256], func=AF.Copy),
                         r=[self.ps_tok[pb]], w=[t_v[t]])
                    yield
                    self.rope(raw[b][:, 0:2, :], t_raw[b], krs[ln][:], t_krs[ln], rtmps[ln], t_rtmps[ln], cos8[:, t, :], sin8[:, t, :], t_tab, 2, 8, 64)
                    yield
                    for r_ in range(2):
                        s.op("dve", lambda e, r_=r_: e.tensor_copy(out=kds[ln][:, :, r_, :], in_=krs[ln][:]), r=[t_krs[ln]], w=[t_kds[ln]])
                    yield
                    ptv = self.bank16(pt).rearrange("p (j c) -> p j c", j=8)
                    for kv in range(2):
                        s.op("pe", lambda e, kv=kv: e.transpose(out=ptv[:, kv, :], in_=kds[ln][:, kv, :, :].rearrange("p r d -> p (r d)"),
                                                                identity=self.ident16[:]), r=[t_kds[ln], self.t_const], w=[self.ps_tok[pt]])
                    yield
                    s.op("act", lambda e: e.activation(out=kTd[:, :, t * 128:(t + 1) * 128], in_=ptv[:, 0:2, :], func=AF.Copy),
                         r=[self.ps_tok[pt]], w=[t_kT[t]])
                    yield

            self.lockstep([kv_lane(0), kv_lane(1)])
            q16 = [self.sb(es, "sq16%d" % i, [128, 16, 64], BF16) for i in range(2)]
            t_q16 = [Tok(), Tok()]
            qT = [self.sb(es, "sqT%d" % i, [128, 8, 128], BF16) for i in range(2)]
            t_qT = [Tok(), Tok()]
            sgd = [self.sb(es, "sgd%d" % i, [128, 1024], F32) for i in range(2)]
            t_sgd = [Tok(), Tok()]
            Sm = [self.sb(es, "sSm%d" % i, [128, 2, 384], F32) for i in range(2)]
            t_Sm = [Tok(), Tok()]
            Pe = [self.sb(es, "sPe%d" % i, [128, 2, 384], BF16) for i in range(2)]
            t_Pe = [Tok(), Tok()]
            stt = [self.sb(es, "sst%d" % i, [128, 8], F32) for i in range(2)]
            t_stt = [Tok(), Tok()]
            PT = [self.sb(es, "sPT%d" % i, [128, 8, 128], BF16) for i in range(2)]
            t_PT = [Tok(), Tok()]
            dens = [self.sb(es, "sdens%d" % i, [128, 32], F32) for i in range(2)]
            t_dens = [Tok(), Tok()]
            densl = [[self.sb(es, "sdensl%d_%d" % (ln, i), [128, 8], F32) for i in range(2)] for ln in range(2)]
            t_densl = [[Tok(), Tok()], [Tok(), Tok()]]
            yd = [self.sb(es, "syd%d" % i, [128, 1024], BF16) for i in range(2)]
            t_yd = [Tok(), Tok()]
            ystage = [self.sb(es, "systage%d" % i, [128, 8, 512], BF16) for i in range(2)]
            t_ys = [Tok(), Tok()]
            def prologue(tt):
                pbq = tt % 2
                for half in range(2):
                    pb = 2 + half
                    self.proj_tok(self.bank(pb), pb, wq, t_wq, half * 512, 512, hT, [hT_tok[tt]],
                                  lambda k: hT[:, k, tt * 128:(tt + 1) * 128])
                    yield
                    s.op("act", lambda e: e.activation(out=raw[pbq][:, half * 8:(half + 1) * 8, :],
                                                       in_=self.bank(pb).rearrange("p (h d) -> p h d", h=8), func=AF.Copy,
                                                       scale=0.125), r=[self.ps_tok[pb]], w=[t_raw[pbq]])
                    yield
                self.rope(raw[pbq][:], t_raw[pbq], q16[pbq][:], t_q16[pbq], rtmp, t_rtmp, cos8[:, tt, :], sin8[:, tt, :], t_tab, 16, 8, 64)
                yield
                pt = self.pick("pt", [6, 7])
                ptv = self.bank16(pt).rearrange("p (j c) -> p j c", j=8)
                for c in range(8):
                    s.op("pe", lambda e, c=c: e.transpose(out=ptv[:, c, :], in_=q16[pbq][:, 2 * c:2 * c + 2, :].rearrange("p h d -> p (h d)"),
                                                          identity=self.ident16[:]), r=[t_q16[pbq], self.t_const], w=[self.ps_tok[pt]])
                yield
                s.op("act", lambda e: e.activation(out=qT[pbq][:], in_=ptv, func=AF.Copy), r=[self.ps_tok[pt]], w=[t_qT[pbq]])
                yield
                for half in range(2):
                    pb = 2 + half
                    self.proj_tok(self.bank(pb), pb, wg, t_wg, half * 512, 512, hT, [hT_tok[tt]],
                                  lambda k: hT[:, k, tt * 128:(tt + 1) * 128])
                    yield
                    s.op("act", lambda e: e.activation(out=sgd[pbq][:, half * 512:(half + 1) * 512], in_=self.bank(pb), func=AF.Silu),
                         r=[self.ps_tok[pb]], w=[t_sgd[pbq]])
                    yield

            self.lockstep([prologue(0)])
            for t in range(T):
                b = t % 2
                blk, tl = t // 4, t % 4
                yb = blk % 2
                kts = [kt for kt in (t - 1, t, t + 1) if 0 <= kt < T]
                nkt = len(kts)
                nk = 128 * nkt
                m0 = (kts[0] - (t - 1)) * 128
                def swa_lane(ln):
                    pp = ln
                    sb0 = 4 if ln == 0 else 2
                    ptb = 6 + ln
                    for c in range(4 * ln, 4 * ln + 4):
                        kv = c // 4
                        for j in range(2):
                            s.op("pe", lambda e, j=j: e.matmul(self.bank(sb0 + j)[:, 0:nk], qT[b][j * 64:(j + 1) * 64, c, :],
                                                               kTd[j * 64:(j + 1) * 64, kv, kts[0] * 128:kts[0] * 128 + nk],
                                                               start=True, stop=True),
                                 r=[t_qT[b]] + [t_kT[kt] for kt in kts], w=[self.ps_tok[sb0 + j]])
                        yield
                        yield from self.softmax_gen([sb0, sb0 + 1], 2, nk, self.mask_swa[:, m0:m0 + nk], self.t_const, Sm[pp], t_Sm[pp], Pe[pp], t_Pe[pp],
                                                    stt[pp], t_stt[pp], sink_ap=sink[:, 2 * c:2 * c + 2], t_sink=t_sink)
                        s.op("dve", lambda e: e.tensor_copy(out=densl[ln][b][:, 2 * (c % 4):2 * (c % 4) + 2], in_=stt[pp][:, 2:4]), r=[t_stt[pp]], w=[t_densl[ln][b]])
                        yield
                        self.transpose_P(Pe[pp], t_Pe[pp], 2, nkt, PT[pp], t_PT[pp], ptb)
                        yield
                        for j in range(2):
                            hq = 2 * c + j
                            ob = hq // 8
                            for kc in range(nkt):
                                s.op("pe", lambda e, j=j, kc=kc: e.matmul(self.bank(ob)[:, (hq % 8) * 64:(hq % 8 + 1) * 64],
                                                                          PT[pp][:, j * nkt + kc, :], vtok[:, kts[kc], kv * 64:(kv + 1) * 64],
                                                                          start=(kc == 0), stop=(kc == nkt - 1)),
                                     r=[t_PT[pp]] + [t_v[kt] for kt in kts], w=[self.ps_tok[ob]])
                        yield

                self.lockstep([swa_lane(0), swa_lane(1)])
                if t + 1 < T:
                    self.lockstep([prologue(t + 1)])
                for ln in range(2):
                    s.op("dve", lambda e, ln=ln: e.tensor_copy(out=dens[b][:, 8 * ln:8 * ln + 8], in_=densl[ln][b][:]), r=[t_densl[ln][b]], w=[t_dens[b]])
                s.op("dve", lambda e: e.reciprocal(out=dens[b][:, 16:32], in_=dens[b][:, 0:16]), r=[t_dens[b]], w=[t_dens[b]])
                for hq in range(16):
                    ob = hq // 8
                    s.op("dve", lambda e, hq=hq: e.scalar_tensor_tensor(out=yd[b][:, hq * 64:(hq + 1) * 64],
                                                                        in0=self.bank(ob)[:, (hq % 8) * 64:(hq % 8 + 1) * 64],
                                                                        scalar=dens[b][:, 16 + hq:17 + hq], in1=sgd[b][:, hq * 64:(hq + 1) * 64],
                                                                        op0=ALU.mult, op1=ALU.mult),
                         r=[self.ps_tok[ob], t_dens[b], t_sgd[b]], ww=[t_yd[b]])
                self.to_ystage(yd[b], t_yd[b], 8, ystage[yb], t_ys[yb], tl, banks=[6, 7])
                if tl == 3 or t == T - 1:
                    self.store_ystage(ystage[yb], t_ys[yb], 8, 4, blk * 512, (tl + 1) * 128)
            s.barrier()

    def dft_fwd(self, L, x_tok, t_x, ncols, cb):
        s = self.s
        T = L // 128
        gc_d, gs_d = self.c_dft[L]
        with ExitStack() as es:
            tc = [self.sb(es, "tc%d" % i, [128, T, 128], BF16) for i in range(2)]
            ts = [self.sb(es, "ts%d" % i, [128, T, 128], BF16) for i in range(2)]
            t_tab = [Tok(), Tok()]
            s.dma("sp", tc[0][:], gc_d[0], w=[t_tab[0]])
            s.dma("sp", ts[0][:], gs_d[0], w=[t_tab[0]])
            for kc in range(T):
                b = kc % 2
                if kc + 1 < T:
                    s.dma("sp", tc[1 - b][:], gc_d[kc + 1], w=[t_tab[1 - b]])
                    s.dma("sp", ts[1 - b][:], gs_d[kc + 1], w=[t_tab[1 - b]])
                for half in range(ncols // 512):
                    bC = self.pick("dftC", [0, 2, 4, 6])
                    bS = bC + 1
                    for nci in range(T):
                        s.op("pe", lambda e, nci=nci: e.matmul(self.bank(bC), tc[b][:, nci, :], x_tok[:, nci, half * 512:(half + 1) * 512],
                                                               start=(nci == 0), stop=(nci == T - 1)),
                             r=[t_tab[b], t_x], w=[self.ps_tok[bC]])
                    for nci in range(T):
                        s.op("pe", lambda e, nci=nci: e.matmul(self.bank(bS), ts[b][:, nci, :], x_tok[:, nci, half * 512:(half + 1) * 512],
                                                               start=(nci == 0), stop=(nci == T - 1)),
                             r=[t_tab[b], t_x], w=[self.ps_tok[bS]])
                    cb(kc, half, bC, bS)
            s.barrier()

    def phase_filter(self, L):
        nc, s = self.nc, self.s
        T = L // 128
        HR, HI = self.H[L]
        CF, SF = self.CFSF
        c = self.c_filt[L]
        with ExitStack() as es:
            embT = self.sb(es, "embT", [33, L], F32)
            w1 = self.sb(es, "fw1", [33, 64], F32)
            w2 = self.sb(es, "fw2", [64, 64], F32)
            w3 = self.sb(es, "fw3", [64, 2048], F32)
            vec = self.sb(es, "fvec", [64, 3], F32)
            hid1 = self.sb(es, "hid1", [64, L], F32)
            hid2 = self.sb(es, "hid2", [64, L], F32)
            tcol = self.sb(es, "tcol", [128, T], F32)
            dbc = self.sb(es, "dbc", [128, 1024], F32)
            psi = self.sb(es, "psi", [128, 2, T], F32)
            t_c = Tok()
            s.dma("sp", embT[:], c["embT"], w=[t_c])
            s.dma("sp", w1[:], self.w["hy_filt_w1"], w=[t_c])
            s.dma("sp", w2[:], self.w["hy_filt_w2"], w=[t_c])
            s.dma("sp", w3[:], self.w["hy_filt_w3"], w=[t_c])
            s.dma("sp", vec[:], self.w["hy_fvec"], w=[t_c])
            s.dma("sp", tcol[:], c["tcol"], w=[t_c])
            s.dma("sp", dbc[:], self.c_delta.partition_broadcast(128), w=[t_c])
            s.dma("sp", psi[:], c["psi"], w=[t_c])
            arg = [self.sb(es, "farg%d" % i, [64, 512], F32) for i in range(2)]
            sn = [self.sb(es, "fsn%d" % i, [64, 512], F32) for i in range(2)]
            t_arg = [Tok(), Tok()]
            t_hid = Tok()
            for layer_i, (wm, kdim, src, dst, bcol) in enumerate(((w1, 33, embT, hid1, 0), (w2, 64, hid1, hid2, 1))):
                for blk in range(L // 512):
                    b = blk % 2
                    pb = self.pick("f", [4, 5])
                    s.op("pe", lambda e: e.matmul(self.bank(pb)[0:64, :], wm[0:kdim, :], src[0:kdim, blk * 512:(blk + 1) * 512],
                                                  start=True, stop=True), r=[t_c, t_hid], w=[self.ps_tok[pb]])
                    s.op("dve", lambda e: e.tensor_scalar(out=arg[b][:], in0=self.bank(pb)[0:64, :], scalar1=vec[:, bcol:bcol + 1],
                                                          scalar2=vec[:, 2:3], op0=ALU.add, op1=ALU.mult),
                         r=[self.ps_tok[pb], t_c], w=[t_arg[b]])
                    s.op("act", lambda e: e.activation(out=sn[b][:], in_=arg[b][:], func=AF.Sin, scale=1.0 / 3.0), r=[t_arg[b]], w=[t_arg[b]])
                    s.op("dve", lambda e: e.tensor_tensor(out=arg[b][:], in0=sn[b][:], in1=sn[b][:], op=ALU.mult), r=[t_arg[b]], w=[t_arg[b]])
                    s.op("dve", lambda e: e.tensor_scalar(out=arg[b][:], in0=arg[b][:], scalar1=-4.0, scalar2=3.0, op0=ALU.mult, op1=ALU.add),
                         r=[t_arg[b]], w=[t_arg[b]])
                    s.op("dve", lambda e: e.tensor_tensor(out=dst[:, blk * 512:(blk + 1) * 512], in0=sn[b][:], in1=arg[b][:], op=ALU.mult),
                         r=[t_arg[b]], w=[t_hid])
            filt = self.sb(es, "filt", [128, T, 1024], BF16)
            t_filt = Tok()
            dec = [self.sb(es, "fdec%d" % i, [128, 1024], F32) for i in range(2)]
            t_dec = [Tok(), Tok()]
            zc = [self.sb(es, "fzc%d" % i, [128, 4, 512], F32) for i in range(2)]
            t_zc = [Tok(), Tok()]
            ho = [self.sb(es, "fho%d" % i, [128, 2, 512], F32) for i in range(2)]
            t_ho = [Tok(), Tok()]
            for dirn in range(2):
                for t in range(T):
                    b = t % 2
                    s.op("act", lambda e: e.activation(out=dec[b][:], in_=dbc[:], func=AF.Exp, scale=tcol[:, t:t + 1]),
                         r=[t_c], w=[t_dec[b]])
                    for hb in range(2):
                        pb = self.pick("f", [4, 5])
                        s.op("pe", lambda e: e.matmul(self.bank(pb), hid2[:, t * 128:(t + 1) * 128],
                                                      w3[:, dirn * 1024 + hb * 512:dirn * 1024 + (hb + 1) * 512], start=True, stop=True),
                             r=[t_hid, t_c], w=[self.ps_tok[pb]])
                        s.op("dve", lambda e: e.tensor_tensor(out=filt[:, t, hb * 512:(hb + 1) * 512], in0=self.bank(pb),
                                                              in1=dec[b][:, hb * 512:(hb + 1) * 512], op=ALU.mult),
                             r=[self.ps_tok[pb], t_dec[b]], w=[t_filt])
                if dirn == 1:
                    s.op("dve", lambda e: e.memset(filt[0:1, 0, :], 0.0), w=[t_filt])

                def cb(kc, half, bC, bS, dirn=dirn):
                    b = (kc * 2 + half) % 2
                    hs = slice(half * 512, (half + 1) * 512)
                    if dirn == 0:
                        s.op("act", lambda e: e.activation(out=zc[b][:, 0, :], in_=self.bank(bC), func=AF.Copy), r=[self.ps_tok[bC]], w=[t_zc[b]])
                        s.op("act", lambda e: e.activation(out=zc[b][:, 1, :], in_=self.bank(bS), func=AF.Copy), r=[self.ps_tok[bS]], w=[t_zc[b]])
                        s.dma("sp", CF[kc, :, hs], zc[b][:, 0, :], r=[t_zc[b]], w=[self.t_cf])
                        s.dma("sp", SF[kc, :, hs], zc[b][:, 1, :], r=[t_zc[b]], w=[self.t_cf])
                    else:
                        s.dma("sp", zc[b][:, 0, :], CF[kc, :, hs], r=[self.t_cf], w=[t_zc[b]])
                        s.dma("sp", zc[b][:, 1, :], SF[kc, :, hs], r=[self.t_cf], w=[t_zc[b]])
                        Z = zc[b]
                        s.op("dve", lambda e: e.tensor_tensor(out=Z[:, 2, :], in0=Z[:, 0, :], in1=self.bank(bC), op=ALU.add),
                             r=[self.ps_tok[bC], t_zc[b]], w=[t_zc[b]])
                        s.op("dve", lambda e: e.tensor_tensor(out=Z[:, 0, :], in0=Z[:, 0, :], in1=self.bank(bC), op=ALU.subtract),
                             r=[self.ps_tok[bC], t_zc[b]], w=[t_zc[b]])
                        s.op("dve", lambda e: e.tensor_tensor(out=Z[:, 3, :], in0=Z[:, 1, :], in1=self.bank(bS), op=ALU.add),
                             r=[self.ps_tok[bS], t_zc[b]], w=[t_zc[b]])
                        s.op("dve", lambda e: e.tensor_tensor(out=Z[:, 1, :], in0=self.bank(bS), in1=Z[:, 1, :], op=ALU.subtract),
                             r=[self.ps_tok[bS], t_zc[b]], w=[t_zc[b]])
                        cps = psi[:, 0, kc:kc + 1]
                        sps = psi[:, 1, kc:kc + 1]
                        s.op("dve", lambda e: e.tensor_scalar(out=ho[b][:, 0, :], in0=Z[:, 2, :], scalar1=cps, scalar2=None, op0=ALU.mult),
                             r=[t_zc[b], t_c], w=[t_ho[b]])
                        s.op("dve", lambda e: e.scalar_tensor_tensor(out=ho[b][:, 0, :], in0=Z[:, 3, :], scalar=sps, in1=ho[b][:, 0, :],
                                                                     op0=ALU.mult, op1=ALU.add), r=[t_zc[b], t_c, t_ho[b]], w=[t_ho[b]])
                        s.op("dve", lambda e: e.tensor_scalar(out=ho[b][:, 1, :], in0=Z[:, 0, :], scalar1=sps, scalar2=None, op0=ALU.mult),
                             r=[t_zc[b], t_c], w=[t_ho[b]])
                        s.op("dve", lambda e: e.scalar_tensor_tensor(out=ho[b][:, 1, :], in0=Z[:, 1, :], scalar=cps, in1=ho[b][:, 1, :],
                                                                     op0=ALU.mult, op1=ALU.add), r=[t_zc[b], t_c, t_ho[b]], w=[t_ho[b]])
                        s.dma("sp", HR[kc, :, hs], ho[b][:, 0, :], r=[t_ho[b]], w=[self.t_H])
                        s.dma("sp", HI[kc, :, hs], ho[b][:, 1, :], r=[t_ho[b]], w=[self.t_H])

                self.dft_fwd(L, filt, t_filt, 1024, cb)
            s.barrier()

    def phase_hy1(self, si, L, hT, hT_tok):
        nc, s = self.nc, self.s
        w_in = self.w["e_w_in"]
        nblk = L // 512
        with ExitStack() as es:
            cw = self.sb(es, "hcw", [128, 24, 3], F32)
            cbias = self.sb(es, "hcb", [128, 24], F32)
            skip = self.sb(es, "hskip", [128, 8], F32)
            t_c = Tok()
            s.dma("sp", cw[:], self.w["hy_conv_w"], w=[t_c])
            s.dma("sp", cbias[:], self.w["hy_conv_b"], w=[t_c])
            s.dma("sp", skip[:], self.w["hy_skip"], w=[t_c])
            wts = [self.sb(es, "hw%d" % i, [128, 8, 4, 128], BF16) for i in range(2)]
            t_wts = [Tok(), Tok()]
            diag = [self.sb(es, "hdiag%d" % i, [128, 9, 128], BF16) for i in range(2)]
            t_diag = [Tok(), Tok()]
            zT = self.sb(es, "hzT", [128, 3, L + 2], BF16)
            t_zT = Tok()
            s.op("dve", lambda e: e.memset(zT[:, :, 0:1], 0.0), w=[t_zT])
            s.op("dve", lambda e: e.memset(zT[:, :, L + 1:L + 2], 0.0), w=[t_zT])
            t_zTb = [Tok() for _ in range(nblk)]
            xa = [self.sb(es, "hxa%d" % i, [128, 4, 512], F32) for i in range(2)]
            t_xa = [Tok(), Tok()]
            o16 = [self.sb(es, "ho16%d" % i, [128, 3, 512], BF16) for i in range(2)]
            t_o16 = [Tok(), Tok()]
            for c in range(8):
                wt, t_w = wts[c % 2], t_wts[c % 2]
                dg, t_dg = diag[c % 2], t_diag[c % 2]
                for a in range(4):
                    s.dma("pool", wt[:, :, a, :], w_in[:, a * 1024 + c * 128:a * 1024 + (c + 1) * 128].rearrange("(k p) c -> p k c", p=128),
                          w=[t_w])
                for a in range(3):
                    for j in range(3):
                        s.op("dve", lambda e, a=a, j=j: e.tensor_scalar(out=dg[:, a * 3 + j, :], in0=self.ident16[:],
                                                                        scalar1=cw[:, a * 8 + c, j:j + 1], scalar2=None, op0=ALU.mult),
                             r=[t_c, self.t_const], w=[t_dg])
                def h1a_lane(ln):
                    pb = 2 + ln
                    for blk in range(ln, nblk, 2):
                        for a in range(3):
                            self.proj_feat(self.bank(pb), pb, wt[:, :, a, :], t_w, 0, hT, hT_tok, blk * 512, 512)
                            yield
                            s.op("act", lambda e, a=a: e.activation(out=zT[:, a, 1 + blk * 512:1 + (blk + 1) * 512], in_=self.bank(pb), func=AF.Copy),
                                 r=[self.ps_tok[pb]], w=[t_zTb[blk]])
                            yield

                self.lockstep([h1a_lane(0), h1a_lane(1)])

                def h1b_lane(ln):
                    b = ln
                    X = xa[b]
                    O = o16[b]
                    pc = 4 + ln
                    pp = 2 + ln
                    for blk in range(ln, nblk, 2):
                        zdeps = [t_zTb[bk] for bk in (blk - 1, blk, blk + 1) if 0 <= bk < nblk] + [t_zT]
                        self.proj_feat(self.bank(pp), pp, wt[:, :, 3, :], t_w, 0, hT, hT_tok, blk * 512, 512)
                        for a in range(3):
                            for j in range(3):
                                s.op("pe", lambda e, a=a, j=j: e.matmul(self.bank(pc), dg[:, a * 3 + j, :], zT[:, a, blk * 512 + j:blk * 512 + j + 512],
                                                                        start=(j == 0), stop=(j == 2)), r=[t_dg] + zdeps, w=[self.ps_tok[pc]])
                            yield
                            s.op("act", lambda e, a=a: e.activation(out=X[:, a, :], in_=self.bank(pc), func=AF.Identity,
                                                                    bias=cbias[:, a * 8 + c:a * 8 + c + 1]), r=[self.ps_tok[pc], t_c], w=[t_xa[b]])
                            yield
                        s.op("act", lambda e: e.activation(out=X[:, 3, :], in_=self.bank(pp), func=AF.Silu), r=[self.ps_tok[pp]], w=[t_xa[b]])
                        yield
                        s.op("dve", lambda e: e.tensor_tensor(out=X[:, 2, :], in0=X[:, 2, :], in1=X[:, 1, :], op=ALU.mult), r=[t_xa[b]], w=[t_xa[b]])
                        s.op("pool", lambda e: e.tensor_tensor(out=X[:, 0, :], in0=X[:, 0, :], in1=X[:, 3, :], op=ALU.mult), r=[t_xa[b]], w=[t_xa[b]])
                        yield
                        s.op("act", lambda e: e.activation(out=O[:, 0, :], in_=X[:, 2, :], func=AF.Copy), r=[t_xa[b]], w=[t_o16[b]])
                        s.op("act", lambda e: e.activation(out=O[:, 1, :], in_=X[:, 0, :], func=AF.Copy), r=[t_xa[b]], w=[t_o16[b]])
                        s.op("dve", lambda e: e.scalar_tensor_tensor(out=O[:, 2, :], in0=X[:, 2, :], scalar=skip[:, c:c + 1], in1=X[:, 0, :],
                                                                     op0=ALU.mult, op1=ALU.mult), r=[t_xa[b], t_c], w=[t_o16[b]])
                        yield
                        for a, dr in enumerate((self.UT, self.AT, self.BT)):
                            s.dma("sp", dr[c * 128:(c + 1) * 128, blk * 512:(blk + 1) * 512], O[:, a, :], r=[t_o16[b]], w=[self.t_uab])
                        yield

                self.lockstep([h1b_lane(0), h1b_lane(1)])
            s.barrier()

    def phase_hy2(self, si, L):
        nc, s = self.nc, self.s
        T = L // 128
        HR, HI = self.H[L]
        ZR, ZS = self.ZRS
        gc_d, gs_d = self.c_dft[L]
        with ExitStack() as es:
            u_tok = self.sb(es, "u_tok", [128, T, 1024], BF16)
            t_u = Tok()
            with ExitStack() as es2:
                ut = [self.sb(es2, "utl%d" % i, [128, L], BF16) for i in range(2)]
                t_ut = [Tok(), Tok()]
                for c in range(8):
                    b = c % 2
                    s.dma("sp", ut[b][:], self.UT[c * 128:(c + 1) * 128, 0:L], r=[self.t_uab], w=[t_ut[b]])
                    for t0 in range(0, T, 8):
                        n = min(8, T - t0)
                        pt = self.pick("pt", [6, 7])
                        ptv = self.bank16(pt).rearrange("p (j c) -> p j c", j=8)
                        for i in range(n):
                            s.op("pe", lambda e, i=i: e.transpose(out=ptv[:, i, :], in_=ut[b][:, (t0 + i) * 128:(t0 + i + 1) * 128],
                                                                  identity=self.ident16[:]), r=[t_ut[b], self.t_const], w=[self.ps_tok[pt]])
                        s.op("act", lambda e: e.activation(out=u_tok[:, t0:t0 + n, c * 128:(c + 1) * 128], in_=ptv[:, 0:n, :], func=AF.Copy),
                             r=[self.ps_tok[pt]], w=[t_u])
                s.barrier()
            hh = [self.sb(es, "hh%d" % i, [128, 2, 512], F32) for i in range(2)]
            t_hh = [Tok(), Tok()]
            cs = [self.sb(es, "cs%d" % i, [128, 2, 512], F32) for i in range(2)]
            t_cs = [Tok(), Tok()]
            tt = [self.sb(es, "tt%d" % i, [128, 4, 512], F32) for i in range(2)]
            t_tt = [Tok(), Tok()]
            zz = [self.sb(es, "zz%d" % i, [128, 2, 512], BF16) for i in range(2)]
            t_zz = [Tok(), Tok()]

            def cb(kc, half, bC, bS):
                b = (kc * 2 + half) % 2
                hs = slice(half * 512, (half + 1) * 512)
                s.dma("sp", hh[b][:, 0, :], HR[kc, :, hs], r=[self.t_H], w=[t_hh[b]])
                s.dma("sp", hh[b][:, 1, :], HI[kc, :, hs], r=[self.t_H], w=[t_hh[b]])
                s.op("act", lambda e: e.activation(out=cs[b][:, 0, :], in_=self.bank(bC), func=AF.Copy), r=[self.ps_tok[bC]], w=[t_cs[b]])
                s.op("act", lambda e: e.activation(out=cs[b][:, 1, :], in_=self.bank(bS), func=AF.Copy), r=[self.ps_tok[bS]], w=[t_cs[b]])
                TT = tt[b]
                s.op("dve", lambda e: e.tensor_tensor(out=TT[:, 0, :], in0=hh[b][:, 0, :], in1=cs[b][:, 0, :], op=ALU.mult), r=[t_hh[b], t_cs[b]], w=[t_tt[b]])
                s.op("pool", lambda e: e.tensor_tensor(out=TT[:, 1, :], in0=hh[b][:, 1, :], in1=cs[b][:, 1, :], op=ALU.mult), r=[t_hh[b], t_cs[b]], w=[t_tt[b]])
                s.op("pool", lambda e: e.tensor_tensor(out=TT[:, 2, :], in0=hh[b][:, 0, :], in1=cs[b][:, 1, :], op=ALU.mult), r=[t_hh[b], t_cs[b]], w=[t_tt[b]])
                s.op("dve", lambda e: e.tensor_tensor(out=TT[:, 3, :], in0=hh[b][:, 1, :], in1=cs[b][:, 0, :], op=ALU.mult), r=[t_hh[b], t_cs[b]], w=[t_tt[b]])
                s.op("dve", lambda e: e.tensor_tensor(out=zz[b][:, 0, :], in0=TT[:, 0, :], in1=TT[:, 1, :], op=ALU.add), r=[t_tt[b]], w=[t_zz[b]])
                s.op("pool", lambda e: e.tensor_tensor(out=zz[b][:, 1, :], in0=TT[:, 2, :], in1=TT[:, 3, :], op=ALU.subtract), r=[t_tt[b]], w=[t_zz[b]])
                s.dma("sp", ZR[kc, :, hs], zz[b][:, 0, :], r=[t_zz[b]], w=[self.t_Z])
                s.dma("sp", ZS[kc, :, hs], zz[b][:, 1, :], r=[t_zz[b]], w=[self.t_Z])

            self.dft_fwd(L, u_tok, t_u, 1024, cb)
        with ExitStack() as es:
            zr = self.sb(es, "zr", [128, T, 512], BF16)
            zs = self.sb(es, "zs", [128, T, 512], BF16)
            t_z = Tok()
            tc = [self.sb(es, "itc%d" % i, [128, T, 128], BF16) for i in range(2)]
            ts = [self.sb(es, "its%d" % i, [128, T, 128], BF16) for i in range(2)]
            t_tab = [Tok(), Tok()]
            y16 = [self.sb(es, "y16%d" % i, [128, 512], BF16) for i in range(2)]
            t_y16 = [Tok(), Tok()]
            ystage = [self.sb(es, "hystage%d" % i, [128, 4, 512], BF16) for i in range(2)]
            t_ys = [Tok(), Tok()]
            ab = [self.sb(es, "hab%d" % i, [128, 2, 4, 512], BF16) for i in range(2)]
            t_ab = [Tok(), Tok()]
            for ch in range(2):
                cs_ = slice(ch * 512, (ch + 1) * 512)
                s.dma("sp", zr[:], ZR[0:T, :, cs_].rearrange("k p c -> p k c"), r=[self.t_Z], w=[t_z])
                s.dma("sp", zs[:], ZS[0:T, :, cs_].rearrange("k p c -> p k c"), r=[self.t_Z], w=[t_z])
                for nci in range(T):
                    b = nci % 2
                    blk, tl = nci // 4, nci % 4
                    yb = blk % 2
                    if tl == 0:
                        rows = slice(ch * 512, (ch + 1) * 512)
                        s.dma("sp", ab[yb][:, 0, :, :], self.AT[rows, blk * 512:(blk + 1) * 512].rearrange("(c p) n -> p c n", p=128),
                              r=[self.t_uab], w=[t_ab[yb]])
                        s.dma("sp", ab[yb][:, 1, :, :], self.BT[rows, blk * 512:(blk + 1) * 512].rearrange("(c p) n -> p c n", p=128),
                              r=[self.t_uab], w=[t_ab[yb]])
                    if nci == 0:
                        s.dma("sp", tc[0][:], gc_d[0], w=[t_tab[0]])
                        s.dma("sp", ts[0][:], gs_d[0], w=[t_tab[0]])
                    if nci + 1 < T:
                        s.dma("sp", tc[1 - b][:], gc_d[nci + 1], w=[t_tab[1 - b]])
                        s.dma("sp", ts[1 - b][:], gs_d[nci + 1], w=[t_tab[1 - b]])
                    pb = self.pick("inv", [0, 1, 2, 3])
                    for kc in range(T):
                        s.op("pe", lambda e, kc=kc: e.matmul(self.bank(pb), tc[b][:, kc, :], zr[:, kc, :], start=(kc == 0), stop=False),
                             r=[t_tab[b], t_z], w=[self.ps_tok[pb]])
                    for kc in range(T):
                        s.op("pe", lambda e, kc=kc: e.matmul(self.bank(pb), ts[b][:, kc, :], zs[:, kc, :], start=False, stop=(kc == T - 1)),
                             r=[t_tab[b], t_z], w=[self.ps_tok[pb]])
                    s.op("act", lambda e: e.activation(out=y16[b][:], in_=self.bank(pb), func=AF.Copy), r=[self.ps_tok[pb]], w=[t_y16[b]])
                    self.to_ystage(y16[b], t_y16[b], 4, ystage[yb], t_ys[yb], tl, banks=[6, 7])
                    if tl == 3:
                        s.op("dve", lambda e: e.tensor_tensor(out=ystage[yb][:], in0=ystage[yb][:], in1=ab[yb][:, 0, :, :], op=ALU.mult),
                             r=[t_ys[yb], t_ab[yb]], w=[t_ys[yb]])
                        s.op("dve", lambda e: e.tensor_tensor(out=ystage[yb][:], in0=ystage[yb][:], in1=ab[yb][:, 1, :, :], op=ALU.add),
                             r=[t_ys[yb], t_ab[yb]], w=[t_ys[yb]])
                        self.store_ystage(ystage[yb], t_ys[yb], 4, ch * 4, blk * 512, 512)
            s.barrier()

    def phase_gdn1(self, si, L, hT, hT_tok):
        nc, s = self.nc, self.s
        T = L // 128
        nblk = L // 512
        w_in = self.w["e_w_in"]
        with ExitStack() as es:
            cw = self.sb(es, "gcw", [128, 24, 5], F32)
            t_c = Tok()
            s.dma("sp", cw[:], self.w["gdn_conv_w"], w=[t_c])
            ones_r = self.sb(es, "ones_r", [128, 128], F32R)
            s.op("dve", lambda e: e.tensor_scalar(out=ones_r[:], in0=self.ident32[:], scalar1=0.0, scalar2=1.0, op0=ALU.mult, op1=ALU.add),
                 r=[self.t_const], w=[t_c])
            wts = [self.sb(es, "gw%d" % i, [128, 8, 3, 128], BF16) for i in range(2)]
            t_wts = [Tok(), Tok()]
            diag = [self.sb(es, "gdiag%d" % i, [128, 15, 128], BF16) for i in range(2)]
            t_diag = [Tok(), Tok()]
            zT = self.sb(es, "gzT", [128, 3, L + 4], BF16)
            t_zT = Tok()
            s.op("dve", lambda e: e.memset(zT[:, :, 0:2], 0.0), w=[t_zT])
            s.op("dve", lambda e: e.memset(zT[:, :, L + 2:L + 4], 0.0), w=[t_zT])
            t_zTb = [Tok() for _ in range(nblk)]
            xs = [self.sb(es, "gx%d" % i, [128, 3, 512], F32) for i in range(2)]
            t_xs = [Tok(), Tok()]
            sq = [self.sb(es, "gsq%d" % i, [128, 512], F32R) for i in range(2)]
            t_sq = [Tok(), Tok()]
            rs = [self.sb(es, "grs%d" % i, [128, 512], F32) for i in range(2)]
            t_rs = [Tok(), Tok()]
            kst = [self.sb(es, "gkst%d" % i, [128, 4, 128], F32) for i in range(2)]
            t_kst = [Tok(), Tok()]
            for h in range(8):
                wt, t_w = wts[h % 2], t_wts[h % 2]
                dg, t_dg = diag[h % 2], t_diag[h % 2]
                for a in range(3):
                    c0 = E_QKV + a * 1024 + h * 128
                    s.dma("pool", wt[:, :, a, :], w_in[:, c0:c0 + 128].rearrange("(k p) c -> p k c", p=128), w=[t_w])
                    for j in range(5):
                        s.op("dve", lambda e, a=a, j=j: e.tensor_scalar(out=dg[:, a * 5 + j, :], in0=self.ident16[:],
                                                                        scalar1=cw[:, a * 8 + h, j:j + 1], scalar2=None, op0=ALU.mult),
                             r=[t_c, self.t_const], w=[t_dg])
                def g1a_lane(ln):
                    pb = 2 + ln
                    for blk in range(ln, nblk, 2):
                        for a in range(3):
                            self.proj_feat(self.bank(pb), pb, wt[:, :, a, :], t_w, 0, hT, hT_tok, blk * 512, 512)
                            yield
                            s.op("act", lambda e, a=a: e.activation(out=zT[:, a, 2 + blk * 512:2 + (blk + 1) * 512], in_=self.bank(pb), func=AF.Copy),
                                 r=[self.ps_tok[pb]], w=[t_zTb[blk]])
                            yield

                self.lockstep([g1a_lane(0), g1a_lane(1)])

                def g1b_lane(ln):
                    b = ln
                    X = xs[b]
                    pc = 4 + ln
                    pss = ln
                    pt = 6 + ln
                    for blk in range(ln, nblk, 2):
                        zdeps = [t_zTb[bk] for bk in (blk - 1, blk, blk + 1) if 0 <= bk < nblk] + [t_zT]
                        for a in range(3):
                            for j in range(5):
                                s.op("pe", lambda e, a=a, j=j: e.matmul(self.bank(pc), dg[:, a * 5 + j, :], zT[:, a, blk * 512 + j:blk * 512 + j + 512],
                                                                        start=(j == 0), stop=(j == 4)), r=[t_dg] + zdeps, w=[self.ps_tok[pc]])
                            yield
                            s.op("act", lambda e, a=a: e.activation(out=X[:, a, :], in_=self.bank(pc), func=AF.Silu), r=[self.ps_tok[pc]], w=[t_xs[b]])
                            yield
                        for a in range(2):
                            s.op("dve", lambda e, a=a: e.tensor_tensor(out=sq[b][:], in0=X[:, a, :], in1=X[:, a, :], op=ALU.mult), r=[t_xs[b]], w=[t_sq[b]])
                            yield
                            s.op("pe", lambda e: e.matmul(self.bank(pss), ones_r[:], sq[b][:], start=True, stop=True), r=[t_c, t_sq[b]], w=[self.ps_tok[pss]])
                            yield
                            s.op("act", lambda e: e.activation(out=rs[b][:], in_=self.bank(pss), func=AF.Ln, bias=self.eps_col[:, 0:1]), r=[self.ps_tok[pss], self.t_const], w=[t_rs[b]])
                            yield
                            s.op("act", lambda e: e.activation(out=rs[b][:], in_=rs[b][:], func=AF.Exp, scale=-0.5), r=[t_rs[b]], w=[t_rs[b]])
                            yield
                            sc = (128.0 ** -0.5) if a == 0 else 1.0
                            s.op("dve", lambda e, a=a: e.scalar_tensor_tensor(out=X[:, a, :], in0=X[:, a, :], scalar=sc, in1=rs[b][:],
                                                                              op0=ALU.mult, op1=ALU.mult), r=[t_xs[b], t_rs[b]], w=[t_xs[b]])
                            dr = self.QT if a == 0 else self.KT
                            s.dma("sp", dr[h * 128:(h + 1) * 128, blk * 512:(blk + 1) * 512], X[:, a, :], r=[t_xs[b]], w=[self.t_gd])
                            yield
                        for a in (1, 2):
                            ptv = self.bank(pt).rearrange("p (j c) -> p j c", j=4)
                            for i in range(4):
                                s.op("pe", lambda e, a=a, i=i: e.transpose(out=ptv[:, i, :], in_=X[:, a, i * 128:(i + 1) * 128], identity=self.ident32[:]),
                                     r=[t_xs[b], self.t_const], w=[self.ps_tok[pt]])
                            yield
                            s.op("act", lambda e: e.activation(out=kst[b][:], in_=ptv, func=AF.Copy), r=[self.ps_tok[pt]], w=[t_kst[b]])
                            dr = self.KTOK if a == 1 else self.VTOK
                            s.dma("sp", dr[blk * 512:(blk + 1) * 512, h * 128:(h + 1) * 128].rearrange("(i p) d -> p i d", p=128), kst[b][:],
                                  r=[t_kst[b]], w=[self.t_gd])
                            yield

                self.lockstep([g1b_lane(0), g1b_lane(1)])
            wg = self.sb(es, "gwg", [128, 8, 1024], BF16)
            wbg = self.sb(es, "gwbg", [128, 8, 32], BF16)
            t_wg = Tok()
            self.load_w(wg, t_wg, w_in, E_GG, 1024)
            self.load_w(wbg, t_wg, w_in, E_BETA, 32)
            rows = self.sb(es, "grows", [128, 2, 16], F32)
            s.dma("sp", rows[:, 0, :], self.w["gdn_A_log"].partition_broadcast(128), w=[t_c])
            s.dma("sp", rows[:, 1, :], self.w["gdn_dt_bias"].partition_broadcast(128), w=[t_c])
            s.op("act", lambda e: e.activation(out=rows[:, 0, :], in_=rows[:, 0, :], func=AF.Exp), r=[t_c], w=[t_c])
            s.op("dve", lambda e: e.tensor_scalar(out=rows[:, 0, :], in0=rows[:, 0, :], scalar1=-1.0, scalar2=None, op0=ALU.mult), r=[t_c], w=[t_c])
            sg = [self.sb(es, "gsg%d" % i, [128, 1024], BF16) for i in range(2)]
            t_sg = [Tok(), Tok()]
            bgt = [self.sb(es, "gbgt%d" % i, [128, 4, 16], F32) for i in range(2)]
            t_bgt = [Tok(), Tok()]
            bgo = [self.sb(es, "gbgo%d" % i, [128, 32], F32) for i in range(2)]
            t_bgo = [Tok(), Tok()]
            for t in range(T):
                b = t % 2
                for half in range(2):
                    pb = 2 + half
                    self.proj_tok(self.bank(pb), pb, wg, t_wg, half * 512, 512, hT, [hT_tok[t]], lambda k: hT[:, k, t * 128:(t + 1) * 128])
                    s.op("act", lambda e: e.activation(out=sg[b][:, half * 512:(half + 1) * 512], in_=self.bank(pb), func=AF.Silu),
                         r=[self.ps_tok[pb]], w=[t_sg[b]])
                s.dma("sp", self.SGG[t * 128:(t + 1) * 128, :], sg[b][:], r=[t_sg[b]], w=[self.t_gd])
                pb = self.pick("conv", [4, 5])
                self.proj_tok(self.bank(pb)[:, 0:32], pb, wbg, t_wg, 0, 32, hT, [hT_tok[t]], lambda k: hT[:, k, t * 128:(t + 1) * 128])
                B = bgt[b]
                s.op("act", lambda e: e.activation(out=bgo[b][:, 0:16], in_=self.bank(pb)[:, 0:16], func=AF.Sigmoid), r=[self.ps_tok[pb]], w=[t_bgo[b]])
                s.op("dve", lambda e: e.tensor_tensor(out=B[:, 0, :], in0=self.bank(pb)[:, 16:32], in1=rows[:, 1, :], op=ALU.add), r=[self.ps_tok[pb], t_c], w=[t_bgt[b]])
                s.op("act", lambda e: e.activation(out=B[:, 1, :], in_=B[:, 0, :], func=AF.Abs), r=[t_bgt[b]], w=[t_bgt[b]])
                s.op("act", lambda e: e.activation(out=B[:, 1, :], in_=B[:, 1, :], func=AF.Exp, scale=-1.0), r=[t_bgt[b]], w=[t_bgt[b]])
                s.op("act", lambda e: e.activation(out=B[:, 1, :], in_=B[:, 1, :], func=AF.Ln, bias=self.eps_col[:, 1:2]), r=[t_bgt[b], self.t_const], w=[t_bgt[b]])
                s.op("dve", lambda e: e.tensor_scalar(out=B[:, 2, :], in0=B[:, 0, :], scalar1=0.0, scalar2=None, op0=ALU.max), r=[t_bgt[b]], w=[t_bgt[b]])
                s.op("dve", lambda e: e.tensor_tensor(out=B[:, 2, :], in0=B[:, 2, :], in1=B[:, 1, :], op=ALU.add), r=[t_bgt[b]], w=[t_bgt[b]])
                s.op("dve", lambda e: e.tensor_tensor(out=bgo[b][:, 16:32], in0=B[:, 2, :], in1=rows[:, 0, :], op=ALU.mult), r=[t_bgt[b], t_c], w=[t_bgo[b]])
                s.dma("sp", self.BG[t * 128:(t + 1) * 128, :], bgo[b][:], r=[t_bgo[b]], w=[self.t_gd])
            s.barrier()

    @staticmethod
    def lockstep(gens):
        gens = list(gens)
        while gens:
            for g in list(gens):
                try:
                    next(g)
                except StopIteration:
                    gens.remove(g)

    def phase_gdn2(self, si, L):
        self.phase_gdnP(L)
        self.phase_gdnR(L)
        self.phase_gdnC(L)

    def phase_gdnP(self, L):
        nc, s = self.nc, self.s
        T = L // 128
        QT3 = self.QT.rearrange("(h d) n -> d h n", d=128)
        KT3 = self.KT.rearrange("(h d) n -> d h n", d=128)
        with ExitStack() as es:
            cm = self.sb(es, "gmask", [128, 18, 128], F32)
            t_c = Tok()
            s.dma("sp", cm[:], self.c_gmask, w=[t_c])
            ones32 = self.sb(es, "ones32", [128, 128], F32)
            s.op("dve", lambda e: e.memset(ones32[:], 1.0), w=[t_c])
            ident_r = self.sb(es, "ident_r", [128, 128], F32R)
            s.op("dve", lambda e: e.tensor_copy(out=ident_r[:], in_=self.ident32[:]), r=[self.t_const], w=[t_c])
            idb = self.ident32[:].unsqueeze(1).to_broadcast([128, 4, 128])
            units = [(dirn, t, qd) for dirn in range(2) for t in range(T) for qd in range(2)]
            NL = 4

            def lane(li):
                def A(nm, dt=F32, shp=(128, 4, 128)):
                    return self.sb(es, "L%d%s" % (li, nm), list(shp), dt), Tok()
                lq, t_lq = A("lq")
                lk, t_lk = A("lk")
                qr, t_qr = A("qr", F32R)
                kr, t_kr = A("kr", F32R)
                lkt, t_lkt = A("lkt")
                lvt, t_lvt = A("lvt")
                ET, t_ET = A("ET")
                Bt, t_Bt = A("Bt")
                Ct, t_Ct = A("Ct")
                Bm, t_Bm = A("Bm", F32R)
                Cm, t_Cm = A("Cm", F32R)
                P, t_P = A("P", F32R)
                Q, t_Q = A("Q", F32R)
                Wn, t_Wn = A("Wn", F32R)
                Vn, t_Vn = A("Vn", F32R)
                QKm, t_QKm = A("QKm")
                ub, t_ub = A("ub")
                wT, t_wT = A("wT")
                kd, t_kd = A("kd")
                bg, t_bg = A("bg", F32, (128, 32))
                sc, t_sc = A("sc", F32, (128, 4, 4))
                scc, t_scc = A("scc", F32, (64, 2, 4))
                glb, t_glb = A("glb", F32, (128, 4, 2))
                Dm, t_Dm = lq, t_lq
                dgG, t_dgG = lk, t_lk
                vr, t_vr = qr, t_qr
                kg, t_kg = kr, t_kr
                bA, bB = 2 * li, 2 * li + 1
                pA, pB = self.ps_tok[bA], self.ps_tok[bB]
                v4 = lambda bk: self.bank(bk).rearrange("p (h c) -> p h c", h=4)
                for ui in range(li, len(units), NL):
                    dirn, t, qd = units[ui]
                    mo = 8 * dirn
                    mo2 = 8 * (1 - dirn)
                    hs = slice(qd * 4, qd * 4 + 4)
                    cs_ = slice(t * 128, (t + 1) * 128)
                    fs = slice(qd * 512, (qd + 1) * 512)
                    s.dma("sp", lq[:], QT3[:, hs, cs_], r=[self.t_gd], w=[t_lq])
                    s.dma("sp", lk[:], KT3[:, hs, cs_], r=[self.t_gd], w=[t_lk])
                    s.dma("sp", lkt[:], self.KTOK[cs_, fs].rearrange("p (h d) -> p h d", h=4), r=[self.t_gd], w=[t_lkt])
                    s.dma("sp", lvt[:], self.VTOK[cs_, fs].rearrange("p (h d) -> p h d", h=4), r=[self.t_gd], w=[t_lvt])
                    s.dma("sp", bg[:], self.BG[cs_, :], r=[self.t_gd], w=[t_bg])
                    yield
                    s.op("act", lambda e: e.activation(out=qr[:], in_=lq[:], func=AF.Copy), r=[t_lq], w=[t_qr])
                    s.op("act", lambda e: e.activation(out=kr[:], in_=lk[:], func=AF.Copy), r=[t_lk], w=[t_kr])
                    beta = bg[:, dirn * 8 + qd * 4:dirn * 8 + qd * 4 + 4]
                    gcol = bg[:, 16 + dirn * 8 + qd * 4:16 + dirn * 8 + qd * 4 + 4]
                    s.op("pe", lambda e: e.matmul(self.bank(bA)[:, 0:4], cm[:, 16 + dirn, :], gcol, start=True, stop=True),
                         r=[t_bg, t_c], w=[pA])
                    for cc in range(2):
                        s.op("pe", lambda e, cc=cc: e.matmul(self.bank(bA)[0:64, 16 + cc * 4:20 + cc * 4], cm[:, 16 + dirn, cc * 64:(cc + 1) * 64], gcol,
                                                             start=True, stop=True), r=[t_bg, t_c], w=[pA])
                    for hh in range(4):
                        s.op("pe", lambda e, hh=hh: e.matmul(self.bank(bB)[:, hh * 128:(hh + 1) * 128], kr[:, hh, :], kr[:, hh, :], start=True, stop=True),
                             r=[t_kr], w=[pB])
                    yield
                    s.op("dve", lambda e: e.tensor_copy(out=sc[:, :, 0], in_=self.bank(bA)[:, 0:4]), r=[pA], w=[t_sc])
                    s.op("act", lambda e: e.activation(out=sc[:, :, 1], in_=self.bank(bA)[:, 0:4], func=AF.Exp), r=[pA], w=[t_sc])
                    s.op("act", lambda e: e.activation(out=scc[:], in_=self.bank(bA)[0:64, 16:24].rearrange("p (c h) -> p c h", c=2), func=AF.Exp),
                         r=[pA], w=[t_scc])
                    yield
                    for hh in range(4):
                        s.op("dve", lambda e, hh=hh: e.tensor_scalar(out=dgG[:, hh, :], in0=self.ident32[:], scalar1=sc[:, hh, 0:1], scalar2=None, op0=ALU.mult),
                             r=[t_sc, self.t_const], w=[t_dgG])
                    s.op("pe", lambda e: e.matmul(self.bank(bA), ones32[:], dgG[:].rearrange("p h c -> p (h c)"), start=True, stop=True),
                         r=[t_c, t_dgG], w=[pA])
                    yield
                    gbc = v4(bA)
                    lastc = (63, 127) if dirn == 0 else (0, 64)
                    s.op("dve", lambda e: e.tensor_tensor(out=Dm[:], in0=gbc, in1=sc[:, :, 0:1].to_broadcast([128, 4, 128]), op=ALU.subtract),
                         r=[pA, t_sc], w=[t_Dm])
                    s.op("act", lambda e: e.activation(out=glb[:], in_=gbc[:, :, lastc[0]:lastc[1] + 1:64], func=AF.Exp), r=[pA], w=[t_glb])
                    for cc in range(2):
                        rr = slice(cc * 64, cc * 64 + 64)
                        s.op("dve", lambda e, cc=cc, rr=rr: e.tensor_tensor(out=sc[rr, :, 2], in0=gbc[rr, :, lastc[cc]], in1=sc[rr, :, 0], op=ALU.subtract),
                             r=[pA, t_sc], w=[t_sc])
                    yield
                    s.op("dve", lambda e: e.scalar_tensor_tensor(out=Dm[:], in0=Dm[:], scalar=0.0, in1=cm[:, mo + 0, :].unsqueeze(1).to_broadcast([128, 4, 128]),
                                                                 op0=ALU.min, op1=ALU.add), r=[t_Dm, t_c], w=[t_Dm])
                    s.op("act", lambda e: e.activation(out=sc[:, :, 2], in_=sc[:, :, 2], func=AF.Exp), r=[t_sc], w=[t_sc])
                    yield
                    s.op("act", lambda e: e.activation(out=ET[:], in_=Dm[:], func=AF.Exp), r=[t_Dm], w=[t_ET])
                    yield
                    s.op("dve", lambda e: e.tensor_tensor(out=Bt[:], in0=v4(bB), in1=ET[:], op=ALU.mult), r=[pB, t_ET], w=[t_Bt])
                    yield
                    for hh in range(4):
                        s.op("pe", lambda e, hh=hh: e.matmul(self.bank(bB)[:, hh * 128:(hh + 1) * 128], kr[:, hh, :], qr[:, hh, :], start=True, stop=True),
                             r=[t_kr, t_qr], w=[pB])
                    s.op("dve", lambda e: e.tensor_tensor(out=Bt[:], in0=Bt[:], in1=cm[:, mo + 1, :].unsqueeze(1).to_broadcast([128, 4, 128]), op=ALU.mult),
                         r=[t_Bt, t_c], w=[t_Bt])
                    yield
                    s.op("dve", lambda e: e.tensor_tensor(out=Bt[:], in0=Bt[:], in1=beta.unsqueeze(2).to_broadcast([128, 4, 128]), op=ALU.mult),
                         r=[t_Bt, t_bg], w=[t_Bt])
                    s.op("dve", lambda e: e.tensor_tensor(out=QKm[:], in0=v4(bB), in1=ET[:], op=ALU.mult), r=[pB, t_ET], w=[t_QKm])
                    s.dma("sp", self.G_QKM[dirn, t, :, hs, :], QKm[:], r=[t_QKm], w=[self.t_gp])
                    yield
                    for hh in range(4):
                        s.op("pe", lambda e, hh=hh: e.transpose(out=self.bank(bA)[:, hh * 128:(hh + 1) * 128], in_=Bt[:, hh, :], identity=self.ident32[:]),
                             r=[t_Bt, self.t_const], w=[pA])
                    s.op("pool", lambda e: e.tensor_tensor(out=kd[:], in0=lkt[:], in1=sc[:, :, 2:3].to_broadcast([128, 4, 128]), op=ALU.mult),
                         r=[t_lkt, t_sc], w=[t_kd])
                    s.dma("sp", self.G_KD[dirn, t, :, fs].rearrange("p (h d) -> p h d", h=4), kd[:], r=[t_kd], w=[self.t_gp])
                    s.dma("sp", self.G_SCC[dirn, t, :, :, hs], scc[:], r=[t_scc], w=[self.t_gp])
                    s.dma("sp", self.G_GLB[dirn, t, :, hs, :], glb[:], r=[t_glb], w=[self.t_gp])
                    yield
                    s.op("act", lambda e: e.activation(out=Ct[:], in_=v4(bA), func=AF.Copy), r=[pA], w=[t_Ct])
                    s.op("dve", lambda e: e.tensor_tensor(out=Bm[:], in0=Bt[:], in1=cm[:, mo + 2, :].unsqueeze(1).to_broadcast([128, 4, 128]), op=ALU.mult),
                         r=[t_Bt, t_c], w=[t_Bm])
                    yield
                    s.op("dve", lambda e: e.tensor_tensor(out=Cm[:], in0=Ct[:], in1=cm[:, mo2 + 2, :].unsqueeze(1).to_broadcast([128, 4, 128]), op=ALU.mult),
                         r=[t_Ct, t_c], w=[t_Cm])
                    s.op("dve", lambda e: e.scalar_tensor_tensor(out=P[:], in0=Bm[:], scalar=-1.0, in1=idb, op0=ALU.mult, op1=ALU.add),
                         r=[t_Bm, self.t_const], w=[t_P])
                    yield
                    s.op("dve", lambda e: e.scalar_tensor_tensor(out=Q[:], in0=Cm[:], scalar=-1.0, in1=idb, op0=ALU.mult, op1=ALU.add),
                         r=[t_Cm, self.t_const], w=[t_Q])
                    yield
                    for lv in range(1, 6):
                        last = (lv == 5)
                        s.op("dve", lambda e, lv=lv: e.tensor_tensor(out=Bm[:], in0=Bt[:], in1=cm[:, mo + 2 + lv, :].unsqueeze(1).to_broadcast([128, 4, 128]),
                                                                     op=ALU.mult), r=[t_Bt, t_c], w=[t_Bm])
                        if not last:
                            s.op("pool", lambda e, lv=lv: e.tensor_tensor(out=Cm[:], in0=Ct[:], in1=cm[:, mo2 + 2 + lv, :].unsqueeze(1).to_broadcast([128, 4, 128]),
                                                                          op=ALU.mult), r=[t_Ct, t_c], w=[t_Cm])
                        yield
                        for hh in range(4):
                            s.op("pe", lambda e, hh=hh: e.matmul(self.bank(bA)[:, hh * 128:(hh + 1) * 128], Bm[:, hh, :], Q[:, hh, :], start=True, stop=True),
                                 r=[t_Bm, t_Q], w=[pA])
                        if not last:
                            for hh in range(4):
                                s.op("pe", lambda e, hh=hh: e.matmul(self.bank(bB)[:, hh * 128:(hh + 1) * 128], Cm[:, hh, :], P[:, hh, :], start=True, stop=True),
                                     r=[t_Cm, t_P], w=[pB])
                        yield
                        s.op("act", lambda e: e.activation(out=Wn[:], in_=v4(bA), func=AF.Copy, scale=-1.0), r=[pA], w=[t_Wn])
                        if not last:
                            s.op("act", lambda e: e.activation(out=Vn[:], in_=v4(bB), func=AF.Copy, scale=-1.0), r=[pB], w=[t_Vn])
                        yield
                        for hh in range(4):
                            s.op("pe", lambda e, hh=hh: e.matmul(self.bank(bA)[:, hh * 128:(hh + 1) * 128], Wn[:, hh, :], P[:, hh, :], start=True, stop=True),
                                 r=[t_Wn, t_P], w=[pA])
                        if not last:
                            for hh in range(4):
                                s.op("pe", lambda e, hh=hh: e.matmul(self.bank(bB)[:, hh * 128:(hh + 1) * 128], Vn[:, hh, :], Q[:, hh, :], start=True, stop=True),
                                     r=[t_Vn, t_Q], w=[pB])
                        yield
                        s.op("dve", lambda e: e.tensor_tensor(out=P[:], in0=P[:].bitcast(F32), in1=v4(bA), op=ALU.add), r=[pA, t_P], w=[t_P])
                        if not last:
                            s.op("dve", lambda e: e.tensor_tensor(out=Q[:], in0=Q[:].bitcast(F32), in1=v4(bB), op=ALU.add), r=[pB, t_Q], w=[t_Q])
                        yield
                    s.op("dve", lambda e: e.tensor_tensor(out=kg[:], in0=lkt[:], in1=sc[:, :, 1:2].to_broadcast([128, 4, 128]), op=ALU.mult),
                         r=[t_lkt, t_sc], w=[t_kg])
                    s.op("act", lambda e: e.activation(out=vr[:], in_=lvt[:], func=AF.Copy), r=[t_lvt], w=[t_vr])
                    yield
                    for hh in range(4):
                        s.op("pe", lambda e, hh=hh: e.matmul(self.bank(bA)[:, hh * 128:(hh + 1) * 128], P[:, hh, :], vr[:, hh, :], start=True, stop=True),
                             r=[t_P, t_vr], w=[pA])
                        s.op("pe", lambda e, hh=hh: e.matmul(self.bank(bB)[:, hh * 128:(hh + 1) * 128], kg[:, hh, :], P[:, hh, :], start=True, stop=True),
                             r=[t_P, t_kg], w=[pB])
                    yield
                    s.op("dve", lambda e: e.tensor_tensor(out=ub[:], in0=v4(bA), in1=beta.unsqueeze(2).to_broadcast([128, 4, 128]), op=ALU.mult),
                         r=[pA, t_bg], w=[t_ub])
                    s.op("act", lambda e: e.activation(out=wT[:], in_=v4(bB), func=AF.Copy), r=[pB], w=[t_wT])
                    s.dma("sp", self.G_UB[dirn, t, :, fs].rearrange("p (h d) -> p h d", h=4), ub[:], r=[t_ub], w=[self.t_gp])
                    s.dma("sp", self.G_WT[dirn, t, :, hs, :], wT[:], r=[t_wT], w=[self.t_gp])
                    yield

            self.lockstep([lane(i) for i in range(NL)])
            s.barrier()

    def phase_gdnR(self, L):
        nc, s = self.nc, self.s
        T = L // 128
        QT3 = self.QT.rearrange("(h d) n -> d h n", d=128)
        with ExitStack() as es:
            def lane(li):
                dirn, qd = li // 2, li % 2
                hs = slice(qd * 4, qd * 4 + 4)
                fs = slice(qd * 512, (qd + 1) * 512)

                def A(nm, dt=F32, shp=(128, 4, 128)):
                    return self.sb(es, "R%d%s" % (li, nm), list(shp), dt), Tok()
                ldb = []
                for i in range(2):
                    ldb.append(dict(wT=A("lwT%d" % i), qk=A("lqk%d" % i), kd=A("lkd%d" % i), ub=A("lub%d" % i), q=A("lq%d" % i),
                                    bg=A("lbg%d" % i, F32, (128, 32)), scc=A("lscc%d" % i, F32, (64, 2, 4)), glb=A("lglb%d" % i, F32, (128, 4, 2))))
                wTr, t_wTr = A("wTr", F32R)
                qkr, t_qkr = A("qkr", F32R)
                kdr, t_kdr = A("kdr", F32R)
                qr, t_qr = A("qr", F32R)
                nb, t_nb = A("nb", F32, (128, 4))
                S, t_S = A("S", F32R)
                vn, t_vn = A("vn", F32R)
                o1s, t_o1s = A("o1s", F32, (64, 4, 128))
                o2s, t_o2s = A("o2s", F32, (64, 4, 128))
                OTs = [A("OT%d" % i, F32, (64, 2, 512)) for i in range(2)]
                bA, bB = 2 * li, 2 * li + 1
                pA, pB = self.ps_tok[bA], self.ps_tok[bB]
                v4 = lambda bk: self.bank(bk).rearrange("p (h c) -> p h c", h=4)
                s.op("dve", lambda e: e.tensor_scalar(out=S[:], in0=self.ident32[:].unsqueeze(1).to_broadcast([128, 4, 128]), scalar1=0.0, scalar2=None,
                                                      op0=ALU.mult), r=[self.t_const], w=[t_S])
                tiles = list(range(T)) if dirn == 0 else list(range(T - 1, -1, -1))

                def load(it):
                    t = tiles[it]
                    Ld = ldb[it % 2]
                    cs_ = slice(t * 128, (t + 1) * 128)
                    s.dma("sp", Ld["wT"][0][:], self.G_WT[dirn, t, :, hs, :], r=[self.t_gp], w=[Ld["wT"][1]])
                    s.dma("sp", Ld["qk"][0][:], self.G_QKM[dirn, t, :, hs, :], r=[self.t_gp], w=[Ld["qk"][1]])
                    s.dma("sp", Ld["kd"][0][:], self.G_KD[dirn, t, :, fs].rearrange("p (h d) -> p h d", h=4), r=[self.t_gp], w=[Ld["kd"][1]])
                    s.dma("sp", Ld["ub"][0][:], self.G_UB[dirn, t, :, fs].rearrange("p (h d) -> p h d", h=4), r=[self.t_gp], w=[Ld["ub"][1]])
                    s.dma("sp", Ld["q"][0][:], QT3[:, hs, cs_], r=[self.t_gd], w=[Ld["q"][1]])
                    s.dma("sp", Ld["bg"][0][:], self.BG[cs_, :], r=[self.t_gd], w=[Ld["bg"][1]])
                    s.dma("sp", Ld["scc"][0][:], self.G_SCC[dirn, t, :, :, hs], r=[self.t_gp], w=[Ld["scc"][1]])
                    s.dma("sp", Ld["glb"][0][:], self.G_GLB[dirn, t, :, hs, :], r=[self.t_gp], w=[Ld["glb"][1]])

                load(0)
                for it, t in enumerate(tiles):
                    Ld = ldb[it % 2]
                    if it + 1 < T:
                        load(it + 1)
                    OT, t_OT = OTs[it % 2]
                    s.op("pool", lambda e: e.tensor_copy(out=wTr[:], in_=Ld["wT"][0][:]), r=[Ld["wT"][1]], w=[t_wTr])
                    s.op("act", lambda e: e.activation(out=qr[:], in_=Ld["q"][0][:], func=AF.Copy), r=[Ld["q"][1]], w=[t_qr])
                    s.op("pool", lambda e: e.tensor_copy(out=qkr[:], in_=Ld["qk"][0][:]), r=[Ld["qk"][1]], w=[t_qkr])
                    s.op("pool", lambda e: e.tensor_copy(out=kdr[:], in_=Ld["kd"][0][:]), r=[Ld["kd"][1]], w=[t_kdr])
                    s.op("dve", lambda e: e.tensor_scalar(out=nb[:], in0=Ld["bg"][0][:, dirn * 8 + qd * 4:dirn * 8 + qd * 4 + 4], scalar1=-1.0, scalar2=None,
                                                          op0=ALU.mult), r=[Ld["bg"][1]], w=[t_nb])
                    ubv, t_ubv = Ld["ub"]
                    sccv, t_sccv = Ld["scc"]
                    glbv, t_glbv = Ld["glb"]
                    yield
                    for cc in ((0, 1) if dirn == 0 else (1, 0)):
                        rr = slice(cc * 64, cc * 64 + 64)
                        M = (cc + 1) * 64
                        for hh in range(4):
                            s.op("pe", lambda e, hh=hh: e.matmul(self.bank(bA)[0:M, hh * 128:(hh + 1) * 128], wTr[:, hh, 0:M], S[:, hh, :], start=True, stop=True),
                                 r=[t_wTr, t_S], w=[pA])
                        for hh in range(4):
                            s.op("pe", lambda e, hh=hh: e.matmul(self.bank(bB)[0:64, hh * 128:(hh + 1) * 128], qr[:, hh, rr], S[:, hh, :], start=True, stop=True),
                                 r=[t_qr, t_S], w=[pB])
                        yield
                        for hh in range(4):
                            s.op("dve", lambda e, hh=hh: e.scalar_tensor_tensor(out=vn[rr, hh, :], in0=self.bank(bA)[rr, hh * 128:(hh + 1) * 128],
                                                                                scalar=nb[rr, hh:hh + 1], in1=ubv[rr, hh, :], op0=ALU.mult, op1=ALU.add),
                                 r=[pA, t_nb, t_ubv], w=[t_vn])
                        s.op("act", lambda e: e.activation(out=o1s[:], in_=self.bank(bB)[0:64, :].rearrange("p (h c) -> p h c", h=4), func=AF.Copy),
                             r=[pB], w=[t_o1s])
                        yield
                        for hh in range(4):
                            s.op("pe", lambda e, hh=hh: e.matmul(self.bank(bB)[:, hh * 128:(hh + 1) * 128], kdr[rr, hh, :], vn[rr, hh, :], start=True, stop=True),
                                 r=[t_kdr, t_vn], w=[pB])
                        for hh in range(4):
                            s.op("pe", lambda e, hh=hh: e.matmul(self.bank(bA)[0:64, hh * 128:(hh + 1) * 128], qkr[rr, hh, rr], vn[rr, hh, :], start=True, stop=True),
                                 r=[t_qkr, t_vn], w=[pA])
                        yield
                        for hh in range(4):
                            s.op("dve", lambda e, hh=hh: e.scalar_tensor_tensor(out=S[:, hh, :], in0=S[:, hh, :].bitcast(F32), scalar=glbv[:, hh, cc:cc + 1],
                                                                                in1=self.bank(bB)[:, hh * 128:(hh + 1) * 128], op0=ALU.mult, op1=ALU.add),
                                 r=[pB, t_glbv, t_S], w=[t_S])
                        s.op("act", lambda e: e.activation(out=o2s[:], in_=self.bank(bA)[0:64, :].rearrange("p (h c) -> p h c", h=4), func=AF.Copy),
                             r=[pA], w=[t_o2s])
                        yield
                        for hh in range(4):
                            s.op("dve", lambda e, hh=hh: e.scalar_tensor_tensor(out=OT[:, cc, hh * 128:(hh + 1) * 128], in0=o1s[:, hh, :],
                                                                                 scalar=sccv[:, cc, hh:hh + 1], in1=o2s[:, hh, :], op0=ALU.mult, op1=ALU.add),
                                 r=[t_o1s, t_sccv, t_o2s], w=[t_OT])
                    s.dma("sp", self.OFB[dirn, t * 128:(t + 1) * 128, fs].rearrange("(c p) f -> p c f", p=64), OT[:], r=[t_OT], w=[self.t_of])
                    yield

            self.lockstep([lane(i) for i in range(4)])
            s.barrier()

    def phase_gdnC(self, L):
        nc, s = self.nc, self.s
        T = L // 128
        with ExitStack() as es:
            gnorm = self.sb(es, "gnormg", [128, 128], F32)
            t_c = Tok()
            s.dma("sp", gnorm[:], self.w["gdn_norm_g"].partition_broadcast(128), w=[t_c])
            of = [self.sb(es, "cof%d" % i, [128, 1024], F32) for i in range(2)]
            ob = [self.sb(es, "cob%d" % i, [128, 1024], F32) for i in range(2)]
            sg = [self.sb(es, "csg%d" % i, [128, 1024], BF16) for i in range(2)]
            t_in = [Tok(), Tok()]
            rst = [self.sb(es, "crst%d" % i, [128, 24], F32) for i in range(2)]
            t_rst = [Tok(), Tok()]
            junk = self.sb(es, "cjunk", [128, 128], F32)
            t_junk = Tok()
            yb16 = [self.sb(es, "cyb%d" % i, [128, 1024], BF16) for i in range(2)]
            t_yb = [Tok(), Tok()]
            ystage = [self.sb(es, "cystage%d" % i, [128, 8, 512], BF16) for i in range(2)]
            t_ys = [Tok(), Tok()]
            for t in range(T):
                b = t % 2
                blk, tl = t // 4, t % 4
                yb = blk % 2
                cs_ = slice(t * 128, (t + 1) * 128)
                s.dma("sp", of[b][:], self.OFB[0, cs_, :], r=[self.t_of], w=[t_in[b]])
                s.dma("sp", ob[b][:], self.OFB[1, cs_, :], r=[self.t_of], w=[t_in[b]])
                s.dma("sp", sg[b][:], self.SGG[cs_, :], r=[self.t_gd], w=[t_in[b]])
                O = of[b]
                s.op("pool", lambda e: e.tensor_tensor(out=O[:], in0=O[:], in1=ob[b][:], op=ALU.add), r=[t_in[b]], w=[t_in[b]])
                for h in range(8):
                    s.op("act", lambda e, h=h: e.activation(out=junk[:], in_=O[:, h * 128:(h + 1) * 128], func=AF.Square, accum_out=rst[b][:, h:h + 1]),
                         r=[t_in[b]], w=[t_junk, t_rst[b]])
                s.op("dve", lambda e: e.tensor_scalar(out=rst[b][:, 8:16], in0=rst[b][:, 0:8], scalar1=1.0 / 128.0, scalar2=EPS, op0=ALU.mult, op1=ALU.add),
                     r=[t_rst[b]], w=[t_rst[b]])
                s.op("act", lambda e: e.activation(out=rst[b][:, 8:16], in_=rst[b][:, 8:16], func=AF.Ln), r=[t_rst[b]], w=[t_rst[b]])
                s.op("act", lambda e: e.activation(out=rst[b][:, 16:24], in_=rst[b][:, 8:16], func=AF.Exp, scale=-0.5), r=[t_rst[b]], w=[t_rst[b]])
                O3 = O[:].rearrange("p (h d) -> p h d", h=8)
                s.op("dve", lambda e: e.tensor_tensor(out=O3, in0=O3, in1=rst[b][:, 16:24].unsqueeze(2).to_broadcast([128, 8, 128]), op=ALU.mult),
                     r=[t_in[b], t_rst[b]], w=[t_in[b]])
                s.op("pool", lambda e: e.tensor_tensor(out=O3, in0=O3, in1=gnorm[:].unsqueeze(1).to_broadcast([128, 8, 128]), op=ALU.mult),
                     r=[t_in[b], t_c], w=[t_in[b]])
                s.op("dve", lambda e: e.tensor_tensor(out=yb16[b][:], in0=O[:], in1=sg[b][:], op=ALU.mult), r=[t_in[b]], w=[t_yb[b]])
                self.to_ystage(yb16[b], t_yb[b], 8, ystage[yb], t_ys[yb], tl, banks=[6, 7])
                if tl == 3 or t == T - 1:
                    self.store_ystage(ystage[yb], t_ys[yb], 8, 8, blk * 512, (tl + 1) * 128)
            s.barrier()

    def phase_dil(self, si, L, hT, hT_tok):
        nc, s = self.nc, self.s
        T = L // 128
        w_in = self.w["o_w_in"]
        OG, LSE = self.OG, self.LSE
        with ExitStack() as es:
            wts = [self.sb(es, "dw%d" % i, [128, 8, 768], BF16) for i in range(2)]
            t_wts = [Tok(), Tok()]
            cos16 = self.sb(es, "cos16", [128, T, 16], F32)
            sin16 = self.sb(es, "sin16", [128, T, 16], F32)
            t_tab = Tok()
            s.dma("sp", cos16[:], self.c_rope[L][0], w=[t_tab])
            s.dma("sp", sin16[:], self.c_rope[L][1], w=[t_tab])
            qT = self.sb(es, "dqT", [128, 2, L], BF16)
            kT = self.sb(es, "dkT", [128, 2, L], BF16)
            t_qk = [Tok() for _ in range(T)]
            vP = self.sb(es, "dvP", [128, T, 256], BF16)
            t_vP = [Tok() for _ in range(T)]
            raw = [self.sb(es, "draw%d" % i, [128, 4, 128], F32) for i in range(2)]
            t_raw = [Tok(), Tok()]
            rtmps = [self.sb(es, "drtmp%d" % i, [128, 4, 4, 16], F32) for i in range(2)]
            t_rtmps = [Tok(), Tok()]
            qk16 = [self.sb(es, "dqk16%d" % i, [128, 4, 128], BF16) for i in range(2)]
            t_qk16 = [Tok(), Tok()]
            Sm = [self.sb(es, "dSm%d" % i, [128, 2, 384], F32) for i in range(2)]
            t_Sm = [Tok(), Tok()]
            Pe = [self.sb(es, "dPe%d" % i, [128, 2, 384], BF16) for i in range(2)]
            t_Pe = [Tok(), Tok()]
            stt = [self.sb(es, "dst%d" % i, [128, 8], F32) for i in range(2)]
            t_stt = [Tok(), Tok()]
            PT = [self.sb(es, "dPT%d" % i, [128, 8, 128], BF16) for i in range(2)]
            t_PT = [Tok(), Tok()]
            og = [self.sb(es, "dog%d" % i, [128, 256], F32) for i in range(2)]
            t_og = [Tok(), Tok()]
            lse = [self.sb(es, "dlse%d" % i, [128, 4], F32) for i in range(2)]
            t_lse = [Tok(), Tok()]
            scale = 128.0 ** -0.5
            it = 0
            for g, d in enumerate(DIL):
                ls = L // d
                tps = ls // 128
                for hp in range(2):
                    wt, t_w = wts[it % 2], t_wts[it % 2]
                    it += 1
                    for qkv in range(3):
                        c0 = O_CQKV + ((qkv * 3 + g) * 4 + hp * 2) * 128
                        s.dma("pool", wt[:, :, qkv * 256:(qkv + 1) * 256],
                              w_in[:, c0:c0 + 256].rearrange("(k p) c -> p k c", p=128), w=[t_w])
                    def qk_lane(ln):
                        b = ln
                        pb = 2 + ln
                        pt = 6 + ln
                        for t in range(ln, T, 2):
                            for qk in range(2):
                                self.proj_tok(self.bank(pb)[:, qk * 256:(qk + 1) * 256], pb, wt, t_w, qk * 256, 256, hT, [hT_tok[t]],
                                              lambda k: hT[:, k, t * 128:(t + 1) * 128])
                            yield
                            s.op("act", lambda e: e.activation(out=raw[b][:], in_=self.bank(pb).rearrange("p (h d) -> p h d", h=4),
                                                               func=AF.Copy), r=[self.ps_tok[pb]], w=[t_raw[b]])
                            yield
                            self.rope(raw[b][:], t_raw[b], qk16[b][:], t_qk16[b], rtmps[ln], t_rtmps[ln], cos16[:, t, :], sin16[:, t, :], t_tab,
                                      4, 16, 128)
                            yield
                            ptv = self.bank16(pt).rearrange("p (j c) -> p j c", j=8)
                            for c in range(4):
                                s.op("pe", lambda e, c=c: e.transpose(out=ptv[:, c, :], in_=qk16[b][:, c, :], identity=self.ident16[:]),
                                     r=[t_qk16[b], self.t_const], w=[self.ps_tok[pt]])
                            yield
                            s.op("act", lambda e: e.activation(out=qT[:, :, t * 128:(t + 1) * 128], in_=ptv[:, 0:2, :], func=AF.Copy,
                                                               scale=scale), r=[self.ps_tok[pt]], w=[t_qk[t]])
                            s.op("dve", lambda e: e.tensor_copy(out=kT[:, :, t * 128:(t + 1) * 128], in_=ptv[:, 2:4, :]),
                                 r=[self.ps_tok[pt]], w=[t_qk[t]])
                            yield

                    self.lockstep([qk_lane(0), qk_lane(1)])

                    def pslice(j):
                        seg, jj = j // tps, j % tps
                        start = seg + d * jj * 128
                        return start, start + d * 127 + 1

                    def ptoks(j):
                        a, bnd = pslice(j)
                        return [t_qk[tt] for tt in range(a // 128, (bnd - 1) // 128 + 1)], \
                               [hT_tok[tt] for tt in range(a // 128, (bnd - 1) // 128 + 1)]

                    def v_lane(ln):
                        pb = 2 + ln
                        for j in range(ln, T, 2):
                            a, bnd = pslice(j)
                            self.proj_tok(self.bank(pb)[:, 0:256], pb, wt, t_w, 512, 256, hT, ptoks(j)[1],
                                          lambda k: hT[:, k, a:bnd:d])
                            yield
                            s.op("act" if ln == 0 else "dve", (lambda e: e.activation(out=vP[:, j, :], in_=self.bank(pb)[:, 0:256], func=AF.Copy)) if ln == 0
                                 else (lambda e: e.tensor_copy(out=vP[:, j, :], in_=self.bank(pb)[:, 0:256])),
                                 r=[self.ps_tok[pb]], w=[t_vP[j]])
                            yield

                    self.lockstep([v_lane(0), v_lane(1)])
                    def dil_lane(ln):
                        pp = ln
                        sb0 = 4 if ln == 0 else 2
                        ptb = 6 + ln
                        ob = ln
                        for j in range(ln, T, 2):
                            a, bnd = pslice(j)
                            kts = [kt for kt in (j - 1, j, j + 1) if 0 <= kt < T and kt // tps == j // tps]
                            nkt = len(kts)
                            nk = 128 * nkt
                            m0 = (kts[0] - (j - 1)) * 128
                            ka = pslice(kts[0])[0]
                            kb_ = pslice(kts[-1])[1]
                            kdeps = []
                            for kt in kts:
                                kdeps += ptoks(kt)[0]
                            for jh in range(2):
                                s.op("pe", lambda e, jh=jh: e.matmul(self.bank(sb0 + jh)[:, 0:nk], qT[:, jh, a:bnd:d], kT[:, jh, ka:kb_:d],
                                                                     start=True, stop=True),
                                     r=ptoks(j)[0] + kdeps, w=[self.ps_tok[sb0 + jh]])
                            yield
                            yield from self.softmax_gen([sb0, sb0 + 1], 2, nk, self.mask_dil[:, m0:m0 + nk], self.t_const, Sm[pp], t_Sm[pp], Pe[pp], t_Pe[pp],
                                                        stt[pp], t_stt[pp])
                            yield
                            self.transpose_P(Pe[pp], t_Pe[pp], 2, nkt, PT[pp], t_PT[pp], ptb)
                            s.op("dve", lambda e: e.reciprocal(out=stt[pp][:, 4:6], in_=stt[pp][:, 2:4]), r=[t_stt[pp]], w=[t_stt[pp]])
                            s.op("act", lambda e: e.activation(out=lse[pp][:, 0:2], in_=stt[pp][:, 2:4], func=AF.Ln), r=[t_stt[pp]], w=[t_lse[pp]])
                            yield
                            for jh in range(2):
                                for kc in range(nkt):
                                    s.op("pe", lambda e, jh=jh, kc=kc: e.matmul(self.bank(ob)[:, jh * 128:(jh + 1) * 128],
                                                                                PT[pp][:, jh * nkt + kc, :],
                                                                                vP[:, kts[kc], jh * 128:(jh + 1) * 128],
                                                                                start=(kc == 0), stop=(kc == nkt - 1)),
                                         r=[t_PT[pp]] + [t_vP[kt] for kt in kts], w=[self.ps_tok[ob]])
                            s.op("dve", lambda e: e.tensor_tensor(out=lse[pp][:, 2:4], in0=lse[pp][:, 0:2], in1=stt[pp][:, 0:2], op=ALU.subtract),
                                 r=[t_stt[pp], t_lse[pp]], w=[t_lse[pp]])
                            yield
                            for jh in range(2):
                                s.op("dve", lambda e, jh=jh: e.tensor_scalar(out=og[pp][:, jh * 128:(jh + 1) * 128],
                                                                             in0=self.bank(ob)[:, jh * 128:(jh + 1) * 128],
                                                                             scalar1=stt[pp][:, 4 + jh:5 + jh], scalar2=None, op0=ALU.mult),
                                     r=[self.ps_tok[ob], t_stt[pp]], ww=[t_og[pp]])
                            s.dma("sp", OG[g, a:bnd:d, hp * 256:(hp + 1) * 256], og[pp][:], r=[t_og[pp]], w=[self.t_og_dram])
                            s.dma("sp", LSE[g, a:bnd:d, hp * 2:(hp + 1) * 2], lse[pp][:, 2:4], r=[t_lse[pp]], w=[self.t_og_dram])
                            yield

                    self.lockstep([dil_lane(0), dil_lane(1)])
            s.barrier()
        with ExitStack() as es:
            wg = self.sb(es, "dwg", [128, 8, 512], BF16)
            t_wg = Tok()
            self.load_w(wg, t_wg, w_in, O_GC, 512)
            og3 = [self.sb(es, "og3%d" % i, [128, 3, 512], F32) for i in range(2)]
            l3 = [self.sb(es, "l3%d" % i, [128, 3, 4], F32) for i in range(2)]
            t_in = [Tok(), Tok()]
            wk = [self.sb(es, "mwk%d" % i, [128, 8, 4], F32) for i in range(2)]
            t_wk = [Tok(), Tok()]
            yacc = [self.sb(es, "yacc%d" % i, [128, 512], F32) for i in range(2)]
            t_ya = [Tok(), Tok()]
            sg = [self.sb(es, "dsg%d" % i, [128, 512], F32) for i in range(2)]
            t_sg = [Tok(), Tok()]
            yc = [self.sb(es, "dyc%d" % i, [128, 512], BF16) for i in range(2)]
            t_yc = [Tok(), Tok()]
            ystage = [self.sb(es, "dystage%d" % i, [128, 4, 512], BF16) for i in range(2)]
            t_ys = [Tok(), Tok()]
            for t in range(T):
                b = t % 2
                blk, tl = t // 4, t % 4
                yb = blk % 2
                s.dma("sp", og3[b][:], OG[:, t * 128:(t + 1) * 128, :].rearrange("g p c -> p g c"), r=[self.t_og_dram], w=[t_in[b]])
                s.dma("sp", l3[b][:], LSE[:, t * 128:(t + 1) * 128, :].rearrange("g p c -> p g c"), r=[self.t_og_dram], w=[t_in[b]])
                pb = self.pick("proj", [2, 3])
                self.proj_tok(self.bank(pb), pb, wg, t_wg, 0, 512, hT, [hT_tok[t]], lambda k: hT[:, k, t * 128:(t + 1) * 128])
                s.op("act", lambda e: e.activation(out=sg[b][:], in_=self.bank(pb), func=AF.Silu), r=[self.ps_tok[pb]], w=[t_sg[b]])
                W = wk[b]
                s.op("dve", lambda e: e.tensor_tensor(out=W[:, 3, :], in0=l3[b][:, 0, :], in1=l3[b][:, 1, :], op=ALU.max), r=[t_in[b]], w=[t_wk[b]])
                s.op("dve", lambda e: e.tensor_tensor(out=W[:, 3, :], in0=W[:, 3, :], in1=l3[b][:, 2, :], op=ALU.max), r=[t_in[b], t_wk[b]], w=[t_wk[b]])
                s.op("dve", lambda e: e.tensor_tensor(out=W[:, 0:3, :], in0=l3[b][:], in1=W[:, 3, :].unsqueeze(1).to_broadcast([128, 3, 4]),
                                                      op=ALU.subtract), r=[t_in[b], t_wk[b]], w=[t_wk[b]])
                s.op("act", lambda e: e.activation(out=W[:, 0:3, :], in_=W[:, 0:3, :], func=AF.Exp), r=[t_wk[b]], w=[t_wk[b]])
                s.op("dve", lambda e: e.tensor_tensor(out=W[:, 4, :], in0=W[:, 0, :], in1=W[:, 1, :], op=ALU.add), r=[t_wk[b]], w=[t_wk[b]])
                s.op("dve", lambda e: e.tensor_tensor(out=W[:, 4, :], in0=W[:, 4, :], in1=W[:, 2, :], op=ALU.add), r=[t_wk[b]], w=[t_wk[b]])
                s.op("dve", lambda e: e.reciprocal(out=W[:, 5, :], in_=W[:, 4, :]), r=[t_wk[b]], w=[t_wk[b]])
                s.op("dve", lambda e: e.tensor_tensor(out=W[:, 0:3, :], in0=W[:, 0:3, :], in1=W[:, 5, :].unsqueeze(1).to_broadcast([128, 3, 4]),
                                                      op=ALU.mult), r=[t_wk[b]], w=[t_wk[b]])
                for h in range(4):
                    hs = slice(h * 128, (h + 1) * 128)
                    s.op("dve", lambda e, h=h, hs=hs: e.tensor_scalar(out=yacc[b][:, hs], in0=og3[b][:, 0, hs], scalar1=W[:, 0, h:h + 1],
                                                                      scalar2=None, op0=ALU.mult), r=[t_in[b], t_wk[b]], w=[t_ya[b]])
                    for g in (1, 2):
                        s.op("dve", lambda e, h=h, hs=hs, g=g: e.scalar_tensor_tensor(out=yacc[b][:, hs], in0=og3[b][:, g, hs],
                                                                                      scalar=W[:, g, h:h + 1], in1=yacc[b][:, hs],
                                                                                      op0=ALU.mult, op1=ALU.add),
                             r=[t_in[b], t_wk[b], t_ya[b]], w=[t_ya[b]])
                s.op("pool", lambda e: e.tensor_tensor(out=yc[b][:], in0=yacc[b][:], in1=sg[b][:], op=ALU.mult),
                     r=[t_ya[b], t_sg[b]], w=[t_yc[b]])
                self.to_ystage(yc[b], t_yc[b], 4, ystage[yb], t_ys[yb], tl, banks=[6, 7])
                if tl == 3 or t == T - 1:
                    self.store_ystage(ystage[yb], t_ys[yb], 4, 0, blk * 512, (tl + 1) * 128)
            s.barrier()

    def zero_YT(self, L, chunks):
        s = self.s
        with ExitStack() as es:
            z = self.sb(es, "zeros", [128, L], BF16)
            t_z = Tok()
            s.op("dve", lambda e: e.memset(z[:], 0.0), w=[t_z])
            for c in chunks:
                s.dma("sp", self.YT[c * 128:(c + 1) * 128, 0:L], z[:], r=[t_z])
            s.barrier()

    def phase_out(self, r0, L, src, dst, pre, nch):
        nc, s = self.nc, self.s
        T = L // 128
        with ExitStack() as es:
            wo = self.sb(es, "wo", [128, nch, D], BF16)
            t_wo = Tok()
            w_out = self.w[pre + "w_out"]
            for c0 in range(0, nch, 4):
                s.dma("pool", wo[:, c0:c0 + 4, :], w_out[c0 * 128:(c0 + 4) * 128, :].rearrange("(c p) n -> p c n", p=128),
                      w=[t_wo])
            gpost = self.sb(es, "gpost", [128, D], F32)
            t_gp = Tok()
            s.dma("sp", gpost[:], self.w[pre + "post_g"].partition_broadcast(128), w=[t_gp])
            yb = [self.sb(es, "yb%d" % i, [128, nch, 512], BF16) for i in range(2)]
            t_yb = [Tok(), Tok()]
            xr = [self.sb(es, "xr%d" % i, [128, D], F32) for i in range(2)]
            t_xr = [Tok(), Tok()]
            st = [self.sb(es, "ost%d" % i, [128, 4], F32) for i in range(2)]
            t_st = [Tok(), Tok()]
            junk = [self.sb(es, "ojunk%d" % i, [128, D], BF16) for i in range(2)]
            t_junk = [Tok(), Tok()]
            tmp = [self.sb(es, "otmp%d" % i, [128, D], F32) for i in range(2)]
            t_tmp = [Tok(), Tok()]
            nblk = (L + 511) // 512
            for blk in range(nblk):
                tok0 = blk * 512
                ntok = min(512, L - tok0)
                bb = blk % 2
                s.dma("sp", yb[bb][:, :, 0:ntok],
                      self.YT[0:nch * 128, tok0:tok0 + ntok].rearrange("(c p) n -> p c n", p=128), w=[t_yb[bb]])
                def o_lane(ln):
                    b = ln
                    p2 = 2 * ln
                    for tl in range(ln, ntok // 128, 2):
                        t = blk * 4 + tl
                        s.dma("sp", xr[b][:], src[r0 + t * 128:r0 + (t + 1) * 128, :], w=[t_xr[b]])
                        for half in range(2):
                            for c in range(nch):
                                s.op("pe", lambda e, c=c, half=half: e.matmul(self.bank(p2 + half), yb[bb][:, c, tl * 128:(tl + 1) * 128],
                                                                              wo[:, c, half * 512:(half + 1) * 512],
                                                                              start=(c == 0), stop=(c == nch - 1)),
                                     r=[t_yb[bb], t_wo], w=[self.ps_tok[p2 + half]])
                        yield
                        pv = self.ps[:, p2 * 512:(p2 + 2) * 512]
                        pr = [self.ps_tok[p2], self.ps_tok[p2 + 1]]
                        s.op("act", lambda e: e.activation(out=junk[b][:], in_=pv, func=AF.Square, accum_out=st[b][:, 0:1]),
                             r=pr, w=[t_junk[b], t_st[b]])
                        yield
                        s.op("dve", lambda e: e.tensor_scalar(out=st[b][:, 1:2], in0=st[b][:, 0:1], scalar1=1.0 / D, scalar2=EPS,
                                                              op0=ALU.mult, op1=ALU.add), r=[t_st[b]], w=[t_st[b]])
                        yield
                        s.op("act", lambda e: e.activation(out=st[b][:, 2:3], in_=st[b][:, 1:2], func=AF.Sqrt), r=[t_st[b]], w=[t_st[b]])
                        yield
                        s.op("dve", lambda e: e.reciprocal(out=st[b][:, 3:4], in_=st[b][:, 2:3]), r=[t_st[b]], w=[t_st[b]])
                        yield
                        s.op("dve", lambda e: e.scalar_tensor_tensor(out=tmp[b][:], in0=pv, scalar=st[b][:, 3:4], in1=gpost[:],
                                                                     op0=ALU.mult, op1=ALU.mult),
                             r=pr + [t_st[b], t_gp], w=[t_tmp[b]])
                        yield
                        s.op("pool", lambda e: e.tensor_tensor(out=tmp[b][:], in0=tmp[b][:], in1=xr[b][:], op=ALU.add),
                             r=[t_xr[b], t_tmp[b]], w=[t_tmp[b]])
                        yield
                        s.dma("sp", dst[r0 + t * 128:r0 + (t + 1) * 128, :], tmp[b][:], r=[t_tmp[b]])
                        yield

                self.lockstep([o_lane(0), o_lane(1)])
            s.barrier()


def host_consts(seq_lens):
    c = {}
    c["c_ident"] = np.eye(128, dtype=np.float32)
    c["c_mask_dil"] = band_mask(64, 192)
    c["c_mask_swa"] = band_mask(0, 256)
    for L in sorted(set(seq_lens)):
        c16, s16 = rope_tables(L, 16)
        c8, s8 = rope_tables(L, 8)
        c["c_cos16_%d" % L] = tok_layout(c16)
        c["c_sin16_%d" % L] = tok_layout(s16)
        c["c_cos8_%d" % L] = tok_layout(c8)
        c["c_sin8_%d" % L] = tok_layout(s8)
    j = np.arange(128)[:, None]
    i = np.arange(128)[None, :]
    same = (j // 64) == (i // 64)
    fw = []
    fw.append(np.where(same & (i >= j), 0.0, NEG))
    fw.append(np.where(same & (i > j), 1.0, 0.0))
    for lv in range(6):
        bsz = 1 << lv
        fw.append(np.where(((j // (2 * bsz)) == (i // (2 * bsz))) & ((j % (2 * bsz)) < bsz) & ((i % (2 * bsz)) >= bsz), 1.0, 0.0))
    bw = [m.T for m in fw]
    tri_f = np.where(same & (j <= i), 1.0, 0.0)
    gm = np.stack(fw + bw + [tri_f, tri_f.T], 0).astype(np.float32)
    c["c_gmask"] = np.ascontiguousarray(gm.transpose(1, 0, 2))
    c["c_delta"] = np.abs(np.linspace(math.log(1e-2) / 1.5, math.log(1e-2) / 0.3, 1024, dtype=np.float32)).reshape(1, 1024)
    for L in sorted(set(seq_lens)):
        T = L // 128
        N = 2 * L
        i = np.arange(L, dtype=np.int64)
        prod = ((2 * i[:, None] + 1) * (2 * i[None, :] + 1)) % (4 * N)
        ang = prod.astype(np.float64) * (2.0 * np.pi / (4 * N))
        for nm, fn in (("c_gc_%d" % L, np.cos), ("c_gs_%d" % L, np.sin)):
            tab = fn(ang).astype(np.float32).astype(ml_dtypes.bfloat16)
            c[nm] = np.ascontiguousarray(tab.reshape(T, 128, T, 128).transpose(2, 1, 0, 3))
        t = np.linspace(0.0, 1.0, L, dtype=np.float32)[:, None]
        wv = (2.0 * math.pi / L) * np.arange(L, dtype=np.float32)[:, None]
        f = np.linspace(1e-4, 15.0, 16, dtype=np.float32)[None, :]
        emb = np.concatenate([t, np.cos(f * wv), -np.sin(f * wv)], axis=-1).astype(np.float32)
        c["c_embT_%d" % L] = np.ascontiguousarray(emb.T)
        c["c_tcol_%d" % L] = np.ascontiguousarray((-t[:, 0]).reshape(T, 128).T)
        k = np.arange(L, dtype=np.float64)
        psi = np.pi * (k + 0.5) / N
        ps = np.stack([(2.0 / N) * np.cos(psi), (2.0 / N) * np.sin(psi)], 0).astype(np.float32)
        c["c_psi_%d" % L] = np.ascontiguousarray(ps.reshape(2, T, 128).transpose(2, 0, 1))
    return c


def col_layout(v):
    return np.ascontiguousarray(np.asarray(v, np.float32).reshape(8, 128).T)


def shared_inputs(inp, seq_lens):
    m = {}
    for nm in ("e_w_in", "e_w_out", "e_w_mem_kv", "o_w_in", "o_w_out", "o_w_mem_kv"):
        m[nm] = np.ascontiguousarray(np.asarray(inp[nm], np.float32)[0])
    for nm in ("e_pre_g", "e_mem_g", "o_pre_g", "o_mem_g"):
        m[nm] = col_layout(np.asarray(inp[nm])[0])
    for nm in ("e_post_g", "o_post_g"):
        m[nm] = np.ascontiguousarray(np.asarray(inp[nm], np.float32).reshape(1, D))
    m["swa_sink"] = np.ascontiguousarray(np.asarray(inp["swa_sink"], np.float32).reshape(1, 16))
    f32 = lambda a: np.asarray(a, np.float32)
    m["hy_filt_w1"] = np.ascontiguousarray(f32(inp["hy_filt_w1"])[0])
    m["hy_filt_w2"] = np.ascontiguousarray(f32(inp["hy_filt_w2"])[0])
    m["hy_filt_w3"] = np.ascontiguousarray(f32(inp["hy_filt_w3"])[0])
    m["hy_fvec"] = np.ascontiguousarray(np.stack([f32(inp["hy_filt_b1"])[0], f32(inp["hy_filt_b2"])[0], f32(inp["hy_freq"])[0]], 1))
    m["hy_conv_w"] = np.ascontiguousarray(f32(inp["hy_conv_w"])[0].reshape(3, 24, 128).transpose(2, 1, 0))
    m["hy_conv_b"] = np.ascontiguousarray(f32(inp["hy_conv_b"])[0].reshape(24, 128).T)
    m["gdn_conv_w"] = np.ascontiguousarray(f32(inp["gdn_conv_w"])[0].reshape(5, 24, 128).transpose(2, 1, 0))
    m["gdn_A_log"] = np.ascontiguousarray(f32(inp["gdn_A_log"]).reshape(1, 16))
    m["gdn_dt_bias"] = np.ascontiguousarray(f32(inp["gdn_dt_bias"]).reshape(1, 16))
    m["gdn_norm_g"] = np.ascontiguousarray(f32(inp["gdn_norm_g"]).reshape(1, 128))
    m["hy_skip"] = np.ascontiguousarray(f32(inp["hy_skip"])[0].reshape(8, 128).T)
    m.update(host_consts(seq_lens))
    return m


_CACHE = {}


def kernel(**inp):
    xp = np.asarray(inp["x_prompt"], np.float32)
    xs = np.asarray(inp["x_sample"], np.float32)
    mp = np.asarray(inp["mem_prompt"], np.float32)
    ms = np.asarray(inp["mem_sample"], np.float32)
    n = 8
    seq_lens = [xs.shape[1], xs.shape[1], xp.shape[1]]
    shared = shared_inputs(inp, seq_lens)
    in_maps = []
    for c in range(n):
        m = dict(shared)
        m["x"] = np.ascontiguousarray(np.concatenate([xs[2 * c], xs[2 * c + 1], xp[c]], axis=0))
        m["mem"] = np.ascontiguousarray(np.concatenate([ms[2 * c], ms[2 * c + 1], mp[c]], axis=0))
        in_maps.append(m)
    nc = KB(seq_lens).build()
    res = run_bass_kernel_spmd(nc, in_maps, core_ids=list(range(n)))
    Ls = xs.shape[1]
    y_p = np.stack([res.results[c]["y"][2 * Ls:] for c in range(n)], axis=0)
    y_s = np.stack([res.results[c]["y"][j * Ls:(j + 1) * Ls] for c in range(n) for j in range(2)], axis=0)
    return (y_p.astype(np.float32), y_s.astype(np.float32))
```
